# Optimizing a Trainium2 kernel written in Bass

```python
import math
import jax, jax.numpy as jnp
from jax import lax
import numpy as np

D_MODEL = 1024
BATCH = 16
SEQ = 4096
DEPTH = 4

GDN_HEADS = 4
GDN_DK = 128
GDN_DV = 128
GDN_CONV = 4
GDN_CHUNK = 64
DIFF_HEADS = 4
DIFF_DH = 64
Q_BLOCK = 128
REL_BUCKETS = 32
REL_MAX_DIST = 128
D_FF = 2816
FFN_CONV = 3
EPS = 1e-6

GDN_QK_W = GDN_HEADS * GDN_DK
GDN_V_W = GDN_HEADS * GDN_DV
DIFF_QK_W = DIFF_HEADS * 2 * DIFF_DH
DIFF_V_W = DIFF_HEADS * 2 * DIFF_DH
MIX_W = GDN_V_W + DIFF_V_W
IN_SPLITS = (GDN_QK_W, GDN_QK_W, GDN_V_W, GDN_V_W, GDN_HEADS, GDN_HEADS, DIFF_QK_W, DIFF_QK_W, DIFF_V_W)
IN_W = sum(IN_SPLITS)
IN_OFFSETS = tuple(int(v) for v in np.cumsum(IN_SPLITS)[:-1])

kernel_name = "hymba_gdn_diffattn_convffn_trunk"

f32 = jnp.float32


def rmsnorm(x, g):
    xf = x.astype(f32)
    y = xf * lax.rsqrt(jnp.mean(xf * xf, axis=-1, keepdims=True) + EPS)
    return (y * g.astype(f32)).astype(x.dtype)


def l2norm(x):
    return x * lax.rsqrt(jnp.sum(x * x, axis=-1, keepdims=True) + EPS)


def causal_dwconv(x, w):
    K, C = w.shape
    return lax.conv_general_dilated(x, w[:, None, :].astype(x.dtype), window_strides=(1,), padding=[(K - 1, 0)],
                                    dimension_numbers=('NWC', 'WIO', 'NWC'), feature_group_count=C)


def t5_causal_bucket(n):
    max_exact = REL_BUCKETS // 2
    nf = jnp.maximum(n, 1).astype(f32)
    large = max_exact + (jnp.log(nf / max_exact) / math.log(REL_MAX_DIST / max_exact)
                         * (REL_BUCKETS - max_exact)).astype(jnp.int32)
    large = jnp.minimum(large, REL_BUCKETS - 1)
    return jnp.where(n < max_exact, n, large)


def chunk_gated_delta_rule(q, k, v, g, beta):
    B, H, T, dk = q.shape
    dv = v.shape[-1]
    C = GDN_CHUNK
    N = T // C
    q = (q * dk ** -0.5).reshape(B, H, N, C, dk)
    k = k.reshape(B, H, N, C, dk)
    v = v.reshape(B, H, N, C, dv)
    beta = beta.reshape(B, H, N, C)[..., None]
    g = jnp.cumsum(g.reshape(B, H, N, C), axis=-1)
    incl = jnp.tril(jnp.ones((C, C), bool))
    strict = jnp.tril(jnp.ones((C, C), bool), -1)
    decay = jnp.exp(jnp.where(incl, g[..., :, None] - g[..., None, :], -jnp.inf))
    k_beta = k * beta
    L = jnp.where(strict, jnp.einsum('bhncd,bhnsd->bhncs', k_beta, k) * decay, 0.0)
    eye = jnp.eye(C, dtype=q.dtype)
    T_inv = lax.linalg.triangular_solve(eye + L, jnp.broadcast_to(eye, L.shape), left_side=True,
                                        lower=True, unit_diagonal=True)
    u = jnp.einsum('bhncs,bhnse->bhnce', T_inv, v * beta)
    w = jnp.einsum('bhncs,bhnsd->bhncd', T_inv, k_beta * jnp.exp(g)[..., None])
    a_intra = jnp.where(incl, jnp.einsum('bhncd,bhnsd->bhncs', q, k) * decay, 0.0)

    def step(S, xs):
        q_c, k_c, u_c, w_c, g_c, a_c = xs
        v_new = u_c - jnp.einsum('bhcd,bhde->bhce', w_c, S)
        o = (jnp.einsum('bhcd,bhde->bhce', q_c * jnp.exp(g_c)[..., None], S)
             + jnp.einsum('bhcs,bhse->bhce', a_c, v_new))
        g_last = g_c[..., -1:]
        S = (S * jnp.exp(g_last)[..., None]
             + jnp.einsum('bhcd,bhce->bhde', k_c * jnp.exp(g_last - g_c)[..., None], v_new))
        return S, o

    xs = tuple(jnp.moveaxis(t, 2, 0) for t in (q, k, u, w, g, a_intra))
    S0 = jnp.zeros((B, H, dk, dv), q.dtype)
    _, o = lax.scan(step, S0, xs)
    return jnp.moveaxis(o, 0, 2).reshape(B, H, T, dv)


def gdn_mixer(q, k, v, gate, b_raw, a_raw, conv_w, a_log, dt_bias, out_norm):
    B, T, _ = q.shape
    qkv = jax.nn.silu(causal_dwconv(jnp.concatenate([q, k, v], axis=-1), conv_w))
    q, k, v = jnp.split(qkv, [GDN_QK_W, 2 * GDN_QK_W], axis=-1)
    heads = lambda t, d: t.reshape(B, T, GDN_HEADS, d).transpose(0, 2, 1, 3).astype(f32)
    q = l2norm(heads(q, GDN_DK))
    k = l2norm(heads(k, GDN_DK))
    v = heads(v, GDN_DV)
    beta = jax.nn.sigmoid(b_raw.astype(f32)).transpose(0, 2, 1)
    g = (-jnp.exp(a_log.astype(f32)) * jax.nn.softplus(a_raw.astype(f32) + dt_bias.astype(f32))).transpose(0, 2, 1)
    o = chunk_gated_delta_rule(q, k, v, g, beta).transpose(0, 2, 1, 3)
    o = rmsnorm(o, out_norm) * jax.nn.silu(gate.reshape(B, T, GDN_HEADS, GDN_DV).astype(f32))
    return o.reshape(B, T, GDN_V_W)


def diff_attn_mixer(q, k, v, q_norm, k_norm, lq1, lk1, lq2, lk2, subln, bias_dist, lambda_init):
    B, T, _ = q.shape
    q = rmsnorm(q.reshape(B, T, DIFF_HEADS, 2, DIFF_DH).astype(f32), q_norm).transpose(0, 2, 3, 1, 4) * DIFF_DH ** -0.5
    k = rmsnorm(k.reshape(B, T, DIFF_HEADS, 2, DIFF_DH).astype(f32), k_norm).transpose(0, 2, 3, 1, 4)
    v = v.reshape(B, T, DIFF_HEADS, 2 * DIFF_DH).astype(f32).transpose(0, 2, 1, 3)
    lam = (jnp.exp(jnp.sum(lq1.astype(f32) * lk1.astype(f32))) - jnp.exp(jnp.sum(lq2.astype(f32) * lk2.astype(f32)))
           + lambda_init)
    kpos = jnp.arange(T)

    def block(start):
        qb = lax.dynamic_slice_in_dim(q, start, Q_BLOCK, axis=3)
        dist = (start + jnp.arange(Q_BLOCK))[:, None] - kpos[None, :]
        bias = bias_dist[:, jnp.clip(dist, 0, T - 1)]
        s = jnp.einsum('bhcqd,bhckd->bhcqk', qb, k) + bias[None, :, None]
        s = jnp.where(dist >= 0, s, -jnp.inf)
        p = jax.nn.softmax(s, axis=-1)
        a = p[:, :, 0] - lam * p[:, :, 1]
        return jnp.einsum('bhqk,bhke->bhqe', a, v)

    o = lax.map(block, jnp.arange(T // Q_BLOCK) * Q_BLOCK)
    o = o.transpose(1, 0, 3, 2, 4).reshape(B, T, DIFF_HEADS, 2 * DIFF_DH)
    o = rmsnorm(o, subln) * (1.0 - lambda_init)
    return o.reshape(B, T, DIFF_V_W)


def conv_ffn(h, w_up, conv_w, conv_b, w_down):
    u = causal_dwconv(h @ w_up, conv_w) + conv_b
    a, g = jnp.split(u, 2, axis=-1)
    return (jax.nn.silu(g) * a) @ w_down


def setup_inputs(seed: int = 0) -> dict:
    key = jax.random.key(seed)
    ks = jax.random.split(key, 26)
    nrm = lambda k, shape, s: jax.random.normal(k, shape, f32) * s
    gain = lambda k, shape: 1.0 + 0.1 * jax.random.normal(k, shape, f32)
    dt = jnp.exp(jax.random.uniform(ks[9], (DEPTH, GDN_HEADS), f32, math.log(1e-3), math.log(1e-1)))
    return {
        "x": nrm(ks[0], (BATCH, SEQ, D_MODEL), 1.0),
        "c": nrm(ks[1], (BATCH, D_MODEL), 1.0),
        "w_ada": nrm(ks[2], (DEPTH, D_MODEL, 6 * D_MODEL), 0.5 * D_MODEL ** -0.5),
        "b_ada": nrm(ks[3], (DEPTH, 6 * D_MODEL), 0.02),
        "norm_mix": gain(ks[4], (DEPTH, D_MODEL)),
        "norm_ffn": gain(ks[5], (DEPTH, D_MODEL)),
        "w_in": nrm(ks[6], (DEPTH, D_MODEL, IN_W), D_MODEL ** -0.5),
        "gdn_conv_w": nrm(ks[7], (DEPTH, GDN_CONV, 2 * GDN_QK_W + GDN_V_W), GDN_CONV ** -0.5),
        "gdn_a_log": jnp.log(jax.random.uniform(ks[8], (DEPTH, GDN_HEADS), f32, 1.0, 16.0)),
        "gdn_dt_bias": dt + jnp.log(-jnp.expm1(-dt)),
        "gdn_out_norm": gain(ks[10], (DEPTH, GDN_DV)),
        "diff_q_norm": gain(ks[11], (DEPTH, DIFF_DH)),
        "diff_k_norm": gain(ks[12], (DEPTH, DIFF_DH)),
        "diff_lambda_q1": nrm(ks[13], (DEPTH, DIFF_DH), 0.1),
        "diff_lambda_k1": nrm(ks[14], (DEPTH, DIFF_DH), 0.1),
        "diff_lambda_q2": nrm(ks[15], (DEPTH, DIFF_DH), 0.1),
        "diff_lambda_k2": nrm(ks[16], (DEPTH, DIFF_DH), 0.1),
        "diff_subln": gain(ks[17], (DEPTH, 2 * DIFF_DH)),
        "rel_bias": nrm(ks[18], (REL_BUCKETS, DIFF_HEADS), 0.5),
        "w_out": nrm(ks[19], (DEPTH, MIX_W, D_MODEL), MIX_W ** -0.5),
        "ffn_up": nrm(ks[20], (DEPTH, D_MODEL, 2 * D_FF), D_MODEL ** -0.5),
        "ffn_conv_w": nrm(ks[21], (DEPTH, FFN_CONV, 2 * D_FF), FFN_CONV ** -0.5),
        "ffn_conv_b": nrm(ks[22], (DEPTH, 2 * D_FF), 0.02),
        "ffn_down": nrm(ks[23], (DEPTH, D_FF, D_MODEL), D_FF ** -0.5),
    }


def reference(x, c, w_ada, b_ada, norm_mix, norm_ffn, w_in, gdn_conv_w, gdn_a_log, gdn_dt_bias, gdn_out_norm,
              diff_q_norm, diff_k_norm, diff_lambda_q1, diff_lambda_k1, diff_lambda_q2, diff_lambda_k2, diff_subln,
              rel_bias, w_out, ffn_up, ffn_conv_w, ffn_conv_b, ffn_down):
    T = x.shape[1]
    bias_dist = rel_bias.astype(f32)[t5_causal_bucket(jnp.arange(T, dtype=jnp.int32))].T
    c_act = jax.nn.silu(c)
    for l in range(DEPTH):
        mod = (c_act @ w_ada[l] + b_ada[l])[:, None, :]
        sh_a, sc_a, gt_a, sh_m, sc_m, gt_m = jnp.split(mod, 6, axis=-1)
        h = rmsnorm(x, norm_mix[l]) * (1.0 + sc_a) + sh_a
        gq, gk, gv, ggate, gb, ga, dq, dk, dv = jnp.split(h @ w_in[l], IN_OFFSETS, axis=-1)
        y_gdn = gdn_mixer(gq, gk, gv, ggate, gb, ga, gdn_conv_w[l], gdn_a_log[l], gdn_dt_bias[l], gdn_out_norm[l])
        lambda_init = 0.8 - 0.6 * math.exp(-0.3 * l)
        y_diff = diff_attn_mixer(dq, dk, dv, diff_q_norm[l], diff_k_norm[l], diff_lambda_q1[l], diff_lambda_k1[l],
                                 diff_lambda_q2[l], diff_lambda_k2[l], diff_subln[l], bias_dist, lambda_init)
        y = jnp.concatenate([y_gdn, y_diff], axis=-1).astype(x.dtype) @ w_out[l]
        x = x + gt_a * y
        h = rmsnorm(x, norm_ffn[l]) * (1.0 + sc_m) + sh_m
        x = x + gt_m * conv_ffn(h, ffn_up[l], ffn_conv_w[l], ffn_conv_b[l], ffn_down[l])
    return x
```

```python
import math
from contextlib import ExitStack
import numpy as np
import concourse.bass as bass
import concourse.mybir as mybir
from concourse.bass_utils import run_bass_kernel_spmd

F32 = mybir.dt.float32
BF16 = mybir.dt.bfloat16
AF = mybir.ActivationFunctionType
ALU = mybir.AluOpType

NCORES = 8
DEPTH = 4
T = 4096
DM = 1024
NEG = -30000.0
EPS = 1e-6
DFF = 2816


class Buf:
    def __init__(self, name, t=None):
        self.name = name
        self.t = t
        self.last_w = None
        self.readers = []
        self.dsem = None
        self.dcount = 0

    def __getitem__(self, k):
        return self.t[k]


class Eng:
    def __init__(self, name, sem):
        self.name = name
        self.sem = sem
        self.count = 0
        self.waited = {}
        self.prog = []


class Sched:
    SEM_WRAP = 30000

    def __init__(self, nc, es):
        self.nc = nc
        self.es = es
        self.engs = {}
        for name in ("sync", "scalar", "vector", "gpsimd", "tensor"):
            self.engs[name] = Eng(name, es.enter_context(nc.semaphore("s_" + name)))
        self.n_instr = 0
        self.dbufs = []
        self.sem_pool = []
        self.rr = 0

    def _waits(self, e, reads, writes):
        toks = []
        own = e.sem
        for b in reads:
            if b.last_w is not None:
                toks.append(b.last_w)
        for b in writes:
            if b.last_w is not None and b.last_w[0] is not own:
                toks.append(b.last_w)
            for r in b.readers:
                if r[0] is not own:
                    toks.append(r)
        need = {}
        for sem, val in toks:
            if e.name == "tensor" and sem is own:
                continue
            k = id(sem)
            if e.waited.get(k, 0) >= val:
                continue
            if k not in need or need[k][1] < val:
                need[k] = (sem, val)
        out = []
        for k, (sem, val) in need.items():
            e.waited[k] = val
            out.append((sem, val))
        return out

    def _record(self, e, fn, waits, sem, inc, reads, writes, tok):
        def run(eng, fn=fn, waits=waits, sem=sem, inc=inc):
            for s, v in waits:
                eng.wait_ge(s, v)
            fn(eng).then_inc(sem, inc)
        e.prog.append(run)
        for b in reads:
            b.readers.append(tok)
            if len(b.readers) > 64:
                b.readers = b.readers[-48:]
        for b in writes:
            b.last_w = tok
            b.readers = []
        self.n_instr += 1

    def op(self, engname, fn, reads=(), writes=()):
        e = self.engs[engname]
        if e.count >= self.SEM_WRAP:
            e.sem = self.es.enter_context(self.nc.semaphore("s_%s_%d" % (engname, self.n_instr)))
            e.count = 0
        waits = self._waits(e, reads, writes)
        e.count += 1
        tok = (e.sem, e.count)
        self._record(e, fn, waits, e.sem, 1, reads, writes, tok)
        return tok

    def dma(self, engname, fn, sembuf, reads=(), writes=()):
        e = self.engs[engname]
        waits = self._waits(e, reads, writes)
        if sembuf.dsem is None:
            if self.sem_pool:
                sembuf.dsem, sembuf.dcount = self.sem_pool.pop()
            else:
                sembuf.dsem = self.es.enter_context(self.nc.semaphore("d%d_%s" % (self.n_instr, sembuf.name)))
        if sembuf not in self.dbufs:
            self.dbufs.append(sembuf)
        sembuf.dcount += 16
        tok = (sembuf.dsem, sembuf.dcount)
        self._record(e, fn, waits, sembuf.dsem, 16, reads, writes, tok)
        return tok

    def barrier(self):
        toks = [(e.sem, e.count) for e in self.engs.values() if e.count > 0]
        toks += [(b.dsem, b.dcount) for b in self.dbufs]
        for name in self.engs:
            self.wait_tokens(name, toks)
        for b in self.dbufs:
            self.sem_pool.append((b.dsem, b.dcount))
            b.dsem = None
        self.dbufs = []

    def wait_tokens(self, engname, toks):
        e = self.engs[engname]
        for sem, val in toks:
            k = id(sem)
            if e.waited.get(k, 0) >= val:
                continue
            e.waited[k] = val
            e.prog.append(lambda eng, s=sem, v=val: eng.wait_ge(s, v))

    def emit(self):
        with self.nc.Block() as block:
            for name in ("sync", "scalar", "vector", "gpsimd", "tensor"):
                def body(eng, name=name):
                    for f in self.engs[name].prog:
                        f(eng)
                getattr(block, name)(body)
        for e in self.engs.values():
            e.prog = []

    def alt(self):
        self.rr ^= 1
        return "vector" if self.rr else "gpsimd"


PHASE_LOG = []


class Phase:
    def __init__(self, S, name):
        self.S = S
        self.nc = S.nc
        self.name = name
        self.es = ExitStack()
        self.k = 0

    def __enter__(self):
        self.es.__enter__()
        return self

    def sb(self, name, shape, dt):
        self.k += 1
        nm = "%s_%s_%d" % (self.name, name, self.k)
        return Buf(nm, self.es.enter_context(self.nc.sbuf_tensor(nm, list(shape), dt)))

    def ps(self, name, shape, dt):
        self.k += 1
        nm = "%s_%s_%d" % (self.name, name, self.k)
        return Buf(nm, self.es.enter_context(self.nc.psum_tensor(nm, list(shape), dt)))

    def __exit__(self, *a):
        if a[0] is None:
            PHASE_LOG.append((self.name, {k: (v.count, id(v.sem)) for k, v in self.S.engs.items()}, self.S.n_instr))
            self.S.barrier()
            self.S.emit()
        return self.es.__exit__(*a)


C_IDENT, C_TRI, C_ONES, C_MASKNEG, C_STRICT, C_BLK64, C_DELTA, C_EPS, C_ONE = 0, 128, 256, 384, 512, 640, 768, 769, 770
C_BM16, C_OFF32, C_OFF64, C_OFF128 = 771, 899, 1027, 1155
NCONST = 1283

PP_CONVW = 0
PP_DTB = 48
PP_ALOG = 52
PP_ONORM = 56
PP_QN = 184
PP_KN = 185
PP_SUBLN = 186
PP_FCW = 187
PP_FCB = 319
PP_NMIX = 363
PP_NFFN = 371
PP_BADA = 379
NPP = 380


def _consts():
    c = np.zeros((128, NCONST), np.float32)
    i = np.arange(128)
    c[:, C_IDENT:C_IDENT + 128] = np.eye(128)
    c[:, C_TRI:C_TRI + 128] = (i[:, None] <= i[None, :])
    c[:, C_ONES:C_ONES + 128] = 1.0
    c[:, C_MASKNEG:C_MASKNEG + 128] = np.where(i[:, None] <= i[None, :], 0.0, NEG)
    c[:, C_STRICT:C_STRICT + 128] = (i[:, None] < i[None, :])
    c[:, C_BLK64:C_BLK64 + 128] = ((i[:, None] // 64) == (i[None, :] // 64))
    c[0, C_DELTA] = 1.0
    bm = lambda m: ((i[:, None] // m) == (i[None, :] // m)).astype(np.float32)
    c[:, C_BM16:C_BM16 + 128] = bm(16)
    c[:, C_OFF32:C_OFF32 + 128] = bm(32) - bm(16)
    c[:, C_OFF64:C_OFF64 + 128] = bm(64) - bm(32)
    c[:, C_OFF128:C_OFF128 + 128] = 1.0 - bm(64)
    c[:, C_EPS] = EPS
    c[:, C_ONE] = 1.0
    return c


def _t5_bucket(n):
    n = np.asarray(n)
    nf = np.maximum(n, 1).astype(np.float32)
    large = 16 + (np.log(nf / np.float32(16)) / np.float32(math.log(128 / 16)) * np.float32(16)).astype(np.int32)
    large = np.minimum(large, 31)
    return np.where(n < 16, n, large)


def _pack_pp(inp):
    pp = np.zeros((DEPTH, 128, NPP), np.float32)
    p = np.arange(128)
    for l in range(DEPTH):
        cw = inp["gdn_conv_w"][l]
        pp[l, :, PP_CONVW:PP_CONVW + 48] = cw.reshape(4, 12, 128).transpose(2, 1, 0).reshape(128, 48)
        pp[l, :, PP_DTB:PP_DTB + 4] = inp["gdn_dt_bias"][l][None, :]
        pp[l, :, PP_ALOG:PP_ALOG + 4] = inp["gdn_a_log"][l][None, :]
        pp[l, :, PP_ONORM:PP_ONORM + 128] = inp["gdn_out_norm"][l][None, :]
        pp[l, :, PP_QN] = inp["diff_q_norm"][l][p % 64]
        pp[l, :, PP_KN] = inp["diff_k_norm"][l][p % 64]
        pp[l, :, PP_SUBLN] = inp["diff_subln"][l]
        fw = inp["ffn_conv_w"][l]
        pp[l, :, PP_FCW:PP_FCW + 132] = fw.reshape(3, 44, 128).transpose(2, 1, 0).reshape(128, 132)
        pp[l, :, PP_FCB:PP_FCB + 44] = inp["ffn_conv_b"][l].reshape(44, 128).T
        pp[l, :, PP_NMIX:PP_NMIX + 8] = inp["norm_mix"][l].reshape(8, 128).T
        pp[l, :, PP_NFFN:PP_NFFN + 8] = inp["norm_ffn"][l].reshape(8, 128).T
    return pp


def build(nlayers=DEPTH, upto="G", debug=False):
    nc = bass.Bass("TRN2", target_bir_lowering=False)
    dk = "ExternalOutput" if debug else "Internal"

    def din(name, shape, dt=F32):
        return nc.dram_tensor(name, list(shape), dt, kind="ExternalInput").ap()

    def dscr(name, shape, dt, dbg=True):
        return nc.dram_tensor(name, list(shape), dt, kind=(dk if dbg else "Internal")).ap()

    x_in = din("x", [2, T, DM])
    cT_in = din("cT", [128, 8, 2])
    consts_in = din("consts", [128, NCONST])
    pp_in = din("pp", [DEPTH, 128, NPP])
    lam_in = din("lamv", [DEPTH, 4, 64])
    tb_in = din("tb", [4, 128, 1024])
    b31_in = din("b31", [128, 4])
    w_ada = din("w_ada", [DEPTH, DM, 6 * DM])
    b_ada = din("b_ada", [DEPTH, 6 * DM])
    w_in = din("w_in", [DEPTH, DM, 3592])
    w_out = din("w_out", [DEPTH, DM, DM])
    ffn_up = din("ffn_up", [DEPTH, DM, 2 * DFF])
    ffn_down = din("ffn_down", [DEPTH, DFF, DM])
    out = nc.dram_tensor("out", [2, T, DM], F32, kind="ExternalOutput").ap()

    MOD = dscr("MOD", [DEPTH, 2, 6 * DM], F32)
    XA = dscr("XA", [2, T, DM], F32)
    XB = dscr("XB", [2, T, DM], F32, dbg=False)
    NCH = T // 128
    KT = dscr("KT", [2, NCH, 128, 4, 128], BF16)
    QGT = dscr("QGT", [2, NCH, 128, 4, 128], BF16)
    QT = dscr("QT", [2, NCH, 128, 4, 128], BF16)
    KBT = dscr("KBT", [2, NCH, 128, 4, 128], BF16)
    KBG = dscr("KBG", [2, NCH, 128, 4, 128], BF16)
    KDEC = dscr("KDEC", [2, NCH, 128, 4, 128], BF16)
    VB = dscr("VB", [2, NCH, 128, 4, 128], BF16)
    DEC = dscr("DEC", [2, NCH, 128, 4, 128], F32)
    EGL = dscr("EGL", [2, NCH, 128, 4], F32)
    GATE = dscr("GATE", [2, NCH, 128, 512], BF16)
    DQT = dscr("DQT", [2, 4, 128, T], BF16)
    DKT = dscr("DKT", [2, 4, 128, T], BF16)
    DV = dscr("DV", [2, NCH, 128, 512], BF16)
    OT = dscr("OT", [2, 8, 128, T], BF16)

    DBG = {}
    if debug:
        for nm in ("Y", "X", "Q", "U", "T1", "X1", "Y1b", "S0", "U1", "PW1", "O1", "VN0"):
            DBG[nm] = dscr("DBG_" + nm, [128, 4, 128], F32)

    with ExitStack() as es0:
        S = Sched(nc, es0)
        G = Phase(S, "glob")
        G.__enter__()
        cst = G.sb("cst", [128, NCONST], F32)
        identB = G.sb("identB", [128, 128], BF16)
        onesB = G.sb("onesB", [128, 128], BF16)
        S.dma("sync", lambda e: e.dma_start(out=cst[:], in_=consts_in[:, :]), cst, writes=[cst])
        S.op("vector", lambda e: e.tensor_copy(identB[:], cst[:, C_IDENT:C_IDENT + 128]), reads=[cst], writes=[identB])
        S.op("vector", lambda e: e.tensor_copy(onesB[:], cst[:, C_ONES:C_ONES + 128]), reads=[cst], writes=[onesB])
        identF = lambda: cst[:, C_IDENT:C_IDENT + 128]
        TRI = lambda: cst[:, C_TRI:C_TRI + 128]
        ONESF = lambda: cst[:, C_ONES:C_ONES + 128]
        epsc = lambda: cst[:, C_EPS:C_EPS + 1]
        onec = lambda: cst[:, C_ONE:C_ONE + 1]

        with Phase(S, "A") as P:
            cT = P.sb("cT", [128, 8, 2], F32)
            S.dma("sync", lambda e: e.dma_start(out=cT[:], in_=cT_in[:, :, :]), cT, writes=[cT])
            S.op("scalar", lambda e: e.activation(cT[:], cT[:], AF.Silu), reads=[cT], writes=[cT])
            wts = [P.sb("wa%d" % i, [128, 8, 512], F32) for i in range(3)]
            pM = [P.ps("pM%d" % i, [2, 512], F32) for i in range(2)]
            k = 0
            bada = P.sb("bada", [2, 6 * DM], F32)
            modt = P.sb("modt", [2, 6 * DM], F32)
            for l in range(nlayers):
                S.dma("sync", lambda e, l=l, bada=bada: e.dma_start(out=bada[:], in_=b_ada[l, :].partition_broadcast(2)), bada, writes=[bada])
                for fb in range(12):
                    wt = wts[k % 3]
                    pm = pM[k % 2]
                    k += 1
                    S.dma("sync", lambda e, l=l, fb=fb, wt=wt: e.dma_start(
                        out=wt[:], in_=w_ada[l, :, fb * 512:(fb + 1) * 512].rearrange("(c p) f -> p c f", p=128)), wt, writes=[wt])
                    for kc in range(8):
                        S.op("tensor", lambda e, kc=kc, wt=wt, pm=pm: e.matmul(pm[:], lhsT=cT[:, kc, :], rhs=wt[:, kc, :],
                                                                              start=(kc == 0), stop=(kc == 7)),
                             reads=[cT, wt], writes=[pm])
                    S.op("vector", lambda e, fb=fb, pm=pm, modt=modt, bada=bada: e.tensor_tensor(
                        modt[:, fb * 512:(fb + 1) * 512], pm[:], bada[:, fb * 512:(fb + 1) * 512], op=ALU.add),
                        reads=[pm, bada], writes=[modt])
                S.dma("sync", lambda e, l=l, modt=modt: e.dma_start(out=MOD[l, :, :], in_=modt[:]), modt, reads=[modt])

        def load_mod_cols(P, l, b, seg, name):
            t = P.sb(name, [128, 8], F32)
            S.dma("sync", lambda e: e.dma_start(out=t[:], in_=MOD[l, b, seg * DM:(seg + 1) * DM].rearrange("(c p) -> p c", p=128),
                                                allow_slow_non_contiguous=True), t, writes=[t])
            return t

        def make_AB(P, l, b, pp, ncol, seg_sh, seg_sc, name):
            sh = load_mod_cols(P, l, b, seg_sh, name + "sh")
            sc = load_mod_cols(P, l, b, seg_sc, name + "sc")
            A = P.sb(name + "A", [128, 8], F32)
            S.op("vector", lambda e: e.scalar_tensor_tensor(A[:], in0=sc[:], scalar=1.0, in1=pp[:, ncol:ncol + 8],
                                                            op0=ALU.add, op1=ALU.mult), reads=[sc, pp], writes=[A])
            return A, sh

        def norm_to_hT(P, xt, nsub, A, Bsh, hT, tmps, pT, W):
            ss, rs, sq, xn, tmpf = tmps
            for sub in range(nsub):
                S.op("scalar", lambda e, sub=sub: e.activation(sq[:], xt[:, sub, :], AF.Square, scale=1.0 / 32,
                                                                accum_out=ss[:, sub:sub + 1]), reads=[xt], writes=[sq, ss])
            S.op("scalar", lambda e: e.activation(rs[:, 0:nsub], ss[:, 0:nsub], AF.Ln, bias=epsc()), reads=[ss, cst], writes=[rs])
            S.op("scalar", lambda e: e.activation(rs[:, 0:nsub], rs[:, 0:nsub], AF.Exp, scale=-0.5), reads=[rs], writes=[rs])
            for sub in range(nsub):
                S.op("vector", lambda e, sub=sub: e.tensor_scalar_mul(xn[:, sub, :], xt[:, sub, :], rs[:, sub:sub + 1]),
                     reads=[xt, rs], writes=[xn])
                for c in range(8):
                    S.op("tensor", lambda e, sub=sub, c=c: e.transpose(pT[:, c, :], xn[:, sub, c * 128:(c + 1) * 128], identB[:]),
                         reads=[xn, identB], writes=[pT])
                S.op("vector", lambda e: e.tensor_tensor(tmpf[:], pT[:], A[:, :, None].to_broadcast([128, 8, 128]), op=ALU.mult),
                     reads=[pT, A], writes=[tmpf])
                S.op("gpsimd", lambda e, sub=sub: e.tensor_tensor(hT[:, :, sub * 128:(sub + 1) * 128], tmpf[:],
                                                                   Bsh[:, :, None].to_broadcast([128, 8, 128]), op=ALU.add),
                     reads=[tmpf, Bsh], writes=[hT])

        def load_w_bf16(P, name, src_ap, kc, ncols):
            t = P.sb(name, [128, kc, ncols], BF16)
            step = max(1, 4096 // ncols)
            for c0 in range(0, kc, step):
                c1 = min(kc, c0 + step)
                S.dma("gpsimd", lambda e, c0=c0, c1=c1: e.dma_start(
                    out=t[:, c0:c1, :], in_=src_ap[c0 * 128:c1 * 128, :].rearrange("(c p) f -> p c f", p=128)), t, writes=[t])
            return t

        xsrc = x_in
        for l in range(nlayers):
            last = (l == nlayers - 1)
            xdst = out if last else XB
            with Phase(S, "C%d" % l) as P:
                pp = P.sb("pp", [128, NPP], F32)
                S.dma("sync", lambda e, l=l: e.dma_start(out=pp[:], in_=pp_in[l, :, :]), pp, writes=[pp])
                Wqkv = load_w_bf16(P, "Wqkv", w_in[l, :, 0:1536], 8, 1536)
                Wgate = load_w_bf16(P, "Wgate", w_in[l, :, 1536:2048], 8, 512)
                Wba = load_w_bf16(P, "Wba", w_in[l, :, 2048:2056], 8, 8)
                Wdqk = load_w_bf16(P, "Wdqk", w_in[l, :, 2056:3080], 8, 1024)
                Wdv = load_w_bf16(P, "Wdv", w_in[l, :, 3080:3592], 8, 512)
                negA = P.sb("negA", [128, 4], F32)
                S.op("scalar", lambda e: e.activation(negA[:], pp[:, PP_ALOG:PP_ALOG + 4], AF.Exp), reads=[pp], writes=[negA])
                S.op("vector", lambda e: e.tensor_scalar_mul(negA[:], negA[:], -1.0), reads=[negA], writes=[negA])
                qgain = P.sb("qgain", [128, 1], F32)
                S.op("vector", lambda e: e.tensor_scalar_mul(qgain[:], pp[:, PP_QN:PP_QN + 1], 0.125), reads=[pp], writes=[qgain])
                xts = [P.sb("xt%d" % i, [128, 4, DM], F32) for i in range(2)]
                ss = P.sb("ss", [128, 4], F32); rs = P.sb("rs", [128, 4], F32)
                sq = P.sb("sq", [128, DM], F32); xn = P.sb("xn", [128, 4, DM], BF16)
                tmpf = P.sb("tmpf", [128, 8, 128], F32)
                hT = P.sb("hT", [128, 8, 512], BF16)
                pT = P.ps("pT", [128, 8, 128], BF16)
                pU = [P.ps("pU%d" % i, [128, 512], F32) for i in range(2)]
                pS = P.ps("pS", [128, 512], F32)
                pSm = P.ps("pSm", [128, 32, 16], F32)
                pD = P.ps("pD", [128, 4, 128], F32)
                pTr = P.ps("pTr", [128, 8, 128], BF16)
                pTok = P.ps("pTok", [128, 512], F32)
                upad = [P.sb("upad%d" % i, [128, 515], F32) for i in range(2)]
                acc = [P.sb("acc%d" % i, [128, 512], F32) for i in range(2)]
                act = [P.sb("act%d" % i, [128, 512], F32) for i in range(2)]
                sqq = [P.sb("sqq%d" % i, [128, 512], F32) for i in range(2)]
                rinv = [P.sb("rinv%d" % i, [128, 512], F32) for i in range(2)]
                halo = P.sb("halo", [128, 12, 3], F32)
                kT = P.sb("kT", [128, 4, 4, 128], BF16)
                qT = P.sb("qT", [128, 4, 4, 128], BF16)
                vT = P.sb("vT", [128, 4, 4, 128], BF16)
                kbT = P.sb("kbT", [128, 4, 4, 128], BF16)
                qgT = P.sb("qgT", [128, 4, 4, 128], BF16)
                kbg = P.sb("kbg", [128, 4, 4, 128], BF16)
                kdec = P.sb("kdec", [128, 4, 4, 128], BF16)
                kb = P.sb("kb", [128, 4, 128], BF16)
                qg = P.sb("qg", [128, 4, 128], BF16)
                vb = P.sb("vb", [128, 4, 4, 128], BF16)
                dec = P.sb("dec", [128, 4, 4, 128], F32)
                gatet = P.sb("gatet", [128, 4, 512], BF16)
                dvt = P.sb("dvt", [128, 4, 512], BF16)
                dqkT = P.sb("dqkT", [128, 8, 512], BF16)
                braw = P.sb("braw", [128, 4, 8], F32)
                beta = P.sb("beta", [128, 4, 4], F32)
                gg = P.sb("gg", [128, 4, 4], F32)
                gcl = P.sb("gcl", [128, 4, 8], F32)
                ngc = P.sb("ngc", [128, 4, 4], F32)
                egc = P.sb("egc", [128, 4, 4], F32)
                bgs = P.sb("bgs", [128, 4, 4], F32)
                qsc = P.sb("qsc", [128, 4, 4], F32)
                kds = P.sb("kds", [128, 4, 4], F32)
                egl = P.sb("egl", [128, 4, 4], F32)
                gbc = P.sb("gbc", [128, 4, 128], F32)

                blocks = [(s, blk) for s in range(2) for blk in range(8)]

                def load_x(i):
                    s, blk = blocks[i]
                    xt = xts[i % 2]
                    S.dma("sync", lambda e: e.dma_start(
                        out=xt[:], in_=xsrc[s, blk * 512:(blk + 1) * 512, :].rearrange("(a p) d -> p a d", p=128)), xt, writes=[xt])

                load_x(0)
                AB = {}
                for b in range(2):
                    AB[b] = make_AB(P, l, b, pp, PP_NMIX, 0, 1, "mix%d" % b)
                cc = 0
                for i, (s, blk) in enumerate(blocks):
                    xt = xts[i % 2]
                    if i + 1 < len(blocks):
                        load_x(i + 1)
                    A, Bsh = AB[s]
                    if blk == 0:
                        S.op("gpsimd", lambda e: e.memset(halo[:], 0.0), writes=[halo])
                    norm_to_hT(P, xt, 4, A, Bsh, hT, (ss, rs, sq, xn, tmpf), pT, None)

                    for sub in range(4):
                        for kc in range(8):
                            S.op("tensor", lambda e, sub=sub, kc=kc: e.matmul(pSm[:, sub, 0:8], lhsT=hT[:, kc, sub * 128:(sub + 1) * 128],
                                                                            rhs=Wba[:, kc, :], start=(kc == 0), stop=(kc == 7)),
                                 reads=[hT, Wba], writes=[pSm])
                    S.op("vector", lambda e: e.tensor_copy(braw[:], pSm[:, 0:4, 0:8]), reads=[pSm], writes=[braw])
                    S.op("scalar", lambda e: e.activation(beta[:], braw[:, :, 0:4], AF.Sigmoid), reads=[braw], writes=[beta])
                    S.op("vector", lambda e: e.tensor_tensor(gg[:], braw[:, :, 4:8], pp[:, None, PP_DTB:PP_DTB + 4].to_broadcast([128, 4, 4]),
                                                             op=ALU.add), reads=[braw, pp], writes=[gg])
                    S.op("scalar", lambda e: e.activation(gg[:], gg[:], AF.Exp), reads=[gg], writes=[gg])
                    S.op("scalar", lambda e: e.activation(gg[:], gg[:], AF.Ln, bias=onec()), reads=[gg, cst], writes=[gg])
                    S.op("vector", lambda e: e.tensor_tensor(gg[:], gg[:], negA[:, None, :].to_broadcast([128, 4, 4]), op=ALU.mult),
                         reads=[gg, negA], writes=[gg])
                    pending = []
                    for fc in range(12):
                        pu = pU[cc % 2]; up = upad[cc % 2]; ac = acc[cc % 2]; at = act[cc % 2]
                        sqt = sqq[cc % 2]; rv = rinv[cc % 2]
                        cc += 1
                        for kc in range(8):
                            S.op("tensor", lambda e, kc=kc, fc=fc, pu=pu: e.matmul(pu[:], lhsT=Wqkv[:, kc, fc * 128:(fc + 1) * 128],
                                                                                    rhs=hT[:, kc, :], start=(kc == 0), stop=(kc == 7)),
                                 reads=[Wqkv, hT], writes=[pu])
                        while pending:
                            pending.pop(0)()
                        S.op("vector", lambda e, fc=fc, up=up: e.tensor_copy(up[:, 0:3], halo[:, fc, :]), reads=[halo], writes=[up])
                        S.op("scalar", lambda e, up=up, pu=pu: e.copy(up[:, 3:515], pu[:]), reads=[pu], writes=[up])
                        S.op("gpsimd", lambda e, fc=fc, up=up: e.tensor_copy(halo[:, fc, :], up[:, 512:515]), reads=[up], writes=[halo])
                        cw = lambda j, fc=fc: pp[:, PP_CONVW + fc * 4 + j:PP_CONVW + fc * 4 + j + 1]
                        S.op("vector", lambda e, up=up, ac=ac, cw=cw: e.tensor_scalar_mul(ac[:], up[:, 3:515], cw(3)), reads=[up, pp], writes=[ac])
                        for j in (2, 1, 0):
                            S.op("vector",
                                 lambda e, up=up, ac=ac, cw=cw, j=j: e.scalar_tensor_tensor(ac[:], in0=up[:, j:j + 512], scalar=cw(j), in1=ac[:],
                                                                                           op0=ALU.mult, op1=ALU.add),
                                 reads=[up, pp, ac], writes=[ac])
                        kind, h = fc // 4, fc % 4
                        if kind == 2:
                            S.op("scalar", lambda e, ac=ac, h=h: e.activation(vT[:, :, h, :], ac[:].rearrange("p (a t) -> p a t", a=4), AF.Silu),
                                 reads=[ac], writes=[vT])
                        else:
                            dst = qT if kind == 0 else kT
                            S.op("scalar", lambda e, ac=ac, at=at: e.activation(at[:], ac[:], AF.Silu), reads=[ac], writes=[at])
                            S.op("gpsimd", lambda e, at=at, sqt=sqt: e.tensor_tensor(sqt[:], at[:], at[:], op=ALU.mult), reads=[at], writes=[sqt])

                            def fin(sqt=sqt, rv=rv, at=at, dst=dst, h=h):
                                S.op("tensor", lambda e: e.matmul(pS[:], lhsT=ONESF(), rhs=sqt[:], start=True, stop=True),
                                     reads=[cst, sqt], writes=[pS])
                                S.op("scalar", lambda e: e.activation(rv[:], pS[:], AF.Ln, bias=epsc()), reads=[pS, cst], writes=[rv])
                                S.op("scalar", lambda e: e.activation(rv[:], rv[:], AF.Exp, scale=-0.5), reads=[rv], writes=[rv])
                                S.op("vector", lambda e: e.tensor_tensor(
                                    dst[:, :, h, :], at[:].rearrange("p (a t) -> p a t", a=4), rv[:].rearrange("p (a t) -> p a t", a=4), op=ALU.mult),
                                    reads=[at, rv], writes=[dst])
                            pending.append(fin)

                    while pending:
                        pending.pop(0)()
                    for sub in range(4):
                        S.op("tensor", lambda e, sub=sub: e.matmul(pSm[:, sub, 8:12], lhsT=TRI(), rhs=gg[:, sub, :], start=True, stop=True),
                             reads=[cst, gg], writes=[pSm])
                        S.op("tensor", lambda e, sub=sub: e.matmul(pSm[:, sub, 12:16], lhsT=ONESF(), rhs=gg[:, sub, :], start=True, stop=True),
                             reads=[cst, gg], writes=[pSm])
                    S.op("vector", lambda e: e.tensor_copy(gcl[:], pSm[:, 0:4, 8:16]), reads=[pSm], writes=[gcl])
                    S.op("vector", lambda e: e.tensor_scalar_mul(ngc[:], gcl[:, :, 0:4], -1.0), reads=[gcl], writes=[ngc])
                    S.op("scalar", lambda e: e.activation(egc[:], gcl[:, :, 0:4], AF.Exp), reads=[gcl], writes=[egc])
                    S.op("scalar", lambda e: e.activation(egl[:], gcl[:, :, 4:8], AF.Exp), reads=[gcl], writes=[egl])
                    S.op("vector", lambda e: e.tensor_tensor(kds[:], gcl[:, :, 4:8], gcl[:, :, 0:4], op=ALU.subtract), reads=[gcl], writes=[kds])
                    S.op("scalar", lambda e: e.activation(kds[:], kds[:], AF.Exp), reads=[kds], writes=[kds])
                    S.op("vector", lambda e: e.tensor_tensor(bgs[:], beta[:], egc[:], op=ALU.mult), reads=[beta, egc], writes=[bgs])
                    S.op("vector", lambda e: e.tensor_scalar_mul(qsc[:], egc[:], 128.0 ** -0.5), reads=[egc], writes=[qsc])
                    S.dma("sync", lambda e, s=s, blk=blk: e.dma_start(out=EGL[s, blk * 4:(blk + 1) * 4, :, :].rearrange("n p h -> p n h"),
                                                                       in_=egl[:]), egl, reads=[egl])
                    for sub in range(4):
                        for h in range(4):
                            S.op("vector", lambda e, sub=sub, h=h: e.tensor_copy(gbc[:, h, :], gg[:, sub, h:h + 1].to_broadcast([128, 128])),
                                 reads=[gg], writes=[gbc])
                        for h in range(4):
                            S.op("tensor", lambda e, h=h: e.matmul(pD[:, h, :], lhsT=gbc[:, h, :], rhs=TRI(), start=True, stop=False),
                                 reads=[gbc, cst], writes=[pD])
                            S.op("tensor", lambda e, h=h: e.matmul(pD[:, h, :], lhsT=identF(), rhs=cst[:, C_MASKNEG:C_MASKNEG + 128],
                                                                   start=False, stop=True), reads=[cst], writes=[pD])
                        for h in range(4):
                            S.op("scalar", lambda e, sub=sub, h=h: e.activation(dec[:, sub, h, :], pD[:, h, :], AF.Exp,
                                                                                  bias=ngc[:, sub, h:h + 1]),
                                 reads=[pD, ngc], writes=[dec])
                    S.dma("sync", lambda e, s=s, blk=blk: e.dma_start(
                        out=DEC[s, blk * 4:(blk + 1) * 4, :, :, :].rearrange("n p h c -> p n h c"), in_=dec[:]), dec, reads=[dec])

                    for fc in range(8):
                        pu = pU[cc % 2]; sqt = sqq[cc % 2]; rv = rinv[cc % 2]
                        cc += 1
                        for kc in range(8):
                            S.op("tensor", lambda e, kc=kc, fc=fc, pu=pu: e.matmul(pu[:], lhsT=Wdqk[:, kc, fc * 128:(fc + 1) * 128],
                                                                                    rhs=hT[:, kc, :], start=(kc == 0), stop=(kc == 7)),
                                 reads=[Wdqk, hT], writes=[pu])
                        while pending:
                            pending.pop(0)()
                        S.op("scalar", lambda e, pu=pu, sqt=sqt: e.activation(sqt[:], pu[:], AF.Square), reads=[pu], writes=[sqt])
                        gain = qgain[:, 0:1] if fc < 4 else pp[:, PP_KN:PP_KN + 1]

                        def fin2(pu=pu, sqt=sqt, rv=rv, fc=fc, gain=gain):
                            S.op("tensor", lambda e: e.matmul(pS[:], lhsT=cst[:, C_BLK64:C_BLK64 + 128], rhs=sqt[:], start=True, stop=True),
                                 reads=[cst, sqt], writes=[pS])
                            S.op("scalar", lambda e: e.activation(rv[:], pS[:], AF.Ln, bias=epsc(), scale=1.0 / 64), reads=[pS, cst], writes=[rv])
                            S.op("scalar", lambda e: e.activation(rv[:], rv[:], AF.Exp, scale=-0.5), reads=[rv], writes=[rv])
                            S.op("vector", lambda e: e.scalar_tensor_tensor(
                                dqkT[:, fc, :], in0=pu[:], scalar=gain, in1=rv[:], op0=ALU.mult, op1=ALU.mult),
                                reads=[pu, rv, qgain, pp], writes=[dqkT])
                        pending.append(fin2)
                    while pending:
                        pending.pop(0)()
                    S.dma("sync", lambda e, s=s, blk=blk: e.dma_start(out=DQT[s, :, :, blk * 512:(blk + 1) * 512].rearrange("h p t -> p h t"),
                                                                       in_=dqkT[:, 0:4, :]), dqkT, reads=[dqkT])
                    S.dma("sync", lambda e, s=s, blk=blk: e.dma_start(out=DKT[s, :, :, blk * 512:(blk + 1) * 512].rearrange("h p t -> p h t"),
                                                                       in_=dqkT[:, 4:8, :]), dqkT, reads=[dqkT])

                    for W, dstt, fn in ((Wgate, gatet, AF.Silu), (Wdv, dvt, AF.Copy)):
                        for sub in range(4):
                            for kc in range(8):
                                S.op("tensor", lambda e, sub=sub, kc=kc, W=W: e.matmul(pTok[:], lhsT=hT[:, kc, sub * 128:(sub + 1) * 128],
                                                                                        rhs=W[:, kc, :], start=(kc == 0), stop=(kc == 7)),
                                     reads=[hT, W], writes=[pTok])
                            S.op("scalar", lambda e, sub=sub, dstt=dstt, fn=fn: e.activation(dstt[:, sub, :], pTok[:], fn),
                                 reads=[pTok], writes=[dstt])
                        dd = GATE if dstt is gatet else DV
                        S.dma("sync", lambda e, dd=dd, dstt=dstt, s=s, blk=blk: e.dma_start(
                            out=dd[s, blk * 4:(blk + 1) * 4, :, :].rearrange("n p f -> p n f"), in_=dstt[:]), dstt, reads=[dstt])
                    for sub in range(4):
                        for h in range(4):
                            S.op("tensor", lambda e, sub=sub, h=h: e.transpose(pTr[:, h, :], kT[:, sub, h, :], identB[:]),
                                 reads=[kT, identB], writes=[pTr])
                        S.op("vector", lambda e, sub=sub: e.tensor_tensor(kbg[:, sub, :, :], pTr[:, 0:4, :], bgs[:, sub, :, None].to_broadcast([128, 4, 128]),
                                                                          op=ALU.mult), reads=[pTr, bgs], writes=[kbg])
                        S.op("vector", lambda e, sub=sub: e.tensor_tensor(kdec[:, sub, :, :], pTr[:, 0:4, :], kds[:, sub, :, None].to_broadcast([128, 4, 128]),
                                                                          op=ALU.mult), reads=[pTr, kds], writes=[kdec])
                        S.op("vector", lambda e, sub=sub: e.tensor_tensor(kb[:], pTr[:, 0:4, :], beta[:, sub, :, None].to_broadcast([128, 4, 128]),
                                                                          op=ALU.mult), reads=[pTr, beta], writes=[kb])
                        for h in range(4):
                            S.op("tensor", lambda e, h=h: e.transpose(pTr[:, h, :], kb[:, h, :], identB[:]), reads=[kb, identB], writes=[pTr])
                        S.op("scalar", lambda e, sub=sub: e.copy(kbT[:, sub, :, :], pTr[:, 0:4, :]), reads=[pTr], writes=[kbT])
                        for h in range(4):
                            S.op("tensor", lambda e, sub=sub, h=h: e.transpose(pTr[:, h, :], qT[:, sub, h, :], identB[:]),
                                 reads=[qT, identB], writes=[pTr])
                        S.op("vector", lambda e, sub=sub: e.tensor_tensor(qg[:], pTr[:, 0:4, :], qsc[:, sub, :, None].to_broadcast([128, 4, 128]),
                                                                          op=ALU.mult), reads=[pTr, qsc], writes=[qg])
                        for h in range(4):
                            S.op("tensor", lambda e, h=h: e.transpose(pTr[:, h, :], qg[:, h, :], identB[:]), reads=[qg, identB], writes=[pTr])
                        S.op("scalar", lambda e, sub=sub: e.copy(qgT[:, sub, :, :], pTr[:, 0:4, :]), reads=[pTr], writes=[qgT])
                        for h in range(4):
                            S.op("tensor", lambda e, sub=sub, h=h: e.transpose(pTr[:, h, :], vT[:, sub, h, :], identB[:]),
                                 reads=[vT, identB], writes=[pTr])
                        S.op("vector", lambda e, sub=sub: e.tensor_tensor(vb[:, sub, :, :], pTr[:, 0:4, :], beta[:, sub, :, None].to_broadcast([128, 4, 128]),
                                                                          op=ALU.mult), reads=[pTr, beta], writes=[vb])
                    for dst, src in ((KT, kT), (QT, qT), (QGT, qgT), (KBT, kbT), (KBG, kbg), (KDEC, kdec), (VB, vb)):
                        S.dma("sync", lambda e, dst=dst, src=src, s=s, blk=blk: e.dma_start(
                            out=dst[s, blk * 4:(blk + 1) * 4, :, :, :].rearrange("n p h t -> p n h t"), in_=src[:]), src, reads=[src])

            if upto == "C":
                break
            with Phase(S, "D%d" % l) as P:
                pp = P.sb("pp", [128, NPP], F32)
                S.dma("sync", lambda e, l=l: e.dma_start(out=pp[:], in_=pp_in[l, :, :]), pp, writes=[pp])
                NSLOT = 3
                names = ("kT", "qT", "qgT", "kbT", "kbg", "kdec", "vb")
                srcs = dict(kT=KT, qT=QT, qgT=QGT, kbT=KBT, kbg=KBG, kdec=KDEC, vb=VB)
                slots = {}
                for s in range(2):
                    for k in range(NSLOT):
                        d_ = {nm: P.sb("%s_%d_%d" % (nm, s, k), [128, 4, 128], BF16) for nm in names}
                        d_["dec"] = P.sb("dec_%d_%d" % (s, k), [128, 4, 128], F32)
                        d_["egl"] = P.sb("egl_%d_%d" % (s, k), [128, 4], F32)
                        d_["gate"] = P.sb("gate_%d_%d" % (s, k), [128, 4, 128], BF16)
                        slots[(s, k)] = d_
                strict = cst[:, None, C_STRICT:C_STRICT + 128].to_broadcast([128, 4, 128])
                identq = cst[:, None, C_IDENT:C_IDENT + 128].to_broadcast([128, 4, 128])
                onormb = pp[:, None, PP_ONORM:PP_ONORM + 128].to_broadcast([128, 4, 128])
                pre = [P.ps("pre%d" % i, [128, 4, 128], F32) for i in range(2)]
                pinv = [P.ps("pinv%d" % i, [128, 4, 128], F32) for i in range(2)]
                pW = P.ps("pW", [128, 4, 128], F32)
                pO = P.ps("pO", [128, 4, 128], F32)
                pSt = P.ps("pSt", [128, 4, 128], F32)
                pTr = P.ps("pTr", [128, 8, 128], BF16)
                st = {}
                for s in range(2):
                    st[s] = dict(
                        Y=[P.sb("Y%d_%d" % (s, i), [128, 4, 128], F32) for i in range(2)],
                        X=[P.sb("X%d_%d" % (s, i), [128, 4, 128], F32) for i in range(2)],
                        Q=P.sb("Q%d" % s, [128, 4, 128], F32), Qb=P.sb("Qb%d" % s, [128, 4, 128], BF16),
                        P=P.sb("Pm%d" % s, [128, 4, 128], F32), Yf=P.sb("Yf%d" % s, [128, 4, 128], F32), Xf=P.sb("Xf%d" % s, [128, 4, 128], F32),
                        t1=P.sb("t1%d" % s, [128, 4, 128], F32),
                        aT=P.sb("aT%d" % s, [128, 4, 128], BF16), u=P.sb("u%d" % s, [128, 4, 128], F32),
                        wT=P.sb("wT%d" % s, [128, 4, 128], BF16), vn=P.sb("vn%d" % s, [128, 4, 128], BF16),
                        Sf=P.sb("Sf%d" % s, [128, 4, 128], F32), Sb=P.sb("Sb%d" % s, [128, 4, 128], BF16),
                        osq=P.sb("osq%d" % s, [128, 4, 128], F32), oss=P.sb("oss%d" % s, [128, 4], F32),
                        o1=P.sb("o1%d" % s, [128, 4, 128], F32), g2=P.sb("g2%d" % s, [128, 4, 128], F32),
                        y=P.sb("y%d" % s, [128, 4, 128], BF16),
                        oT=P.sb("oT%d" % s, [128, 4, 512], BF16),
                    )
                    S.op("gpsimd", lambda e, s=s: e.memset(st[s]["Sf"][:], 0.0), writes=[st[s]["Sf"]])
                    S.op("gpsimd", lambda e, s=s: e.memset(st[s]["Sb"][:], 0.0), writes=[st[s]["Sb"]])

                def dbg(nm, tile, s, n, nn=0):
                    if debug and s == 0 and n == nn and l == 0:
                        S.dma("sync", lambda e: e.dma_start(out=DBG[nm][:, :, :], in_=tile[:]), tile, reads=[tile])

                def loadD(s, n):
                    sl = slots[(s, n % NSLOT)]
                    for nm in names:
                        S.dma("sync", lambda e, nm=nm, sl=sl: e.dma_start(out=sl[nm][:], in_=srcs[nm][s, n, :, :, :]), sl[nm], writes=[sl[nm]])
                    S.dma("sync", lambda e, sl=sl: e.dma_start(out=sl["dec"][:], in_=DEC[s, n, :, :, :]), sl["dec"], writes=[sl["dec"]])
                    S.dma("sync", lambda e, sl=sl: e.dma_start(out=sl["egl"][:], in_=EGL[s, n, :, :]), sl["egl"], writes=[sl["egl"]])
                    S.dma("sync", lambda e, sl=sl: e.dma_start(out=sl["gate"][:], in_=GATE[s, n, :, :].rearrange("p (h e) -> p h e", h=4)),
                          sl["gate"], writes=[sl["gate"]])

                for s in range(2):
                    loadD(s, 0)
                    loadD(s, 1)
                pk = [0, 0]

                def mm4(out, lhs, rhs, reads, acc=None):
                    for h in range(4):
                        S.op("tensor", lambda e, h=h: e.matmul(out[:, h, :], lhsT=lhs[:, h, :], rhs=rhs[:, h, :],
                                                               start=(acc in (None, "start")), stop=(acc in (None, "stop"))),
                             reads=reads, writes=[out])

                for n in range(NCH):
                    SL = {s: slots[(s, n % NSLOT)] for s in range(2)}
                    if n + 2 < NCH:
                        for s in range(2):
                            loadD(s, n + 2)
                    for s in range(2):
                        sl, q = SL[s], st[s]
                        pg = pre[s]
                        q["pg"] = pg
                        mm4(pg, sl["kT"], sl["kbT"], [sl["kT"], sl["kbT"]])
                    bc = lambda c0: cst[:, None, c0:c0 + 128].to_broadcast([128, 4, 128])
                    for s in range(2):
                        sl, q = SL[s], st[s]
                        pg = q["pg"]
                        S.op("vector", lambda e, q=q, sl=sl, pg=pg: e.scalar_tensor_tensor(q["t1"][:], in0=pg[:], scalar=-1.0, in1=sl["dec"][:],
                                                                                        op0=ALU.mult, op1=ALU.mult), reads=[pg, sl["dec"]], writes=[q["t1"]])
                        S.op("gpsimd", lambda e, q=q: e.tensor_tensor(q["Yf"][:], q["t1"][:], strict, op=ALU.mult),
                             reads=[q["t1"], cst], writes=[q["Yf"]])
                    for s in range(2):
                        q = st[s]
                        for h in range(4):
                            S.op("tensor", lambda e, h=h, q=q, pi=pinv[s]: e.transpose(pi[:, h, :], q["Yf"][:, h, :], identF()),
                                 reads=[q["Yf"], cst], writes=[pinv[s]])
                    for s in range(2):
                        q = st[s]
                        S.op("scalar", lambda e, q=q, pi_=pinv[s]: e.copy(q["Xf"][:], pi_[:]), reads=[pinv[s]], writes=[q["Xf"]])
                        S.op("gpsimd", lambda e, q=q: e.tensor_tensor(q["Y"][0][:], q["Yf"][:], bc(C_BM16), op=ALU.mult), reads=[q["Yf"], cst], writes=[q["Y"][0]])
                        S.op("gpsimd", lambda e, q=q: e.tensor_tensor(q["X"][0][:], q["Xf"][:], bc(C_BM16), op=ALU.mult), reads=[q["Xf"], cst], writes=[q["X"][0]])
                        S.op("gpsimd", lambda e, q=q: e.tensor_tensor(q["Q"][:], q["Y"][0][:], identq, op=ALU.add), reads=[q["Y"][0], cst], writes=[q["Q"]])
                        S.op("gpsimd", lambda e, q=q: e.tensor_tensor(q["P"][:], q["X"][0][:], identq, op=ALU.add), reads=[q["X"][0], cst], writes=[q["P"]])
                    for lev in range(1, 4):
                        a, b = (lev - 1) % 2, lev % 2
                        for s in range(2):
                            q = st[s]
                            mm4(pinv[s], q["Y"][a], q["X"][a], [q["Y"][a], q["X"][a]])
                            mm4(pre[s], q["X"][a], q["Y"][a], [q["Y"][a], q["X"][a]])
                        for s in range(2):
                            q = st[s]
                            S.op("scalar", lambda e, q=q, b=b, px_=pinv[s]: e.copy(q["X"][b][:], px_[:]), reads=[pinv[s]], writes=[q["X"][b]])
                            S.op("vector", lambda e, q=q, b=b, py_=pre[s]: e.tensor_copy(q["Y"][b][:], py_[:]), reads=[pre[s]], writes=[q["Y"][b]])
                        for s in range(2):
                            q = st[s]
                            mm4(pinv[s], q["X"][b], q["Q"], [q["X"][b], q["Q"]])
                            mm4(pre[s], q["Y"][b], q["P"], [q["Y"][b], q["P"]])
                        for s in range(2):
                            q = st[s]
                            S.op("vector", lambda e, q=q, pq_=pinv[s]: e.tensor_tensor(q["Q"][:], q["Q"][:], pq_[:], op=ALU.add),
                                 reads=[q["Q"], pinv[s]], writes=[q["Q"]])
                            S.op("vector", lambda e, q=q, pp_=pre[s]: e.tensor_tensor(q["P"][:], q["P"][:], pp_[:], op=ALU.add),
                                 reads=[q["P"], pre[s]], writes=[q["P"]])
                    for mi, coff in enumerate((C_OFF32, C_OFF64, C_OFF128)):
                        lastm = (mi == 2)
                        for s in range(2):
                            q = st[s]
                            S.op("gpsimd", lambda e, q=q, coff=coff: e.tensor_tensor(q["Y"][0][:], q["Yf"][:], bc(coff), op=ALU.mult), reads=[q["Yf"], cst], writes=[q["Y"][0]])
                            S.op("gpsimd", lambda e, q=q, coff=coff: e.tensor_tensor(q["X"][0][:], q["Xf"][:], bc(coff), op=ALU.mult), reads=[q["Xf"], cst], writes=[q["X"][0]])
                        for s in range(2):
                            q = st[s]
                            mm4(pinv[s], q["X"][0], q["Q"], [q["X"][0], q["Q"]])
                            if not lastm:
                                mm4(pre[s], q["Y"][0], q["P"], [q["Y"][0], q["P"]])
                        for s in range(2):
                            q = st[s]
                            S.op("scalar", lambda e, q=q, p_=pinv[s]: e.copy(q["X"][1][:], p_[:]), reads=[pinv[s]], writes=[q["X"][1]])
                            if not lastm:
                                S.op("vector", lambda e, q=q, p_=pre[s]: e.tensor_copy(q["Y"][1][:], p_[:]), reads=[pre[s]], writes=[q["Y"][1]])
                        for s in range(2):
                            q = st[s]
                            mm4(pinv[s], q["P"], q["X"][1], [q["P"], q["X"][1]])
                            if not lastm:
                                mm4(pre[s], q["Q"], q["Y"][1], [q["Q"], q["Y"][1]])
                        for s in range(2):
                            q = st[s]
                            S.op("vector", lambda e, q=q, p_=pinv[s]: e.tensor_tensor(q["Q"][:], q["Q"][:], p_[:], op=ALU.add),
                                 reads=[q["Q"], pinv[s]], writes=[q["Q"]])
                            if not lastm:
                                S.op("vector", lambda e, q=q, p_=pre[s]: e.tensor_tensor(q["P"][:], q["P"][:], p_[:], op=ALU.add),
                                     reads=[q["P"], pre[s]], writes=[q["P"]])
                    for s in range(2):
                        sl, q = SL[s], st[s]
                        S.op("scalar", lambda e, q=q: e.copy(q["Qb"][:], q["Q"][:]), reads=[q["Q"]], writes=[q["Qb"]])
                        dbg("Q", q["Q"], s, n)
                        pa = pre[s]
                        q["pa"] = pa
                        mm4(pa, sl["kT"], sl["qT"], [sl["kT"], sl["qT"]])
                    for s in range(2):
                        sl, q = SL[s], st[s]
                        S.op("vector", lambda e, q=q, sl=sl, pa_=q["pa"]: e.scalar_tensor_tensor(q["aT"][:], in0=pa_[:], scalar=128.0 ** -0.5, in1=sl["dec"][:],
                                                                                     op0=ALU.mult, op1=ALU.mult), reads=[q["pa"], sl["dec"]], writes=[q["aT"]])
                        pu_ = pinv[s]
                        q["pu"] = pu_
                        mm4(pu_, q["Qb"], sl["vb"], [q["Qb"], sl["vb"]])
                    for s in range(2):
                        sl, q = SL[s], st[s]
                        S.op("scalar", lambda e, q=q, pu_=q["pu"]: e.copy(q["u"][:], pu_[:]), reads=[q["pu"]], writes=[q["u"]])
                        dbg("U", q["u"], s, n)
                        pw_ = pre[s]
                        q["pw"] = pw_
                        mm4(pw_, sl["kbg"], q["Qb"], [q["Qb"], sl["kbg"]])
                    for s in range(2):
                        q = st[s]
                        S.op("scalar", lambda e, q=q, pw_=q["pw"]: e.copy(q["wT"][:], pw_[:]), reads=[q["pw"]], writes=[q["wT"]])
                    for s in range(2):
                        sl, q = SL[s], st[s]
                        mm4(pW, q["wT"], q["Sb"], [q["wT"], q["Sb"]])
                        S.op("vector", lambda e, q=q: e.tensor_tensor(q["vn"][:], q["u"][:], pW[:], op=ALU.subtract),
                             reads=[q["u"], pW], writes=[q["vn"]])
                        if debug and s == 0 and n == 1 and l == 0:
                            S.op("vector", lambda e, q=q: e.tensor_copy(q["o1"][:], pW[:]), reads=[pW], writes=[q["o1"]])
                            dbg("PW1", q["o1"], s, n, 1)
                        for h in range(4):
                            S.op("tensor", lambda e, h=h, q=q, sl=sl: e.matmul(pO[:, h, :], lhsT=sl["qgT"][:, h, :], rhs=q["Sb"][:, h, :], start=True, stop=False),
                                 reads=[sl["qgT"], q["Sb"]], writes=[pO])
                            S.op("tensor", lambda e, h=h, q=q: e.matmul(pO[:, h, :], lhsT=q["aT"][:, h, :], rhs=q["vn"][:, h, :], start=False, stop=True),
                                 reads=[q["aT"], q["vn"]], writes=[pO])
                        mm4(pSt, sl["kdec"], q["vn"], [sl["kdec"], q["vn"]])
                        S.op("gpsimd", lambda e, q=q, sl=sl: e.tensor_tensor(q["Sf"][:], q["Sf"][:], sl["egl"][:, :, None].to_broadcast([128, 4, 128]),
                                                                            op=ALU.mult), reads=[q["Sf"], sl["egl"]], writes=[q["Sf"]])
                        S.op("vector", lambda e, q=q: e.tensor_tensor(q["Sf"][:], q["Sf"][:], pSt[:], op=ALU.add), reads=[q["Sf"], pSt], writes=[q["Sf"]])
                        S.op("scalar", lambda e, q=q: e.copy(q["Sb"][:], q["Sf"][:]), reads=[q["Sf"]], writes=[q["Sb"]])
                        dbg("S0", q["Sf"], s, n)
                        dbg("U1", q["u"], s, n, 1)
                        S.op("scalar", lambda e, q=q: e.activation(q["osq"][:], pO[:], AF.Square), reads=[pO], writes=[q["osq"]])
                        S.op("vector", lambda e, q=q: e.reduce_sum(q["oss"][:], q["osq"][:], axis=mybir.AxisListType.X), reads=[q["osq"]], writes=[q["oss"]])
                        S.op("scalar", lambda e, q=q: e.activation(q["oss"][:], q["oss"][:], AF.Ln, bias=epsc(), scale=1.0 / 128), reads=[q["oss"], cst], writes=[q["oss"]])
                        S.op("scalar", lambda e, q=q: e.activation(q["oss"][:], q["oss"][:], AF.Exp, scale=-0.5), reads=[q["oss"]], writes=[q["oss"]])
                        S.op("vector", lambda e, q=q: e.tensor_tensor(q["o1"][:], pO[:], q["oss"][:, :, None].to_broadcast([128, 4, 128]), op=ALU.mult),
                             reads=[pO, q["oss"]], writes=[q["o1"]])
                        S.op("gpsimd", lambda e, q=q, sl=sl: e.tensor_tensor(q["g2"][:], sl["gate"][:], onormb, op=ALU.mult), reads=[sl["gate"], pp], writes=[q["g2"]])
                        S.op("gpsimd", lambda e, q=q: e.tensor_tensor(q["y"][:], q["o1"][:], q["g2"][:], op=ALU.mult), reads=[q["o1"], q["g2"]], writes=[q["y"]])
                        for h in range(4):
                            S.op("tensor", lambda e, h=h, q=q: e.transpose(pTr[:, h, :], q["y"][:, h, :], identB[:]), reads=[q["y"], identB], writes=[pTr])
                        j = n % 4
                        S.op("scalar", lambda e, q=q, j=j: e.copy(q["oT"][:, :, j * 128:(j + 1) * 128], pTr[:, 0:4, :]), reads=[pTr], writes=[q["oT"]])
                        if j == 3:
                            t0 = (n - 3) * 128
                            S.dma("sync", lambda e, q=q, s=s, t0=t0: e.dma_start(out=OT[s, 0:4, :, t0:t0 + 512].rearrange("h p t -> p h t"), in_=q["oT"][:]),
                                  q["oT"], reads=[q["oT"]])
            if upto == "D":
                break
            with Phase(S, "E%d" % l) as P:
                pp = P.sb("pp", [128, NPP], F32)
                S.dma("sync", lambda e, l=l: e.dma_start(out=pp[:], in_=pp_in[l, :, :]), pp, writes=[pp])
                lam_init = 0.8 - 0.6 * math.exp(-0.3 * l)
                lamr = P.sb("lamr", [1, 4, 64], F32)
                S.dma("sync", lambda e, l=l: e.dma_start(out=lamr[:], in_=lam_in[l:l + 1, :, :]), lamr, writes=[lamr])
                lprod = P.sb("lprod", [1, 2, 64], F32)
                lsum = P.sb("lsum", [1, 2], F32)
                nlam1 = P.sb("nlam1", [1, 1], F32)
                nlam = P.sb("nlam", [128, 1], F32)
                subg = P.sb("subg", [128, 1], F32)
                b31 = P.sb("b31", [128, 4], F32)
                S.dma("sync", lambda e: e.dma_start(out=b31[:], in_=b31_in[:, :]), b31, writes=[b31])
                S.op("vector", lambda e: e.tensor_tensor(lprod[:], lamr[:, 0:4:2, :], lamr[:, 1:4:2, :], op=ALU.mult), reads=[lamr], writes=[lprod])
                S.op("vector", lambda e: e.reduce_sum(lsum[:], lprod[:], axis=mybir.AxisListType.X), reads=[lprod], writes=[lsum])
                S.op("scalar", lambda e: e.activation(lsum[:], lsum[:], AF.Exp), reads=[lsum], writes=[lsum])
                S.op("vector", lambda e: e.scalar_tensor_tensor(nlam1[:], in0=lsum[:, 1:2], scalar=-lam_init, in1=lsum[:, 0:1],
                                                                op0=ALU.add, op1=ALU.subtract), reads=[lsum], writes=[nlam1])
                pS0 = [P.ps("pS0_%d" % i, [128, 512], F32) for i in range(2)]
                pS1 = [P.ps("pS1_%d" % i, [128, 512], F32) for i in range(2)]
                pO0 = P.ps("pO0", [128, 512], F32); pO1 = P.ps("pO1", [128, 512], F32)
                pZ0 = P.ps("pZ0", [128, 512], F32); pZ1 = P.ps("pZ1", [128, 512], F32)
                S.op("tensor", lambda e: e.matmul(pZ0[:, 0:1], lhsT=cst[0:1, C_ONES:C_ONES + 128], rhs=nlam1[:], start=True, stop=True),
                     reads=[cst, nlam1], writes=[pZ0])
                S.op("vector", lambda e: e.tensor_copy(nlam[:], pZ0[:, 0:1]), reads=[pZ0], writes=[nlam])
                S.op("vector", lambda e: e.tensor_scalar_mul(subg[:], pp[:, PP_SUBLN:PP_SUBLN + 1], 1.0 - lam_init), reads=[pp], writes=[subg])
                expB = []
                for h in range(4):
                    t = P.sb("expB%d" % h, [128, 1024], F32)
                    S.dma("sync", lambda e, h=h, t=t: e.dma_start(out=t[:], in_=tb_in[h, :, :]), t, writes=[t])
                    S.op("scalar", lambda e, t=t: e.activation(t[:], t[:], AF.Exp), reads=[t], writes=[t])
                    expB.append(t)
                qts = [P.sb("qt%d" % i, [128, T], BF16) for i in range(2)]
                kts = [P.sb("kt%d" % i, [128, T], BF16) for i in range(2)]
                vts = [P.sb("vt%d" % i, [128, NCH, 128], BF16) for i in range(2)]
                E0 = [P.sb("E0_%d" % i, [128, 512], BF16) for i in range(3)]
                E1 = [P.sb("E1_%d" % i, [128, 512], BF16) for i in range(3)]
                Ef = [P.sb("Ef_%d" % i, [128, 512], F32) for i in range(2)]
                rz0 = P.sb("rz0", [128, 512], F32); rz1 = P.sb("rz1", [128, 512], F32)
                oo = P.sb("oo", [128, 512], F32); osq = P.sb("osq", [128, 512], F32)
                rin = P.sb("rin", [128, 512], F32)
                oTs = [P.sb("oTs%d" % i, [128, 512], BF16) for i in range(2)]
                heads = [(s, h) for s in range(2) for h in range(4)]

                def loadE(i):
                    s, h = heads[i]
                    S.dma("sync", lambda e: e.dma_start(out=qts[i % 2][:], in_=DQT[s, h, :, :]), qts[i % 2], writes=[qts[i % 2]])
                    S.dma("sync", lambda e: e.dma_start(out=kts[i % 2][:], in_=DKT[s, h, :, :]), kts[i % 2], writes=[kts[i % 2]])
                    S.dma("sync", lambda e: e.dma_start(out=vts[i % 2][:], in_=DV[s, :, :, h * 128:(h + 1) * 128].rearrange("n p e -> p n e")),
                          vts[i % 2], writes=[vts[i % 2]])

                loadE(0)
                steps = []
                for i, (s, h) in enumerate(heads):
                    for j in range(8):
                        nk = 4 * j + 4
                        for ki in range(nk):
                            steps.append((i, s, h, j, ki, nk))
                cnt = {"ek": 0, "fk": 0, "ok": 0, "loaded": 0}
                live = {}

                def stageA(idx):
                    i, s, h, j, ki, nk = steps[idx]
                    qt, kt = qts[i % 2], kts[i % 2]
                    ek = cnt["ek"]; cnt["ek"] += 1
                    p0, p1 = pS0[ek % 2], pS1[ek % 2]
                    e0, e1 = E0[ek % 3], E1[ek % 3]
                    S.op("tensor", lambda e: e.matmul(p0[:], lhsT=kt[0:64, ki * 128:(ki + 1) * 128], rhs=qt[0:64, j * 512:(j + 1) * 512],
                                                      start=True, stop=True), reads=[kt, qt], writes=[p0])
                    S.op("tensor", lambda e: e.matmul(p1[:], lhsT=kt[64:128, ki * 128:(ki + 1) * 128], rhs=qt[64:128, j * 512:(j + 1) * 512],
                                                      start=True, stop=True), reads=[kt, qt], writes=[p1])
                    near = ki >= 4 * j - 1
                    if not near:
                        S.op("scalar", lambda e: e.activation(e0[:], p0[:], AF.Exp, bias=b31[:, h:h + 1]), reads=[p0, b31], writes=[e0])
                        S.op("scalar", lambda e: e.activation(e1[:], p1[:], AF.Exp, bias=b31[:, h:h + 1]), reads=[p1, b31], writes=[e1])
                    else:
                        c0 = 512 * j - 128 * ki + 384
                        for pc, ec in ((p0, e0), (p1, e1)):
                            ef = Ef[cnt["fk"] % 2]; cnt["fk"] += 1
                            S.op("scalar", lambda e, pc=pc, ef=ef: e.activation(ef[:], pc[:], AF.Exp), reads=[pc], writes=[ef])
                            S.op("vector" if cnt["fk"] % 2 else "gpsimd",
                                 lambda e, ef=ef, ec=ec: e.tensor_tensor(ec[:], ef[:], expB[h][:, c0:c0 + 512], op=ALU.mult),
                                 reads=[ef, expB[h]], writes=[ec])
                    live[idx] = (e0, e1)

                def stageB(idx):
                    i, s, h, j, ki, nk = steps[idx]
                    if j == 0 and ki == 0 and i + 1 < len(heads):
                        loadE(i + 1)
                    vt = vts[i % 2]
                    e0, e1 = live.pop(idx)
                    first, lastk = (ki == 0), (ki == nk - 1)
                    S.op("tensor", lambda e: e.matmul(pO0[:], lhsT=vt[:, ki, :], rhs=e0[:], start=first, stop=lastk), reads=[vt, e0], writes=[pO0])
                    S.op("tensor", lambda e: e.matmul(pZ0[:], lhsT=onesB[:], rhs=e0[:], start=first, stop=lastk), reads=[onesB, e0], writes=[pZ0])
                    S.op("tensor", lambda e: e.matmul(pO1[:], lhsT=vt[:, ki, :], rhs=e1[:], start=first, stop=lastk), reads=[vt, e1], writes=[pO1])
                    S.op("tensor", lambda e: e.matmul(pZ1[:], lhsT=onesB[:], rhs=e1[:], start=first, stop=lastk), reads=[onesB, e1], writes=[pZ1])
                    if not lastk:
                        return
                    S.op("vector", lambda e: e.reciprocal(rz0[:], pZ0[:]), reads=[pZ0], writes=[rz0])
                    S.op("vector", lambda e: e.reciprocal(rz1[:], pZ1[:]), reads=[pZ1], writes=[rz1])
                    S.op("vector", lambda e: e.tensor_tensor(rz0[:], pO0[:], rz0[:], op=ALU.mult), reads=[pO0, rz0], writes=[rz0])
                    S.op("vector", lambda e: e.tensor_tensor(rz1[:], pO1[:], rz1[:], op=ALU.mult), reads=[pO1, rz1], writes=[rz1])
                    S.op("vector", lambda e: e.scalar_tensor_tensor(oo[:], in0=rz1[:], scalar=nlam[:, 0:1], in1=rz0[:], op0=ALU.mult, op1=ALU.add),
                         reads=[rz0, rz1, nlam], writes=[oo])
                    S.op("gpsimd", lambda e: e.tensor_tensor(osq[:], oo[:], oo[:], op=ALU.mult), reads=[oo], writes=[osq])
                    S.op("tensor", lambda e: e.matmul(pZ0[:], lhsT=ONESF(), rhs=osq[:], start=True, stop=True), reads=[cst, osq], writes=[pZ0])
                    S.op("scalar", lambda e: e.activation(rin[:], pZ0[:], AF.Ln, bias=epsc(), scale=1.0 / 128), reads=[pZ0, cst], writes=[rin])
                    S.op("scalar", lambda e: e.activation(rin[:], rin[:], AF.Exp, scale=-0.5), reads=[rin], writes=[rin])
                    ot = oTs[cnt["ok"] % 2]; cnt["ok"] += 1
                    S.op("vector", lambda e: e.scalar_tensor_tensor(ot[:], in0=oo[:], scalar=subg[:, 0:1], in1=rin[:], op0=ALU.mult, op1=ALU.mult),
                         reads=[oo, subg, rin], writes=[ot])
                    S.dma("sync", lambda e: e.dma_start(out=OT[s, 4 + h, :, j * 512:(j + 1) * 512], in_=ot[:]), ot, reads=[ot])

                LOOK = 2
                for idx in range(min(LOOK, len(steps))):
                    stageA(idx)
                for idx in range(len(steps)):
                    stageB(idx)
                    if idx + LOOK < len(steps):
                        stageA(idx + LOOK)
            if upto == "E":
                break
            with Phase(S, "F%d" % l) as P:
                Wo = load_w_bf16(P, "Wo", w_out[l, :, :], 8, DM)
                gtB = []
                for b in range(2):
                    t = P.sb("gtB%d" % b, [128, DM], F32)
                    S.dma("sync", lambda e, b=b, t=t, l=l: e.dma_start(out=t[:], in_=MOD[l, b, 2 * DM:3 * DM].partition_broadcast(128)), t, writes=[t])
                    gtB.append(t)
                xts = [P.sb("xt%d" % i, [128, 4, DM], F32) for i in range(2)]
                ots = [P.sb("ot%d" % i, [128, 8, 512], BF16) for i in range(2)]
                tmp = [P.sb("tmp%d" % i, [128, 512], F32) for i in range(2)]
                pY = [P.ps("pY%d" % i, [128, 512], F32) for i in range(4)]
                blocks = [(s, blk) for s in range(2) for blk in range(8)]

                def loadF(i):
                    s, blk = blocks[i]
                    S.dma("sync", lambda e: e.dma_start(out=xts[i % 2][:], in_=xsrc[s, blk * 512:(blk + 1) * 512, :].rearrange("(a p) d -> p a d", p=128)),
                          xts[i % 2], writes=[xts[i % 2]])
                    S.dma("sync", lambda e: e.dma_start(out=ots[i % 2][:], in_=OT[s, :, :, blk * 512:(blk + 1) * 512].rearrange("c p t -> p c t")),
                          ots[i % 2], writes=[ots[i % 2]])

                loadF(0)
                yk = 0
                for i, (s, blk) in enumerate(blocks):
                    if i + 1 < len(blocks):
                        loadF(i + 1)
                    xt, ot = xts[i % 2], ots[i % 2]
                    for sub in range(4):
                        for dh in range(2):
                            py = pY[yk % 4]; tp = tmp[yk % 2]; yk += 1
                            for c in range(8):
                                S.op("tensor", lambda e, c=c, sub=sub, dh=dh, py=py, ot=ot: e.matmul(py[:], lhsT=ot[:, c, sub * 128:(sub + 1) * 128],
                                                                                                  rhs=Wo[:, c, dh * 512:(dh + 1) * 512], start=(c == 0), stop=(c == 7)),
                                     reads=[ot, Wo], writes=[py])
                            S.op("vector", lambda e, py=py, tp=tp, dh=dh, s=s: e.tensor_tensor(tp[:], py[:], gtB[s][:, dh * 512:(dh + 1) * 512], op=ALU.mult),
                                 reads=[py, gtB[s]], writes=[tp])
                            S.op("gpsimd", lambda e, tp=tp, sub=sub, dh=dh, xt=xt: e.tensor_tensor(xt[:, sub, dh * 512:(dh + 1) * 512], xt[:, sub, dh * 512:(dh + 1) * 512],
                                                                                                   tp[:], op=ALU.add), reads=[tp, xt], writes=[xt])
                    S.dma("sync", lambda e, xt=xt, s=s, blk=blk: e.dma_start(out=XA[s, blk * 512:(blk + 1) * 512, :].rearrange("(a p) d -> p a d", p=128), in_=xt[:]),
                          xt, reads=[xt])
            if upto == "F":
                break
            with Phase(S, "G%d" % l) as P:
                pp = P.sb("pp", [128, NPP], F32)
                S.dma("sync", lambda e, l=l: e.dma_start(out=pp[:], in_=pp_in[l, :, :]), pp, writes=[pp])
                Wup = load_w_bf16(P, "Wup", ffn_up[l, :, :], 8, 2 * DFF)
                Wdn = load_w_bf16(P, "Wdn", ffn_down[l, :, :], 22, DM)
                gtB = []
                AB = {}
                for b in range(2):
                    t = P.sb("gtB%d" % b, [128, DM], F32)
                    S.dma("sync", lambda e, b=b, t=t, l=l: e.dma_start(out=t[:], in_=MOD[l, b, 5 * DM:6 * DM].partition_broadcast(128)), t, writes=[t])
                    gtB.append(t)
                    AB[b] = make_AB(P, l, b, pp, PP_NFFN, 3, 4, "ffn%d" % b)
                NB = 256
                xts = [P.sb("xt%d" % i, [128, 2, DM], F32) for i in range(2)]
                ss = P.sb("ss", [128, 4], F32); rs = P.sb("rs", [128, 4], F32)
                sq = P.sb("sq", [128, DM], F32); xn = P.sb("xn", [128, 2, DM], BF16)
                tmpf = P.sb("tmpf", [128, 8, 128], F32)
                hT = P.sb("hT", [128, 8, NB], BF16)
                GT = P.sb("GT", [128, 22, NB], BF16)
                pT = P.ps("pT", [128, 8, 128], BF16)
                pUf = [P.ps("pU%d" % i, [128, 512], F32) for i in range(4)]
                pY = [P.ps("pY%d" % i, [128, 512], F32) for i in range(2)]
                upad = [P.sb("upad%d" % i, [128, NB + 2], F32) for i in range(4)]
                acc = [P.sb("acc%d" % i, [128, NB], F32) for i in range(4)]
                sg = [P.sb("sg%d" % i, [128, NB], F32) for i in range(2)]
                tmp = [P.sb("tmp%d" % i, [128, 512], F32) for i in range(2)]
                halo = P.sb("halo", [128, 44, 2], F32)
                blocks = [(s, blk) for s in range(2) for blk in range(T // NB)]

                def loadG(i):
                    s, blk = blocks[i]
                    S.dma("sync", lambda e: e.dma_start(out=xts[i % 2][:], in_=XA[s, blk * NB:(blk + 1) * NB, :].rearrange("(a p) d -> p a d", p=128)),
                          xts[i % 2], writes=[xts[i % 2]])

                loadG(0)
                uk = 0; yk = 0
                for i, (s, blk) in enumerate(blocks):
                    if i + 1 < len(blocks):
                        loadG(i + 1)
                    xt = xts[i % 2]
                    A, Bsh = AB[s]
                    if blk == 0:
                        S.op("gpsimd", lambda e: e.memset(halo[:], 0.0), writes=[halo])
                    norm_to_hT(P, xt, 2, A, Bsh, hT, (ss, rs, sq, xn, tmpf), pT, None)
                    for fc in range(22):
                        res = []
                        for half in range(2):
                            f = fc + 22 * half
                            pu = pUf[uk % 4]; up = upad[uk % 4]; ac = acc[uk % 4]; uk += 1
                            for kc in range(8):
                                S.op("tensor", lambda e, kc=kc, f=f, pu=pu: e.matmul(pu[:, 0:NB], lhsT=Wup[:, kc, f * 128:(f + 1) * 128], rhs=hT[:, kc, :],
                                                                                    start=(kc == 0), stop=(kc == 7)), reads=[Wup, hT], writes=[pu])
                            S.op("gpsimd", lambda e, f=f, up=up: e.tensor_copy(up[:, 0:2], halo[:, f, :]), reads=[halo], writes=[up])
                            S.op("scalar", lambda e, up=up, pu=pu: e.copy(up[:, 2:NB + 2], pu[:, 0:NB]), reads=[pu], writes=[up])
                            S.op("gpsimd", lambda e, f=f, up=up: e.tensor_copy(halo[:, f, :], up[:, NB:NB + 2]), reads=[up], writes=[halo])
                            cw = lambda j, f=f: pp[:, PP_FCW + f * 3 + j:PP_FCW + f * 3 + j + 1]
                            S.op("vector", lambda e, up=up, ac=ac, cw=cw, f=f: e.tensor_scalar(ac[:], up[:, 2:NB + 2], cw(2), pp[:, PP_FCB + f:PP_FCB + f + 1],
                                                                                               op0=ALU.mult, op1=ALU.add), reads=[up, pp], writes=[ac])
                            for j in (1, 0):
                                S.op("vector", lambda e, up=up, ac=ac, cw=cw, j=j: e.scalar_tensor_tensor(ac[:], in0=up[:, j:j + NB], scalar=cw(j), in1=ac[:],
                                                                                                          op0=ALU.mult, op1=ALU.add), reads=[up, pp, ac], writes=[ac])
                            res.append(ac)
                        sgt = sg[fc % 2]
                        S.op("scalar", lambda e, sgt=sgt, g_=res[1]: e.activation(sgt[:], g_[:], AF.Silu), reads=[res[1]], writes=[sgt])
                        S.op("gpsimd", lambda e, sgt=sgt, a_=res[0], fc=fc: e.tensor_tensor(GT[:, fc, :], a_[:], sgt[:], op=ALU.mult), reads=[res[0], sgt], writes=[GT])
                    for sub in range(2):
                        for dh in range(2):
                            py = pY[yk % 2]; tp = tmp[yk % 2]; yk += 1
                            for fc in range(22):
                                S.op("tensor", lambda e, fc=fc, sub=sub, dh=dh, py=py: e.matmul(py[:], lhsT=GT[:, fc, sub * 128:(sub + 1) * 128],
                                                                                                    rhs=Wdn[:, fc, dh * 512:(dh + 1) * 512], start=(fc == 0), stop=(fc == 21)),
                                     reads=[GT, Wdn], writes=[py])
                            S.op("vector", lambda e, py=py, tp=tp, dh=dh, s=s: e.tensor_tensor(tp[:], py[:], gtB[s][:, dh * 512:(dh + 1) * 512], op=ALU.mult),
                                 reads=[py, gtB[s]], writes=[tp])
                            S.op("gpsimd", lambda e, tp=tp, sub=sub, dh=dh, xt=xt: e.tensor_tensor(xt[:, sub, dh * 512:(dh + 1) * 512], xt[:, sub, dh * 512:(dh + 1) * 512],
                                                                                                   tp[:], op=ALU.add), reads=[tp, xt], writes=[xt])
                    S.dma("sync", lambda e, xt=xt, s=s, blk=blk: e.dma_start(out=xdst[s, blk * NB:(blk + 1) * NB, :].rearrange("(a p) d -> p a d", p=128), in_=xt[:]),
                          xt, reads=[xt])
            xsrc = xdst
        G.__exit__(None, None, None)
    return nc


def _prep(inputs):
    inp = {k: np.asarray(v) for k, v in inputs.items()}
    consts = _consts()
    pp = _pack_pp(inp)
    lamv = np.stack([inp["diff_lambda_q1"], inp["diff_lambda_k1"], inp["diff_lambda_q2"], inp["diff_lambda_k2"]], axis=1)
    lamv = np.ascontiguousarray(lamv.astype(np.float32))
    kk = np.arange(128)[:, None]
    cc = np.arange(1024)[None, :]
    dist = cc - kk - 384
    bidx = _t5_bucket(np.maximum(dist, 0))
    rb = inp["rel_bias"].astype(np.float32)
    tb = np.empty((4, 128, 1024), np.float32)
    for h in range(4):
        tb[h] = np.where(dist >= 0, rb[bidx, h], np.float32(NEG))
    b31 = np.ascontiguousarray(np.broadcast_to(rb[31][None, :], (128, 4))).astype(np.float32)
    shared = dict(consts=consts, pp=pp, lamv=lamv, tb=tb, b31=b31,
                  w_ada=inp["w_ada"], b_ada=inp["b_ada"], w_in=inp["w_in"], w_out=inp["w_out"],
                  ffn_up=inp["ffn_up"], ffn_down=inp["ffn_down"])
    in_maps = []
    for c in range(NCORES):
        m = dict(shared)
        m["x"] = np.ascontiguousarray(inp["x"][2 * c:2 * c + 2])
        cc_ = inp["c"][2 * c:2 * c + 2]
        m["cT"] = np.ascontiguousarray(cc_.reshape(2, 8, 128).transpose(2, 1, 0))
        in_maps.append(m)
    return in_maps


def kernel(**inputs):
    in_maps = _prep(inputs)
    nc = build()
    res = run_bass_kernel_spmd(nc, in_maps, core_ids=list(range(NCORES)))
    return np.concatenate([r["out"] for r in res.results], axis=0).astype(np.float32)
```

```python
import math
from contextlib import ExitStack
import numpy as np
import concourse.bass as bass
import concourse.mybir as mybir
from concourse.bass_utils import run_bass_kernel_spmd

F32 = mybir.dt.float32
BF16 = mybir.dt.bfloat16
AF = mybir.ActivationFunctionType
ALU = mybir.AluOpType

NCORES = 8
DEPTH = 4
T = 4096
DM = 1024
NEG = -30000.0
EPS = 1e-6
DFF = 2816


class Buf:
    def __init__(self, name, t=None):
        self.name = name
        self.t = t
        self.last_w = None
        self.readers = []
        self.dsem = None
        self.dcount = 0

    def __getitem__(self, k):
        return self.t[k]


class Eng:
    def __init__(self, name, sem):
        self.name = name
        self.sem = sem
        self.count = 0
        self.waited = {}
        self.prog = []


class Sched:
    SEM_WRAP = 30000

    def __init__(self, nc, es):
        self.nc = nc
        self.es = es
        self.engs = {}
        for name in ("sync", "scalar", "vector", "gpsimd", "tensor"):
            self.engs[name] = Eng(name, es.enter_context(nc.semaphore("s_" + name)))
        self.n_instr = 0
        self.dbufs = []
        self.sem_pool = []
        self.rr = 0

    def _waits(self, e, reads, writes):
        toks = []
        own = e.sem
        for b in reads:
            if b.last_w is not None:
                toks.append(b.last_w)
        for b in writes:
            if b.last_w is not None and b.last_w[0] is not own:
                toks.append(b.last_w)
            for r in b.readers:
                if r[0] is not own:
                    toks.append(r)
        need = {}
        for sem, val in toks:
            if e.name == "tensor" and sem is own:
                continue
            k = id(sem)
            if e.waited.get(k, 0) >= val:
                continue
            if k not in need or need[k][1] < val:
                need[k] = (sem, val)
        out = []
        for k, (sem, val) in need.items():
            e.waited[k] = val
            out.append((sem, val))
        return out

    def _record(self, e, fn, waits, sem, inc, reads, writes, tok):
        def run(eng, fn=fn, waits=waits, sem=sem, inc=inc):
            for s, v in waits:
                eng.wait_ge(s, v)
            fn(eng).then_inc(sem, inc)
        e.prog.append(run)
        for b in reads:
            b.readers.append(tok)
            if len(b.readers) > 64:
                b.readers = b.readers[-48:]
        for b in writes:
            b.last_w = tok
            b.readers = []
        self.n_instr += 1

    def op(self, engname, fn, reads=(), writes=()):
        e = self.engs[engname]
        if e.count >= self.SEM_WRAP:
            e.sem = self.es.enter_context(self.nc.semaphore("s_%s_%d" % (engname, self.n_instr)))
            e.count = 0
        waits = self._waits(e, reads, writes)
        e.count += 1
        tok = (e.sem, e.count)
        self._record(e, fn, waits, e.sem, 1, reads, writes, tok)
        return tok

    def dma(self, engname, fn, sembuf, reads=(), writes=()):
        e = self.engs[engname]
        waits = self._waits(e, reads, writes)
        if sembuf.dsem is None:
            if self.sem_pool:
                sembuf.dsem, sembuf.dcount = self.sem_pool.pop()
            else:
                sembuf.dsem = self.es.enter_context(self.nc.semaphore("d%d_%s" % (self.n_instr, sembuf.name)))
        if sembuf not in self.dbufs:
            self.dbufs.append(sembuf)
        sembuf.dcount += 16
        tok = (sembuf.dsem, sembuf.dcount)
        self._record(e, fn, waits, sembuf.dsem, 16, reads, writes, tok)
        return tok

    def barrier(self):
        toks = [(e.sem, e.count) for e in self.engs.values() if e.count > 0]
        toks += [(b.dsem, b.dcount) for b in self.dbufs]
        for name in self.engs:
            self.wait_tokens(name, toks)
        for b in self.dbufs:
            self.sem_pool.append((b.dsem, b.dcount))
            b.dsem = None
        self.dbufs = []

    def wait_tokens(self, engname, toks):
        e = self.engs[engname]
        for sem, val in toks:
            k = id(sem)
            if e.waited.get(k, 0) >= val:
                continue
            e.waited[k] = val
            e.prog.append(lambda eng, s=sem, v=val: eng.wait_ge(s, v))

    def emit(self):
        with self.nc.Block() as block:
            for name in ("sync", "scalar", "vector", "gpsimd", "tensor"):
                def body(eng, name=name):
                    for f in self.engs[name].prog:
                        f(eng)
                getattr(block, name)(body)
        for e in self.engs.values():
            e.prog = []

    def alt(self):
        self.rr ^= 1
        return "vector" if self.rr else "gpsimd"


PHASE_LOG = []


class Phase:
    def __init__(self, S, name):
        self.S = S
        self.nc = S.nc
        self.name = name
        self.es = ExitStack()
        self.k = 0

    def __enter__(self):
        self.es.__enter__()
        return self

    def sb(self, name, shape, dt):
        self.k += 1
        nm = "%s_%s_%d" % (self.name, name, self.k)
        return Buf(nm, self.es.enter_context(self.nc.sbuf_tensor(nm, list(shape), dt)))

    def ps(self, name, shape, dt):
        self.k += 1
        nm = "%s_%s_%d" % (self.name, name, self.k)
        return Buf(nm, self.es.enter_context(self.nc.psum_tensor(nm, list(shape), dt)))

    def __exit__(self, *a):
        if a[0] is None:
            PHASE_LOG.append((self.name, {k: (v.count, id(v.sem)) for k, v in self.S.engs.items()}, self.S.n_instr))
            self.S.barrier()
            self.S.emit()
        return self.es.__exit__(*a)


C_IDENT, C_TRI, C_ONES, C_MASKNEG, C_STRICT, C_BLK64, C_DELTA, C_EPS, C_ONE = 0, 128, 256, 384, 512, 640, 768, 769, 770
C_BM16, C_OFF32, C_OFF64, C_OFF128 = 771, 899, 1027, 1155
NCONST = 1283

PP_CONVW = 0
PP_DTB = 48
PP_ALOG = 52
PP_ONORM = 56
PP_QN = 184
PP_KN = 185
PP_SUBLN = 186
PP_FCW = 187
PP_FCB = 319
PP_NMIX = 363
PP_NFFN = 371
PP_BADA = 379
NPP = 380


def _consts():
    c = np.zeros((128, NCONST), np.float32)
    i = np.arange(128)
    c[:, C_IDENT:C_IDENT + 128] = np.eye(128)
    c[:, C_TRI:C_TRI + 128] = (i[:, None] <= i[None, :])
    c[:, C_ONES:C_ONES + 128] = 1.0
    c[:, C_MASKNEG:C_MASKNEG + 128] = np.where(i[:, None] <= i[None, :], 0.0, NEG)
    c[:, C_STRICT:C_STRICT + 128] = (i[:, None] < i[None, :])
    c[:, C_BLK64:C_BLK64 + 128] = ((i[:, None] // 64) == (i[None, :] // 64))
    c[0, C_DELTA] = 1.0
    bm = lambda m: ((i[:, None] // m) == (i[None, :] // m)).astype(np.float32)
    c[:, C_BM16:C_BM16 + 128] = bm(16)
    c[:, C_OFF32:C_OFF32 + 128] = bm(32) - bm(16)
    c[:, C_OFF64:C_OFF64 + 128] = bm(64) - bm(32)
    c[:, C_OFF128:C_OFF128 + 128] = 1.0 - bm(64)
    c[:, C_EPS] = EPS
    c[:, C_ONE] = 1.0
    return c


def _t5_bucket(n):
    n = np.asarray(n)
    nf = np.maximum(n, 1).astype(np.float32)
    large = 16 + (np.log(nf / np.float32(16)) / np.float32(math.log(128 / 16)) * np.float32(16)).astype(np.int32)
    large = np.minimum(large, 31)
    return np.where(n < 16, n, large)


def _pack_pp(inp):
    pp = np.zeros((DEPTH, 128, NPP), np.float32)
    p = np.arange(128)
    for l in range(DEPTH):
        cw = inp["gdn_conv_w"][l]
        pp[l, :, PP_CONVW:PP_CONVW + 48] = cw.reshape(4, 12, 128).transpose(2, 1, 0).reshape(128, 48)
        pp[l, :, PP_DTB:PP_DTB + 4] = inp["gdn_dt_bias"][l][None, :]
        pp[l, :, PP_ALOG:PP_ALOG + 4] = inp["gdn_a_log"][l][None, :]
        pp[l, :, PP_ONORM:PP_ONORM + 128] = inp["gdn_out_norm"][l][None, :]
        pp[l, :, PP_QN] = inp["diff_q_norm"][l][p % 64]
        pp[l, :, PP_KN] = inp["diff_k_norm"][l][p % 64]
        pp[l, :, PP_SUBLN] = inp["diff_subln"][l]
        fw = inp["ffn_conv_w"][l]
        pp[l, :, PP_FCW:PP_FCW + 132] = fw.reshape(3, 44, 128).transpose(2, 1, 0).reshape(128, 132)
        pp[l, :, PP_FCB:PP_FCB + 44] = inp["ffn_conv_b"][l].reshape(44, 128).T
        pp[l, :, PP_NMIX:PP_NMIX + 8] = inp["norm_mix"][l].reshape(8, 128).T
        pp[l, :, PP_NFFN:PP_NFFN + 8] = inp["norm_ffn"][l].reshape(8, 128).T
    return pp


def build(nlayers=DEPTH, upto="G", debug=False):
    nc = bass.Bass("TRN2", target_bir_lowering=False)
    dk = "ExternalOutput" if debug else "Internal"

    def din(name, shape, dt=F32):
        return nc.dram_tensor(name, list(shape), dt, kind="ExternalInput").ap()

    def dscr(name, shape, dt, dbg=True):
        return nc.dram_tensor(name, list(shape), dt, kind=(dk if dbg else "Internal")).ap()

    x_in = din("x", [2, T, DM])
    cT_in = din("cT", [128, 8, 2])
    consts_in = din("consts", [128, NCONST])
    pp_in = din("pp", [DEPTH, 128, NPP])
    lam_in = din("lamv", [DEPTH, 4, 64])
    tb_in = din("tb", [4, 128, 1024])
    b31_in = din("b31", [128, 4])
    w_ada = din("w_ada", [DEPTH, DM, 6 * DM])
    b_ada = din("b_ada", [DEPTH, 6 * DM])
    w_in = din("w_in", [DEPTH, DM, 3592])
    w_out = din("w_out", [DEPTH, DM, DM])
    ffn_up = din("ffn_up", [DEPTH, DM, 2 * DFF])
    ffn_down = din("ffn_down", [DEPTH, DFF, DM])
    out = nc.dram_tensor("out", [2, T, DM], F32, kind="ExternalOutput").ap()

    MOD = dscr("MOD", [DEPTH, 2, 6 * DM], F32)
    XA = dscr("XA", [2, T, DM], F32)
    XB = dscr("XB", [2, T, DM], F32, dbg=False)
    NCH = T // 128
    KT = dscr("KT", [2, NCH, 128, 4, 128], BF16)
    QGT = dscr("QGT", [2, NCH, 128, 4, 128], BF16)
    QT = dscr("QT", [2, NCH, 128, 4, 128], BF16)
    KBT = dscr("KBT", [2, NCH, 128, 4, 128], BF16)
    KBG = dscr("KBG", [2, NCH, 128, 4, 128], BF16)
    KDEC = dscr("KDEC", [2, NCH, 128, 4, 128], BF16)
    VB = dscr("VB", [2, NCH, 128, 4, 128], BF16)
    DEC = dscr("DEC", [2, NCH, 128, 4, 128], F32)
    EGL = dscr("EGL", [2, NCH, 128, 4], F32)
    GATE = dscr("GATE", [2, NCH, 128, 512], BF16)
    DQT = dscr("DQT", [2, 4, 128, T], BF16)
    DKT = dscr("DKT", [2, 4, 128, T], BF16)
    DV = dscr("DV", [2, NCH, 128, 512], BF16)
    OT = dscr("OT", [2, 8, 128, T], BF16)
    GTD = dscr("GTD", [2, 22, 128, T], BF16, dbg=False)

    DBG = {}
    if debug:
        for nm in ("Y", "X", "Q", "U", "T1", "X1", "Y1b", "S0", "U1", "PW1", "O1", "VN0"):
            DBG[nm] = dscr("DBG_" + nm, [128, 4, 128], F32)

    with ExitStack() as es0:
        S = Sched(nc, es0)
        G = Phase(S, "glob")
        G.__enter__()
        cst = G.sb("cst", [128, NCONST], F32)
        identB = G.sb("identB", [128, 128], BF16)
        onesB = G.sb("onesB", [128, 128], BF16)
        S.dma("sync", lambda e: e.dma_start(out=cst[:], in_=consts_in[:, :]), cst, writes=[cst])
        S.op("vector", lambda e: e.tensor_copy(identB[:], cst[:, C_IDENT:C_IDENT + 128]), reads=[cst], writes=[identB])
        S.op("vector", lambda e: e.tensor_copy(onesB[:], cst[:, C_ONES:C_ONES + 128]), reads=[cst], writes=[onesB])
        identF = lambda: cst[:, C_IDENT:C_IDENT + 128]
        TRI = lambda: cst[:, C_TRI:C_TRI + 128]
        ONESF = lambda: cst[:, C_ONES:C_ONES + 128]
        epsc = lambda: cst[:, C_EPS:C_EPS + 1]
        onec = lambda: cst[:, C_ONE:C_ONE + 1]

        with Phase(S, "A") as P:
            cT = P.sb("cT", [128, 8, 2], F32)
            S.dma("sync", lambda e: e.dma_start(out=cT[:], in_=cT_in[:, :, :]), cT, writes=[cT])
            S.op("scalar", lambda e: e.activation(cT[:], cT[:], AF.Silu), reads=[cT], writes=[cT])
            wts = [P.sb("wa%d" % i, [128, 8, 512], F32) for i in range(3)]
            pM = [P.ps("pM%d" % i, [2, 512], F32) for i in range(2)]
            k = 0
            bada = P.sb("bada", [2, 6 * DM], F32)
            modt = P.sb("modt", [2, 6 * DM], F32)
            for l in range(nlayers):
                S.dma("sync", lambda e, l=l, bada=bada: e.dma_start(out=bada[:], in_=b_ada[l, :].partition_broadcast(2)), bada, writes=[bada])
                for fb in range(12):
                    wt = wts[k % 3]
                    pm = pM[k % 2]
                    k += 1
                    S.dma("sync", lambda e, l=l, fb=fb, wt=wt: e.dma_start(
                        out=wt[:], in_=w_ada[l, :, fb * 512:(fb + 1) * 512].rearrange("(c p) f -> p c f", p=128)), wt, writes=[wt])
                    for kc in range(8):
                        S.op("tensor", lambda e, kc=kc, wt=wt, pm=pm: e.matmul(pm[:], lhsT=cT[:, kc, :], rhs=wt[:, kc, :],
                                                                              start=(kc == 0), stop=(kc == 7)),
                             reads=[cT, wt], writes=[pm])
                    S.op("vector", lambda e, fb=fb, pm=pm, modt=modt, bada=bada: e.tensor_tensor(
                        modt[:, fb * 512:(fb + 1) * 512], pm[:], bada[:, fb * 512:(fb + 1) * 512], op=ALU.add),
                        reads=[pm, bada], writes=[modt])
                S.dma("sync", lambda e, l=l, modt=modt: e.dma_start(out=MOD[l, :, :], in_=modt[:]), modt, reads=[modt])

        def load_mod_cols(P, l, b, seg, name):
            t = P.sb(name, [128, 8], F32)
            S.dma("sync", lambda e: e.dma_start(out=t[:], in_=MOD[l, b, seg * DM:(seg + 1) * DM].rearrange("(c p) -> p c", p=128),
                                                allow_slow_non_contiguous=True), t, writes=[t])
            return t

        def make_AB(P, l, b, pp, ncol, seg_sh, seg_sc, name):
            sh = load_mod_cols(P, l, b, seg_sh, name + "sh")
            sc = load_mod_cols(P, l, b, seg_sc, name + "sc")
            A = P.sb(name + "A", [128, 8], F32)
            S.op("vector", lambda e: e.scalar_tensor_tensor(A[:], in0=sc[:], scalar=1.0, in1=pp[:, ncol:ncol + 8],
                                                            op0=ALU.add, op1=ALU.mult), reads=[sc, pp], writes=[A])
            return A, sh

        def norm_to_hT(P, xt, nsub, A, Bsh, hT, tmps, pT, W):
            ss, rs, sq, xn, tmpf = tmps
            for sub in range(nsub):
                S.op("scalar", lambda e, sub=sub: e.activation(sq[:], xt[:, sub, :], AF.Square, scale=1.0 / 32,
                                                                accum_out=ss[:, sub:sub + 1]), reads=[xt], writes=[sq, ss])
            S.op("scalar", lambda e: e.activation(rs[:, 0:nsub], ss[:, 0:nsub], AF.Ln, bias=epsc()), reads=[ss, cst], writes=[rs])
            S.op("scalar", lambda e: e.activation(rs[:, 0:nsub], rs[:, 0:nsub], AF.Exp, scale=-0.5), reads=[rs], writes=[rs])
            for sub in range(nsub):
                S.op("vector", lambda e, sub=sub: e.tensor_scalar_mul(xn[:, sub, :], xt[:, sub, :], rs[:, sub:sub + 1]),
                     reads=[xt, rs], writes=[xn])
                for c in range(8):
                    S.op("tensor", lambda e, sub=sub, c=c: e.transpose(pT[:, c, :], xn[:, sub, c * 128:(c + 1) * 128], identB[:]),
                         reads=[xn, identB], writes=[pT])
                S.op("vector", lambda e: e.tensor_tensor(tmpf[:], pT[:], A[:, :, None].to_broadcast([128, 8, 128]), op=ALU.mult),
                     reads=[pT, A], writes=[tmpf])
                S.op("gpsimd", lambda e, sub=sub: e.tensor_tensor(hT[:, :, sub * 128:(sub + 1) * 128], tmpf[:],
                                                                   Bsh[:, :, None].to_broadcast([128, 8, 128]), op=ALU.add),
                     reads=[tmpf, Bsh], writes=[hT])

        def load_w_bf16(P, name, src_ap, kc, ncols):
            t = P.sb(name, [128, kc, ncols], BF16)
            step = max(1, 4096 // ncols)
            for c0 in range(0, kc, step):
                c1 = min(kc, c0 + step)
                S.dma("gpsimd", lambda e, c0=c0, c1=c1: e.dma_start(
                    out=t[:, c0:c1, :], in_=src_ap[c0 * 128:c1 * 128, :].rearrange("(c p) f -> p c f", p=128)), t, writes=[t])
            return t

        xsrc = x_in
        for l in range(nlayers):
            last = (l == nlayers - 1)
            xdst = out if last else XB
            with Phase(S, "C%d" % l) as P:
                pp = P.sb("pp", [128, NPP], F32)
                S.dma("sync", lambda e, l=l: e.dma_start(out=pp[:], in_=pp_in[l, :, :]), pp, writes=[pp])
                Wqkv = load_w_bf16(P, "Wqkv", w_in[l, :, 0:1536], 8, 1536)
                Wgate = load_w_bf16(P, "Wgate", w_in[l, :, 1536:2048], 8, 512)
                Wba = load_w_bf16(P, "Wba", w_in[l, :, 2048:2056], 8, 8)
                Wdqk = load_w_bf16(P, "Wdqk", w_in[l, :, 2056:3080], 8, 1024)
                Wdv = load_w_bf16(P, "Wdv", w_in[l, :, 3080:3592], 8, 512)
                negA = P.sb("negA", [128, 4], F32)
                S.op("scalar", lambda e: e.activation(negA[:], pp[:, PP_ALOG:PP_ALOG + 4], AF.Exp), reads=[pp], writes=[negA])
                S.op("vector", lambda e: e.tensor_scalar_mul(negA[:], negA[:], -1.0), reads=[negA], writes=[negA])
                qgain = P.sb("qgain", [128, 1], F32)
                S.op("vector", lambda e: e.tensor_scalar_mul(qgain[:], pp[:, PP_QN:PP_QN + 1], 0.125), reads=[pp], writes=[qgain])
                xts = [P.sb("xt%d" % i, [128, 4, DM], F32) for i in range(2)]
                ss = P.sb("ss", [128, 4], F32); rs = P.sb("rs", [128, 4], F32)
                sq = P.sb("sq", [128, DM], F32); xn = P.sb("xn", [128, 4, DM], BF16)
                tmpf = P.sb("tmpf", [128, 8, 128], F32)
                hT = P.sb("hT", [128, 8, 512], BF16)
                pT = P.ps("pT", [128, 8, 128], BF16)
                pU = [P.ps("pU%d" % i, [128, 512], F32) for i in range(2)]
                pS = P.ps("pS", [128, 512], F32)
                pSm = P.ps("pSm", [128, 32, 16], F32)
                pD = P.ps("pD", [128, 4, 128], F32)
                pTr = P.ps("pTr", [128, 8, 128], BF16)
                pTok = P.ps("pTok", [128, 512], F32)
                upad = [P.sb("upad%d" % i, [128, 515], F32) for i in range(2)]
                acc = [P.sb("acc%d" % i, [128, 512], F32) for i in range(2)]
                act = [P.sb("act%d" % i, [128, 512], F32) for i in range(2)]
                sqq = [P.sb("sqq%d" % i, [128, 512], F32) for i in range(2)]
                rinv = [P.sb("rinv%d" % i, [128, 512], F32) for i in range(2)]
                halo = P.sb("halo", [128, 12, 3], F32)
                kT = P.sb("kT", [128, 4, 4, 128], BF16)
                qT = P.sb("qT", [128, 4, 4, 128], BF16)
                vT = P.sb("vT", [128, 4, 4, 128], BF16)
                kbT = P.sb("kbT", [128, 4, 4, 128], BF16)
                qgT = P.sb("qgT", [128, 4, 4, 128], BF16)
                kbg = P.sb("kbg", [128, 4, 4, 128], BF16)
                kdec = P.sb("kdec", [128, 4, 4, 128], BF16)
                kb = P.sb("kb", [128, 4, 128], BF16)
                qg = P.sb("qg", [128, 4, 128], BF16)
                vb = P.sb("vb", [128, 4, 4, 128], BF16)
                dec = P.sb("dec", [128, 4, 4, 128], F32)
                gatet = P.sb("gatet", [128, 4, 512], BF16)
                dvt = P.sb("dvt", [128, 4, 512], BF16)
                dqkT = P.sb("dqkT", [128, 8, 512], BF16)
                braw = P.sb("braw", [128, 4, 8], F32)
                beta = P.sb("beta", [128, 4, 4], F32)
                gg = P.sb("gg", [128, 4, 4], F32)
                gcl = P.sb("gcl", [128, 4, 8], F32)
                ngc = P.sb("ngc", [128, 4, 4], F32)
                egc = P.sb("egc", [128, 4, 4], F32)
                bgs = P.sb("bgs", [128, 4, 4], F32)
                qsc = P.sb("qsc", [128, 4, 4], F32)
                kds = P.sb("kds", [128, 4, 4], F32)
                egl = P.sb("egl", [128, 4, 4], F32)
                gbc = P.sb("gbc", [128, 4, 128], F32)

                blocks = [(s, blk) for s in range(2) for blk in range(8)]

                def load_x(i):
                    s, blk = blocks[i]
                    xt = xts[i % 2]
                    S.dma("sync", lambda e: e.dma_start(
                        out=xt[:], in_=xsrc[s, blk * 512:(blk + 1) * 512, :].rearrange("(a p) d -> p a d", p=128)), xt, writes=[xt])

                load_x(0)
                AB = {}
                for b in range(2):
                    AB[b] = make_AB(P, l, b, pp, PP_NMIX, 0, 1, "mix%d" % b)
                cc = 0
                for i, (s, blk) in enumerate(blocks):
                    xt = xts[i % 2]
                    if i + 1 < len(blocks):
                        load_x(i + 1)
                    A, Bsh = AB[s]
                    if blk == 0:
                        S.op("gpsimd", lambda e: e.memset(halo[:], 0.0), writes=[halo])
                    norm_to_hT(P, xt, 4, A, Bsh, hT, (ss, rs, sq, xn, tmpf), pT, None)

                    for sub in range(4):
                        for kc in range(8):
                            S.op("tensor", lambda e, sub=sub, kc=kc: e.matmul(pSm[:, sub, 0:8], lhsT=hT[:, kc, sub * 128:(sub + 1) * 128],
                                                                            rhs=Wba[:, kc, :], start=(kc == 0), stop=(kc == 7)),
                                 reads=[hT, Wba], writes=[pSm])
                    S.op("vector", lambda e: e.tensor_copy(braw[:], pSm[:, 0:4, 0:8]), reads=[pSm], writes=[braw])
                    S.op("scalar", lambda e: e.activation(beta[:], braw[:, :, 0:4], AF.Sigmoid), reads=[braw], writes=[beta])
                    S.op("vector", lambda e: e.tensor_tensor(gg[:], braw[:, :, 4:8], pp[:, None, PP_DTB:PP_DTB + 4].to_broadcast([128, 4, 4]),
                                                             op=ALU.add), reads=[braw, pp], writes=[gg])
                    S.op("scalar", lambda e: e.activation(gg[:], gg[:], AF.Exp), reads=[gg], writes=[gg])
                    S.op("scalar", lambda e: e.activation(gg[:], gg[:], AF.Ln, bias=onec()), reads=[gg, cst], writes=[gg])
                    S.op("vector", lambda e: e.tensor_tensor(gg[:], gg[:], negA[:, None, :].to_broadcast([128, 4, 4]), op=ALU.mult),
                         reads=[gg, negA], writes=[gg])
                    pending = []
                    for fc in range(12):
                        pu = pU[cc % 2]; up = upad[cc % 2]; ac = acc[cc % 2]; at = act[cc % 2]
                        sqt = sqq[cc % 2]; rv = rinv[cc % 2]
                        cc += 1
                        for kc in range(8):
                            S.op("tensor", lambda e, kc=kc, fc=fc, pu=pu: e.matmul(pu[:], lhsT=Wqkv[:, kc, fc * 128:(fc + 1) * 128],
                                                                                    rhs=hT[:, kc, :], start=(kc == 0), stop=(kc == 7)),
                                 reads=[Wqkv, hT], writes=[pu])
                        while pending:
                            pending.pop(0)()
                        S.op("vector", lambda e, fc=fc, up=up: e.tensor_copy(up[:, 0:3], halo[:, fc, :]), reads=[halo], writes=[up])
                        S.op("scalar", lambda e, up=up, pu=pu: e.copy(up[:, 3:515], pu[:]), reads=[pu], writes=[up])
                        S.op("gpsimd", lambda e, fc=fc, up=up: e.tensor_copy(halo[:, fc, :], up[:, 512:515]), reads=[up], writes=[halo])
                        cw = lambda j, fc=fc: pp[:, PP_CONVW + fc * 4 + j:PP_CONVW + fc * 4 + j + 1]
                        S.op("vector", lambda e, up=up, ac=ac, cw=cw: e.tensor_scalar_mul(ac[:], up[:, 3:515], cw(3)), reads=[up, pp], writes=[ac])
                        for j in (2, 1, 0):
                            S.op("vector",
                                 lambda e, up=up, ac=ac, cw=cw, j=j: e.scalar_tensor_tensor(ac[:], in0=up[:, j:j + 512], scalar=cw(j), in1=ac[:],
                                                                                           op0=ALU.mult, op1=ALU.add),
                                 reads=[up, pp, ac], writes=[ac])
                        kind, h = fc // 4, fc % 4
                        if kind == 2:
                            S.op("scalar", lambda e, ac=ac, h=h: e.activation(vT[:, :, h, :], ac[:].rearrange("p (a t) -> p a t", a=4), AF.Silu),
                                 reads=[ac], writes=[vT])
                        else:
                            dst = qT if kind == 0 else kT
                            S.op("scalar", lambda e, ac=ac, at=at: e.activation(at[:], ac[:], AF.Silu), reads=[ac], writes=[at])
                            S.op("gpsimd", lambda e, at=at, sqt=sqt: e.tensor_tensor(sqt[:], at[:], at[:], op=ALU.mult), reads=[at], writes=[sqt])

                            def fin(sqt=sqt, rv=rv, at=at, dst=dst, h=h):
                                S.op("tensor", lambda e: e.matmul(pS[:], lhsT=ONESF(), rhs=sqt[:], start=True, stop=True),
                                     reads=[cst, sqt], writes=[pS])
                                S.op("scalar", lambda e: e.activation(rv[:], pS[:], AF.Ln, bias=epsc()), reads=[pS, cst], writes=[rv])
                                S.op("scalar", lambda e: e.activation(rv[:], rv[:], AF.Exp, scale=-0.5), reads=[rv], writes=[rv])
                                S.op("vector", lambda e: e.tensor_tensor(
                                    dst[:, :, h, :], at[:].rearrange("p (a t) -> p a t", a=4), rv[:].rearrange("p (a t) -> p a t", a=4), op=ALU.mult),
                                    reads=[at, rv], writes=[dst])
                            pending.append(fin)

                    while pending:
                        pending.pop(0)()
                    for sub in range(4):
                        S.op("tensor", lambda e, sub=sub: e.matmul(pSm[:, sub, 8:12], lhsT=TRI(), rhs=gg[:, sub, :], start=True, stop=True),
                             reads=[cst, gg], writes=[pSm])
                        S.op("tensor", lambda e, sub=sub: e.matmul(pSm[:, sub, 12:16], lhsT=ONESF(), rhs=gg[:, sub, :], start=True, stop=True),
                             reads=[cst, gg], writes=[pSm])
                    S.op("vector", lambda e: e.tensor_copy(gcl[:], pSm[:, 0:4, 8:16]), reads=[pSm], writes=[gcl])
                    S.op("vector", lambda e: e.tensor_scalar_mul(ngc[:], gcl[:, :, 0:4], -1.0), reads=[gcl], writes=[ngc])
                    S.op("scalar", lambda e: e.activation(egc[:], gcl[:, :, 0:4], AF.Exp), reads=[gcl], writes=[egc])
                    S.op("scalar", lambda e: e.activation(egl[:], gcl[:, :, 4:8], AF.Exp), reads=[gcl], writes=[egl])
                    S.op("vector", lambda e: e.tensor_tensor(kds[:], gcl[:, :, 4:8], gcl[:, :, 0:4], op=ALU.subtract), reads=[gcl], writes=[kds])
                    S.op("scalar", lambda e: e.activation(kds[:], kds[:], AF.Exp), reads=[kds], writes=[kds])
                    S.op("vector", lambda e: e.tensor_tensor(bgs[:], beta[:], egc[:], op=ALU.mult), reads=[beta, egc], writes=[bgs])
                    S.op("vector", lambda e: e.tensor_scalar_mul(qsc[:], egc[:], 128.0 ** -0.5), reads=[egc], writes=[qsc])
                    S.dma("sync", lambda e, s=s, blk=blk: e.dma_start(out=EGL[s, blk * 4:(blk + 1) * 4, :, :].rearrange("n p h -> p n h"),
                                                                       in_=egl[:]), egl, reads=[egl])
                    for sub in range(4):
                        for h in range(4):
                            S.op("vector", lambda e, sub=sub, h=h: e.tensor_copy(gbc[:, h, :], gg[:, sub, h:h + 1].to_broadcast([128, 128])),
                                 reads=[gg], writes=[gbc])
                        for h in range(4):
                            S.op("tensor", lambda e, h=h: e.matmul(pD[:, h, :], lhsT=gbc[:, h, :], rhs=TRI(), start=True, stop=False),
                                 reads=[gbc, cst], writes=[pD])
                            S.op("tensor", lambda e, h=h: e.matmul(pD[:, h, :], lhsT=identF(), rhs=cst[:, C_MASKNEG:C_MASKNEG + 128],
                                                                   start=False, stop=True), reads=[cst], writes=[pD])
                        for h in range(4):
                            S.op("scalar", lambda e, sub=sub, h=h: e.activation(dec[:, sub, h, :], pD[:, h, :], AF.Exp,
                                                                                  bias=ngc[:, sub, h:h + 1]),
                                 reads=[pD, ngc], writes=[dec])
                    S.dma("sync", lambda e, s=s, blk=blk: e.dma_start(
                        out=DEC[s, blk * 4:(blk + 1) * 4, :, :, :].rearrange("n p h c -> p n h c"), in_=dec[:]), dec, reads=[dec])

                    for fc in range(8):
                        pu = pU[cc % 2]; sqt = sqq[cc % 2]; rv = rinv[cc % 2]
                        cc += 1
                        for kc in range(8):
                            S.op("tensor", lambda e, kc=kc, fc=fc, pu=pu: e.matmul(pu[:], lhsT=Wdqk[:, kc, fc * 128:(fc + 1) * 128],
                                                                                    rhs=hT[:, kc, :], start=(kc == 0), stop=(kc == 7)),
                                 reads=[Wdqk, hT], writes=[pu])
                        while pending:
                            pending.pop(0)()
                        S.op("scalar", lambda e, pu=pu, sqt=sqt: e.activation(sqt[:], pu[:], AF.Square), reads=[pu], writes=[sqt])
                        gain = qgain[:, 0:1] if fc < 4 else pp[:, PP_KN:PP_KN + 1]

                        def fin2(pu=pu, sqt=sqt, rv=rv, fc=fc, gain=gain):
                            S.op("tensor", lambda e: e.matmul(pS[:], lhsT=cst[:, C_BLK64:C_BLK64 + 128], rhs=sqt[:], start=True, stop=True),
                                 reads=[cst, sqt], writes=[pS])
                            S.op("scalar", lambda e: e.activation(rv[:], pS[:], AF.Ln, bias=epsc(), scale=1.0 / 64), reads=[pS, cst], writes=[rv])
                            S.op("scalar", lambda e: e.activation(rv[:], rv[:], AF.Exp, scale=-0.5), reads=[rv], writes=[rv])
                            S.op("vector", lambda e: e.scalar_tensor_tensor(
                                dqkT[:, fc, :], in0=pu[:], scalar=gain, in1=rv[:], op0=ALU.mult, op1=ALU.mult),
                                reads=[pu, rv, qgain, pp], writes=[dqkT])
                        pending.append(fin2)
                    while pending:
                        pending.pop(0)()
                    S.dma("sync", lambda e, s=s, blk=blk: e.dma_start(out=DQT[s, :, :, blk * 512:(blk + 1) * 512].rearrange("h p t -> p h t"),
                                                                       in_=dqkT[:, 0:4, :]), dqkT, reads=[dqkT])
                    S.dma("sync", lambda e, s=s, blk=blk: e.dma_start(out=DKT[s, :, :, blk * 512:(blk + 1) * 512].rearrange("h p t -> p h t"),
                                                                       in_=dqkT[:, 4:8, :]), dqkT, reads=[dqkT])

                    for W, dstt, fn in ((Wgate, gatet, AF.Silu), (Wdv, dvt, AF.Copy)):
                        for sub in range(4):
                            for kc in range(8):
                                S.op("tensor", lambda e, sub=sub, kc=kc, W=W: e.matmul(pTok[:], lhsT=hT[:, kc, sub * 128:(sub + 1) * 128],
                                                                                        rhs=W[:, kc, :], start=(kc == 0), stop=(kc == 7)),
                                     reads=[hT, W], writes=[pTok])
                            S.op("scalar", lambda e, sub=sub, dstt=dstt, fn=fn: e.activation(dstt[:, sub, :], pTok[:], fn),
                                 reads=[pTok], writes=[dstt])
                        dd = GATE if dstt is gatet else DV
                        S.dma("sync", lambda e, dd=dd, dstt=dstt, s=s, blk=blk: e.dma_start(
                            out=dd[s, blk * 4:(blk + 1) * 4, :, :].rearrange("n p f -> p n f"), in_=dstt[:]), dstt, reads=[dstt])
                    for sub in range(4):
                        for h in range(4):
                            S.op("tensor", lambda e, sub=sub, h=h: e.transpose(pTr[:, h, :], kT[:, sub, h, :], identB[:]),
                                 reads=[kT, identB], writes=[pTr])
                        S.op("vector", lambda e, sub=sub: e.tensor_tensor(kbg[:, sub, :, :], pTr[:, 0:4, :], bgs[:, sub, :, None].to_broadcast([128, 4, 128]),
                                                                          op=ALU.mult), reads=[pTr, bgs], writes=[kbg])
                        S.op("vector", lambda e, sub=sub: e.tensor_tensor(kdec[:, sub, :, :], pTr[:, 0:4, :], kds[:, sub, :, None].to_broadcast([128, 4, 128]),
                                                                          op=ALU.mult), reads=[pTr, kds], writes=[kdec])
                        S.op("vector", lambda e, sub=sub: e.tensor_tensor(kb[:], pTr[:, 0:4, :], beta[:, sub, :, None].to_broadcast([128, 4, 128]),
                                                                          op=ALU.mult), reads=[pTr, beta], writes=[kb])
                        for h in range(4):
                            S.op("tensor", lambda e, h=h: e.transpose(pTr[:, h, :], kb[:, h, :], identB[:]), reads=[kb, identB], writes=[pTr])
                        S.op("scalar", lambda e, sub=sub: e.copy(kbT[:, sub, :, :], pTr[:, 0:4, :]), reads=[pTr], writes=[kbT])
                        for h in range(4):
                            S.op("tensor", lambda e, sub=sub, h=h: e.transpose(pTr[:, h, :], qT[:, sub, h, :], identB[:]),
                                 reads=[qT, identB], writes=[pTr])
                        S.op("vector", lambda e, sub=sub: e.tensor_tensor(qg[:], pTr[:, 0:4, :], qsc[:, sub, :, None].to_broadcast([128, 4, 128]),
                                                                          op=ALU.mult), reads=[pTr, qsc], writes=[qg])
                        for h in range(4):
                            S.op("tensor", lambda e, h=h: e.transpose(pTr[:, h, :], qg[:, h, :], identB[:]), reads=[qg, identB], writes=[pTr])
                        S.op("scalar", lambda e, sub=sub: e.copy(qgT[:, sub, :, :], pTr[:, 0:4, :]), reads=[pTr], writes=[qgT])
                        for h in range(4):
                            S.op("tensor", lambda e, sub=sub, h=h: e.transpose(pTr[:, h, :], vT[:, sub, h, :], identB[:]),
                                 reads=[vT, identB], writes=[pTr])
                        S.op("vector", lambda e, sub=sub: e.tensor_tensor(vb[:, sub, :, :], pTr[:, 0:4, :], beta[:, sub, :, None].to_broadcast([128, 4, 128]),
                                                                          op=ALU.mult), reads=[pTr, beta], writes=[vb])
                    for dst, src in ((KT, kT), (QT, qT), (QGT, qgT), (KBT, kbT), (KBG, kbg), (KDEC, kdec), (VB, vb)):
                        S.dma("sync", lambda e, dst=dst, src=src, s=s, blk=blk: e.dma_start(
                            out=dst[s, blk * 4:(blk + 1) * 4, :, :, :].rearrange("n p h t -> p n h t"), in_=src[:]), src, reads=[src])

            if upto == "C":
                break
            with Phase(S, "D%d" % l) as P:
                pp = P.sb("pp", [128, NPP], F32)
                S.dma("sync", lambda e, l=l: e.dma_start(out=pp[:], in_=pp_in[l, :, :]), pp, writes=[pp])
                NSLOT = 3
                names = ("kT", "qT", "qgT", "kbT", "kbg", "kdec", "vb")
                srcs = dict(kT=KT, qT=QT, qgT=QGT, kbT=KBT, kbg=KBG, kdec=KDEC, vb=VB)
                slots = {}
                for s in range(2):
                    for k in range(NSLOT):
                        d_ = {nm: P.sb("%s_%d_%d" % (nm, s, k), [128, 4, 128], BF16) for nm in names}
                        d_["dec"] = P.sb("dec_%d_%d" % (s, k), [128, 4, 128], F32)
                        d_["egl"] = P.sb("egl_%d_%d" % (s, k), [128, 4], F32)
                        d_["gate"] = P.sb("gate_%d_%d" % (s, k), [128, 4, 128], BF16)
                        slots[(s, k)] = d_
                strict = cst[:, None, C_STRICT:C_STRICT + 128].to_broadcast([128, 4, 128])
                identq = cst[:, None, C_IDENT:C_IDENT + 128].to_broadcast([128, 4, 128])
                onormb = pp[:, None, PP_ONORM:PP_ONORM + 128].to_broadcast([128, 4, 128])
                pre = [P.ps("pre%d" % i, [128, 4, 128], F32) for i in range(2)]
                pinv = [P.ps("pinv%d" % i, [128, 4, 128], F32) for i in range(2)]
                pW = P.ps("pW", [128, 4, 128], F32)
                pO = P.ps("pO", [128, 4, 128], F32)
                pSt = P.ps("pSt", [128, 4, 128], F32)
                pTr = P.ps("pTr", [128, 8, 128], BF16)
                st = {}
                for s in range(2):
                    st[s] = dict(
                        Y=[P.sb("Y%d_%d" % (s, i), [128, 4, 128], F32) for i in range(2)],
                        X=[P.sb("X%d_%d" % (s, i), [128, 4, 128], F32) for i in range(2)],
                        Q=P.sb("Q%d" % s, [128, 4, 128], F32), Qb=P.sb("Qb%d" % s, [128, 4, 128], BF16),
                        P=P.sb("Pm%d" % s, [128, 4, 128], F32), Yf=P.sb("Yf%d" % s, [128, 4, 128], F32), Xf=P.sb("Xf%d" % s, [128, 4, 128], F32),
                        t1=P.sb("t1%d" % s, [128, 4, 128], F32),
                        aT=P.sb("aT%d" % s, [128, 4, 128], BF16), u=P.sb("u%d" % s, [128, 4, 128], F32),
                        wT=P.sb("wT%d" % s, [128, 4, 128], BF16), vn=P.sb("vn%d" % s, [128, 4, 128], BF16),
                        Sf=P.sb("Sf%d" % s, [128, 4, 128], F32), Sb=P.sb("Sb%d" % s, [128, 4, 128], BF16),
                        osq=P.sb("osq%d" % s, [128, 4, 128], F32), oss=P.sb("oss%d" % s, [128, 4], F32),
                        o1=P.sb("o1%d" % s, [128, 4, 128], F32), g2=P.sb("g2%d" % s, [128, 4, 128], F32),
                        y=P.sb("y%d" % s, [128, 4, 128], BF16),
                        oT=P.sb("oT%d" % s, [128, 4, 512], BF16),
                    )
                    S.op("gpsimd", lambda e, s=s: e.memset(st[s]["Sf"][:], 0.0), writes=[st[s]["Sf"]])
                    S.op("gpsimd", lambda e, s=s: e.memset(st[s]["Sb"][:], 0.0), writes=[st[s]["Sb"]])

                def dbg(nm, tile, s, n, nn=0):
                    if debug and s == 0 and n == nn and l == 0:
                        S.dma("sync", lambda e: e.dma_start(out=DBG[nm][:, :, :], in_=tile[:]), tile, reads=[tile])

                def loadD(s, n):
                    sl = slots[(s, n % NSLOT)]
                    for nm in names:
                        S.dma("sync", lambda e, nm=nm, sl=sl: e.dma_start(out=sl[nm][:], in_=srcs[nm][s, n, :, :, :]), sl[nm], writes=[sl[nm]])
                    S.dma("sync", lambda e, sl=sl: e.dma_start(out=sl["dec"][:], in_=DEC[s, n, :, :, :]), sl["dec"], writes=[sl["dec"]])
                    S.dma("sync", lambda e, sl=sl: e.dma_start(out=sl["egl"][:], in_=EGL[s, n, :, :]), sl["egl"], writes=[sl["egl"]])
                    S.dma("sync", lambda e, sl=sl: e.dma_start(out=sl["gate"][:], in_=GATE[s, n, :, :].rearrange("p (h e) -> p h e", h=4)),
                          sl["gate"], writes=[sl["gate"]])

                for s in range(2):
                    loadD(s, 0)
                    loadD(s, 1)
                pk = [0, 0]

                def mm4(out, lhs, rhs, reads, acc=None):
                    for h in range(4):
                        S.op("tensor", lambda e, h=h: e.matmul(out[:, h, :], lhsT=lhs[:, h, :], rhs=rhs[:, h, :],
                                                               start=(acc in (None, "start")), stop=(acc in (None, "stop"))),
                             reads=reads, writes=[out])

                for n in range(NCH):
                    SL = {s: slots[(s, n % NSLOT)] for s in range(2)}
                    if n + 2 < NCH:
                        for s in range(2):
                            loadD(s, n + 2)
                    for s in range(2):
                        sl, q = SL[s], st[s]
                        pg = pre[s]
                        q["pg"] = pg
                        mm4(pg, sl["kT"], sl["kbT"], [sl["kT"], sl["kbT"]])
                    bc = lambda c0: cst[:, None, c0:c0 + 128].to_broadcast([128, 4, 128])
                    for s in range(2):
                        sl, q = SL[s], st[s]
                        pg = q["pg"]
                        S.op("vector", lambda e, q=q, sl=sl, pg=pg: e.scalar_tensor_tensor(q["t1"][:], in0=pg[:], scalar=-1.0, in1=sl["dec"][:],
                                                                                        op0=ALU.mult, op1=ALU.mult), reads=[pg, sl["dec"]], writes=[q["t1"]])
                        S.op("gpsimd", lambda e, q=q: e.tensor_tensor(q["Yf"][:], q["t1"][:], strict, op=ALU.mult),
                             reads=[q["t1"], cst], writes=[q["Yf"]])
                    for s in range(2):
                        q = st[s]
                        for h in range(4):
                            S.op("tensor", lambda e, h=h, q=q, pi=pinv[s]: e.transpose(pi[:, h, :], q["Yf"][:, h, :], identF()),
                                 reads=[q["Yf"], cst], writes=[pinv[s]])
                    for s in range(2):
                        q = st[s]
                        S.op("scalar", lambda e, q=q, pi_=pinv[s]: e.copy(q["Xf"][:], pi_[:]), reads=[pinv[s]], writes=[q["Xf"]])
                        S.op("gpsimd", lambda e, q=q: e.tensor_tensor(q["Y"][0][:], q["Yf"][:], bc(C_BM16), op=ALU.mult), reads=[q["Yf"], cst], writes=[q["Y"][0]])
                        S.op("gpsimd", lambda e, q=q: e.tensor_tensor(q["X"][0][:], q["Xf"][:], bc(C_BM16), op=ALU.mult), reads=[q["Xf"], cst], writes=[q["X"][0]])
                        S.op("gpsimd", lambda e, q=q: e.tensor_tensor(q["Q"][:], q["Y"][0][:], identq, op=ALU.add), reads=[q["Y"][0], cst], writes=[q["Q"]])
                        S.op("gpsimd", lambda e, q=q: e.tensor_tensor(q["P"][:], q["X"][0][:], identq, op=ALU.add), reads=[q["X"][0], cst], writes=[q["P"]])
                    for lev in range(1, 4):
                        a, b = (lev - 1) % 2, lev % 2
                        for s in range(2):
                            q = st[s]
                            mm4(pinv[s], q["Y"][a], q["X"][a], [q["Y"][a], q["X"][a]])
                            mm4(pre[s], q["X"][a], q["Y"][a], [q["Y"][a], q["X"][a]])
                        for s in range(2):
                            q = st[s]
                            S.op("scalar", lambda e, q=q, b=b, px_=pinv[s]: e.copy(q["X"][b][:], px_[:]), reads=[pinv[s]], writes=[q["X"][b]])
                            S.op("vector", lambda e, q=q, b=b, py_=pre[s]: e.tensor_copy(q["Y"][b][:], py_[:]), reads=[pre[s]], writes=[q["Y"][b]])
                        for s in range(2):
                            q = st[s]
                            mm4(pinv[s], q["X"][b], q["Q"], [q["X"][b], q["Q"]])
                            mm4(pre[s], q["Y"][b], q["P"], [q["Y"][b], q["P"]])
                        for s in range(2):
                            q = st[s]
                            S.op("vector", lambda e, q=q, pq_=pinv[s]: e.tensor_tensor(q["Q"][:], q["Q"][:], pq_[:], op=ALU.add),
                                 reads=[q["Q"], pinv[s]], writes=[q["Q"]])
                            S.op("vector", lambda e, q=q, pp_=pre[s]: e.tensor_tensor(q["P"][:], q["P"][:], pp_[:], op=ALU.add),
                                 reads=[q["P"], pre[s]], writes=[q["P"]])
                    for mi, coff in enumerate((C_OFF32, C_OFF64, C_OFF128)):
                        lastm = (mi == 2)
                        for s in range(2):
                            q = st[s]
                            S.op("gpsimd", lambda e, q=q, coff=coff: e.tensor_tensor(q["Y"][0][:], q["Yf"][:], bc(coff), op=ALU.mult), reads=[q["Yf"], cst], writes=[q["Y"][0]])
                            S.op("gpsimd", lambda e, q=q, coff=coff: e.tensor_tensor(q["X"][0][:], q["Xf"][:], bc(coff), op=ALU.mult), reads=[q["Xf"], cst], writes=[q["X"][0]])
                        for s in range(2):
                            q = st[s]
                            mm4(pinv[s], q["X"][0], q["Q"], [q["X"][0], q["Q"]])
                            if not lastm:
                                mm4(pre[s], q["Y"][0], q["P"], [q["Y"][0], q["P"]])
                        for s in range(2):
                            q = st[s]
                            S.op("scalar", lambda e, q=q, p_=pinv[s]: e.copy(q["X"][1][:], p_[:]), reads=[pinv[s]], writes=[q["X"][1]])
                            if not lastm:
                                S.op("vector", lambda e, q=q, p_=pre[s]: e.tensor_copy(q["Y"][1][:], p_[:]), reads=[pre[s]], writes=[q["Y"][1]])
                        for s in range(2):
                            q = st[s]
                            mm4(pinv[s], q["P"], q["X"][1], [q["P"], q["X"][1]])
                            if not lastm:
                                mm4(pre[s], q["Q"], q["Y"][1], [q["Q"], q["Y"][1]])
                        for s in range(2):
                            q = st[s]
                            S.op("vector", lambda e, q=q, p_=pinv[s]: e.tensor_tensor(q["Q"][:], q["Q"][:], p_[:], op=ALU.add),
                                 reads=[q["Q"], pinv[s]], writes=[q["Q"]])
                            if not lastm:
                                S.op("vector", lambda e, q=q, p_=pre[s]: e.tensor_tensor(q["P"][:], q["P"][:], p_[:], op=ALU.add),
                                     reads=[q["P"], pre[s]], writes=[q["P"]])
                    for s in range(2):
                        sl, q = SL[s], st[s]
                        S.op("scalar", lambda e, q=q: e.copy(q["Qb"][:], q["Q"][:]), reads=[q["Q"]], writes=[q["Qb"]])
                        dbg("Q", q["Q"], s, n)
                        pa = pre[s]
                        q["pa"] = pa
                        mm4(pa, sl["kT"], sl["qT"], [sl["kT"], sl["qT"]])
                    for s in range(2):
                        sl, q = SL[s], st[s]
                        S.op("vector", lambda e, q=q, sl=sl, pa_=q["pa"]: e.scalar_tensor_tensor(q["aT"][:], in0=pa_[:], scalar=128.0 ** -0.5, in1=sl["dec"][:],
                                                                                     op0=ALU.mult, op1=ALU.mult), reads=[q["pa"], sl["dec"]], writes=[q["aT"]])
                        pu_ = pinv[s]
                        q["pu"] = pu_
                        mm4(pu_, q["Qb"], sl["vb"], [q["Qb"], sl["vb"]])
                    for s in range(2):
                        sl, q = SL[s], st[s]
                        S.op("scalar", lambda e, q=q, pu_=q["pu"]: e.copy(q["u"][:], pu_[:]), reads=[q["pu"]], writes=[q["u"]])
                        dbg("U", q["u"], s, n)
                        pw_ = pre[s]
                        q["pw"] = pw_
                        mm4(pw_, sl["kbg"], q["Qb"], [q["Qb"], sl["kbg"]])
                    for s in range(2):
                        q = st[s]
                        S.op("scalar", lambda e, q=q, pw_=q["pw"]: e.copy(q["wT"][:], pw_[:]), reads=[q["pw"]], writes=[q["wT"]])
                    for s in range(2):
                        sl, q = SL[s], st[s]
                        mm4(pW, q["wT"], q["Sb"], [q["wT"], q["Sb"]])
                        S.op("vector", lambda e, q=q: e.tensor_tensor(q["vn"][:], q["u"][:], pW[:], op=ALU.subtract),
                             reads=[q["u"], pW], writes=[q["vn"]])
                        if debug and s == 0 and n == 1 and l == 0:
                            S.op("vector", lambda e, q=q: e.tensor_copy(q["o1"][:], pW[:]), reads=[pW], writes=[q["o1"]])
                            dbg("PW1", q["o1"], s, n, 1)
                        for h in range(4):
                            S.op("tensor", lambda e, h=h, q=q, sl=sl: e.matmul(pO[:, h, :], lhsT=sl["qgT"][:, h, :], rhs=q["Sb"][:, h, :], start=True, stop=False),
                                 reads=[sl["qgT"], q["Sb"]], writes=[pO])
                            S.op("tensor", lambda e, h=h, q=q: e.matmul(pO[:, h, :], lhsT=q["aT"][:, h, :], rhs=q["vn"][:, h, :], start=False, stop=True),
                                 reads=[q["aT"], q["vn"]], writes=[pO])
                        mm4(pSt, sl["kdec"], q["vn"], [sl["kdec"], q["vn"]])
                        S.op("gpsimd", lambda e, q=q, sl=sl: e.tensor_tensor(q["Sf"][:], q["Sf"][:], sl["egl"][:, :, None].to_broadcast([128, 4, 128]),
                                                                            op=ALU.mult), reads=[q["Sf"], sl["egl"]], writes=[q["Sf"]])
                        S.op("vector", lambda e, q=q: e.tensor_tensor(q["Sf"][:], q["Sf"][:], pSt[:], op=ALU.add), reads=[q["Sf"], pSt], writes=[q["Sf"]])
                        S.op("scalar", lambda e, q=q: e.copy(q["Sb"][:], q["Sf"][:]), reads=[q["Sf"]], writes=[q["Sb"]])
                        dbg("S0", q["Sf"], s, n)
                        dbg("U1", q["u"], s, n, 1)
                        S.op("scalar", lambda e, q=q: e.activation(q["osq"][:], pO[:], AF.Square), reads=[pO], writes=[q["osq"]])
                        S.op("vector", lambda e, q=q: e.reduce_sum(q["oss"][:], q["osq"][:], axis=mybir.AxisListType.X), reads=[q["osq"]], writes=[q["oss"]])
                        S.op("scalar", lambda e, q=q: e.activation(q["oss"][:], q["oss"][:], AF.Ln, bias=epsc(), scale=1.0 / 128), reads=[q["oss"], cst], writes=[q["oss"]])
                        S.op("scalar", lambda e, q=q: e.activation(q["oss"][:], q["oss"][:], AF.Exp, scale=-0.5), reads=[q["oss"]], writes=[q["oss"]])
                        S.op("vector", lambda e, q=q: e.tensor_tensor(q["o1"][:], pO[:], q["oss"][:, :, None].to_broadcast([128, 4, 128]), op=ALU.mult),
                             reads=[pO, q["oss"]], writes=[q["o1"]])
                        S.op("gpsimd", lambda e, q=q, sl=sl: e.tensor_tensor(q["g2"][:], sl["gate"][:], onormb, op=ALU.mult), reads=[sl["gate"], pp], writes=[q["g2"]])
                        S.op("gpsimd", lambda e, q=q: e.tensor_tensor(q["y"][:], q["o1"][:], q["g2"][:], op=ALU.mult), reads=[q["o1"], q["g2"]], writes=[q["y"]])
                        for h in range(4):
                            S.op("tensor", lambda e, h=h, q=q: e.transpose(pTr[:, h, :], q["y"][:, h, :], identB[:]), reads=[q["y"], identB], writes=[pTr])
                        j = n % 4
                        S.op("scalar", lambda e, q=q, j=j: e.copy(q["oT"][:, :, j * 128:(j + 1) * 128], pTr[:, 0:4, :]), reads=[pTr], writes=[q["oT"]])
                        if j == 3:
                            t0 = (n - 3) * 128
                            S.dma("sync", lambda e, q=q, s=s, t0=t0: e.dma_start(out=OT[s, 0:4, :, t0:t0 + 512].rearrange("h p t -> p h t"), in_=q["oT"][:]),
                                  q["oT"], reads=[q["oT"]])
            if upto == "D":
                break
            with Phase(S, "E%d" % l) as P:
                pp = P.sb("pp", [128, NPP], F32)
                S.dma("sync", lambda e, l=l: e.dma_start(out=pp[:], in_=pp_in[l, :, :]), pp, writes=[pp])
                lam_init = 0.8 - 0.6 * math.exp(-0.3 * l)
                lamr = P.sb("lamr", [1, 4, 64], F32)
                S.dma("sync", lambda e, l=l: e.dma_start(out=lamr[:], in_=lam_in[l:l + 1, :, :]), lamr, writes=[lamr])
                lprod = P.sb("lprod", [1, 2, 64], F32)
                lsum = P.sb("lsum", [1, 2], F32)
                nlam1 = P.sb("nlam1", [1, 1], F32)
                nlam = P.sb("nlam", [128, 1], F32)
                subg = P.sb("subg", [128, 1], F32)
                b31 = P.sb("b31", [128, 4], F32)
                S.dma("sync", lambda e: e.dma_start(out=b31[:], in_=b31_in[:, :]), b31, writes=[b31])
                S.op("vector", lambda e: e.tensor_tensor(lprod[:], lamr[:, 0:4:2, :], lamr[:, 1:4:2, :], op=ALU.mult), reads=[lamr], writes=[lprod])
                S.op("vector", lambda e: e.reduce_sum(lsum[:], lprod[:], axis=mybir.AxisListType.X), reads=[lprod], writes=[lsum])
                S.op("scalar", lambda e: e.activation(lsum[:], lsum[:], AF.Exp), reads=[lsum], writes=[lsum])
                S.op("vector", lambda e: e.scalar_tensor_tensor(nlam1[:], in0=lsum[:, 1:2], scalar=-lam_init, in1=lsum[:, 0:1],
                                                                op0=ALU.add, op1=ALU.subtract), reads=[lsum], writes=[nlam1])
                pS0 = [P.ps("pS0_%d" % i, [128, 512], F32) for i in range(2)]
                pS1 = [P.ps("pS1_%d" % i, [128, 512], F32) for i in range(2)]
                pO0 = P.ps("pO0", [128, 512], F32); pO1 = P.ps("pO1", [128, 512], F32)
                pZ0 = P.ps("pZ0", [128, 512], F32); pZ1 = P.ps("pZ1", [128, 512], F32)
                S.op("tensor", lambda e: e.matmul(pZ0[:, 0:1], lhsT=cst[0:1, C_ONES:C_ONES + 128], rhs=nlam1[:], start=True, stop=True),
                     reads=[cst, nlam1], writes=[pZ0])
                S.op("vector", lambda e: e.tensor_copy(nlam[:], pZ0[:, 0:1]), reads=[pZ0], writes=[nlam])
                S.op("vector", lambda e: e.tensor_scalar_mul(subg[:], pp[:, PP_SUBLN:PP_SUBLN + 1], 1.0 - lam_init), reads=[pp], writes=[subg])
                expB = []
                for h in range(4):
                    t = P.sb("expB%d" % h, [128, 1024], F32)
                    S.dma("sync", lambda e, h=h, t=t: e.dma_start(out=t[:], in_=tb_in[h, :, :]), t, writes=[t])
                    S.op("scalar", lambda e, t=t: e.activation(t[:], t[:], AF.Exp), reads=[t], writes=[t])
                    expB.append(t)
                qts = [P.sb("qt%d" % i, [128, T], BF16) for i in range(2)]
                kts = [P.sb("kt%d" % i, [128, T], BF16) for i in range(2)]
                vts = [P.sb("vt%d" % i, [128, NCH, 128], BF16) for i in range(2)]
                E0 = [P.sb("E0_%d" % i, [128, 512], BF16) for i in range(3)]
                E1 = [P.sb("E1_%d" % i, [128, 512], BF16) for i in range(3)]
                Ef = [P.sb("Ef_%d" % i, [128, 512], F32) for i in range(2)]
                rz0 = P.sb("rz0", [128, 512], F32); rz1 = P.sb("rz1", [128, 512], F32)
                zc0 = P.sb("zc0", [128, 512], F32); zc1 = P.sb("zc1", [128, 512], F32)
                oc0 = P.sb("oc0", [128, 512], F32); oc1 = P.sb("oc1", [128, 512], F32)
                oo = P.sb("oo", [128, 512], F32); osq = P.sb("osq", [128, 512], F32)
                rin = P.sb("rin", [128, 512], F32)
                oTs = [P.sb("oTs%d" % i, [128, 512], BF16) for i in range(2)]
                heads = [(s, h) for s in range(2) for h in range(4)]

                def loadE(i):
                    s, h = heads[i]
                    S.dma("sync", lambda e: e.dma_start(out=qts[i % 2][:], in_=DQT[s, h, :, :]), qts[i % 2], writes=[qts[i % 2]])
                    S.dma("sync", lambda e: e.dma_start(out=kts[i % 2][:], in_=DKT[s, h, :, :]), kts[i % 2], writes=[kts[i % 2]])
                    S.dma("sync", lambda e: e.dma_start(out=vts[i % 2][:], in_=DV[s, :, :, h * 128:(h + 1) * 128].rearrange("n p e -> p n e")),
                          vts[i % 2], writes=[vts[i % 2]])

                loadE(0)
                steps = []
                for i, (s, h) in enumerate(heads):
                    for j in range(8):
                        nk = 4 * j + 4
                        for ki in range(nk):
                            steps.append((i, s, h, j, ki, nk))
                cnt = {"ek": 0, "fk": 0, "ok": 0, "loaded": 0}
                live = {}

                def stageA(idx):
                    i, s, h, j, ki, nk = steps[idx]
                    qt, kt = qts[i % 2], kts[i % 2]
                    ek = cnt["ek"]; cnt["ek"] += 1
                    p0, p1 = pS0[ek % 2], pS1[ek % 2]
                    e0, e1 = E0[ek % 3], E1[ek % 3]
                    S.op("tensor", lambda e: e.matmul(p0[:], lhsT=kt[0:64, ki * 128:(ki + 1) * 128], rhs=qt[0:64, j * 512:(j + 1) * 512],
                                                      start=True, stop=True), reads=[kt, qt], writes=[p0])
                    S.op("tensor", lambda e: e.matmul(p1[:], lhsT=kt[64:128, ki * 128:(ki + 1) * 128], rhs=qt[64:128, j * 512:(j + 1) * 512],
                                                      start=True, stop=True), reads=[kt, qt], writes=[p1])
                    near = ki >= 4 * j - 1
                    if not near:
                        S.op("scalar", lambda e: e.activation(e0[:], p0[:], AF.Exp, bias=b31[:, h:h + 1]), reads=[p0, b31], writes=[e0])
                        S.op("scalar", lambda e: e.activation(e1[:], p1[:], AF.Exp, bias=b31[:, h:h + 1]), reads=[p1, b31], writes=[e1])
                    else:
                        c0 = 512 * j - 128 * ki + 384
                        for pc, ec in ((p0, e0), (p1, e1)):
                            ef = Ef[cnt["fk"] % 2]; cnt["fk"] += 1
                            S.op("scalar", lambda e, pc=pc, ef=ef: e.activation(ef[:], pc[:], AF.Exp), reads=[pc], writes=[ef])
                            S.op("vector" if cnt["fk"] % 2 else "gpsimd",
                                 lambda e, ef=ef, ec=ec: e.tensor_tensor(ec[:], ef[:], expB[h][:, c0:c0 + 512], op=ALU.mult),
                                 reads=[ef, expB[h]], writes=[ec])
                    live[idx] = (e0, e1)

                def stageB(idx):
                    i, s, h, j, ki, nk = steps[idx]
                    if j == 0 and ki == 0 and i + 1 < len(heads):
                        loadE(i + 1)
                    vt = vts[i % 2]
                    e0, e1 = live.pop(idx)
                    first, lastk = (ki == 0), (ki == nk - 1)
                    S.op("tensor", lambda e: e.matmul(pO0[:], lhsT=vt[:, ki, :], rhs=e0[:], start=first, stop=lastk), reads=[vt, e0], writes=[pO0])
                    S.op("tensor", lambda e: e.matmul(pZ0[:], lhsT=onesB[:], rhs=e0[:], start=first, stop=lastk), reads=[onesB, e0], writes=[pZ0])
                    S.op("tensor", lambda e: e.matmul(pO1[:], lhsT=vt[:, ki, :], rhs=e1[:], start=first, stop=lastk), reads=[vt, e1], writes=[pO1])
                    S.op("tensor", lambda e: e.matmul(pZ1[:], lhsT=onesB[:], rhs=e1[:], start=first, stop=lastk), reads=[onesB, e1], writes=[pZ1])
                    if not lastk:
                        return
                    S.op("scalar", lambda e: e.copy(zc0[:], pZ0[:]), reads=[pZ0], writes=[zc0])
                    S.op("vector", lambda e: e.tensor_copy(zc1[:], pZ1[:]), reads=[pZ1], writes=[zc1])
                    S.op("scalar", lambda e: e.copy(oc0[:], pO0[:]), reads=[pO0], writes=[oc0])
                    S.op("vector", lambda e: e.tensor_copy(oc1[:], pO1[:]), reads=[pO1], writes=[oc1])
                    pend_epi.append((s, h, j))
                    return

                def epilogue_tail():
                    s, h, j = pend_epi.pop(0)
                    S.op("vector", lambda e: e.reciprocal(rz0[:], zc0[:]), reads=[zc0], writes=[rz0])
                    S.op("vector", lambda e: e.reciprocal(rz1[:], zc1[:]), reads=[zc1], writes=[rz1])
                    S.op("gpsimd", lambda e: e.tensor_tensor(rz0[:], oc0[:], rz0[:], op=ALU.mult), reads=[oc0, rz0], writes=[rz0])
                    S.op("gpsimd", lambda e: e.tensor_tensor(rz1[:], oc1[:], rz1[:], op=ALU.mult), reads=[oc1, rz1], writes=[rz1])
                    S.op("vector", lambda e: e.scalar_tensor_tensor(oo[:], in0=rz1[:], scalar=nlam[:, 0:1], in1=rz0[:], op0=ALU.mult, op1=ALU.add),
                         reads=[rz0, rz1, nlam], writes=[oo])
                    S.op("gpsimd", lambda e: e.tensor_tensor(osq[:], oo[:], oo[:], op=ALU.mult), reads=[oo], writes=[osq])
                    ek = cnt["ek"]; cnt["ek"] += 1
                    pss = pS0[ek % 2]
                    S.op("tensor", lambda e: e.matmul(pss[:], lhsT=ONESF(), rhs=osq[:], start=True, stop=True), reads=[cst, osq], writes=[pss])
                    S.op("scalar", lambda e: e.activation(rin[:], pss[:], AF.Ln, bias=epsc(), scale=1.0 / 128), reads=[pss, cst], writes=[rin])
                    S.op("scalar", lambda e: e.activation(rin[:], rin[:], AF.Exp, scale=-0.5), reads=[rin], writes=[rin])
                    ot = oTs[cnt["ok"] % 2]; cnt["ok"] += 1
                    S.op("vector", lambda e: e.scalar_tensor_tensor(ot[:], in0=oo[:], scalar=subg[:, 0:1], in1=rin[:], op0=ALU.mult, op1=ALU.mult),
                         reads=[oo, subg, rin], writes=[ot])
                    S.dma("sync", lambda e: e.dma_start(out=OT[s, 4 + h, :, j * 512:(j + 1) * 512], in_=ot[:]), ot, reads=[ot])

                pend_epi = []
                LOOK = 2
                for idx in range(min(LOOK, len(steps))):
                    stageA(idx)
                for idx in range(len(steps)):
                    had = len(pend_epi)
                    stageB(idx)
                    if idx + LOOK < len(steps):
                        stageA(idx + LOOK)
                    if had:
                        epilogue_tail()
                while pend_epi:
                    epilogue_tail()
            if upto == "E":
                break
            with Phase(S, "F%d" % l) as P:
                Wo = load_w_bf16(P, "Wo", w_out[l, :, :], 8, DM)
                gtB = []
                for b in range(2):
                    t = P.sb("gtB%d" % b, [128, DM], F32)
                    S.dma("sync", lambda e, b=b, t=t, l=l: e.dma_start(out=t[:], in_=MOD[l, b, 2 * DM:3 * DM].partition_broadcast(128)), t, writes=[t])
                    gtB.append(t)
                xts = [P.sb("xt%d" % i, [128, 4, DM], F32) for i in range(2)]
                ots = [P.sb("ot%d" % i, [128, 8, 512], BF16) for i in range(2)]
                tmp = [P.sb("tmp%d" % i, [128, 512], F32) for i in range(2)]
                pY = [P.ps("pY%d" % i, [128, 512], F32) for i in range(4)]
                blocks = [(s, blk) for s in range(2) for blk in range(8)]

                def loadF(i):
                    s, blk = blocks[i]
                    S.dma("sync", lambda e: e.dma_start(out=xts[i % 2][:], in_=xsrc[s, blk * 512:(blk + 1) * 512, :].rearrange("(a p) d -> p a d", p=128)),
                          xts[i % 2], writes=[xts[i % 2]])
                    S.dma("sync", lambda e: e.dma_start(out=ots[i % 2][:], in_=OT[s, :, :, blk * 512:(blk + 1) * 512].rearrange("c p t -> p c t")),
                          ots[i % 2], writes=[ots[i % 2]])

                loadF(0)
                yk = 0
                for i, (s, blk) in enumerate(blocks):
                    if i + 1 < len(blocks):
                        loadF(i + 1)
                    xt, ot = xts[i % 2], ots[i % 2]
                    for sub in range(4):
                        for dh in range(2):
                            py = pY[yk % 4]; tp = tmp[yk % 2]; yk += 1
                            for c in range(8):
                                S.op("tensor", lambda e, c=c, sub=sub, dh=dh, py=py, ot=ot: e.matmul(py[:], lhsT=ot[:, c, sub * 128:(sub + 1) * 128],
                                                                                                  rhs=Wo[:, c, dh * 512:(dh + 1) * 512], start=(c == 0), stop=(c == 7)),
                                     reads=[ot, Wo], writes=[py])
                            S.op("vector", lambda e, py=py, tp=tp, dh=dh, s=s: e.tensor_tensor(tp[:], py[:], gtB[s][:, dh * 512:(dh + 1) * 512], op=ALU.mult),
                                 reads=[py, gtB[s]], writes=[tp])
                            S.op("gpsimd", lambda e, tp=tp, sub=sub, dh=dh, xt=xt: e.tensor_tensor(xt[:, sub, dh * 512:(dh + 1) * 512], xt[:, sub, dh * 512:(dh + 1) * 512],
                                                                                                   tp[:], op=ALU.add), reads=[tp, xt], writes=[xt])
                    S.dma("sync", lambda e, xt=xt, s=s, blk=blk: e.dma_start(out=XA[s, blk * 512:(blk + 1) * 512, :].rearrange("(a p) d -> p a d", p=128), in_=xt[:]),
                          xt, reads=[xt])
            if upto == "F":
                break
            with Phase(S, "G%d" % l) as P:
                pp = P.sb("pp", [128, NPP], F32)
                S.dma("sync", lambda e, l=l: e.dma_start(out=pp[:], in_=pp_in[l, :, :]), pp, writes=[pp])
                Wup = load_w_bf16(P, "Wup", ffn_up[l, :, :], 8, 2 * DFF)
                AB = {}
                for b in range(2):
                    AB[b] = make_AB(P, l, b, pp, PP_NFFN, 3, 4, "ffn%d" % b)
                NB = 512
                xts = [P.sb("xt%d" % i, [128, 4, DM], F32) for i in range(2)]
                ss = P.sb("ss", [128, 4], F32); rs = P.sb("rs", [128, 4], F32)
                sq = P.sb("sq", [128, DM], BF16); xn = P.sb("xn", [128, 4, DM], BF16)
                tmpf = P.sb("tmpf", [128, 8, 128], F32)
                hT = P.sb("hT", [128, 8, NB], BF16)
                gts = [P.sb("gt%d" % i, [128, NB], BF16) for i in range(4)]
                pT = P.ps("pT", [128, 8, 128], BF16)
                pUf = [P.ps("pU%d" % i, [128, 512], F32) for i in range(6)]
                upad = [P.sb("upad%d" % i, [128, NB + 2], F32) for i in range(4)]
                acc = [P.sb("acc%d" % i, [128, NB], F32) for i in range(4)]
                sg = [P.sb("sg%d" % i, [128, NB], F32) for i in range(2)]
                halo = P.sb("halo", [128, 44, 2], F32)
                blocks = [(s, blk) for s in range(2) for blk in range(T // NB)]

                def loadG(i):
                    s, blk = blocks[i]
                    S.dma("sync", lambda e: e.dma_start(out=xts[i % 2][:], in_=XA[s, blk * NB:(blk + 1) * NB, :].rearrange("(a p) d -> p a d", p=128)),
                          xts[i % 2], writes=[xts[i % 2]])

                loadG(0)
                uk = 0; pk = 0; gk = 0
                for i, (s, blk) in enumerate(blocks):
                    if i + 1 < len(blocks):
                        loadG(i + 1)
                    xt = xts[i % 2]
                    A, Bsh = AB[s]
                    if blk == 0:
                        S.op("gpsimd", lambda e: e.memset(halo[:], 0.0), writes=[halo])
                    norm_to_hT(P, xt, 4, A, Bsh, hT, (ss, rs, sq, xn, tmpf), pT, None)
                    for fc in range(22):
                        res = []
                        for half in range(2):
                            f = fc + 22 * half
                            pu = pUf[pk % 6]; pk += 1
                            up = upad[uk % 4]; ac = acc[uk % 4]; uk += 1
                            for kc in range(8):
                                S.op("tensor", lambda e, kc=kc, f=f, pu=pu: e.matmul(pu[:], lhsT=Wup[:, kc, f * 128:(f + 1) * 128], rhs=hT[:, kc, :],
                                                                                    start=(kc == 0), stop=(kc == 7)), reads=[Wup, hT], writes=[pu])
                            S.op("gpsimd", lambda e, f=f, up=up: e.tensor_copy(up[:, 0:2], halo[:, f, :]), reads=[halo], writes=[up])
                            S.op("scalar", lambda e, up=up, pu=pu: e.copy(up[:, 2:NB + 2], pu[:]), reads=[pu], writes=[up])
                            S.op("gpsimd", lambda e, f=f, up=up: e.tensor_copy(halo[:, f, :], up[:, NB:NB + 2]), reads=[up], writes=[halo])
                            cw = lambda j, f=f: pp[:, PP_FCW + f * 3 + j:PP_FCW + f * 3 + j + 1]
                            S.op("vector", lambda e, up=up, ac=ac, cw=cw, f=f: e.tensor_scalar(ac[:], up[:, 2:NB + 2], cw(2), pp[:, PP_FCB + f:PP_FCB + f + 1],
                                                                                               op0=ALU.mult, op1=ALU.add), reads=[up, pp], writes=[ac])
                            for j in (1, 0):
                                S.op("vector", lambda e, up=up, ac=ac, cw=cw, j=j: e.scalar_tensor_tensor(ac[:], in0=up[:, j:j + NB], scalar=cw(j), in1=ac[:],
                                                                                                          op0=ALU.mult, op1=ALU.add), reads=[up, pp, ac], writes=[ac])
                            res.append(ac)
                        sgt = sg[fc % 2]
                        gt = gts[gk % 4]; gk += 1
                        S.op("scalar", lambda e, sgt=sgt, g_=res[1]: e.activation(sgt[:], g_[:], AF.Silu), reads=[res[1]], writes=[sgt])
                        S.op("gpsimd", lambda e, sgt=sgt, a_=res[0], gt=gt: e.tensor_tensor(gt[:], a_[:], sgt[:], op=ALU.mult), reads=[res[0], sgt], writes=[gt])
                        S.dma("sync", lambda e, gt=gt, s=s, blk=blk, fc=fc: e.dma_start(out=GTD[s, fc, :, blk * NB:(blk + 1) * NB], in_=gt[:]), gt, reads=[gt])
            with Phase(S, "H%d" % l) as P:
                Wdn = load_w_bf16(P, "Wdn", ffn_down[l, :, :], 22, DM)
                gtB = []
                for b in range(2):
                    t = P.sb("gtB%d" % b, [128, DM], F32)
                    S.dma("sync", lambda e, b=b, t=t, l=l: e.dma_start(out=t[:], in_=MOD[l, b, 5 * DM:6 * DM].partition_broadcast(128)), t, writes=[t])
                    gtB.append(t)
                NB = 512
                xts = [P.sb("xt%d" % i, [128, 4, DM], F32) for i in range(2)]
                gin = [P.sb("gin%d" % i, [128, 22, NB], BF16) for i in range(2)]
                tmp = [P.sb("tmp%d" % i, [128, 512], F32) for i in range(2)]
                pY = [P.ps("pY%d" % i, [128, 512], F32) for i in range(4)]
                blocks = [(s, blk) for s in range(2) for blk in range(T // NB)]

                def loadH(i):
                    s, blk = blocks[i]
                    S.dma("sync", lambda e: e.dma_start(out=xts[i % 2][:], in_=XA[s, blk * NB:(blk + 1) * NB, :].rearrange("(a p) d -> p a d", p=128)),
                          xts[i % 2], writes=[xts[i % 2]])
                    for f0 in (0, 11):
                        S.dma("sync", lambda e, f0=f0: e.dma_start(out=gin[i % 2][:, f0:f0 + 11, :],
                                                                   in_=GTD[s, f0:f0 + 11, :, blk * NB:(blk + 1) * NB].rearrange("f p t -> p f t")),
                              gin[i % 2], writes=[gin[i % 2]])

                loadH(0)
                yk = 0
                for i, (s, blk) in enumerate(blocks):
                    if i + 1 < len(blocks):
                        loadH(i + 1)
                    xt, gi = xts[i % 2], gin[i % 2]
                    for sub in range(4):
                        for dh in range(2):
                            py = pY[yk % 4]; tp = tmp[yk % 2]; yk += 1
                            for fc in range(22):
                                S.op("tensor", lambda e, fc=fc, sub=sub, dh=dh, py=py, gi=gi: e.matmul(py[:], lhsT=gi[:, fc, sub * 128:(sub + 1) * 128],
                                                                                                           rhs=Wdn[:, fc, dh * 512:(dh + 1) * 512], start=(fc == 0), stop=(fc == 21)),
                                     reads=[gi, Wdn], writes=[py])
                            S.op("vector", lambda e, py=py, tp=tp, dh=dh, s=s: e.tensor_tensor(tp[:], py[:], gtB[s][:, dh * 512:(dh + 1) * 512], op=ALU.mult),
                                 reads=[py, gtB[s]], writes=[tp])
                            S.op("gpsimd", lambda e, tp=tp, sub=sub, dh=dh, xt=xt: e.tensor_tensor(xt[:, sub, dh * 512:(dh + 1) * 512], xt[:, sub, dh * 512:(dh + 1) * 512],
                                                                                                   tp[:], op=ALU.add), reads=[tp, xt], writes=[xt])
                    S.dma("sync", lambda e, xt=xt, s=s, blk=blk: e.dma_start(out=xdst[s, blk * NB:(blk + 1) * NB, :].rearrange("(a p) d -> p a d", p=128), in_=xt[:]),
                          xt, reads=[xt])
            xsrc = xdst
        G.__exit__(None, None, None)
    return nc


def _prep(inputs):
    inp = {k: np.asarray(v) for k, v in inputs.items()}
    consts = _consts()
    pp = _pack_pp(inp)
    lamv = np.stack([inp["diff_lambda_q1"], inp["diff_lambda_k1"], inp["diff_lambda_q2"], inp["diff_lambda_k2"]], axis=1)
    lamv = np.ascontiguousarray(lamv.astype(np.float32))
    kk = np.arange(128)[:, None]
    cc = np.arange(1024)[None, :]
    dist = cc - kk - 384
    bidx = _t5_bucket(np.maximum(dist, 0))
    rb = inp["rel_bias"].astype(np.float32)
    tb = np.empty((4, 128, 1024), np.float32)
    for h in range(4):
        tb[h] = np.where(dist >= 0, rb[bidx, h], np.float32(NEG))
    b31 = np.ascontiguousarray(np.broadcast_to(rb[31][None, :], (128, 4))).astype(np.float32)
    shared = dict(consts=consts, pp=pp, lamv=lamv, tb=tb, b31=b31,
                  w_ada=inp["w_ada"], b_ada=inp["b_ada"], w_in=inp["w_in"], w_out=inp["w_out"],
                  ffn_up=inp["ffn_up"], ffn_down=inp["ffn_down"])
    in_maps = []
    for c in range(NCORES):
        m = dict(shared)
        m["x"] = np.ascontiguousarray(inp["x"][2 * c:2 * c + 2])
        cc_ = inp["c"][2 * c:2 * c + 2]
        m["cT"] = np.ascontiguousarray(cc_.reshape(2, 8, 128).transpose(2, 1, 0))
        in_maps.append(m)
    return in_maps


def kernel(**inputs):
    in_maps = _prep(inputs)
    nc = build()
    res = run_bass_kernel_spmd(nc, in_maps, core_ids=list(range(NCORES)))
    return np.concatenate([r["out"] for r in res.results], axis=0).astype(np.float32)
```

```python
import math
from contextlib import ExitStack
import numpy as np
import concourse.bass as bass
import concourse.mybir as mybir
from concourse.bass_utils import run_bass_kernel_spmd

F32 = mybir.dt.float32
BF16 = mybir.dt.bfloat16
AF = mybir.ActivationFunctionType
ALU = mybir.AluOpType

NCORES = 8
DEPTH = 4
T = 4096
DM = 1024
NEG = -30000.0
EPS = 1e-6
DFF = 2816


class Buf:
    def __init__(self, name, t=None):
        self.name = name
        self.t = t
        self.last_w = None
        self.readers = []
        self.dsem = None
        self.dcount = 0

    def __getitem__(self, k):
        return self.t[k]


class Eng:
    def __init__(self, name, sem):
        self.name = name
        self.sem = sem
        self.count = 0
        self.waited = {}
        self.prog = []


class Sched:
    SEM_WRAP = 30000

    def __init__(self, nc, es):
        self.nc = nc
        self.es = es
        self.engs = {}
        for name in ("sync", "scalar", "vector", "gpsimd", "tensor"):
            self.engs[name] = Eng(name, es.enter_context(nc.semaphore("s_" + name)))
        self.n_instr = 0
        self.dbufs = []
        self.sem_pool = []
        self.rr = 0

    def _waits(self, e, reads, writes):
        toks = []
        own = e.sem
        for b in reads:
            if b.last_w is not None:
                toks.append(b.last_w)
        for b in writes:
            if b.last_w is not None and b.last_w[0] is not own:
                toks.append(b.last_w)
            for r in b.readers:
                if r[0] is not own:
                    toks.append(r)
        need = {}
        for sem, val in toks:
            if e.name == "tensor" and sem is own:
                continue
            k = id(sem)
            if e.waited.get(k, 0) >= val:
                continue
            if k not in need or need[k][1] < val:
                need[k] = (sem, val)
        out = []
        for k, (sem, val) in need.items():
            e.waited[k] = val
            out.append((sem, val))
        return out

    def _record(self, e, fn, waits, sem, inc, reads, writes, tok):
        def run(eng, fn=fn, waits=waits, sem=sem, inc=inc):
            for s, v in waits:
                eng.wait_ge(s, v)
            fn(eng).then_inc(sem, inc)
        e.prog.append(run)
        for b in reads:
            b.readers.append(tok)
            if len(b.readers) > 64:
                b.readers = b.readers[-48:]
        for b in writes:
            b.last_w = tok
            b.readers = []
        self.n_instr += 1

    def op(self, engname, fn, reads=(), writes=()):
        e = self.engs[engname]
        if e.count >= self.SEM_WRAP:
            e.sem = self.es.enter_context(self.nc.semaphore("s_%s_%d" % (engname, self.n_instr)))
            e.count = 0
        waits = self._waits(e, reads, writes)
        e.count += 1
        tok = (e.sem, e.count)
        self._record(e, fn, waits, e.sem, 1, reads, writes, tok)
        return tok

    def dma(self, engname, fn, sembuf, reads=(), writes=()):
        e = self.engs[engname]
        waits = self._waits(e, reads, writes)
        if sembuf.dsem is None:
            if self.sem_pool:
                sembuf.dsem, sembuf.dcount = self.sem_pool.pop()
            else:
                sembuf.dsem = self.es.enter_context(self.nc.semaphore("d%d_%s" % (self.n_instr, sembuf.name)))
        if sembuf not in self.dbufs:
            self.dbufs.append(sembuf)
        sembuf.dcount += 16
        tok = (sembuf.dsem, sembuf.dcount)
        self._record(e, fn, waits, sembuf.dsem, 16, reads, writes, tok)
        return tok

    def barrier(self):
        toks = [(e.sem, e.count) for e in self.engs.values() if e.count > 0]
        toks += [(b.dsem, b.dcount) for b in self.dbufs]
        for name in self.engs:
            self.wait_tokens(name, toks)
        for b in self.dbufs:
            self.sem_pool.append((b.dsem, b.dcount))
            b.dsem = None
        self.dbufs = []

    def wait_tokens(self, engname, toks):
        e = self.engs[engname]
        for sem, val in toks:
            k = id(sem)
            if e.waited.get(k, 0) >= val:
                continue
            e.waited[k] = val
            e.prog.append(lambda eng, s=sem, v=val: eng.wait_ge(s, v))

    def emit(self):
        with self.nc.Block() as block:
            for name in ("sync", "scalar", "vector", "gpsimd", "tensor"):
                def body(eng, name=name):
                    for f in self.engs[name].prog:
                        f(eng)
                getattr(block, name)(body)
        for e in self.engs.values():
            e.prog = []

    def alt(self):
        self.rr ^= 1
        return "vector" if self.rr else "gpsimd"


PHASE_LOG = []


class Phase:
    def __init__(self, S, name):
        self.S = S
        self.nc = S.nc
        self.name = name
        self.es = ExitStack()
        self.k = 0

    def __enter__(self):
        self.es.__enter__()
        return self

    def sb(self, name, shape, dt):
        self.k += 1
        nm = "%s_%s_%d" % (self.name, name, self.k)
        return Buf(nm, self.es.enter_context(self.nc.sbuf_tensor(nm, list(shape), dt)))

    def ps(self, name, shape, dt):
        self.k += 1
        nm = "%s_%s_%d" % (self.name, name, self.k)
        return Buf(nm, self.es.enter_context(self.nc.psum_tensor(nm, list(shape), dt)))

    def __exit__(self, *a):
        if a[0] is None:
            PHASE_LOG.append((self.name, {k: (v.count, id(v.sem)) for k, v in self.S.engs.items()}, self.S.n_instr))
            self.S.barrier()
            self.S.emit()
        return self.es.__exit__(*a)


C_IDENT, C_TRI, C_ONES, C_MASKNEG, C_STRICT, C_BLK64, C_DELTA, C_EPS, C_ONE = 0, 128, 256, 384, 512, 640, 768, 769, 770
C_BM16, C_OFF32, C_OFF64, C_OFF128 = 771, 899, 1027, 1155
NCONST = 1283

PP_CONVW = 0
PP_DTB = 48
PP_ALOG = 52
PP_ONORM = 56
PP_QN = 184
PP_KN = 185
PP_SUBLN = 186
PP_FCW = 187
PP_FCB = 319
PP_NMIX = 363
PP_NFFN = 371
PP_BADA = 379
NPP = 380


def _consts():
    c = np.zeros((128, NCONST), np.float32)
    i = np.arange(128)
    c[:, C_IDENT:C_IDENT + 128] = np.eye(128)
    c[:, C_TRI:C_TRI + 128] = (i[:, None] <= i[None, :])
    c[:, C_ONES:C_ONES + 128] = 1.0
    c[:, C_MASKNEG:C_MASKNEG + 128] = np.where(i[:, None] <= i[None, :], 0.0, NEG)
    c[:, C_STRICT:C_STRICT + 128] = (i[:, None] < i[None, :])
    c[:, C_BLK64:C_BLK64 + 128] = ((i[:, None] // 64) == (i[None, :] // 64))
    c[0, C_DELTA] = 1.0
    bm = lambda m: ((i[:, None] // m) == (i[None, :] // m)).astype(np.float32)
    c[:, C_BM16:C_BM16 + 128] = bm(16)
    c[:, C_OFF32:C_OFF32 + 128] = bm(32) - bm(16)
    c[:, C_OFF64:C_OFF64 + 128] = bm(64) - bm(32)
    c[:, C_OFF128:C_OFF128 + 128] = 1.0 - bm(64)
    c[:, C_EPS] = EPS
    c[:, C_ONE] = 1.0
    return c


def _t5_bucket(n):
    n = np.asarray(n)
    nf = np.maximum(n, 1).astype(np.float32)
    large = 16 + (np.log(nf / np.float32(16)) / np.float32(math.log(128 / 16)) * np.float32(16)).astype(np.int32)
    large = np.minimum(large, 31)
    return np.where(n < 16, n, large)


def _pack_pp(inp):
    pp = np.zeros((DEPTH, 128, NPP), np.float32)
    p = np.arange(128)
    for l in range(DEPTH):
        cw = inp["gdn_conv_w"][l]
        pp[l, :, PP_CONVW:PP_CONVW + 48] = cw.reshape(4, 12, 128).transpose(2, 1, 0).reshape(128, 48)
        pp[l, :, PP_DTB:PP_DTB + 4] = inp["gdn_dt_bias"][l][None, :]
        pp[l, :, PP_ALOG:PP_ALOG + 4] = inp["gdn_a_log"][l][None, :]
        pp[l, :, PP_ONORM:PP_ONORM + 128] = inp["gdn_out_norm"][l][None, :]
        pp[l, :, PP_QN] = inp["diff_q_norm"][l][p % 64]
        pp[l, :, PP_KN] = inp["diff_k_norm"][l][p % 64]
        pp[l, :, PP_SUBLN] = inp["diff_subln"][l]
        fw = inp["ffn_conv_w"][l]
        pp[l, :, PP_FCW:PP_FCW + 132] = fw.reshape(3, 44, 128).transpose(2, 1, 0).reshape(128, 132)
        pp[l, :, PP_FCB:PP_FCB + 44] = inp["ffn_conv_b"][l].reshape(44, 128).T
        pp[l, :, PP_NMIX:PP_NMIX + 8] = inp["norm_mix"][l].reshape(8, 128).T
        pp[l, :, PP_NFFN:PP_NFFN + 8] = inp["norm_ffn"][l].reshape(8, 128).T
    return pp


def build(nlayers=DEPTH, upto="G", debug=False):
    nc = bass.Bass("TRN2", target_bir_lowering=False)
    dk = "ExternalOutput" if debug else "Internal"

    def din(name, shape, dt=F32):
        return nc.dram_tensor(name, list(shape), dt, kind="ExternalInput").ap()

    def dscr(name, shape, dt, dbg=True):
        return nc.dram_tensor(name, list(shape), dt, kind=(dk if dbg else "Internal")).ap()

    x_in = din("x", [2, T, DM])
    cT_in = din("cT", [128, 8, 2])
    consts_in = din("consts", [128, NCONST])
    pp_in = din("pp", [DEPTH, 128, NPP])
    lam_in = din("lamv", [DEPTH, 4, 64])
    tb_in = din("tb", [4, 128, 1024])
    b31_in = din("b31", [128, 4])
    w_ada = din("w_ada", [DEPTH, DM, 6 * DM])
    b_ada = din("b_ada", [DEPTH, 6 * DM])
    w_in = din("w_in", [DEPTH, DM, 3592])
    w_out = din("w_out", [DEPTH, DM, DM])
    ffn_up = din("ffn_up", [DEPTH, DM, 2 * DFF])
    ffn_down = din("ffn_down", [DEPTH, DFF, DM])
    out = nc.dram_tensor("out", [2, T, DM], F32, kind="ExternalOutput").ap()

    MOD = dscr("MOD", [DEPTH, 2, 6 * DM], F32)
    XA = dscr("XA", [2, T, DM], F32)
    XB = dscr("XB", [2, T, DM], F32, dbg=False)
    NCH = T // 128
    KT = dscr("KT", [2, NCH, 128, 4, 128], BF16)
    QGT = dscr("QGT", [2, NCH, 128, 4, 128], BF16)
    QT = dscr("QT", [2, NCH, 128, 4, 128], BF16)
    KBT = dscr("KBT", [2, NCH, 128, 4, 128], BF16)
    KBG = dscr("KBG", [2, NCH, 128, 4, 128], BF16)
    KDEC = dscr("KDEC", [2, NCH, 128, 4, 128], BF16)
    VB = dscr("VB", [2, NCH, 128, 4, 128], BF16)
    DEC = dscr("DEC", [2, NCH, 128, 4, 128], F32)
    EGL = dscr("EGL", [2, NCH, 128, 4], F32)
    GATE = dscr("GATE", [2, NCH, 128, 512], BF16)
    DQT = dscr("DQT", [2, 4, 128, T], BF16)
    DKT = dscr("DKT", [2, 4, 128, T], BF16)
    DV = dscr("DV", [2, NCH, 128, 512], BF16)
    OT = dscr("OT", [2, 8, 128, T], BF16)
    GTD = dscr("GTD", [2, 22, 128, T], BF16, dbg=False)

    DBG = {}
    if debug:
        for nm in ("Y", "X", "Q", "U", "T1", "X1", "Y1b", "S0", "U1", "PW1", "O1", "VN0"):
            DBG[nm] = dscr("DBG_" + nm, [128, 4, 128], F32)

    with ExitStack() as es0:
        S = Sched(nc, es0)
        G = Phase(S, "glob")
        G.__enter__()
        cst = G.sb("cst", [128, NCONST], F32)
        identB = G.sb("identB", [128, 128], BF16)
        onesB = G.sb("onesB", [128, 128], BF16)
        S.dma("sync", lambda e: e.dma_start(out=cst[:], in_=consts_in[:, :]), cst, writes=[cst])
        S.op("vector", lambda e: e.tensor_copy(identB[:], cst[:, C_IDENT:C_IDENT + 128]), reads=[cst], writes=[identB])
        S.op("vector", lambda e: e.tensor_copy(onesB[:], cst[:, C_ONES:C_ONES + 128]), reads=[cst], writes=[onesB])
        identF = lambda: cst[:, C_IDENT:C_IDENT + 128]
        TRI = lambda: cst[:, C_TRI:C_TRI + 128]
        ONESF = lambda: cst[:, C_ONES:C_ONES + 128]
        epsc = lambda: cst[:, C_EPS:C_EPS + 1]
        onec = lambda: cst[:, C_ONE:C_ONE + 1]

        with Phase(S, "A") as P:
            cT = P.sb("cT", [128, 8, 2], F32)
            S.dma("sync", lambda e: e.dma_start(out=cT[:], in_=cT_in[:, :, :]), cT, writes=[cT])
            S.op("scalar", lambda e: e.activation(cT[:], cT[:], AF.Silu), reads=[cT], writes=[cT])
            wts = [P.sb("wa%d" % i, [128, 8, 512], F32) for i in range(3)]
            pM = [P.ps("pM%d" % i, [2, 512], F32) for i in range(2)]
            k = 0
            bada = P.sb("bada", [2, 6 * DM], F32)
            modt = P.sb("modt", [2, 6 * DM], F32)
            for l in range(nlayers):
                S.dma("sync", lambda e, l=l, bada=bada: e.dma_start(out=bada[:], in_=b_ada[l, :].partition_broadcast(2)), bada, writes=[bada])
                for fb in range(12):
                    wt = wts[k % 3]
                    pm = pM[k % 2]
                    k += 1
                    S.dma("sync", lambda e, l=l, fb=fb, wt=wt: e.dma_start(
                        out=wt[:], in_=w_ada[l, :, fb * 512:(fb + 1) * 512].rearrange("(c p) f -> p c f", p=128)), wt, writes=[wt])
                    for kc in range(8):
                        S.op("tensor", lambda e, kc=kc, wt=wt, pm=pm: e.matmul(pm[:], lhsT=cT[:, kc, :], rhs=wt[:, kc, :],
                                                                              start=(kc == 0), stop=(kc == 7)),
                             reads=[cT, wt], writes=[pm])
                    S.op("vector", lambda e, fb=fb, pm=pm, modt=modt, bada=bada: e.tensor_tensor(
                        modt[:, fb * 512:(fb + 1) * 512], pm[:], bada[:, fb * 512:(fb + 1) * 512], op=ALU.add),
                        reads=[pm, bada], writes=[modt])
                S.dma("sync", lambda e, l=l, modt=modt: e.dma_start(out=MOD[l, :, :], in_=modt[:]), modt, reads=[modt])

        def load_mod_cols(P, l, b, seg, name):
            t = P.sb(name, [128, 8], F32)
            S.dma("sync", lambda e: e.dma_start(out=t[:], in_=MOD[l, b, seg * DM:(seg + 1) * DM].rearrange("(c p) -> p c", p=128),
                                                allow_slow_non_contiguous=True), t, writes=[t])
            return t

        def make_AB(P, l, b, pp, ncol, seg_sh, seg_sc, name):
            sh = load_mod_cols(P, l, b, seg_sh, name + "sh")
            sc = load_mod_cols(P, l, b, seg_sc, name + "sc")
            A = P.sb(name + "A", [128, 8], F32)
            S.op("vector", lambda e: e.scalar_tensor_tensor(A[:], in0=sc[:], scalar=1.0, in1=pp[:, ncol:ncol + 8],
                                                            op0=ALU.add, op1=ALU.mult), reads=[sc, pp], writes=[A])
            return A, sh

        def norm_to_hT(P, xt, nsub, A, Bsh, hT, tmps, pT, W):
            ss, rs, sq, xn, tmpf = tmps
            for sub in range(nsub):
                S.op("scalar", lambda e, sub=sub: e.activation(sq[:], xt[:, sub, :], AF.Square, scale=1.0 / 32,
                                                                accum_out=ss[:, sub:sub + 1]), reads=[xt], writes=[sq, ss])
            S.op("scalar", lambda e: e.activation(rs[:, 0:nsub], ss[:, 0:nsub], AF.Ln, bias=epsc()), reads=[ss, cst], writes=[rs])
            S.op("scalar", lambda e: e.activation(rs[:, 0:nsub], rs[:, 0:nsub], AF.Exp, scale=-0.5), reads=[rs], writes=[rs])
            for sub in range(nsub):
                S.op("vector", lambda e, sub=sub: e.tensor_scalar_mul(xn[:, sub, :], xt[:, sub, :], rs[:, sub:sub + 1]),
                     reads=[xt, rs], writes=[xn])
                for c in range(8):
                    S.op("tensor", lambda e, sub=sub, c=c: e.transpose(pT[:, c, :], xn[:, sub, c * 128:(c + 1) * 128], identB[:]),
                         reads=[xn, identB], writes=[pT])
                S.op("vector", lambda e: e.tensor_tensor(tmpf[:], pT[:], A[:, :, None].to_broadcast([128, 8, 128]), op=ALU.mult),
                     reads=[pT, A], writes=[tmpf])
                S.op("gpsimd", lambda e, sub=sub: e.tensor_tensor(hT[:, :, sub * 128:(sub + 1) * 128], tmpf[:],
                                                                   Bsh[:, :, None].to_broadcast([128, 8, 128]), op=ALU.add),
                     reads=[tmpf, Bsh], writes=[hT])

        def load_w_bf16(P, name, src_ap, kc, ncols):
            t = P.sb(name, [128, kc, ncols], BF16)
            step = max(1, 4096 // ncols)
            for c0 in range(0, kc, step):
                c1 = min(kc, c0 + step)
                S.dma("gpsimd", lambda e, c0=c0, c1=c1: e.dma_start(
                    out=t[:, c0:c1, :], in_=src_ap[c0 * 128:c1 * 128, :].rearrange("(c p) f -> p c f", p=128)), t, writes=[t])
            return t

        xsrc = x_in
        for l in range(nlayers):
            last = (l == nlayers - 1)
            xdst = out if last else XB
            with Phase(S, "C%d" % l) as P:
                pp = P.sb("pp", [128, NPP], F32)
                S.dma("sync", lambda e, l=l: e.dma_start(out=pp[:], in_=pp_in[l, :, :]), pp, writes=[pp])
                Wqkv = load_w_bf16(P, "Wqkv", w_in[l, :, 0:1536], 8, 1536)
                Wgate = load_w_bf16(P, "Wgate", w_in[l, :, 1536:2048], 8, 512)
                Wba = load_w_bf16(P, "Wba", w_in[l, :, 2048:2056], 8, 8)
                Wdqk = load_w_bf16(P, "Wdqk", w_in[l, :, 2056:3080], 8, 1024)
                Wdv = load_w_bf16(P, "Wdv", w_in[l, :, 3080:3592], 8, 512)
                negA = P.sb("negA", [128, 4], F32)
                S.op("scalar", lambda e: e.activation(negA[:], pp[:, PP_ALOG:PP_ALOG + 4], AF.Exp), reads=[pp], writes=[negA])
                S.op("vector", lambda e: e.tensor_scalar_mul(negA[:], negA[:], -1.0), reads=[negA], writes=[negA])
                qgain = P.sb("qgain", [128, 1], F32)
                S.op("vector", lambda e: e.tensor_scalar_mul(qgain[:], pp[:, PP_QN:PP_QN + 1], 0.125), reads=[pp], writes=[qgain])
                xts = [P.sb("xt%d" % i, [128, 4, DM], F32) for i in range(2)]
                ss = P.sb("ss", [128, 4], F32); rs = P.sb("rs", [128, 4], F32)
                sq = P.sb("sq", [128, DM], F32); xn = P.sb("xn", [128, 4, DM], BF16)
                tmpf = P.sb("tmpf", [128, 8, 128], F32)
                hT = P.sb("hT", [128, 8, 512], BF16)
                pT = P.ps("pT", [128, 8, 128], BF16)
                pU = [P.ps("pU%d" % i, [128, 512], F32) for i in range(2)]
                pS = P.ps("pS", [128, 512], F32)
                pSm = P.ps("pSm", [128, 32, 16], F32)
                pD = P.ps("pD", [128, 4, 128], F32)
                pTr = P.ps("pTr", [128, 8, 128], BF16)
                pTok = P.ps("pTok", [128, 512], F32)
                upad = [P.sb("upad%d" % i, [128, 515], F32) for i in range(2)]
                acc = [P.sb("acc%d" % i, [128, 512], F32) for i in range(2)]
                act = [P.sb("act%d" % i, [128, 512], F32) for i in range(2)]
                sqq = [P.sb("sqq%d" % i, [128, 512], F32) for i in range(2)]
                rinv = [P.sb("rinv%d" % i, [128, 512], F32) for i in range(2)]
                halo = P.sb("halo", [128, 12, 3], F32)
                kT = P.sb("kT", [128, 4, 4, 128], BF16)
                qT = P.sb("qT", [128, 4, 4, 128], BF16)
                vT = P.sb("vT", [128, 4, 4, 128], BF16)
                kbT = P.sb("kbT", [128, 4, 4, 128], BF16)
                qgT = P.sb("qgT", [128, 4, 4, 128], BF16)
                kbg = P.sb("kbg", [128, 4, 4, 128], BF16)
                kdec = P.sb("kdec", [128, 4, 4, 128], BF16)
                kb = P.sb("kb", [128, 4, 128], BF16)
                qg = P.sb("qg", [128, 4, 128], BF16)
                vb = P.sb("vb", [128, 4, 4, 128], BF16)
                dec = P.sb("dec", [128, 4, 4, 128], F32)
                gatet = P.sb("gatet", [128, 4, 512], BF16)
                dvt = P.sb("dvt", [128, 4, 512], BF16)
                dqkT = P.sb("dqkT", [128, 8, 512], BF16)
                braw = P.sb("braw", [128, 4, 8], F32)
                beta = P.sb("beta", [128, 4, 4], F32)
                gg = P.sb("gg", [128, 4, 4], F32)
                gcl = P.sb("gcl", [128, 4, 8], F32)
                ngc = P.sb("ngc", [128, 4, 4], F32)
                egc = P.sb("egc", [128, 4, 4], F32)
                bgs = P.sb("bgs", [128, 4, 4], F32)
                qsc = P.sb("qsc", [128, 4, 4], F32)
                kds = P.sb("kds", [128, 4, 4], F32)
                egl = P.sb("egl", [128, 4, 4], F32)
                gbc = P.sb("gbc", [128, 4, 128], F32)

                blocks = [(s, blk) for s in range(2) for blk in range(8)]

                def load_x(i):
                    s, blk = blocks[i]
                    xt = xts[i % 2]
                    S.dma("sync", lambda e: e.dma_start(
                        out=xt[:], in_=xsrc[s, blk * 512:(blk + 1) * 512, :].rearrange("(a p) d -> p a d", p=128)), xt, writes=[xt])

                load_x(0)
                AB = {}
                for b in range(2):
                    AB[b] = make_AB(P, l, b, pp, PP_NMIX, 0, 1, "mix%d" % b)
                cc = 0
                for i, (s, blk) in enumerate(blocks):
                    xt = xts[i % 2]
                    if i + 1 < len(blocks):
                        load_x(i + 1)
                    A, Bsh = AB[s]
                    if blk == 0:
                        S.op("gpsimd", lambda e: e.memset(halo[:], 0.0), writes=[halo])
                    norm_to_hT(P, xt, 4, A, Bsh, hT, (ss, rs, sq, xn, tmpf), pT, None)

                    for sub in range(4):
                        for kc in range(8):
                            S.op("tensor", lambda e, sub=sub, kc=kc: e.matmul(pSm[:, sub, 0:8], lhsT=hT[:, kc, sub * 128:(sub + 1) * 128],
                                                                            rhs=Wba[:, kc, :], start=(kc == 0), stop=(kc == 7)),
                                 reads=[hT, Wba], writes=[pSm])
                    S.op("vector", lambda e: e.tensor_copy(braw[:], pSm[:, 0:4, 0:8]), reads=[pSm], writes=[braw])
                    S.op("scalar", lambda e: e.activation(beta[:], braw[:, :, 0:4], AF.Sigmoid), reads=[braw], writes=[beta])
                    S.op("vector", lambda e: e.tensor_tensor(gg[:], braw[:, :, 4:8], pp[:, None, PP_DTB:PP_DTB + 4].to_broadcast([128, 4, 4]),
                                                             op=ALU.add), reads=[braw, pp], writes=[gg])
                    S.op("scalar", lambda e: e.activation(gg[:], gg[:], AF.Exp), reads=[gg], writes=[gg])
                    S.op("scalar", lambda e: e.activation(gg[:], gg[:], AF.Ln, bias=onec()), reads=[gg, cst], writes=[gg])
                    S.op("vector", lambda e: e.tensor_tensor(gg[:], gg[:], negA[:, None, :].to_broadcast([128, 4, 4]), op=ALU.mult),
                         reads=[gg, negA], writes=[gg])
                    pending = []
                    for fc in range(12):
                        pu = pU[cc % 2]; up = upad[cc % 2]; ac = acc[cc % 2]; at = act[cc % 2]
                        sqt = sqq[cc % 2]; rv = rinv[cc % 2]
                        cc += 1
                        for kc in range(8):
                            S.op("tensor", lambda e, kc=kc, fc=fc, pu=pu: e.matmul(pu[:], lhsT=Wqkv[:, kc, fc * 128:(fc + 1) * 128],
                                                                                    rhs=hT[:, kc, :], start=(kc == 0), stop=(kc == 7)),
                                 reads=[Wqkv, hT], writes=[pu])
                        while pending:
                            pending.pop(0)()
                        S.op("vector", lambda e, fc=fc, up=up: e.tensor_copy(up[:, 0:3], halo[:, fc, :]), reads=[halo], writes=[up])
                        S.op("scalar", lambda e, up=up, pu=pu: e.copy(up[:, 3:515], pu[:]), reads=[pu], writes=[up])
                        S.op("gpsimd", lambda e, fc=fc, up=up: e.tensor_copy(halo[:, fc, :], up[:, 512:515]), reads=[up], writes=[halo])
                        cw = lambda j, fc=fc: pp[:, PP_CONVW + fc * 4 + j:PP_CONVW + fc * 4 + j + 1]
                        S.op("vector", lambda e, up=up, ac=ac, cw=cw: e.tensor_scalar_mul(ac[:], up[:, 3:515], cw(3)), reads=[up, pp], writes=[ac])
                        for j in (2, 1, 0):
                            S.op("vector",
                                 lambda e, up=up, ac=ac, cw=cw, j=j: e.scalar_tensor_tensor(ac[:], in0=up[:, j:j + 512], scalar=cw(j), in1=ac[:],
                                                                                           op0=ALU.mult, op1=ALU.add),
                                 reads=[up, pp, ac], writes=[ac])
                        kind, h = fc // 4, fc % 4
                        if kind == 2:
                            S.op("scalar", lambda e, ac=ac, h=h: e.activation(vT[:, :, h, :], ac[:].rearrange("p (a t) -> p a t", a=4), AF.Silu),
                                 reads=[ac], writes=[vT])
                        else:
                            dst = qT if kind == 0 else kT
                            S.op("scalar", lambda e, ac=ac, at=at: e.activation(at[:], ac[:], AF.Silu), reads=[ac], writes=[at])
                            S.op("gpsimd", lambda e, at=at, sqt=sqt: e.tensor_tensor(sqt[:], at[:], at[:], op=ALU.mult), reads=[at], writes=[sqt])

                            def fin(sqt=sqt, rv=rv, at=at, dst=dst, h=h):
                                S.op("tensor", lambda e: e.matmul(pS[:], lhsT=ONESF(), rhs=sqt[:], start=True, stop=True),
                                     reads=[cst, sqt], writes=[pS])
                                S.op("scalar", lambda e: e.activation(rv[:], pS[:], AF.Ln, bias=epsc()), reads=[pS, cst], writes=[rv])
                                S.op("scalar", lambda e: e.activation(rv[:], rv[:], AF.Exp, scale=-0.5), reads=[rv], writes=[rv])
                                S.op("vector", lambda e: e.tensor_tensor(
                                    dst[:, :, h, :], at[:].rearrange("p (a t) -> p a t", a=4), rv[:].rearrange("p (a t) -> p a t", a=4), op=ALU.mult),
                                    reads=[at, rv], writes=[dst])
                            pending.append(fin)

                    while pending:
                        pending.pop(0)()
                    for sub in range(4):
                        S.op("tensor", lambda e, sub=sub: e.matmul(pSm[:, sub, 8:12], lhsT=TRI(), rhs=gg[:, sub, :], start=True, stop=True),
                             reads=[cst, gg], writes=[pSm])
                        S.op("tensor", lambda e, sub=sub: e.matmul(pSm[:, sub, 12:16], lhsT=ONESF(), rhs=gg[:, sub, :], start=True, stop=True),
                             reads=[cst, gg], writes=[pSm])
                    S.op("vector", lambda e: e.tensor_copy(gcl[:], pSm[:, 0:4, 8:16]), reads=[pSm], writes=[gcl])
                    S.op("vector", lambda e: e.tensor_scalar_mul(ngc[:], gcl[:, :, 0:4], -1.0), reads=[gcl], writes=[ngc])
                    S.op("scalar", lambda e: e.activation(egc[:], gcl[:, :, 0:4], AF.Exp), reads=[gcl], writes=[egc])
                    S.op("scalar", lambda e: e.activation(egl[:], gcl[:, :, 4:8], AF.Exp), reads=[gcl], writes=[egl])
                    S.op("vector", lambda e: e.tensor_tensor(kds[:], gcl[:, :, 4:8], gcl[:, :, 0:4], op=ALU.subtract), reads=[gcl], writes=[kds])
                    S.op("scalar", lambda e: e.activation(kds[:], kds[:], AF.Exp), reads=[kds], writes=[kds])
                    S.op("vector", lambda e: e.tensor_tensor(bgs[:], beta[:], egc[:], op=ALU.mult), reads=[beta, egc], writes=[bgs])
                    S.op("vector", lambda e: e.tensor_scalar_mul(qsc[:], egc[:], 128.0 ** -0.5), reads=[egc], writes=[qsc])
                    S.dma("sync", lambda e, s=s, blk=blk: e.dma_start(out=EGL[s, blk * 4:(blk + 1) * 4, :, :].rearrange("n p h -> p n h"),
                                                                       in_=egl[:]), egl, reads=[egl])
                    for sub in range(4):
                        for h in range(4):
                            S.op("vector", lambda e, sub=sub, h=h: e.tensor_copy(gbc[:, h, :], gg[:, sub, h:h + 1].to_broadcast([128, 128])),
                                 reads=[gg], writes=[gbc])
                        for h in range(4):
                            S.op("tensor", lambda e, h=h: e.matmul(pD[:, h, :], lhsT=gbc[:, h, :], rhs=TRI(), start=True, stop=False),
                                 reads=[gbc, cst], writes=[pD])
                            S.op("tensor", lambda e, h=h: e.matmul(pD[:, h, :], lhsT=identF(), rhs=cst[:, C_MASKNEG:C_MASKNEG + 128],
                                                                   start=False, stop=True), reads=[cst], writes=[pD])
                        for h in range(4):
                            S.op("scalar", lambda e, sub=sub, h=h: e.activation(dec[:, sub, h, :], pD[:, h, :], AF.Exp,
                                                                                  bias=ngc[:, sub, h:h + 1]),
                                 reads=[pD, ngc], writes=[dec])
                    S.dma("sync", lambda e, s=s, blk=blk: e.dma_start(
                        out=DEC[s, blk * 4:(blk + 1) * 4, :, :, :].rearrange("n p h c -> p n h c"), in_=dec[:]), dec, reads=[dec])

                    for fc in range(8):
                        pu = pU[cc % 2]; sqt = sqq[cc % 2]; rv = rinv[cc % 2]
                        cc += 1
                        for kc in range(8):
                            S.op("tensor", lambda e, kc=kc, fc=fc, pu=pu: e.matmul(pu[:], lhsT=Wdqk[:, kc, fc * 128:(fc + 1) * 128],
                                                                                    rhs=hT[:, kc, :], start=(kc == 0), stop=(kc == 7)),
                                 reads=[Wdqk, hT], writes=[pu])
                        while pending:
                            pending.pop(0)()
                        S.op("scalar", lambda e, pu=pu, sqt=sqt: e.activation(sqt[:], pu[:], AF.Square), reads=[pu], writes=[sqt])
                        gain = qgain[:, 0:1] if fc < 4 else pp[:, PP_KN:PP_KN + 1]

                        def fin2(pu=pu, sqt=sqt, rv=rv, fc=fc, gain=gain):
                            S.op("tensor", lambda e: e.matmul(pS[:], lhsT=cst[:, C_BLK64:C_BLK64 + 128], rhs=sqt[:], start=True, stop=True),
                                 reads=[cst, sqt], writes=[pS])
                            S.op("scalar", lambda e: e.activation(rv[:], pS[:], AF.Ln, bias=epsc(), scale=1.0 / 64), reads=[pS, cst], writes=[rv])
                            S.op("scalar", lambda e: e.activation(rv[:], rv[:], AF.Exp, scale=-0.5), reads=[rv], writes=[rv])
                            S.op("vector", lambda e: e.scalar_tensor_tensor(
                                dqkT[:, fc, :], in0=pu[:], scalar=gain, in1=rv[:], op0=ALU.mult, op1=ALU.mult),
                                reads=[pu, rv, qgain, pp], writes=[dqkT])
                        pending.append(fin2)
                    while pending:
                        pending.pop(0)()
                    S.dma("sync", lambda e, s=s, blk=blk: e.dma_start(out=DQT[s, :, :, blk * 512:(blk + 1) * 512].rearrange("h p t -> p h t"),
                                                                       in_=dqkT[:, 0:4, :]), dqkT, reads=[dqkT])
                    S.dma("sync", lambda e, s=s, blk=blk: e.dma_start(out=DKT[s, :, :, blk * 512:(blk + 1) * 512].rearrange("h p t -> p h t"),
                                                                       in_=dqkT[:, 4:8, :]), dqkT, reads=[dqkT])

                    for W, dstt, fn in ((Wgate, gatet, AF.Silu), (Wdv, dvt, AF.Copy)):
                        for sub in range(4):
                            for kc in range(8):
                                S.op("tensor", lambda e, sub=sub, kc=kc, W=W: e.matmul(pTok[:], lhsT=hT[:, kc, sub * 128:(sub + 1) * 128],
                                                                                        rhs=W[:, kc, :], start=(kc == 0), stop=(kc == 7)),
                                     reads=[hT, W], writes=[pTok])
                            S.op("scalar", lambda e, sub=sub, dstt=dstt, fn=fn: e.activation(dstt[:, sub, :], pTok[:], fn),
                                 reads=[pTok], writes=[dstt])
                        dd = GATE if dstt is gatet else DV
                        S.dma("sync", lambda e, dd=dd, dstt=dstt, s=s, blk=blk: e.dma_start(
                            out=dd[s, blk * 4:(blk + 1) * 4, :, :].rearrange("n p f -> p n f"), in_=dstt[:]), dstt, reads=[dstt])
                    for sub in range(4):
                        for h in range(4):
                            S.op("tensor", lambda e, sub=sub, h=h: e.transpose(pTr[:, h, :], kT[:, sub, h, :], identB[:]),
                                 reads=[kT, identB], writes=[pTr])
                        S.op("vector", lambda e, sub=sub: e.tensor_tensor(kbg[:, sub, :, :], pTr[:, 0:4, :], bgs[:, sub, :, None].to_broadcast([128, 4, 128]),
                                                                          op=ALU.mult), reads=[pTr, bgs], writes=[kbg])
                        S.op("vector", lambda e, sub=sub: e.tensor_tensor(kdec[:, sub, :, :], pTr[:, 0:4, :], kds[:, sub, :, None].to_broadcast([128, 4, 128]),
                                                                          op=ALU.mult), reads=[pTr, kds], writes=[kdec])
                        S.op("vector", lambda e, sub=sub: e.tensor_tensor(kb[:], pTr[:, 0:4, :], beta[:, sub, :, None].to_broadcast([128, 4, 128]),
                                                                          op=ALU.mult), reads=[pTr, beta], writes=[kb])
                        for h in range(4):
                            S.op("tensor", lambda e, h=h: e.transpose(pTr[:, h, :], kb[:, h, :], identB[:]), reads=[kb, identB], writes=[pTr])
                        S.op("scalar", lambda e, sub=sub: e.copy(kbT[:, sub, :, :], pTr[:, 0:4, :]), reads=[pTr], writes=[kbT])
                        for h in range(4):
                            S.op("tensor", lambda e, sub=sub, h=h: e.transpose(pTr[:, h, :], qT[:, sub, h, :], identB[:]),
                                 reads=[qT, identB], writes=[pTr])
                        S.op("vector", lambda e, sub=sub: e.tensor_tensor(qg[:], pTr[:, 0:4, :], qsc[:, sub, :, None].to_broadcast([128, 4, 128]),
                                                                          op=ALU.mult), reads=[pTr, qsc], writes=[qg])
                        for h in range(4):
                            S.op("tensor", lambda e, h=h: e.transpose(pTr[:, h, :], qg[:, h, :], identB[:]), reads=[qg, identB], writes=[pTr])
                        S.op("scalar", lambda e, sub=sub: e.copy(qgT[:, sub, :, :], pTr[:, 0:4, :]), reads=[pTr], writes=[qgT])
                        for h in range(4):
                            S.op("tensor", lambda e, sub=sub, h=h: e.transpose(pTr[:, h, :], vT[:, sub, h, :], identB[:]),
                                 reads=[vT, identB], writes=[pTr])
                        S.op("vector", lambda e, sub=sub: e.tensor_tensor(vb[:, sub, :, :], pTr[:, 0:4, :], beta[:, sub, :, None].to_broadcast([128, 4, 128]),
                                                                          op=ALU.mult), reads=[pTr, beta], writes=[vb])
                    for dst, src in ((KT, kT), (QT, qT), (QGT, qgT), (KBT, kbT), (KBG, kbg), (KDEC, kdec), (VB, vb)):
                        S.dma("sync", lambda e, dst=dst, src=src, s=s, blk=blk: e.dma_start(
                            out=dst[s, blk * 4:(blk + 1) * 4, :, :, :].rearrange("n p h t -> p n h t"), in_=src[:]), src, reads=[src])

            if upto == "C":
                break
            with Phase(S, "D%d" % l) as P:
                pp = P.sb("pp", [128, NPP], F32)
                S.dma("sync", lambda e, l=l: e.dma_start(out=pp[:], in_=pp_in[l, :, :]), pp, writes=[pp])
                NSLOT = 3
                names = ("kT", "qT", "qgT", "kbT", "kbg", "kdec", "vb")
                srcs = dict(kT=KT, qT=QT, qgT=QGT, kbT=KBT, kbg=KBG, kdec=KDEC, vb=VB)
                slots = {}
                for s in range(2):
                    for k in range(NSLOT):
                        d_ = {nm: P.sb("%s_%d_%d" % (nm, s, k), [128, 4, 128], BF16) for nm in names}
                        d_["dec"] = P.sb("dec_%d_%d" % (s, k), [128, 4, 128], F32)
                        d_["egl"] = P.sb("egl_%d_%d" % (s, k), [128, 4], F32)
                        d_["gate"] = P.sb("gate_%d_%d" % (s, k), [128, 4, 128], BF16)
                        slots[(s, k)] = d_
                strict = cst[:, None, C_STRICT:C_STRICT + 128].to_broadcast([128, 4, 128])
                identq = cst[:, None, C_IDENT:C_IDENT + 128].to_broadcast([128, 4, 128])
                onormb = pp[:, None, PP_ONORM:PP_ONORM + 128].to_broadcast([128, 4, 128])
                pre = [P.ps("pre%d" % i, [128, 4, 128], F32) for i in range(2)]
                pinv = [P.ps("pinv%d" % i, [128, 4, 128], F32) for i in range(2)]
                pW = P.ps("pW", [128, 4, 128], F32)
                pO = P.ps("pO", [128, 4, 128], F32)
                pSt = P.ps("pSt", [128, 4, 128], F32)
                pTr = P.ps("pTr", [128, 8, 128], BF16)
                st = {}
                for s in range(2):
                    st[s] = dict(
                        Y=[P.sb("Y%d_%d" % (s, i), [128, 4, 128], F32) for i in range(2)],
                        X=[P.sb("X%d_%d" % (s, i), [128, 4, 128], F32) for i in range(2)],
                        Q=P.sb("Q%d" % s, [128, 4, 128], F32), Qb=P.sb("Qb%d" % s, [128, 4, 128], BF16),
                        P=P.sb("Pm%d" % s, [128, 4, 128], F32), Yf=P.sb("Yf%d" % s, [128, 4, 128], F32), Xf=P.sb("Xf%d" % s, [128, 4, 128], F32),
                        t1=P.sb("t1%d" % s, [128, 4, 128], F32),
                        aT=P.sb("aT%d" % s, [128, 4, 128], BF16), u=P.sb("u%d" % s, [128, 4, 128], F32),
                        wT=P.sb("wT%d" % s, [128, 4, 128], BF16), vn=P.sb("vn%d" % s, [128, 4, 128], BF16),
                        Sf=P.sb("Sf%d" % s, [128, 4, 128], F32), Sb=P.sb("Sb%d" % s, [128, 4, 128], BF16),
                        osq=P.sb("osq%d" % s, [128, 4, 128], F32), oss=P.sb("oss%d" % s, [128, 4], F32),
                        o1=P.sb("o1%d" % s, [128, 4, 128], F32), g2=P.sb("g2%d" % s, [128, 4, 128], F32),
                        y=P.sb("y%d" % s, [128, 4, 128], BF16),
                        oT=P.sb("oT%d" % s, [128, 4, 512], BF16),
                    )
                    S.op("gpsimd", lambda e, s=s: e.memset(st[s]["Sf"][:], 0.0), writes=[st[s]["Sf"]])
                    S.op("gpsimd", lambda e, s=s: e.memset(st[s]["Sb"][:], 0.0), writes=[st[s]["Sb"]])

                def dbg(nm, tile, s, n, nn=0):
                    if debug and s == 0 and n == nn and l == 0:
                        S.dma("sync", lambda e: e.dma_start(out=DBG[nm][:, :, :], in_=tile[:]), tile, reads=[tile])

                def loadD(s, n):
                    sl = slots[(s, n % NSLOT)]
                    for nm in names:
                        S.dma("sync", lambda e, nm=nm, sl=sl: e.dma_start(out=sl[nm][:], in_=srcs[nm][s, n, :, :, :]), sl[nm], writes=[sl[nm]])
                    S.dma("sync", lambda e, sl=sl: e.dma_start(out=sl["dec"][:], in_=DEC[s, n, :, :, :]), sl["dec"], writes=[sl["dec"]])
                    S.dma("sync", lambda e, sl=sl: e.dma_start(out=sl["egl"][:], in_=EGL[s, n, :, :]), sl["egl"], writes=[sl["egl"]])
                    S.dma("sync", lambda e, sl=sl: e.dma_start(out=sl["gate"][:], in_=GATE[s, n, :, :].rearrange("p (h e) -> p h e", h=4)),
                          sl["gate"], writes=[sl["gate"]])

                for s in range(2):
                    loadD(s, 0)
                    loadD(s, 1)
                pk = [0, 0]

                def mm4(out, lhs, rhs, reads, acc=None):
                    for h in range(4):
                        S.op("tensor", lambda e, h=h: e.matmul(out[:, h, :], lhsT=lhs[:, h, :], rhs=rhs[:, h, :],
                                                               start=(acc in (None, "start")), stop=(acc in (None, "stop"))),
                             reads=reads, writes=[out])

                for n in range(NCH):
                    SL = {s: slots[(s, n % NSLOT)] for s in range(2)}
                    if n + 2 < NCH:
                        for s in range(2):
                            loadD(s, n + 2)
                    for s in range(2):
                        sl, q = SL[s], st[s]
                        pg = pre[s]
                        q["pg"] = pg
                        mm4(pg, sl["kT"], sl["kbT"], [sl["kT"], sl["kbT"]])
                    bc = lambda c0: cst[:, None, c0:c0 + 128].to_broadcast([128, 4, 128])
                    for s in range(2):
                        sl, q = SL[s], st[s]
                        pg = q["pg"]
                        S.op("vector", lambda e, q=q, sl=sl, pg=pg: e.scalar_tensor_tensor(q["t1"][:], in0=pg[:], scalar=-1.0, in1=sl["dec"][:],
                                                                                        op0=ALU.mult, op1=ALU.mult), reads=[pg, sl["dec"]], writes=[q["t1"]])
                        S.op("gpsimd", lambda e, q=q: e.tensor_tensor(q["Yf"][:], q["t1"][:], strict, op=ALU.mult),
                             reads=[q["t1"], cst], writes=[q["Yf"]])
                    for s in range(2):
                        q = st[s]
                        for h in range(4):
                            S.op("tensor", lambda e, h=h, q=q, pi=pinv[s]: e.transpose(pi[:, h, :], q["Yf"][:, h, :], identF()),
                                 reads=[q["Yf"], cst], writes=[pinv[s]])
                    for s in range(2):
                        q = st[s]
                        S.op("scalar", lambda e, q=q, pi_=pinv[s]: e.copy(q["Xf"][:], pi_[:]), reads=[pinv[s]], writes=[q["Xf"]])
                        S.op("gpsimd", lambda e, q=q: e.tensor_tensor(q["Y"][0][:], q["Yf"][:], bc(C_BM16), op=ALU.mult), reads=[q["Yf"], cst], writes=[q["Y"][0]])
                        S.op("gpsimd", lambda e, q=q: e.tensor_tensor(q["X"][0][:], q["Xf"][:], bc(C_BM16), op=ALU.mult), reads=[q["Xf"], cst], writes=[q["X"][0]])
                        S.op("gpsimd", lambda e, q=q: e.tensor_tensor(q["Q"][:], q["Y"][0][:], identq, op=ALU.add), reads=[q["Y"][0], cst], writes=[q["Q"]])
                        S.op("gpsimd", lambda e, q=q: e.tensor_tensor(q["P"][:], q["X"][0][:], identq, op=ALU.add), reads=[q["X"][0], cst], writes=[q["P"]])
                    for lev in range(1, 4):
                        a, b = (lev - 1) % 2, lev % 2
                        for s in range(2):
                            q = st[s]
                            mm4(pinv[s], q["Y"][a], q["X"][a], [q["Y"][a], q["X"][a]])
                            mm4(pre[s], q["X"][a], q["Y"][a], [q["Y"][a], q["X"][a]])
                        for s in range(2):
                            q = st[s]
                            S.op("scalar", lambda e, q=q, b=b, px_=pinv[s]: e.copy(q["X"][b][:], px_[:]), reads=[pinv[s]], writes=[q["X"][b]])
                            S.op("vector", lambda e, q=q, b=b, py_=pre[s]: e.tensor_copy(q["Y"][b][:], py_[:]), reads=[pre[s]], writes=[q["Y"][b]])
                        for s in range(2):
                            q = st[s]
                            mm4(pinv[s], q["X"][b], q["Q"], [q["X"][b], q["Q"]])
                            mm4(pre[s], q["Y"][b], q["P"], [q["Y"][b], q["P"]])
                        for s in range(2):
                            q = st[s]
                            S.op("vector", lambda e, q=q, pq_=pinv[s]: e.tensor_tensor(q["Q"][:], q["Q"][:], pq_[:], op=ALU.add),
                                 reads=[q["Q"], pinv[s]], writes=[q["Q"]])
                            S.op("vector", lambda e, q=q, pp_=pre[s]: e.tensor_tensor(q["P"][:], q["P"][:], pp_[:], op=ALU.add),
                                 reads=[q["P"], pre[s]], writes=[q["P"]])
                    for mi, coff in enumerate((C_OFF32, C_OFF64, C_OFF128)):
                        lastm = (mi == 2)
                        for s in range(2):
                            q = st[s]
                            S.op("gpsimd", lambda e, q=q, coff=coff: e.tensor_tensor(q["Y"][0][:], q["Yf"][:], bc(coff), op=ALU.mult), reads=[q["Yf"], cst], writes=[q["Y"][0]])
                            S.op("gpsimd", lambda e, q=q, coff=coff: e.tensor_tensor(q["X"][0][:], q["Xf"][:], bc(coff), op=ALU.mult), reads=[q["Xf"], cst], writes=[q["X"][0]])
                        for s in range(2):
                            q = st[s]
                            mm4(pinv[s], q["X"][0], q["Q"], [q["X"][0], q["Q"]])
                            if not lastm:
                                mm4(pre[s], q["Y"][0], q["P"], [q["Y"][0], q["P"]])
                        for s in range(2):
                            q = st[s]
                            S.op("scalar", lambda e, q=q, p_=pinv[s]: e.copy(q["X"][1][:], p_[:]), reads=[pinv[s]], writes=[q["X"][1]])
                            if not lastm:
                                S.op("vector", lambda e, q=q, p_=pre[s]: e.tensor_copy(q["Y"][1][:], p_[:]), reads=[pre[s]], writes=[q["Y"][1]])
                        for s in range(2):
                            q = st[s]
                            mm4(pinv[s], q["P"], q["X"][1], [q["P"], q["X"][1]])
                            if not lastm:
                                mm4(pre[s], q["Q"], q["Y"][1], [q["Q"], q["Y"][1]])
                        for s in range(2):
                            q = st[s]
                            S.op("vector", lambda e, q=q, p_=pinv[s]: e.tensor_tensor(q["Q"][:], q["Q"][:], p_[:], op=ALU.add),
                                 reads=[q["Q"], pinv[s]], writes=[q["Q"]])
                            if not lastm:
                                S.op("vector", lambda e, q=q, p_=pre[s]: e.tensor_tensor(q["P"][:], q["P"][:], p_[:], op=ALU.add),
                                     reads=[q["P"], pre[s]], writes=[q["P"]])
                    for s in range(2):
                        sl, q = SL[s], st[s]
                        S.op("scalar", lambda e, q=q: e.copy(q["Qb"][:], q["Q"][:]), reads=[q["Q"]], writes=[q["Qb"]])
                        dbg("Q", q["Q"], s, n)
                        pa = pre[s]
                        q["pa"] = pa
                        mm4(pa, sl["kT"], sl["qT"], [sl["kT"], sl["qT"]])
                    for s in range(2):
                        sl, q = SL[s], st[s]
                        S.op("vector", lambda e, q=q, sl=sl, pa_=q["pa"]: e.scalar_tensor_tensor(q["aT"][:], in0=pa_[:], scalar=128.0 ** -0.5, in1=sl["dec"][:],
                                                                                     op0=ALU.mult, op1=ALU.mult), reads=[q["pa"], sl["dec"]], writes=[q["aT"]])
                        pu_ = pinv[s]
                        q["pu"] = pu_
                        mm4(pu_, q["Qb"], sl["vb"], [q["Qb"], sl["vb"]])
                    for s in range(2):
                        sl, q = SL[s], st[s]
                        S.op("scalar", lambda e, q=q, pu_=q["pu"]: e.copy(q["u"][:], pu_[:]), reads=[q["pu"]], writes=[q["u"]])
                        dbg("U", q["u"], s, n)
                        pw_ = pre[s]
                        q["pw"] = pw_
                        mm4(pw_, sl["kbg"], q["Qb"], [q["Qb"], sl["kbg"]])
                    for s in range(2):
                        q = st[s]
                        S.op("scalar", lambda e, q=q, pw_=q["pw"]: e.copy(q["wT"][:], pw_[:]), reads=[q["pw"]], writes=[q["wT"]])
                    for s in range(2):
                        sl, q = SL[s], st[s]
                        mm4(pW, q["wT"], q["Sb"], [q["wT"], q["Sb"]])
                        S.op("vector", lambda e, q=q: e.tensor_tensor(q["vn"][:], q["u"][:], pW[:], op=ALU.subtract),
                             reads=[q["u"], pW], writes=[q["vn"]])
                        if debug and s == 0 and n == 1 and l == 0:
                            S.op("vector", lambda e, q=q: e.tensor_copy(q["o1"][:], pW[:]), reads=[pW], writes=[q["o1"]])
                            dbg("PW1", q["o1"], s, n, 1)
                        for h in range(4):
                            S.op("tensor", lambda e, h=h, q=q, sl=sl: e.matmul(pO[:, h, :], lhsT=sl["qgT"][:, h, :], rhs=q["Sb"][:, h, :], start=True, stop=False),
                                 reads=[sl["qgT"], q["Sb"]], writes=[pO])
                            S.op("tensor", lambda e, h=h, q=q: e.matmul(pO[:, h, :], lhsT=q["aT"][:, h, :], rhs=q["vn"][:, h, :], start=False, stop=True),
                                 reads=[q["aT"], q["vn"]], writes=[pO])
                        mm4(pSt, sl["kdec"], q["vn"], [sl["kdec"], q["vn"]])
                        S.op("gpsimd", lambda e, q=q, sl=sl: e.tensor_tensor(q["Sf"][:], q["Sf"][:], sl["egl"][:, :, None].to_broadcast([128, 4, 128]),
                                                                            op=ALU.mult), reads=[q["Sf"], sl["egl"]], writes=[q["Sf"]])
                        S.op("vector", lambda e, q=q: e.tensor_tensor(q["Sf"][:], q["Sf"][:], pSt[:], op=ALU.add), reads=[q["Sf"], pSt], writes=[q["Sf"]])
                        S.op("scalar", lambda e, q=q: e.copy(q["Sb"][:], q["Sf"][:]), reads=[q["Sf"]], writes=[q["Sb"]])
                        dbg("S0", q["Sf"], s, n)
                        dbg("U1", q["u"], s, n, 1)
                        S.op("scalar", lambda e, q=q: e.activation(q["osq"][:], pO[:], AF.Square), reads=[pO], writes=[q["osq"]])
                        S.op("vector", lambda e, q=q: e.reduce_sum(q["oss"][:], q["osq"][:], axis=mybir.AxisListType.X), reads=[q["osq"]], writes=[q["oss"]])
                        S.op("scalar", lambda e, q=q: e.activation(q["oss"][:], q["oss"][:], AF.Ln, bias=epsc(), scale=1.0 / 128), reads=[q["oss"], cst], writes=[q["oss"]])
                        S.op("scalar", lambda e, q=q: e.activation(q["oss"][:], q["oss"][:], AF.Exp, scale=-0.5), reads=[q["oss"]], writes=[q["oss"]])
                        S.op("vector", lambda e, q=q: e.tensor_tensor(q["o1"][:], pO[:], q["oss"][:, :, None].to_broadcast([128, 4, 128]), op=ALU.mult),
                             reads=[pO, q["oss"]], writes=[q["o1"]])
                        S.op("gpsimd", lambda e, q=q, sl=sl: e.tensor_tensor(q["g2"][:], sl["gate"][:], onormb, op=ALU.mult), reads=[sl["gate"], pp], writes=[q["g2"]])
                        S.op("gpsimd", lambda e, q=q: e.tensor_tensor(q["y"][:], q["o1"][:], q["g2"][:], op=ALU.mult), reads=[q["o1"], q["g2"]], writes=[q["y"]])
                        for h in range(4):
                            S.op("tensor", lambda e, h=h, q=q: e.transpose(pTr[:, h, :], q["y"][:, h, :], identB[:]), reads=[q["y"], identB], writes=[pTr])
                        j = n % 4
                        S.op("scalar", lambda e, q=q, j=j: e.copy(q["oT"][:, :, j * 128:(j + 1) * 128], pTr[:, 0:4, :]), reads=[pTr], writes=[q["oT"]])
                        if j == 3:
                            t0 = (n - 3) * 128
                            S.dma("sync", lambda e, q=q, s=s, t0=t0: e.dma_start(out=OT[s, 0:4, :, t0:t0 + 512].rearrange("h p t -> p h t"), in_=q["oT"][:]),
                                  q["oT"], reads=[q["oT"]])
            if upto == "D":
                break
            with Phase(S, "E%d" % l) as P:
                pp = P.sb("pp", [128, NPP], F32)
                S.dma("sync", lambda e, l=l: e.dma_start(out=pp[:], in_=pp_in[l, :, :]), pp, writes=[pp])
                lam_init = 0.8 - 0.6 * math.exp(-0.3 * l)
                lamr = P.sb("lamr", [1, 4, 64], F32)
                S.dma("sync", lambda e, l=l: e.dma_start(out=lamr[:], in_=lam_in[l:l + 1, :, :]), lamr, writes=[lamr])
                lprod = P.sb("lprod", [1, 2, 64], F32)
                lsum = P.sb("lsum", [1, 2], F32)
                nlam1 = P.sb("nlam1", [1, 1], F32)
                nlam = P.sb("nlam", [128, 1], F32)
                subg = P.sb("subg", [128, 1], F32)
                b31 = P.sb("b31", [128, 4], F32)
                S.dma("sync", lambda e: e.dma_start(out=b31[:], in_=b31_in[:, :]), b31, writes=[b31])
                S.op("vector", lambda e: e.tensor_tensor(lprod[:], lamr[:, 0:4:2, :], lamr[:, 1:4:2, :], op=ALU.mult), reads=[lamr], writes=[lprod])
                S.op("vector", lambda e: e.reduce_sum(lsum[:], lprod[:], axis=mybir.AxisListType.X), reads=[lprod], writes=[lsum])
                S.op("scalar", lambda e: e.activation(lsum[:], lsum[:], AF.Exp), reads=[lsum], writes=[lsum])
                S.op("vector", lambda e: e.scalar_tensor_tensor(nlam1[:], in0=lsum[:, 1:2], scalar=-lam_init, in1=lsum[:, 0:1],
                                                                op0=ALU.add, op1=ALU.subtract), reads=[lsum], writes=[nlam1])
                pS0 = [P.ps("pS0_%d" % i, [128, 512], F32) for i in range(2)]
                pS1 = [P.ps("pS1_%d" % i, [128, 512], F32) for i in range(2)]
                pO0 = P.ps("pO0", [128, 512], F32); pO1 = P.ps("pO1", [128, 512], F32)
                pZ0 = P.ps("pZ0", [128, 512], F32); pZ1 = P.ps("pZ1", [128, 512], F32)
                S.op("tensor", lambda e: e.matmul(pZ0[:, 0:1], lhsT=cst[0:1, C_ONES:C_ONES + 128], rhs=nlam1[:], start=True, stop=True),
                     reads=[cst, nlam1], writes=[pZ0])
                S.op("vector", lambda e: e.tensor_copy(nlam[:], pZ0[:, 0:1]), reads=[pZ0], writes=[nlam])
                S.op("vector", lambda e: e.tensor_scalar_mul(subg[:], pp[:, PP_SUBLN:PP_SUBLN + 1], 1.0 - lam_init), reads=[pp], writes=[subg])
                expB = []
                for h in range(4):
                    t = P.sb("expB%d" % h, [128, 1024], F32)
                    S.dma("sync", lambda e, h=h, t=t: e.dma_start(out=t[:], in_=tb_in[h, :, :]), t, writes=[t])
                    S.op("scalar", lambda e, t=t: e.activation(t[:], t[:], AF.Exp), reads=[t], writes=[t])
                    expB.append(t)
                qts = [P.sb("qt%d" % i, [128, T], BF16) for i in range(2)]
                kts = [P.sb("kt%d" % i, [128, T], BF16) for i in range(2)]
                vts = [P.sb("vt%d" % i, [128, NCH, 128], BF16) for i in range(2)]
                E0 = [P.sb("E0_%d" % i, [128, 512], BF16) for i in range(3)]
                E1 = [P.sb("E1_%d" % i, [128, 512], BF16) for i in range(3)]
                Ef = [P.sb("Ef_%d" % i, [128, 512], F32) for i in range(2)]
                rz0 = P.sb("rz0", [128, 512], F32); rz1 = P.sb("rz1", [128, 512], F32)
                zc0 = P.sb("zc0", [128, 512], F32); zc1 = P.sb("zc1", [128, 512], F32)
                oc0 = P.sb("oc0", [128, 512], F32); oc1 = P.sb("oc1", [128, 512], F32)
                oo = P.sb("oo", [128, 512], F32); osq = P.sb("osq", [128, 512], F32)
                rin = P.sb("rin", [128, 512], F32)
                oTs = [P.sb("oTs%d" % i, [128, 512], BF16) for i in range(2)]
                heads = [(s, h) for s in range(2) for h in range(4)]

                def loadE(i):
                    s, h = heads[i]
                    S.dma("sync", lambda e: e.dma_start(out=qts[i % 2][:], in_=DQT[s, h, :, :]), qts[i % 2], writes=[qts[i % 2]])
                    S.dma("sync", lambda e: e.dma_start(out=kts[i % 2][:], in_=DKT[s, h, :, :]), kts[i % 2], writes=[kts[i % 2]])
                    S.dma("sync", lambda e: e.dma_start(out=vts[i % 2][:], in_=DV[s, :, :, h * 128:(h + 1) * 128].rearrange("n p e -> p n e")),
                          vts[i % 2], writes=[vts[i % 2]])

                loadE(0)
                steps = []
                for i, (s, h) in enumerate(heads):
                    for j in range(8):
                        nk = 4 * j + 4
                        for ki in range(nk):
                            steps.append((i, s, h, j, ki, nk))
                cnt = {"ek": 0, "fk": 0, "ok": 0, "loaded": 0}
                live = {}

                def stageA(idx):
                    i, s, h, j, ki, nk = steps[idx]
                    qt, kt = qts[i % 2], kts[i % 2]
                    ek = cnt["ek"]; cnt["ek"] += 1
                    p0, p1 = pS0[ek % 2], pS1[ek % 2]
                    e0, e1 = E0[ek % 3], E1[ek % 3]
                    S.op("tensor", lambda e: e.matmul(p0[:], lhsT=kt[0:64, ki * 128:(ki + 1) * 128], rhs=qt[0:64, j * 512:(j + 1) * 512],
                                                      start=True, stop=True), reads=[kt, qt], writes=[p0])
                    S.op("tensor", lambda e: e.matmul(p1[:], lhsT=kt[64:128, ki * 128:(ki + 1) * 128], rhs=qt[64:128, j * 512:(j + 1) * 512],
                                                      start=True, stop=True), reads=[kt, qt], writes=[p1])
                    near = ki >= 4 * j - 1
                    if not near:
                        S.op("scalar", lambda e: e.activation(e0[:], p0[:], AF.Exp, bias=b31[:, h:h + 1]), reads=[p0, b31], writes=[e0])
                        S.op("scalar", lambda e: e.activation(e1[:], p1[:], AF.Exp, bias=b31[:, h:h + 1]), reads=[p1, b31], writes=[e1])
                    else:
                        c0 = 512 * j - 128 * ki + 384
                        for pc, ec in ((p0, e0), (p1, e1)):
                            ef = Ef[cnt["fk"] % 2]; cnt["fk"] += 1
                            S.op("scalar", lambda e, pc=pc, ef=ef: e.activation(ef[:], pc[:], AF.Exp), reads=[pc], writes=[ef])
                            S.op("vector",
                                 lambda e, ef=ef, ec=ec: e.tensor_tensor(ec[:], ef[:], expB[h][:, c0:c0 + 512], op=ALU.mult),
                                 reads=[ef, expB[h]], writes=[ec])
                    live[idx] = (e0, e1)

                def stageB(idx):
                    i, s, h, j, ki, nk = steps[idx]
                    if j == 0 and ki == 0 and i + 1 < len(heads):
                        loadE(i + 1)
                    vt = vts[i % 2]
                    e0, e1 = live.pop(idx)
                    first, lastk = (ki == 0), (ki == nk - 1)
                    S.op("tensor", lambda e: e.matmul(pO0[:], lhsT=vt[:, ki, :], rhs=e0[:], start=first, stop=lastk), reads=[vt, e0], writes=[pO0])
                    S.op("tensor", lambda e: e.matmul(pZ0[:], lhsT=onesB[:], rhs=e0[:], start=first, stop=lastk), reads=[onesB, e0], writes=[pZ0])
                    S.op("tensor", lambda e: e.matmul(pO1[:], lhsT=vt[:, ki, :], rhs=e1[:], start=first, stop=lastk), reads=[vt, e1], writes=[pO1])
                    S.op("tensor", lambda e: e.matmul(pZ1[:], lhsT=onesB[:], rhs=e1[:], start=first, stop=lastk), reads=[onesB, e1], writes=[pZ1])
                    if not lastk:
                        return
                    S.op("scalar", lambda e: e.copy(zc0[:], pZ0[:]), reads=[pZ0], writes=[zc0])
                    S.op("vector", lambda e: e.tensor_copy(zc1[:], pZ1[:]), reads=[pZ1], writes=[zc1])
                    S.op("scalar", lambda e: e.copy(oc0[:], pO0[:]), reads=[pO0], writes=[oc0])
                    S.op("vector", lambda e: e.tensor_copy(oc1[:], pO1[:]), reads=[pO1], writes=[oc1])
                    pend_epi.append((s, h, j))
                    return

                def epilogue_tail():
                    s, h, j = pend_epi.pop(0)
                    S.op("vector", lambda e: e.reciprocal(rz0[:], zc0[:]), reads=[zc0], writes=[rz0])
                    S.op("vector", lambda e: e.reciprocal(rz1[:], zc1[:]), reads=[zc1], writes=[rz1])
                    S.op("gpsimd", lambda e: e.tensor_tensor(rz0[:], oc0[:], rz0[:], op=ALU.mult), reads=[oc0, rz0], writes=[rz0])
                    S.op("gpsimd", lambda e: e.tensor_tensor(rz1[:], oc1[:], rz1[:], op=ALU.mult), reads=[oc1, rz1], writes=[rz1])
                    S.op("vector", lambda e: e.scalar_tensor_tensor(oo[:], in0=rz1[:], scalar=nlam[:, 0:1], in1=rz0[:], op0=ALU.mult, op1=ALU.add),
                         reads=[rz0, rz1, nlam], writes=[oo])
                    S.op("gpsimd", lambda e: e.tensor_tensor(osq[:], oo[:], oo[:], op=ALU.mult), reads=[oo], writes=[osq])
                    pend_epi2.append([s, h, j, 8])

                def epilogue_tail2():
                    s, h, j, _ = pend_epi2.pop(0)
                    ek = cnt["ek"]; cnt["ek"] += 1
                    pss = pS0[ek % 2]
                    S.op("tensor", lambda e: e.matmul(pss[:], lhsT=ONESF(), rhs=osq[:], start=True, stop=True), reads=[cst, osq], writes=[pss])
                    S.op("scalar", lambda e: e.activation(rin[:], pss[:], AF.Ln, bias=epsc(), scale=1.0 / 128), reads=[pss, cst], writes=[rin])
                    S.op("scalar", lambda e: e.activation(rin[:], rin[:], AF.Exp, scale=-0.5), reads=[rin], writes=[rin])
                    ot = oTs[cnt["ok"] % 2]; cnt["ok"] += 1
                    S.op("vector", lambda e: e.scalar_tensor_tensor(ot[:], in0=oo[:], scalar=subg[:, 0:1], in1=rin[:], op0=ALU.mult, op1=ALU.mult),
                         reads=[oo, subg, rin], writes=[ot])
                    S.dma("sync", lambda e: e.dma_start(out=OT[s, 4 + h, :, j * 512:(j + 1) * 512], in_=ot[:]), ot, reads=[ot])

                pend_epi = []
                pend_epi2 = []
                LOOK = 2
                for idx in range(min(LOOK, len(steps))):
                    stageA(idx)
                for idx in range(len(steps)):
                    had = len(pend_epi)
                    stageB(idx)
                    if idx + LOOK < len(steps):
                        stageA(idx + LOOK)
                    for it in pend_epi2:
                        it[3] -= 1
                    if had:
                        while pend_epi2:
                            epilogue_tail2()
                        epilogue_tail()
                    while pend_epi2 and pend_epi2[0][3] <= 0:
                        epilogue_tail2()
                while pend_epi:
                    while pend_epi2:
                        epilogue_tail2()
                    epilogue_tail()
                while pend_epi2:
                    epilogue_tail2()
            if upto == "E":
                break
            with Phase(S, "F%d" % l) as P:
                Wo = load_w_bf16(P, "Wo", w_out[l, :, :], 8, DM)
                gtB = []
                for b in range(2):
                    t = P.sb("gtB%d" % b, [128, DM], F32)
                    S.dma("sync", lambda e, b=b, t=t, l=l: e.dma_start(out=t[:], in_=MOD[l, b, 2 * DM:3 * DM].partition_broadcast(128)), t, writes=[t])
                    gtB.append(t)
                xts = [P.sb("xt%d" % i, [128, 4, DM], F32) for i in range(2)]
                ots = [P.sb("ot%d" % i, [128, 8, 512], BF16) for i in range(2)]
                tmp = [P.sb("tmp%d" % i, [128, 512], F32) for i in range(2)]
                pY = [P.ps("pY%d" % i, [128, 512], F32) for i in range(4)]
                blocks = [(s, blk) for s in range(2) for blk in range(8)]

                def loadF(i):
                    s, blk = blocks[i]
                    S.dma("sync", lambda e: e.dma_start(out=xts[i % 2][:], in_=xsrc[s, blk * 512:(blk + 1) * 512, :].rearrange("(a p) d -> p a d", p=128)),
                          xts[i % 2], writes=[xts[i % 2]])
                    S.dma("sync", lambda e: e.dma_start(out=ots[i % 2][:], in_=OT[s, :, :, blk * 512:(blk + 1) * 512].rearrange("c p t -> p c t")),
                          ots[i % 2], writes=[ots[i % 2]])

                loadF(0)
                yk = 0
                for i, (s, blk) in enumerate(blocks):
                    if i + 1 < len(blocks):
                        loadF(i + 1)
                    xt, ot = xts[i % 2], ots[i % 2]
                    for sub in range(4):
                        for dh in range(2):
                            py = pY[yk % 4]; tp = tmp[yk % 2]; yk += 1
                            for c in range(8):
                                S.op("tensor", lambda e, c=c, sub=sub, dh=dh, py=py, ot=ot: e.matmul(py[:], lhsT=ot[:, c, sub * 128:(sub + 1) * 128],
                                                                                                  rhs=Wo[:, c, dh * 512:(dh + 1) * 512], start=(c == 0), stop=(c == 7)),
                                     reads=[ot, Wo], writes=[py])
                            S.op("vector", lambda e, py=py, tp=tp, dh=dh, s=s: e.tensor_tensor(tp[:], py[:], gtB[s][:, dh * 512:(dh + 1) * 512], op=ALU.mult),
                                 reads=[py, gtB[s]], writes=[tp])
                            S.op("gpsimd", lambda e, tp=tp, sub=sub, dh=dh, xt=xt: e.tensor_tensor(xt[:, sub, dh * 512:(dh + 1) * 512], xt[:, sub, dh * 512:(dh + 1) * 512],
                                                                                                   tp[:], op=ALU.add), reads=[tp, xt], writes=[xt])
                    S.dma("sync", lambda e, xt=xt, s=s, blk=blk: e.dma_start(out=XA[s, blk * 512:(blk + 1) * 512, :].rearrange("(a p) d -> p a d", p=128), in_=xt[:]),
                          xt, reads=[xt])
            if upto == "F":
                break
            with Phase(S, "G%d" % l) as P:
                pp = P.sb("pp", [128, NPP], F32)
                S.dma("sync", lambda e, l=l: e.dma_start(out=pp[:], in_=pp_in[l, :, :]), pp, writes=[pp])
                Wup = load_w_bf16(P, "Wup", ffn_up[l, :, :], 8, 2 * DFF)
                AB = {}
                for b in range(2):
                    AB[b] = make_AB(P, l, b, pp, PP_NFFN, 3, 4, "ffn%d" % b)
                NB = 512
                xts = [P.sb("xt%d" % i, [128, 4, DM], F32) for i in range(2)]
                ss = P.sb("ss", [128, 4], F32); rs = P.sb("rs", [128, 4], F32)
                sq = P.sb("sq", [128, DM], BF16); xn = P.sb("xn", [128, 4, DM], BF16)
                tmpf = P.sb("tmpf", [128, 8, 128], F32)
                hT = P.sb("hT", [128, 8, NB], BF16)
                gts = [P.sb("gt%d" % i, [128, NB], BF16) for i in range(4)]
                pT = P.ps("pT", [128, 8, 128], BF16)
                pUf = [P.ps("pU%d" % i, [128, 512], F32) for i in range(6)]
                upad = [P.sb("upad%d" % i, [128, NB + 2], F32) for i in range(4)]
                acc = [P.sb("acc%d" % i, [128, NB], F32) for i in range(4)]
                sg = [P.sb("sg%d" % i, [128, NB], F32) for i in range(2)]
                halo = P.sb("halo", [128, 44, 2], F32)
                blocks = [(s, blk) for s in range(2) for blk in range(T // NB)]

                def loadG(i):
                    s, blk = blocks[i]
                    S.dma("sync", lambda e: e.dma_start(out=xts[i % 2][:], in_=XA[s, blk * NB:(blk + 1) * NB, :].rearrange("(a p) d -> p a d", p=128)),
                          xts[i % 2], writes=[xts[i % 2]])

                loadG(0)
                uk = 0; pk = 0; gk = 0
                for i, (s, blk) in enumerate(blocks):
                    if i + 1 < len(blocks):
                        loadG(i + 1)
                    xt = xts[i % 2]
                    A, Bsh = AB[s]
                    if blk == 0:
                        S.op("gpsimd", lambda e: e.memset(halo[:], 0.0), writes=[halo])
                    norm_to_hT(P, xt, 4, A, Bsh, hT, (ss, rs, sq, xn, tmpf), pT, None)
                    for fc in range(22):
                        res = []
                        for half in range(2):
                            f = fc + 22 * half
                            pu = pUf[pk % 6]; pk += 1
                            up = upad[uk % 4]; ac = acc[uk % 4]; uk += 1
                            for kc in range(8):
                                S.op("tensor", lambda e, kc=kc, f=f, pu=pu: e.matmul(pu[:], lhsT=Wup[:, kc, f * 128:(f + 1) * 128], rhs=hT[:, kc, :],
                                                                                    start=(kc == 0), stop=(kc == 7)), reads=[Wup, hT], writes=[pu])
                            S.op("gpsimd", lambda e, f=f, up=up: e.tensor_copy(up[:, 0:2], halo[:, f, :]), reads=[halo], writes=[up])
                            S.op("scalar", lambda e, up=up, pu=pu: e.copy(up[:, 2:NB + 2], pu[:]), reads=[pu], writes=[up])
                            S.op("gpsimd", lambda e, f=f, up=up: e.tensor_copy(halo[:, f, :], up[:, NB:NB + 2]), reads=[up], writes=[halo])
                            cw = lambda j, f=f: pp[:, PP_FCW + f * 3 + j:PP_FCW + f * 3 + j + 1]
                            S.op("vector", lambda e, up=up, ac=ac, cw=cw, f=f: e.tensor_scalar(ac[:], up[:, 2:NB + 2], cw(2), pp[:, PP_FCB + f:PP_FCB + f + 1],
                                                                                               op0=ALU.mult, op1=ALU.add), reads=[up, pp], writes=[ac])
                            for j in (1, 0):
                                S.op("vector", lambda e, up=up, ac=ac, cw=cw, j=j: e.scalar_tensor_tensor(ac[:], in0=up[:, j:j + NB], scalar=cw(j), in1=ac[:],
                                                                                                          op0=ALU.mult, op1=ALU.add), reads=[up, pp, ac], writes=[ac])
                            res.append(ac)
                        sgt = sg[fc % 2]
                        gt = gts[gk % 4]; gk += 1
                        S.op("scalar", lambda e, sgt=sgt, g_=res[1]: e.activation(sgt[:], g_[:], AF.Silu), reads=[res[1]], writes=[sgt])
                        S.op("gpsimd", lambda e, sgt=sgt, a_=res[0], gt=gt: e.tensor_tensor(gt[:], a_[:], sgt[:], op=ALU.mult), reads=[res[0], sgt], writes=[gt])
                        S.dma("sync", lambda e, gt=gt, s=s, blk=blk, fc=fc: e.dma_start(out=GTD[s, fc, :, blk * NB:(blk + 1) * NB], in_=gt[:]), gt, reads=[gt])
            with Phase(S, "H%d" % l) as P:
                Wdn = load_w_bf16(P, "Wdn", ffn_down[l, :, :], 22, DM)
                gtB = []
                for b in range(2):
                    t = P.sb("gtB%d" % b, [128, DM], F32)
                    S.dma("sync", lambda e, b=b, t=t, l=l: e.dma_start(out=t[:], in_=MOD[l, b, 5 * DM:6 * DM].partition_broadcast(128)), t, writes=[t])
                    gtB.append(t)
                NB = 512
                xts = [P.sb("xt%d" % i, [128, 4, DM], F32) for i in range(2)]
                gin = [P.sb("gin%d" % i, [128, 22, NB], BF16) for i in range(2)]
                tmp = [P.sb("tmp%d" % i, [128, 512], F32) for i in range(2)]
                pY = [P.ps("pY%d" % i, [128, 512], F32) for i in range(4)]
                blocks = [(s, blk) for s in range(2) for blk in range(T // NB)]

                def loadH(i):
                    s, blk = blocks[i]
                    S.dma("sync", lambda e: e.dma_start(out=xts[i % 2][:], in_=XA[s, blk * NB:(blk + 1) * NB, :].rearrange("(a p) d -> p a d", p=128)),
                          xts[i % 2], writes=[xts[i % 2]])
                    for f0 in (0, 11):
                        S.dma("sync", lambda e, f0=f0: e.dma_start(out=gin[i % 2][:, f0:f0 + 11, :],
                                                                   in_=GTD[s, f0:f0 + 11, :, blk * NB:(blk + 1) * NB].rearrange("f p t -> p f t")),
                              gin[i % 2], writes=[gin[i % 2]])

                loadH(0)
                yk = 0
                for i, (s, blk) in enumerate(blocks):
                    if i + 1 < len(blocks):
                        loadH(i + 1)
                    xt, gi = xts[i % 2], gin[i % 2]
                    for sub in range(4):
                        for dh in range(2):
                            py = pY[yk % 4]; tp = tmp[yk % 2]; yk += 1
                            for fc in range(22):
                                S.op("tensor", lambda e, fc=fc, sub=sub, dh=dh, py=py, gi=gi: e.matmul(py[:], lhsT=gi[:, fc, sub * 128:(sub + 1) * 128],
                                                                                                           rhs=Wdn[:, fc, dh * 512:(dh + 1) * 512], start=(fc == 0), stop=(fc == 21)),
                                     reads=[gi, Wdn], writes=[py])
                            S.op("vector", lambda e, py=py, tp=tp, dh=dh, s=s: e.tensor_tensor(tp[:], py[:], gtB[s][:, dh * 512:(dh + 1) * 512], op=ALU.mult),
                                 reads=[py, gtB[s]], writes=[tp])
                            S.op("gpsimd", lambda e, tp=tp, sub=sub, dh=dh, xt=xt: e.tensor_tensor(xt[:, sub, dh * 512:(dh + 1) * 512], xt[:, sub, dh * 512:(dh + 1) * 512],
                                                                                                   tp[:], op=ALU.add), reads=[tp, xt], writes=[xt])
                    S.dma("sync", lambda e, xt=xt, s=s, blk=blk: e.dma_start(out=xdst[s, blk * NB:(blk + 1) * NB, :].rearrange("(a p) d -> p a d", p=128), in_=xt[:]),
                          xt, reads=[xt])
            xsrc = xdst
        G.__exit__(None, None, None)
    return nc


def _prep(inputs):
    inp = {k: np.asarray(v) for k, v in inputs.items()}
    consts = _consts()
    pp = _pack_pp(inp)
    lamv = np.stack([inp["diff_lambda_q1"], inp["diff_lambda_k1"], inp["diff_lambda_q2"], inp["diff_lambda_k2"]], axis=1)
    lamv = np.ascontiguousarray(lamv.astype(np.float32))
    kk = np.arange(128)[:, None]
    cc = np.arange(1024)[None, :]
    dist = cc - kk - 384
    bidx = _t5_bucket(np.maximum(dist, 0))
    rb = inp["rel_bias"].astype(np.float32)
    tb = np.empty((4, 128, 1024), np.float32)
    for h in range(4):
        tb[h] = np.where(dist >= 0, rb[bidx, h], np.float32(NEG))
    b31 = np.ascontiguousarray(np.broadcast_to(rb[31][None, :], (128, 4))).astype(np.float32)
    shared = dict(consts=consts, pp=pp, lamv=lamv, tb=tb, b31=b31,
                  w_ada=inp["w_ada"], b_ada=inp["b_ada"], w_in=inp["w_in"], w_out=inp["w_out"],
                  ffn_up=inp["ffn_up"], ffn_down=inp["ffn_down"])
    in_maps = []
    for c in range(NCORES):
        m = dict(shared)
        m["x"] = np.ascontiguousarray(inp["x"][2 * c:2 * c + 2])
        cc_ = inp["c"][2 * c:2 * c + 2]
        m["cT"] = np.ascontiguousarray(cc_.reshape(2, 8, 128).transpose(2, 1, 0))
        in_maps.append(m)
    return in_maps


def kernel(**inputs):
    in_maps = _prep(inputs)
    nc = build()
    res = run_bass_kernel_spmd(nc, in_maps, core_ids=list(range(NCORES)))
    return np.concatenate([r["out"] for r in res.results], axis=0).astype(np.float32)
```

```python
import math
from contextlib import ExitStack
import numpy as np
import concourse.bass as bass
import concourse.mybir as mybir
from concourse.bass_utils import run_bass_kernel_spmd

F32 = mybir.dt.float32
BF16 = mybir.dt.bfloat16
AF = mybir.ActivationFunctionType
ALU = mybir.AluOpType

NCORES = 8
DEPTH = 4
T = 4096
DM = 1024
NEG = -30000.0
EPS = 1e-6
DFF = 2816


class Buf:
    def __init__(self, name, t=None):
        self.name = name
        self.t = t
        self.last_w = None
        self.readers = []
        self.dsem = None
        self.dcount = 0

    def __getitem__(self, k):
        return self.t[k]


class Eng:
    def __init__(self, name, sem):
        self.name = name
        self.sem = sem
        self.count = 0
        self.waited = {}
        self.prog = []


class Sched:
    SEM_WRAP = 30000

    def __init__(self, nc, es):
        self.nc = nc
        self.es = es
        self.engs = {}
        for name in ("sync", "scalar", "vector", "gpsimd", "tensor"):
            self.engs[name] = Eng(name, es.enter_context(nc.semaphore("s_" + name)))
        self.n_instr = 0
        self.dbufs = []
        self.sem_pool = []
        self.rr = 0

    def _waits(self, e, reads, writes):
        toks = []
        own = e.sem
        for b in reads:
            if b.last_w is not None:
                toks.append(b.last_w)
        for b in writes:
            if b.last_w is not None and b.last_w[0] is not own:
                toks.append(b.last_w)
            for r in b.readers:
                if r[0] is not own:
                    toks.append(r)
        need = {}
        for sem, val in toks:
            if e.name == "tensor" and sem is own:
                continue
            k = id(sem)
            if e.waited.get(k, 0) >= val:
                continue
            if k not in need or need[k][1] < val:
                need[k] = (sem, val)
        out = []
        for k, (sem, val) in need.items():
            e.waited[k] = val
            out.append((sem, val))
        return out

    def _record(self, e, fn, waits, sem, inc, reads, writes, tok):
        def run(eng, fn=fn, waits=waits, sem=sem, inc=inc):
            for s, v in waits:
                eng.wait_ge(s, v)
            fn(eng).then_inc(sem, inc)
        e.prog.append(run)
        for b in reads:
            b.readers.append(tok)
            if len(b.readers) > 64:
                b.readers = b.readers[-48:]
        for b in writes:
            b.last_w = tok
            b.readers = []
        self.n_instr += 1

    def op(self, engname, fn, reads=(), writes=()):
        e = self.engs[engname]
        if e.count >= self.SEM_WRAP:
            e.sem = self.es.enter_context(self.nc.semaphore("s_%s_%d" % (engname, self.n_instr)))
            e.count = 0
        waits = self._waits(e, reads, writes)
        e.count += 1
        tok = (e.sem, e.count)
        self._record(e, fn, waits, e.sem, 1, reads, writes, tok)
        return tok

    def dma(self, engname, fn, sembuf, reads=(), writes=()):
        e = self.engs[engname]
        waits = self._waits(e, reads, writes)
        if sembuf.dsem is None:
            if self.sem_pool:
                sembuf.dsem, sembuf.dcount = self.sem_pool.pop()
            else:
                sembuf.dsem = self.es.enter_context(self.nc.semaphore("d%d_%s" % (self.n_instr, sembuf.name)))
        if sembuf not in self.dbufs:
            self.dbufs.append(sembuf)
        sembuf.dcount += 16
        tok = (sembuf.dsem, sembuf.dcount)
        self._record(e, fn, waits, sembuf.dsem, 16, reads, writes, tok)
        return tok

    def barrier(self):
        toks = [(e.sem, e.count) for e in self.engs.values() if e.count > 0]
        toks += [(b.dsem, b.dcount) for b in self.dbufs]
        for name in self.engs:
            self.wait_tokens(name, toks)
        for b in self.dbufs:
            self.sem_pool.append((b.dsem, b.dcount))
            b.dsem = None
        self.dbufs = []

    def wait_tokens(self, engname, toks):
        e = self.engs[engname]
        for sem, val in toks:
            k = id(sem)
            if e.waited.get(k, 0) >= val:
                continue
            e.waited[k] = val
            e.prog.append(lambda eng, s=sem, v=val: eng.wait_ge(s, v))

    def emit(self):
        with self.nc.Block() as block:
            for name in ("sync", "scalar", "vector", "gpsimd", "tensor"):
                def body(eng, name=name):
                    for f in self.engs[name].prog:
                        f(eng)
                getattr(block, name)(body)
        for e in self.engs.values():
            e.prog = []

    def alt(self):
        self.rr ^= 1
        return "vector" if self.rr else "gpsimd"


PHASE_LOG = []


class Phase:
    def __init__(self, S, name):
        self.S = S
        self.nc = S.nc
        self.name = name
        self.es = ExitStack()
        self.k = 0

    def __enter__(self):
        self.es.__enter__()
        return self

    def sb(self, name, shape, dt):
        self.k += 1
        nm = "%s_%s_%d" % (self.name, name, self.k)
        return Buf(nm, self.es.enter_context(self.nc.sbuf_tensor(nm, list(shape), dt)))

    def ps(self, name, shape, dt):
        self.k += 1
        nm = "%s_%s_%d" % (self.name, name, self.k)
        return Buf(nm, self.es.enter_context(self.nc.psum_tensor(nm, list(shape), dt)))

    def __exit__(self, *a):
        if a[0] is None:
            PHASE_LOG.append((self.name, {k: (v.count, id(v.sem)) for k, v in self.S.engs.items()}, self.S.n_instr))
            self.S.barrier()
            self.S.emit()
        return self.es.__exit__(*a)


C_IDENT, C_TRI, C_ONES, C_MASKNEG, C_STRICT, C_BLK64, C_DELTA, C_EPS, C_ONE = 0, 128, 256, 384, 512, 640, 768, 769, 770
C_BM16, C_OFF32, C_OFF64, C_OFF128 = 771, 899, 1027, 1155
NCONST = 1283

PP_CONVW = 0
PP_DTB = 48
PP_ALOG = 52
PP_ONORM = 56
PP_QN = 184
PP_KN = 185
PP_SUBLN = 186
PP_FCW = 187
PP_FCB = 319
PP_NMIX = 363
PP_NFFN = 371
PP_BADA = 379
NPP = 380


def _consts():
    c = np.zeros((128, NCONST), np.float32)
    i = np.arange(128)
    c[:, C_IDENT:C_IDENT + 128] = np.eye(128)
    c[:, C_TRI:C_TRI + 128] = (i[:, None] <= i[None, :])
    c[:, C_ONES:C_ONES + 128] = 1.0
    c[:, C_MASKNEG:C_MASKNEG + 128] = np.where(i[:, None] <= i[None, :], 0.0, NEG)
    c[:, C_STRICT:C_STRICT + 128] = (i[:, None] < i[None, :])
    c[:, C_BLK64:C_BLK64 + 128] = ((i[:, None] // 64) == (i[None, :] // 64))
    c[0, C_DELTA] = 1.0
    bm = lambda m: ((i[:, None] // m) == (i[None, :] // m)).astype(np.float32)
    c[:, C_BM16:C_BM16 + 128] = bm(16)
    c[:, C_OFF32:C_OFF32 + 128] = bm(32) - bm(16)
    c[:, C_OFF64:C_OFF64 + 128] = bm(64) - bm(32)
    c[:, C_OFF128:C_OFF128 + 128] = 1.0 - bm(64)
    c[:, C_EPS] = EPS
    c[:, C_ONE] = 1.0
    return c


def _t5_bucket(n):
    n = np.asarray(n)
    nf = np.maximum(n, 1).astype(np.float32)
    large = 16 + (np.log(nf / np.float32(16)) / np.float32(math.log(128 / 16)) * np.float32(16)).astype(np.int32)
    large = np.minimum(large, 31)
    return np.where(n < 16, n, large)


def _pack_pp(inp):
    pp = np.zeros((DEPTH, 128, NPP), np.float32)
    p = np.arange(128)
    for l in range(DEPTH):
        cw = inp["gdn_conv_w"][l]
        pp[l, :, PP_CONVW:PP_CONVW + 48] = cw.reshape(4, 12, 128).transpose(2, 1, 0).reshape(128, 48)
        pp[l, :, PP_DTB:PP_DTB + 4] = inp["gdn_dt_bias"][l][None, :]
        pp[l, :, PP_ALOG:PP_ALOG + 4] = inp["gdn_a_log"][l][None, :]
        pp[l, :, PP_ONORM:PP_ONORM + 128] = inp["gdn_out_norm"][l][None, :]
        pp[l, :, PP_QN] = inp["diff_q_norm"][l][p % 64]
        pp[l, :, PP_KN] = inp["diff_k_norm"][l][p % 64]
        pp[l, :, PP_SUBLN] = inp["diff_subln"][l]
        fw = inp["ffn_conv_w"][l]
        pp[l, :, PP_FCW:PP_FCW + 132] = fw.reshape(3, 44, 128).transpose(2, 1, 0).reshape(128, 132)
        pp[l, :, PP_FCB:PP_FCB + 44] = inp["ffn_conv_b"][l].reshape(44, 128).T
        pp[l, :, PP_NMIX:PP_NMIX + 8] = inp["norm_mix"][l].reshape(8, 128).T
        pp[l, :, PP_NFFN:PP_NFFN + 8] = inp["norm_ffn"][l].reshape(8, 128).T
    return pp


def build(nlayers=DEPTH, upto="G", debug=False):
    nc = bass.Bass("TRN2", target_bir_lowering=False)
    dk = "ExternalOutput" if debug else "Internal"

    def din(name, shape, dt=F32):
        return nc.dram_tensor(name, list(shape), dt, kind="ExternalInput").ap()

    def dscr(name, shape, dt, dbg=True):
        return nc.dram_tensor(name, list(shape), dt, kind=(dk if dbg else "Internal")).ap()

    x_in = din("x", [2, T, DM])
    cT_in = din("cT", [128, 8, 2])
    consts_in = din("consts", [128, NCONST])
    pp_in = din("pp", [DEPTH, 128, NPP])
    lam_in = din("lamv", [DEPTH, 4, 64])
    tb_in = din("tb", [4, 128, 1024])
    b31_in = din("b31", [128, 4])
    w_ada = din("w_ada", [DEPTH, DM, 6 * DM])
    b_ada = din("b_ada", [DEPTH, 6 * DM])
    w_in = din("w_in", [DEPTH, DM, 3592])
    w_out = din("w_out", [DEPTH, DM, DM])
    ffn_up = din("ffn_up", [DEPTH, DM, 2 * DFF])
    ffn_down = din("ffn_down", [DEPTH, DFF, DM])
    out = nc.dram_tensor("out", [2, T, DM], F32, kind="ExternalOutput").ap()

    MOD = dscr("MOD", [DEPTH, 2, 6 * DM], F32)
    XA = dscr("XA", [2, T, DM], F32)
    XB = dscr("XB", [2, T, DM], F32, dbg=False)
    NCH = T // 128
    KT = dscr("KT", [2, NCH, 128, 4, 128], BF16)
    QGT = dscr("QGT", [2, NCH, 128, 4, 128], BF16)
    QT = dscr("QT", [2, NCH, 128, 4, 128], BF16)
    KBT = dscr("KBT", [2, NCH, 128, 4, 128], BF16)
    KBG = dscr("KBG", [2, NCH, 128, 4, 128], BF16)
    KDEC = dscr("KDEC", [2, NCH, 128, 4, 128], BF16)
    VB = dscr("VB", [2, NCH, 128, 4, 128], BF16)
    DEC = dscr("DEC", [2, NCH, 128, 4, 128], F32)
    EGL = dscr("EGL", [2, NCH, 128, 4], F32)
    GATE = dscr("GATE", [2, NCH, 128, 512], BF16)
    DQT = dscr("DQT", [2, 4, 128, T], BF16)
    DKT = dscr("DKT", [2, 4, 128, T], BF16)
    DV = dscr("DV", [2, NCH, 128, 512], BF16)
    OT = dscr("OT", [2, 8, 128, T], BF16)
    GTD = dscr("GTD", [2, 22, 128, T], BF16, dbg=False)

    DBG = {}
    if debug:
        for nm in ("Y", "X", "Q", "U", "T1", "X1", "Y1b", "S0", "U1", "PW1", "O1", "VN0"):
            DBG[nm] = dscr("DBG_" + nm, [128, 4, 128], F32)

    with ExitStack() as es0:
        S = Sched(nc, es0)
        G = Phase(S, "glob")
        G.__enter__()
        cst = G.sb("cst", [128, NCONST], F32)
        identB = G.sb("identB", [128, 128], BF16)
        onesB = G.sb("onesB", [128, 128], BF16)
        S.dma("sync", lambda e: e.dma_start(out=cst[:], in_=consts_in[:, :]), cst, writes=[cst])
        S.op("vector", lambda e: e.tensor_copy(identB[:], cst[:, C_IDENT:C_IDENT + 128]), reads=[cst], writes=[identB])
        S.op("vector", lambda e: e.tensor_copy(onesB[:], cst[:, C_ONES:C_ONES + 128]), reads=[cst], writes=[onesB])
        identF = lambda: cst[:, C_IDENT:C_IDENT + 128]
        TRI = lambda: cst[:, C_TRI:C_TRI + 128]
        ONESF = lambda: cst[:, C_ONES:C_ONES + 128]
        epsc = lambda: cst[:, C_EPS:C_EPS + 1]
        onec = lambda: cst[:, C_ONE:C_ONE + 1]

        with Phase(S, "A") as P:
            cT = P.sb("cT", [128, 8, 2], F32)
            S.dma("sync", lambda e: e.dma_start(out=cT[:], in_=cT_in[:, :, :]), cT, writes=[cT])
            S.op("scalar", lambda e: e.activation(cT[:], cT[:], AF.Silu), reads=[cT], writes=[cT])
            wts = [P.sb("wa%d" % i, [128, 8, 512], F32) for i in range(3)]
            pM = [P.ps("pM%d" % i, [2, 512], F32) for i in range(2)]
            k = 0
            bada = P.sb("bada", [2, 6 * DM], F32)
            modt = P.sb("modt", [2, 6 * DM], F32)
            for l in range(nlayers):
                S.dma("sync", lambda e, l=l, bada=bada: e.dma_start(out=bada[:], in_=b_ada[l, :].partition_broadcast(2)), bada, writes=[bada])
                for fb in range(12):
                    wt = wts[k % 3]
                    pm = pM[k % 2]
                    k += 1
                    S.dma("sync", lambda e, l=l, fb=fb, wt=wt: e.dma_start(
                        out=wt[:], in_=w_ada[l, :, fb * 512:(fb + 1) * 512].rearrange("(c p) f -> p c f", p=128)), wt, writes=[wt])
                    for kc in range(8):
                        S.op("tensor", lambda e, kc=kc, wt=wt, pm=pm: e.matmul(pm[:], lhsT=cT[:, kc, :], rhs=wt[:, kc, :],
                                                                              start=(kc == 0), stop=(kc == 7)),
                             reads=[cT, wt], writes=[pm])
                    S.op("vector", lambda e, fb=fb, pm=pm, modt=modt, bada=bada: e.tensor_tensor(
                        modt[:, fb * 512:(fb + 1) * 512], pm[:], bada[:, fb * 512:(fb + 1) * 512], op=ALU.add),
                        reads=[pm, bada], writes=[modt])
                S.dma("sync", lambda e, l=l, modt=modt: e.dma_start(out=MOD[l, :, :], in_=modt[:]), modt, reads=[modt])

        def load_mod_cols(P, l, b, seg, name):
            t = P.sb(name, [128, 8], F32)
            S.dma("sync", lambda e: e.dma_start(out=t[:], in_=MOD[l, b, seg * DM:(seg + 1) * DM].rearrange("(c p) -> p c", p=128),
                                                allow_slow_non_contiguous=True), t, writes=[t])
            return t

        def make_AB(P, l, b, pp, ncol, seg_sh, seg_sc, name):
            sh = load_mod_cols(P, l, b, seg_sh, name + "sh")
            sc = load_mod_cols(P, l, b, seg_sc, name + "sc")
            A = P.sb(name + "A", [128, 8], F32)
            S.op("vector", lambda e: e.scalar_tensor_tensor(A[:], in0=sc[:], scalar=1.0, in1=pp[:, ncol:ncol + 8],
                                                            op0=ALU.add, op1=ALU.mult), reads=[sc, pp], writes=[A])
            return A, sh

        def norm_to_hT(P, xt, nsub, A, Bsh, hT, tmps, pT, W):
            ss, rs, sq, xn, tmpf = tmps
            for sub in range(nsub):
                S.op("scalar", lambda e, sub=sub: e.activation(sq[:], xt[:, sub, :], AF.Square, scale=1.0 / 32,
                                                                accum_out=ss[:, sub:sub + 1]), reads=[xt], writes=[sq, ss])
            S.op("scalar", lambda e: e.activation(rs[:, 0:nsub], ss[:, 0:nsub], AF.Ln, bias=epsc()), reads=[ss, cst], writes=[rs])
            S.op("scalar", lambda e: e.activation(rs[:, 0:nsub], rs[:, 0:nsub], AF.Exp, scale=-0.5), reads=[rs], writes=[rs])
            for sub in range(nsub):
                S.op("vector", lambda e, sub=sub: e.tensor_scalar_mul(xn[:, sub, :], xt[:, sub, :], rs[:, sub:sub + 1]),
                     reads=[xt, rs], writes=[xn])
                for c in range(8):
                    S.op("tensor", lambda e, sub=sub, c=c: e.transpose(pT[:, c, :], xn[:, sub, c * 128:(c + 1) * 128], identB[:]),
                         reads=[xn, identB], writes=[pT])
                S.op("vector", lambda e: e.tensor_tensor(tmpf[:], pT[:], A[:, :, None].to_broadcast([128, 8, 128]), op=ALU.mult),
                     reads=[pT, A], writes=[tmpf])
                S.op("gpsimd", lambda e, sub=sub: e.tensor_tensor(hT[:, :, sub * 128:(sub + 1) * 128], tmpf[:],
                                                                   Bsh[:, :, None].to_broadcast([128, 8, 128]), op=ALU.add),
                     reads=[tmpf, Bsh], writes=[hT])

        def load_w_bf16(P, name, src_ap, kc, ncols):
            t = P.sb(name, [128, kc, ncols], BF16)
            step = max(1, 4096 // ncols)
            for c0 in range(0, kc, step):
                c1 = min(kc, c0 + step)
                S.dma("gpsimd", lambda e, c0=c0, c1=c1: e.dma_start(
                    out=t[:, c0:c1, :], in_=src_ap[c0 * 128:c1 * 128, :].rearrange("(c p) f -> p c f", p=128)), t, writes=[t])
            return t

        xsrc = x_in
        for l in range(nlayers):
            last = (l == nlayers - 1)
            xdst = out if last else XB
            with Phase(S, "C%d" % l) as P:
                pp = P.sb("pp", [128, NPP], F32)
                S.dma("sync", lambda e, l=l: e.dma_start(out=pp[:], in_=pp_in[l, :, :]), pp, writes=[pp])
                Wqkv = load_w_bf16(P, "Wqkv", w_in[l, :, 0:1536], 8, 1536)
                Wgate = load_w_bf16(P, "Wgate", w_in[l, :, 1536:2048], 8, 512)
                Wba = load_w_bf16(P, "Wba", w_in[l, :, 2048:2056], 8, 8)
                Wdqk = load_w_bf16(P, "Wdqk", w_in[l, :, 2056:3080], 8, 1024)
                Wdv = load_w_bf16(P, "Wdv", w_in[l, :, 3080:3592], 8, 512)
                negA = P.sb("negA", [128, 4], F32)
                S.op("scalar", lambda e: e.activation(negA[:], pp[:, PP_ALOG:PP_ALOG + 4], AF.Exp), reads=[pp], writes=[negA])
                S.op("vector", lambda e: e.tensor_scalar_mul(negA[:], negA[:], -1.0), reads=[negA], writes=[negA])
                qgain = P.sb("qgain", [128, 1], F32)
                S.op("vector", lambda e: e.tensor_scalar_mul(qgain[:], pp[:, PP_QN:PP_QN + 1], 0.125), reads=[pp], writes=[qgain])
                xts = [P.sb("xt%d" % i, [128, 4, DM], F32) for i in range(2)]
                ss = P.sb("ss", [128, 4], F32); rs = P.sb("rs", [128, 4], F32)
                sq = P.sb("sq", [128, DM], BF16); xn = P.sb("xn", [128, 4, DM], BF16)
                tmpf = P.sb("tmpf", [128, 8, 128], F32)
                hT = P.sb("hT", [128, 8, 512], BF16)
                pT = P.ps("pT", [128, 8, 128], BF16)
                pU = [P.ps("pU%d" % i, [128, 512], F32) for i in range(2)]
                pS = P.ps("pS", [128, 512], F32)
                pSm = P.ps("pSm", [128, 512], F32)
                pD = P.ps("pD", [128, 512], F32)
                pSmv = lambda: pSm[:].rearrange("p (a b) -> p a b", b=16)
                pDv = lambda: pD[:].rearrange("p (h c) -> p h c", c=128)
                pTr = P.ps("pTr", [128, 8, 128], BF16)
                pTok = P.ps("pTok", [128, 512], F32)
                PUS = [pU[0], pU[1], pSm, pD, pTok]
                upad = [P.sb("upad%d" % i, [128, 515], F32) for i in range(2)]
                acc = [P.sb("acc%d" % i, [128, 512], F32) for i in range(2)]
                act = [P.sb("act%d" % i, [128, 512], F32) for i in range(4)]
                sqq = [P.sb("sqq%d" % i, [128, 512], F32) for i in range(4)]
                rinv = [P.sb("rinv%d" % i, [128, 512], F32) for i in range(2)]
                halo = P.sb("halo", [128, 12, 3], F32)
                kT = P.sb("kT", [128, 4, 4, 128], BF16)
                qT = P.sb("qT", [128, 4, 4, 128], BF16)
                vT = P.sb("vT", [128, 4, 4, 128], BF16)
                kbT = P.sb("kbT", [128, 4, 4, 128], BF16)
                qgT = P.sb("qgT", [128, 4, 4, 128], BF16)
                kbg = P.sb("kbg", [128, 4, 4, 128], BF16)
                kdec = P.sb("kdec", [128, 4, 4, 128], BF16)
                kb = P.sb("kb", [128, 4, 128], BF16)
                qg = P.sb("qg", [128, 4, 128], BF16)
                vb = P.sb("vb", [128, 4, 4, 128], BF16)
                dec = P.sb("dec", [128, 4, 4, 128], F32)
                gatet = P.sb("gatet", [128, 4, 512], BF16)
                dvt = P.sb("dvt", [128, 4, 512], BF16)
                dqkT = P.sb("dqkT", [128, 8, 512], BF16)
                braw = P.sb("braw", [128, 4, 8], F32)
                beta = P.sb("beta", [128, 4, 4], F32)
                gg = P.sb("gg", [128, 4, 4], F32)
                gcl = P.sb("gcl", [128, 4, 8], F32)
                ngc = P.sb("ngc", [128, 4, 4], F32)
                egc = P.sb("egc", [128, 4, 4], F32)
                bgs = P.sb("bgs", [128, 4, 4], F32)
                qsc = P.sb("qsc", [128, 4, 4], F32)
                kds = P.sb("kds", [128, 4, 4], F32)
                egl = P.sb("egl", [128, 4, 4], F32)
                gbc = P.sb("gbc", [128, 4, 128], F32)

                blocks = [(s, blk) for s in range(2) for blk in range(8)]

                def load_x(i):
                    s, blk = blocks[i]
                    xt = xts[i % 2]
                    S.dma("sync", lambda e: e.dma_start(
                        out=xt[:], in_=xsrc[s, blk * 512:(blk + 1) * 512, :].rearrange("(a p) d -> p a d", p=128)), xt, writes=[xt])

                load_x(0)
                AB = {}
                for b in range(2):
                    AB[b] = make_AB(P, l, b, pp, PP_NMIX, 0, 1, "mix%d" % b)
                cc = 0
                for i, (s, blk) in enumerate(blocks):
                    xt = xts[i % 2]
                    if i + 1 < len(blocks):
                        load_x(i + 1)
                    A, Bsh = AB[s]
                    if blk == 0:
                        S.op("gpsimd", lambda e: e.memset(halo[:], 0.0), writes=[halo])
                    norm_to_hT(P, xt, 4, A, Bsh, hT, (ss, rs, sq, xn, tmpf), pT, None)

                    for sub in range(4):
                        for kc in range(8):
                            S.op("tensor", lambda e, sub=sub, kc=kc: e.matmul(pSmv()[:, sub, 0:8], lhsT=hT[:, kc, sub * 128:(sub + 1) * 128],
                                                                            rhs=Wba[:, kc, :], start=(kc == 0), stop=(kc == 7)),
                                 reads=[hT, Wba], writes=[pSm])
                    S.op("vector", lambda e: e.tensor_copy(braw[:], pSmv()[:, 0:4, 0:8]), reads=[pSm], writes=[braw])
                    S.op("scalar", lambda e: e.activation(beta[:], braw[:, :, 0:4], AF.Sigmoid), reads=[braw], writes=[beta])
                    S.op("vector", lambda e: e.tensor_tensor(gg[:], braw[:, :, 4:8], pp[:, None, PP_DTB:PP_DTB + 4].to_broadcast([128, 4, 4]),
                                                             op=ALU.add), reads=[braw, pp], writes=[gg])
                    S.op("scalar", lambda e: e.activation(gg[:], gg[:], AF.Exp), reads=[gg], writes=[gg])
                    S.op("scalar", lambda e: e.activation(gg[:], gg[:], AF.Ln, bias=onec()), reads=[gg, cst], writes=[gg])
                    S.op("vector", lambda e: e.tensor_tensor(gg[:], gg[:], negA[:, None, :].to_broadcast([128, 4, 4]), op=ALU.mult),
                         reads=[gg, negA], writes=[gg])
                    pending = []
                    for fc in range(12):
                        pu = PUS[cc % 5]; up = upad[cc % 2]; ac = acc[cc % 2]; at = act[cc % 4]
                        sqt = sqq[cc % 4]; rv = rinv[cc % 2]
                        cc += 1
                        for kc in range(8):
                            S.op("tensor", lambda e, kc=kc, fc=fc, pu=pu: e.matmul(pu[:], lhsT=Wqkv[:, kc, fc * 128:(fc + 1) * 128],
                                                                                    rhs=hT[:, kc, :], start=(kc == 0), stop=(kc == 7)),
                                 reads=[Wqkv, hT], writes=[pu])
                        while len(pending) > 2:
                            pending.pop(0)()
                        S.op("vector", lambda e, fc=fc, up=up: e.tensor_copy(up[:, 0:3], halo[:, fc, :]), reads=[halo], writes=[up])
                        S.op("scalar", lambda e, up=up, pu=pu: e.copy(up[:, 3:515], pu[:]), reads=[pu], writes=[up])
                        S.op("gpsimd", lambda e, fc=fc, up=up: e.tensor_copy(halo[:, fc, :], up[:, 512:515]), reads=[up], writes=[halo])
                        cw = lambda j, fc=fc: pp[:, PP_CONVW + fc * 4 + j:PP_CONVW + fc * 4 + j + 1]
                        S.op("vector", lambda e, up=up, ac=ac, cw=cw: e.tensor_scalar_mul(ac[:], up[:, 3:515], cw(3)), reads=[up, pp], writes=[ac])
                        for j in (2, 1, 0):
                            S.op("vector",
                                 lambda e, up=up, ac=ac, cw=cw, j=j: e.scalar_tensor_tensor(ac[:], in0=up[:, j:j + 512], scalar=cw(j), in1=ac[:],
                                                                                           op0=ALU.mult, op1=ALU.add),
                                 reads=[up, pp, ac], writes=[ac])
                        kind, h = fc // 4, fc % 4
                        if kind == 2:
                            S.op("scalar", lambda e, ac=ac, h=h: e.activation(vT[:, :, h, :], ac[:].rearrange("p (a t) -> p a t", a=4), AF.Silu),
                                 reads=[ac], writes=[vT])
                        else:
                            dst = qT if kind == 0 else kT
                            S.op("scalar", lambda e, ac=ac, at=at: e.activation(at[:], ac[:], AF.Silu), reads=[ac], writes=[at])
                            S.op("gpsimd", lambda e, at=at, sqt=sqt: e.tensor_tensor(sqt[:], at[:], at[:], op=ALU.mult), reads=[at], writes=[sqt])

                            def fin(sqt=sqt, rv=rv, at=at, dst=dst, h=h):
                                S.op("tensor", lambda e: e.matmul(pS[:], lhsT=ONESF(), rhs=sqt[:], start=True, stop=True),
                                     reads=[cst, sqt], writes=[pS])
                                S.op("scalar", lambda e: e.activation(rv[:], pS[:], AF.Ln, bias=epsc()), reads=[pS, cst], writes=[rv])
                                S.op("scalar", lambda e: e.activation(rv[:], rv[:], AF.Exp, scale=-0.5), reads=[rv], writes=[rv])
                                S.op("vector", lambda e: e.tensor_tensor(
                                    dst[:, :, h, :], at[:].rearrange("p (a t) -> p a t", a=4), rv[:].rearrange("p (a t) -> p a t", a=4), op=ALU.mult),
                                    reads=[at, rv], writes=[dst])
                            pending.append(fin)

                    while pending:
                        pending.pop(0)()
                    for sub in range(4):
                        S.op("tensor", lambda e, sub=sub: e.matmul(pSmv()[:, sub, 8:12], lhsT=TRI(), rhs=gg[:, sub, :], start=True, stop=True),
                             reads=[cst, gg], writes=[pSm])
                        S.op("tensor", lambda e, sub=sub: e.matmul(pSmv()[:, sub, 12:16], lhsT=ONESF(), rhs=gg[:, sub, :], start=True, stop=True),
                             reads=[cst, gg], writes=[pSm])
                    S.op("vector", lambda e: e.tensor_copy(gcl[:], pSmv()[:, 0:4, 8:16]), reads=[pSm], writes=[gcl])
                    S.op("vector", lambda e: e.tensor_scalar_mul(ngc[:], gcl[:, :, 0:4], -1.0), reads=[gcl], writes=[ngc])
                    S.op("scalar", lambda e: e.activation(egc[:], gcl[:, :, 0:4], AF.Exp), reads=[gcl], writes=[egc])
                    S.op("scalar", lambda e: e.activation(egl[:], gcl[:, :, 4:8], AF.Exp), reads=[gcl], writes=[egl])
                    S.op("vector", lambda e: e.tensor_tensor(kds[:], gcl[:, :, 4:8], gcl[:, :, 0:4], op=ALU.subtract), reads=[gcl], writes=[kds])
                    S.op("scalar", lambda e: e.activation(kds[:], kds[:], AF.Exp), reads=[kds], writes=[kds])
                    S.op("vector", lambda e: e.tensor_tensor(bgs[:], beta[:], egc[:], op=ALU.mult), reads=[beta, egc], writes=[bgs])
                    S.op("vector", lambda e: e.tensor_scalar_mul(qsc[:], egc[:], 128.0 ** -0.5), reads=[egc], writes=[qsc])
                    S.dma("sync", lambda e, s=s, blk=blk: e.dma_start(out=EGL[s, blk * 4:(blk + 1) * 4, :, :].rearrange("n p h -> p n h"),
                                                                       in_=egl[:]), egl, reads=[egl])
                    for sub in range(4):
                        for h in range(4):
                            S.op("vector", lambda e, sub=sub, h=h: e.tensor_copy(gbc[:, h, :], gg[:, sub, h:h + 1].to_broadcast([128, 128])),
                                 reads=[gg], writes=[gbc])
                        for h in range(4):
                            S.op("tensor", lambda e, h=h: e.matmul(pDv()[:, h, :], lhsT=gbc[:, h, :], rhs=TRI(), start=True, stop=False),
                                 reads=[gbc, cst], writes=[pD])
                            S.op("tensor", lambda e, h=h: e.matmul(pDv()[:, h, :], lhsT=identF(), rhs=cst[:, C_MASKNEG:C_MASKNEG + 128],
                                                                   start=False, stop=True), reads=[cst], writes=[pD])
                        for h in range(4):
                            S.op("scalar", lambda e, sub=sub, h=h: e.activation(dec[:, sub, h, :], pDv()[:, h, :], AF.Exp,
                                                                                  bias=ngc[:, sub, h:h + 1]),
                                 reads=[pD, ngc], writes=[dec])
                    S.dma("sync", lambda e, s=s, blk=blk: e.dma_start(
                        out=DEC[s, blk * 4:(blk + 1) * 4, :, :, :].rearrange("n p h c -> p n h c"), in_=dec[:]), dec, reads=[dec])

                    for fc in range(8):
                        pu = PUS[cc % 5]; sqt = sqq[cc % 4]; rv = rinv[cc % 2]
                        cc += 1
                        for kc in range(8):
                            S.op("tensor", lambda e, kc=kc, fc=fc, pu=pu: e.matmul(pu[:], lhsT=Wdqk[:, kc, fc * 128:(fc + 1) * 128],
                                                                                    rhs=hT[:, kc, :], start=(kc == 0), stop=(kc == 7)),
                                 reads=[Wdqk, hT], writes=[pu])
                        while len(pending) > 2:
                            pending.pop(0)()
                        S.op("scalar", lambda e, pu=pu, sqt=sqt: e.activation(sqt[:], pu[:], AF.Square), reads=[pu], writes=[sqt])
                        gain = qgain[:, 0:1] if fc < 4 else pp[:, PP_KN:PP_KN + 1]

                        def fin2(pu=pu, sqt=sqt, rv=rv, fc=fc, gain=gain):
                            S.op("tensor", lambda e: e.matmul(pS[:], lhsT=cst[:, C_BLK64:C_BLK64 + 128], rhs=sqt[:], start=True, stop=True),
                                 reads=[cst, sqt], writes=[pS])
                            S.op("scalar", lambda e: e.activation(rv[:], pS[:], AF.Ln, bias=epsc(), scale=1.0 / 64), reads=[pS, cst], writes=[rv])
                            S.op("scalar", lambda e: e.activation(rv[:], rv[:], AF.Exp, scale=-0.5), reads=[rv], writes=[rv])
                            S.op("vector", lambda e: e.scalar_tensor_tensor(
                                dqkT[:, fc, :], in0=pu[:], scalar=gain, in1=rv[:], op0=ALU.mult, op1=ALU.mult),
                                reads=[pu, rv, qgain, pp], writes=[dqkT])
                        pending.append(fin2)
                    while pending:
                        pending.pop(0)()
                    S.dma("sync", lambda e, s=s, blk=blk: e.dma_start(out=DQT[s, :, :, blk * 512:(blk + 1) * 512].rearrange("h p t -> p h t"),
                                                                       in_=dqkT[:, 0:4, :]), dqkT, reads=[dqkT])
                    S.dma("sync", lambda e, s=s, blk=blk: e.dma_start(out=DKT[s, :, :, blk * 512:(blk + 1) * 512].rearrange("h p t -> p h t"),
                                                                       in_=dqkT[:, 4:8, :]), dqkT, reads=[dqkT])

                    for W, dstt, fn in ((Wgate, gatet, AF.Silu), (Wdv, dvt, AF.Copy)):
                        for sub in range(4):
                            for kc in range(8):
                                S.op("tensor", lambda e, sub=sub, kc=kc, W=W: e.matmul(pTok[:], lhsT=hT[:, kc, sub * 128:(sub + 1) * 128],
                                                                                        rhs=W[:, kc, :], start=(kc == 0), stop=(kc == 7)),
                                     reads=[hT, W], writes=[pTok])
                            S.op("scalar", lambda e, sub=sub, dstt=dstt, fn=fn: e.activation(dstt[:, sub, :], pTok[:], fn),
                                 reads=[pTok], writes=[dstt])
                        dd = GATE if dstt is gatet else DV
                        S.dma("sync", lambda e, dd=dd, dstt=dstt, s=s, blk=blk: e.dma_start(
                            out=dd[s, blk * 4:(blk + 1) * 4, :, :].rearrange("n p f -> p n f"), in_=dstt[:]), dstt, reads=[dstt])
                    for sub in range(4):
                        for h in range(4):
                            S.op("tensor", lambda e, sub=sub, h=h: e.transpose(pTr[:, h, :], kT[:, sub, h, :], identB[:]),
                                 reads=[kT, identB], writes=[pTr])
                        S.op("vector", lambda e, sub=sub: e.tensor_tensor(kbg[:, sub, :, :], pTr[:, 0:4, :], bgs[:, sub, :, None].to_broadcast([128, 4, 128]),
                                                                          op=ALU.mult), reads=[pTr, bgs], writes=[kbg])
                        S.op("vector", lambda e, sub=sub: e.tensor_tensor(kdec[:, sub, :, :], pTr[:, 0:4, :], kds[:, sub, :, None].to_broadcast([128, 4, 128]),
                                                                          op=ALU.mult), reads=[pTr, kds], writes=[kdec])
                        S.op("vector", lambda e, sub=sub: e.tensor_tensor(kb[:], pTr[:, 0:4, :], beta[:, sub, :, None].to_broadcast([128, 4, 128]),
                                                                          op=ALU.mult), reads=[pTr, beta], writes=[kb])
                        for h in range(4):
                            S.op("tensor", lambda e, h=h: e.transpose(pTr[:, h, :], kb[:, h, :], identB[:]), reads=[kb, identB], writes=[pTr])
                        S.op("scalar", lambda e, sub=sub: e.copy(kbT[:, sub, :, :], pTr[:, 0:4, :]), reads=[pTr], writes=[kbT])
                        for h in range(4):
                            S.op("tensor", lambda e, sub=sub, h=h: e.transpose(pTr[:, h, :], qT[:, sub, h, :], identB[:]),
                                 reads=[qT, identB], writes=[pTr])
                        S.op("vector", lambda e, sub=sub: e.tensor_tensor(qg[:], pTr[:, 0:4, :], qsc[:, sub, :, None].to_broadcast([128, 4, 128]),
                                                                          op=ALU.mult), reads=[pTr, qsc], writes=[qg])
                        for h in range(4):
                            S.op("tensor", lambda e, h=h: e.transpose(pTr[:, h, :], qg[:, h, :], identB[:]), reads=[qg, identB], writes=[pTr])
                        S.op("scalar", lambda e, sub=sub: e.copy(qgT[:, sub, :, :], pTr[:, 0:4, :]), reads=[pTr], writes=[qgT])
                        for h in range(4):
                            S.op("tensor", lambda e, sub=sub, h=h: e.transpose(pTr[:, h, :], vT[:, sub, h, :], identB[:]),
                                 reads=[vT, identB], writes=[pTr])
                        S.op("vector", lambda e, sub=sub: e.tensor_tensor(vb[:, sub, :, :], pTr[:, 0:4, :], beta[:, sub, :, None].to_broadcast([128, 4, 128]),
                                                                          op=ALU.mult), reads=[pTr, beta], writes=[vb])
                    for dst, src in ((KT, kT), (QT, qT), (QGT, qgT), (KBT, kbT), (KBG, kbg), (KDEC, kdec), (VB, vb)):
                        S.dma("sync", lambda e, dst=dst, src=src, s=s, blk=blk: e.dma_start(
                            out=dst[s, blk * 4:(blk + 1) * 4, :, :, :].rearrange("n p h t -> p n h t"), in_=src[:]), src, reads=[src])

            if upto == "C":
                break
            with Phase(S, "D%d" % l) as P:
                pp = P.sb("pp", [128, NPP], F32)
                S.dma("sync", lambda e, l=l: e.dma_start(out=pp[:], in_=pp_in[l, :, :]), pp, writes=[pp])
                NSLOT = 3
                names = ("kT", "qT", "qgT", "kbT", "kbg", "kdec", "vb")
                srcs = dict(kT=KT, qT=QT, qgT=QGT, kbT=KBT, kbg=KBG, kdec=KDEC, vb=VB)
                slots = {}
                for s in range(2):
                    for k in range(NSLOT):
                        d_ = {nm: P.sb("%s_%d_%d" % (nm, s, k), [128, 4, 128], BF16) for nm in names}
                        d_["dec"] = P.sb("dec_%d_%d" % (s, k), [128, 4, 128], F32)
                        d_["egl"] = P.sb("egl_%d_%d" % (s, k), [128, 4], F32)
                        d_["gate"] = P.sb("gate_%d_%d" % (s, k), [128, 4, 128], BF16)
                        slots[(s, k)] = d_
                strict = cst[:, None, C_STRICT:C_STRICT + 128].to_broadcast([128, 4, 128])
                identq = cst[:, None, C_IDENT:C_IDENT + 128].to_broadcast([128, 4, 128])
                onormb = pp[:, None, PP_ONORM:PP_ONORM + 128].to_broadcast([128, 4, 128])
                pre = [P.ps("pre%d" % i, [128, 4, 128], F32) for i in range(2)]
                pinv = [P.ps("pinv%d" % i, [128, 4, 128], F32) for i in range(2)]
                pW = P.ps("pW", [128, 4, 128], F32)
                pO = P.ps("pO", [128, 4, 128], F32)
                pSt = P.ps("pSt", [128, 4, 128], F32)
                pTr = P.ps("pTr", [128, 8, 128], BF16)
                st = {}
                for s in range(2):
                    st[s] = dict(
                        Y=[P.sb("Y%d_%d" % (s, i), [128, 4, 128], F32) for i in range(2)],
                        X=[P.sb("X%d_%d" % (s, i), [128, 4, 128], F32) for i in range(2)],
                        Q=P.sb("Q%d" % s, [128, 4, 128], F32), Qb=P.sb("Qb%d" % s, [128, 4, 128], BF16),
                        P=P.sb("Pm%d" % s, [128, 4, 128], F32), Yf=P.sb("Yf%d" % s, [128, 4, 128], F32), Xf=P.sb("Xf%d" % s, [128, 4, 128], F32),
                        t1=P.sb("t1%d" % s, [128, 4, 128], F32),
                        aT=P.sb("aT%d" % s, [128, 4, 128], BF16), u=P.sb("u%d" % s, [128, 4, 128], F32),
                        wT=P.sb("wT%d" % s, [128, 4, 128], BF16), vn=P.sb("vn%d" % s, [128, 4, 128], BF16),
                        Sf=P.sb("Sf%d" % s, [128, 4, 128], F32), Sb=P.sb("Sb%d" % s, [128, 4, 128], BF16),
                        osq=P.sb("osq%d" % s, [128, 4, 128], F32), oss=P.sb("oss%d" % s, [128, 4], F32),
                        o1=P.sb("o1%d" % s, [128, 4, 128], F32), g2=P.sb("g2%d" % s, [128, 4, 128], F32),
                        y=P.sb("y%d" % s, [128, 4, 128], BF16),
                        oT=P.sb("oT%d" % s, [128, 4, 512], BF16),
                    )
                    S.op("gpsimd", lambda e, s=s: e.memset(st[s]["Sf"][:], 0.0), writes=[st[s]["Sf"]])
                    S.op("gpsimd", lambda e, s=s: e.memset(st[s]["Sb"][:], 0.0), writes=[st[s]["Sb"]])

                def dbg(nm, tile, s, n, nn=0):
                    if debug and s == 0 and n == nn and l == 0:
                        S.dma("sync", lambda e: e.dma_start(out=DBG[nm][:, :, :], in_=tile[:]), tile, reads=[tile])

                def loadD(s, n):
                    sl = slots[(s, n % NSLOT)]
                    for nm in names:
                        S.dma("sync", lambda e, nm=nm, sl=sl: e.dma_start(out=sl[nm][:], in_=srcs[nm][s, n, :, :, :]), sl[nm], writes=[sl[nm]])
                    S.dma("sync", lambda e, sl=sl: e.dma_start(out=sl["dec"][:], in_=DEC[s, n, :, :, :]), sl["dec"], writes=[sl["dec"]])
                    S.dma("sync", lambda e, sl=sl: e.dma_start(out=sl["egl"][:], in_=EGL[s, n, :, :]), sl["egl"], writes=[sl["egl"]])
                    S.dma("sync", lambda e, sl=sl: e.dma_start(out=sl["gate"][:], in_=GATE[s, n, :, :].rearrange("p (h e) -> p h e", h=4)),
                          sl["gate"], writes=[sl["gate"]])

                for s in range(2):
                    loadD(s, 0)
                    loadD(s, 1)
                pk = [0, 0]

                def mm4(out, lhs, rhs, reads, acc=None):
                    for h in range(4):
                        S.op("tensor", lambda e, h=h: e.matmul(out[:, h, :], lhsT=lhs[:, h, :], rhs=rhs[:, h, :],
                                                               start=(acc in (None, "start")), stop=(acc in (None, "stop"))),
                             reads=reads, writes=[out])

                for n in range(NCH):
                    SL = {s: slots[(s, n % NSLOT)] for s in range(2)}
                    if n + 2 < NCH:
                        for s in range(2):
                            loadD(s, n + 2)
                    for s in range(2):
                        sl, q = SL[s], st[s]
                        pg = pre[s]
                        q["pg"] = pg
                        mm4(pg, sl["kT"], sl["kbT"], [sl["kT"], sl["kbT"]])
                    bc = lambda c0: cst[:, None, c0:c0 + 128].to_broadcast([128, 4, 128])
                    for s in range(2):
                        sl, q = SL[s], st[s]
                        pg = q["pg"]
                        S.op("vector", lambda e, q=q, sl=sl, pg=pg: e.scalar_tensor_tensor(q["t1"][:], in0=pg[:], scalar=-1.0, in1=sl["dec"][:],
                                                                                        op0=ALU.mult, op1=ALU.mult), reads=[pg, sl["dec"]], writes=[q["t1"]])
                        S.op("gpsimd", lambda e, q=q: e.tensor_tensor(q["Yf"][:], q["t1"][:], strict, op=ALU.mult),
                             reads=[q["t1"], cst], writes=[q["Yf"]])
                    for s in range(2):
                        q = st[s]
                        for h in range(4):
                            S.op("tensor", lambda e, h=h, q=q, pi=pinv[s]: e.transpose(pi[:, h, :], q["Yf"][:, h, :], identF()),
                                 reads=[q["Yf"], cst], writes=[pinv[s]])
                    for s in range(2):
                        q = st[s]
                        S.op("scalar", lambda e, q=q, pi_=pinv[s]: e.copy(q["Xf"][:], pi_[:]), reads=[pinv[s]], writes=[q["Xf"]])
                        S.op("gpsimd", lambda e, q=q: e.tensor_tensor(q["Y"][0][:], q["Yf"][:], bc(C_BM16), op=ALU.mult), reads=[q["Yf"], cst], writes=[q["Y"][0]])
                        S.op("gpsimd", lambda e, q=q: e.tensor_tensor(q["X"][0][:], q["Xf"][:], bc(C_BM16), op=ALU.mult), reads=[q["Xf"], cst], writes=[q["X"][0]])
                        S.op("gpsimd", lambda e, q=q: e.tensor_tensor(q["Q"][:], q["Y"][0][:], identq, op=ALU.add), reads=[q["Y"][0], cst], writes=[q["Q"]])
                        S.op("gpsimd", lambda e, q=q: e.tensor_tensor(q["P"][:], q["X"][0][:], identq, op=ALU.add), reads=[q["X"][0], cst], writes=[q["P"]])
                    for lev in range(1, 4):
                        a, b = (lev - 1) % 2, lev % 2
                        for s in range(2):
                            q = st[s]
                            mm4(pinv[s], q["Y"][a], q["X"][a], [q["Y"][a], q["X"][a]])
                            mm4(pre[s], q["X"][a], q["Y"][a], [q["Y"][a], q["X"][a]])
                        for s in range(2):
                            q = st[s]
                            S.op("scalar", lambda e, q=q, b=b, px_=pinv[s]: e.copy(q["X"][b][:], px_[:]), reads=[pinv[s]], writes=[q["X"][b]])
                            S.op("vector", lambda e, q=q, b=b, py_=pre[s]: e.tensor_copy(q["Y"][b][:], py_[:]), reads=[pre[s]], writes=[q["Y"][b]])
                        for s in range(2):
                            q = st[s]
                            mm4(pinv[s], q["X"][b], q["Q"], [q["X"][b], q["Q"]])
                            mm4(pre[s], q["Y"][b], q["P"], [q["Y"][b], q["P"]])
                        for s in range(2):
                            q = st[s]
                            S.op("vector", lambda e, q=q, pq_=pinv[s]: e.tensor_tensor(q["Q"][:], q["Q"][:], pq_[:], op=ALU.add),
                                 reads=[q["Q"], pinv[s]], writes=[q["Q"]])
                            S.op("vector", lambda e, q=q, pp_=pre[s]: e.tensor_tensor(q["P"][:], q["P"][:], pp_[:], op=ALU.add),
                                 reads=[q["P"], pre[s]], writes=[q["P"]])
                    for mi, coff in enumerate((C_OFF32, C_OFF64, C_OFF128)):
                        lastm = (mi == 2)
                        for s in range(2):
                            q = st[s]
                            S.op("gpsimd", lambda e, q=q, coff=coff: e.tensor_tensor(q["Y"][0][:], q["Yf"][:], bc(coff), op=ALU.mult), reads=[q["Yf"], cst], writes=[q["Y"][0]])
                            S.op("gpsimd", lambda e, q=q, coff=coff: e.tensor_tensor(q["X"][0][:], q["Xf"][:], bc(coff), op=ALU.mult), reads=[q["Xf"], cst], writes=[q["X"][0]])
                        for s in range(2):
                            q = st[s]
                            mm4(pinv[s], q["X"][0], q["Q"], [q["X"][0], q["Q"]])
                            if not lastm:
                                mm4(pre[s], q["Y"][0], q["P"], [q["Y"][0], q["P"]])
                        for s in range(2):
                            q = st[s]
                            S.op("scalar", lambda e, q=q, p_=pinv[s]: e.copy(q["X"][1][:], p_[:]), reads=[pinv[s]], writes=[q["X"][1]])
                            if not lastm:
                                S.op("vector", lambda e, q=q, p_=pre[s]: e.tensor_copy(q["Y"][1][:], p_[:]), reads=[pre[s]], writes=[q["Y"][1]])
                        for s in range(2):
                            q = st[s]
                            mm4(pinv[s], q["P"], q["X"][1], [q["P"], q["X"][1]])
                            if not lastm:
                                mm4(pre[s], q["Q"], q["Y"][1], [q["Q"], q["Y"][1]])
                        for s in range(2):
                            q = st[s]
                            S.op("vector", lambda e, q=q, p_=pinv[s]: e.tensor_tensor(q["Q"][:], q["Q"][:], p_[:], op=ALU.add),
                                 reads=[q["Q"], pinv[s]], writes=[q["Q"]])
                            if not lastm:
                                S.op("vector", lambda e, q=q, p_=pre[s]: e.tensor_tensor(q["P"][:], q["P"][:], p_[:], op=ALU.add),
                                     reads=[q["P"], pre[s]], writes=[q["P"]])
                    for s in range(2):
                        sl, q = SL[s], st[s]
                        S.op("scalar", lambda e, q=q: e.copy(q["Qb"][:], q["Q"][:]), reads=[q["Q"]], writes=[q["Qb"]])
                        dbg("Q", q["Q"], s, n)
                        pa = pre[s]
                        q["pa"] = pa
                        mm4(pa, sl["kT"], sl["qT"], [sl["kT"], sl["qT"]])
                    for s in range(2):
                        sl, q = SL[s], st[s]
                        S.op("vector", lambda e, q=q, sl=sl, pa_=q["pa"]: e.scalar_tensor_tensor(q["aT"][:], in0=pa_[:], scalar=128.0 ** -0.5, in1=sl["dec"][:],
                                                                                     op0=ALU.mult, op1=ALU.mult), reads=[q["pa"], sl["dec"]], writes=[q["aT"]])
                        pu_ = pinv[s]
                        q["pu"] = pu_
                        mm4(pu_, q["Qb"], sl["vb"], [q["Qb"], sl["vb"]])
                    for s in range(2):
                        sl, q = SL[s], st[s]
                        S.op("scalar", lambda e, q=q, pu_=q["pu"]: e.copy(q["u"][:], pu_[:]), reads=[q["pu"]], writes=[q["u"]])
                        dbg("U", q["u"], s, n)
                        pw_ = pre[s]
                        q["pw"] = pw_
                        mm4(pw_, sl["kbg"], q["Qb"], [q["Qb"], sl["kbg"]])
                    for s in range(2):
                        q = st[s]
                        S.op("scalar", lambda e, q=q, pw_=q["pw"]: e.copy(q["wT"][:], pw_[:]), reads=[q["pw"]], writes=[q["wT"]])
                    for s in range(2):
                        sl, q = SL[s], st[s]
                        mm4(pW, q["wT"], q["Sb"], [q["wT"], q["Sb"]])
                        S.op("vector", lambda e, q=q: e.tensor_tensor(q["vn"][:], q["u"][:], pW[:], op=ALU.subtract),
                             reads=[q["u"], pW], writes=[q["vn"]])
                        if debug and s == 0 and n == 1 and l == 0:
                            S.op("vector", lambda e, q=q: e.tensor_copy(q["o1"][:], pW[:]), reads=[pW], writes=[q["o1"]])
                            dbg("PW1", q["o1"], s, n, 1)
                        for h in range(4):
                            S.op("tensor", lambda e, h=h, q=q, sl=sl: e.matmul(pO[:, h, :], lhsT=sl["qgT"][:, h, :], rhs=q["Sb"][:, h, :], start=True, stop=False),
                                 reads=[sl["qgT"], q["Sb"]], writes=[pO])
                            S.op("tensor", lambda e, h=h, q=q: e.matmul(pO[:, h, :], lhsT=q["aT"][:, h, :], rhs=q["vn"][:, h, :], start=False, stop=True),
                                 reads=[q["aT"], q["vn"]], writes=[pO])
                        mm4(pSt, sl["kdec"], q["vn"], [sl["kdec"], q["vn"]])
                        S.op("gpsimd", lambda e, q=q, sl=sl: e.tensor_tensor(q["Sf"][:], q["Sf"][:], sl["egl"][:, :, None].to_broadcast([128, 4, 128]),
                                                                            op=ALU.mult), reads=[q["Sf"], sl["egl"]], writes=[q["Sf"]])
                        S.op("vector", lambda e, q=q: e.tensor_tensor(q["Sf"][:], q["Sf"][:], pSt[:], op=ALU.add), reads=[q["Sf"], pSt], writes=[q["Sf"]])
                        S.op("scalar", lambda e, q=q: e.copy(q["Sb"][:], q["Sf"][:]), reads=[q["Sf"]], writes=[q["Sb"]])
                        dbg("S0", q["Sf"], s, n)
                        dbg("U1", q["u"], s, n, 1)
                        S.op("scalar", lambda e, q=q: e.activation(q["osq"][:], pO[:], AF.Square), reads=[pO], writes=[q["osq"]])
                        S.op("vector", lambda e, q=q: e.reduce_sum(q["oss"][:], q["osq"][:], axis=mybir.AxisListType.X), reads=[q["osq"]], writes=[q["oss"]])
                        S.op("scalar", lambda e, q=q: e.activation(q["oss"][:], q["oss"][:], AF.Ln, bias=epsc(), scale=1.0 / 128), reads=[q["oss"], cst], writes=[q["oss"]])
                        S.op("scalar", lambda e, q=q: e.activation(q["oss"][:], q["oss"][:], AF.Exp, scale=-0.5), reads=[q["oss"]], writes=[q["oss"]])
                        S.op("vector", lambda e, q=q: e.tensor_tensor(q["o1"][:], pO[:], q["oss"][:, :, None].to_broadcast([128, 4, 128]), op=ALU.mult),
                             reads=[pO, q["oss"]], writes=[q["o1"]])
                        S.op("gpsimd", lambda e, q=q, sl=sl: e.tensor_tensor(q["g2"][:], sl["gate"][:], onormb, op=ALU.mult), reads=[sl["gate"], pp], writes=[q["g2"]])
                        S.op("gpsimd", lambda e, q=q: e.tensor_tensor(q["y"][:], q["o1"][:], q["g2"][:], op=ALU.mult), reads=[q["o1"], q["g2"]], writes=[q["y"]])
                        for h in range(4):
                            S.op("tensor", lambda e, h=h, q=q: e.transpose(pTr[:, h, :], q["y"][:, h, :], identB[:]), reads=[q["y"], identB], writes=[pTr])
                        j = n % 4
                        S.op("scalar", lambda e, q=q, j=j: e.copy(q["oT"][:, :, j * 128:(j + 1) * 128], pTr[:, 0:4, :]), reads=[pTr], writes=[q["oT"]])
                        if j == 3:
                            t0 = (n - 3) * 128
                            S.dma("sync", lambda e, q=q, s=s, t0=t0: e.dma_start(out=OT[s, 0:4, :, t0:t0 + 512].rearrange("h p t -> p h t"), in_=q["oT"][:]),
                                  q["oT"], reads=[q["oT"]])
            if upto == "D":
                break
            with Phase(S, "E%d" % l) as P:
                pp = P.sb("pp", [128, NPP], F32)
                S.dma("sync", lambda e, l=l: e.dma_start(out=pp[:], in_=pp_in[l, :, :]), pp, writes=[pp])
                lam_init = 0.8 - 0.6 * math.exp(-0.3 * l)
                lamr = P.sb("lamr", [1, 4, 64], F32)
                S.dma("sync", lambda e, l=l: e.dma_start(out=lamr[:], in_=lam_in[l:l + 1, :, :]), lamr, writes=[lamr])
                lprod = P.sb("lprod", [1, 2, 64], F32)
                lsum = P.sb("lsum", [1, 2], F32)
                nlam1 = P.sb("nlam1", [1, 1], F32)
                nlam = P.sb("nlam", [128, 1], F32)
                subg = P.sb("subg", [128, 1], F32)
                b31 = P.sb("b31", [128, 4], F32)
                S.dma("sync", lambda e: e.dma_start(out=b31[:], in_=b31_in[:, :]), b31, writes=[b31])
                S.op("vector", lambda e: e.tensor_tensor(lprod[:], lamr[:, 0:4:2, :], lamr[:, 1:4:2, :], op=ALU.mult), reads=[lamr], writes=[lprod])
                S.op("vector", lambda e: e.reduce_sum(lsum[:], lprod[:], axis=mybir.AxisListType.X), reads=[lprod], writes=[lsum])
                S.op("scalar", lambda e: e.activation(lsum[:], lsum[:], AF.Exp), reads=[lsum], writes=[lsum])
                S.op("vector", lambda e: e.scalar_tensor_tensor(nlam1[:], in0=lsum[:, 1:2], scalar=-lam_init, in1=lsum[:, 0:1],
                                                                op0=ALU.add, op1=ALU.subtract), reads=[lsum], writes=[nlam1])
                pS0 = [P.ps("pS0_%d" % i, [128, 512], F32) for i in range(2)]
                pS1 = [P.ps("pS1_%d" % i, [128, 512], F32) for i in range(2)]
                pO0 = P.ps("pO0", [128, 512], F32); pO1 = P.ps("pO1", [128, 512], F32)
                pZ0 = P.ps("pZ0", [128, 512], F32); pZ1 = P.ps("pZ1", [128, 512], F32)
                S.op("tensor", lambda e: e.matmul(pZ0[:, 0:1], lhsT=cst[0:1, C_ONES:C_ONES + 128], rhs=nlam1[:], start=True, stop=True),
                     reads=[cst, nlam1], writes=[pZ0])
                S.op("vector", lambda e: e.tensor_copy(nlam[:], pZ0[:, 0:1]), reads=[pZ0], writes=[nlam])
                S.op("vector", lambda e: e.tensor_scalar_mul(subg[:], pp[:, PP_SUBLN:PP_SUBLN + 1], 1.0 - lam_init), reads=[pp], writes=[subg])
                expB = []
                for h in range(4):
                    t = P.sb("expB%d" % h, [128, 1024], F32)
                    S.dma("sync", lambda e, h=h, t=t: e.dma_start(out=t[:], in_=tb_in[h, :, :]), t, writes=[t])
                    S.op("scalar", lambda e, t=t: e.activation(t[:], t[:], AF.Exp), reads=[t], writes=[t])
                    expB.append(t)
                qts = [P.sb("qt%d" % i, [128, T], BF16) for i in range(2)]
                kts = [P.sb("kt%d" % i, [128, T], BF16) for i in range(2)]
                vts = [P.sb("vt%d" % i, [128, NCH, 128], BF16) for i in range(2)]
                E0 = [P.sb("E0_%d" % i, [128, 512], BF16) for i in range(3)]
                E1 = [P.sb("E1_%d" % i, [128, 512], BF16) for i in range(3)]
                Ef = [P.sb("Ef_%d" % i, [128, 512], F32) for i in range(2)]
                rz0 = P.sb("rz0", [128, 512], F32); rz1 = P.sb("rz1", [128, 512], F32)
                zc0 = P.sb("zc0", [128, 512], F32); zc1 = P.sb("zc1", [128, 512], F32)
                oc0 = P.sb("oc0", [128, 512], F32); oc1 = P.sb("oc1", [128, 512], F32)
                oo = P.sb("oo", [128, 512], F32); osq = P.sb("osq", [128, 512], F32)
                rin = P.sb("rin", [128, 512], F32)
                oTs = [P.sb("oTs%d" % i, [128, 512], BF16) for i in range(2)]
                heads = [(s, h) for s in range(2) for h in range(4)]

                def loadE(i):
                    s, h = heads[i]
                    S.dma("sync", lambda e: e.dma_start(out=qts[i % 2][:], in_=DQT[s, h, :, :]), qts[i % 2], writes=[qts[i % 2]])
                    S.dma("sync", lambda e: e.dma_start(out=kts[i % 2][:], in_=DKT[s, h, :, :]), kts[i % 2], writes=[kts[i % 2]])
                    S.dma("sync", lambda e: e.dma_start(out=vts[i % 2][:], in_=DV[s, :, :, h * 128:(h + 1) * 128].rearrange("n p e -> p n e")),
                          vts[i % 2], writes=[vts[i % 2]])

                loadE(0)
                steps = []
                for i, (s, h) in enumerate(heads):
                    for j in range(8):
                        nk = 4 * j + 4
                        for ki in range(nk):
                            steps.append((i, s, h, j, ki, nk))
                cnt = {"ek": 0, "fk": 0, "ok": 0, "loaded": 0}
                live = {}

                def stageA(idx):
                    i, s, h, j, ki, nk = steps[idx]
                    qt, kt = qts[i % 2], kts[i % 2]
                    ek = cnt["ek"]; cnt["ek"] += 1
                    p0, p1 = pS0[ek % 2], pS1[ek % 2]
                    e0, e1 = E0[ek % 3], E1[ek % 3]
                    S.op("tensor", lambda e: e.matmul(p0[:], lhsT=kt[0:64, ki * 128:(ki + 1) * 128], rhs=qt[0:64, j * 512:(j + 1) * 512],
                                                      start=True, stop=True), reads=[kt, qt], writes=[p0])
                    S.op("tensor", lambda e: e.matmul(p1[:], lhsT=kt[64:128, ki * 128:(ki + 1) * 128], rhs=qt[64:128, j * 512:(j + 1) * 512],
                                                      start=True, stop=True), reads=[kt, qt], writes=[p1])
                    near = ki >= 4 * j - 1
                    if not near:
                        S.op("scalar", lambda e: e.activation(e0[:], p0[:], AF.Exp, bias=b31[:, h:h + 1]), reads=[p0, b31], writes=[e0])
                        S.op("scalar", lambda e: e.activation(e1[:], p1[:], AF.Exp, bias=b31[:, h:h + 1]), reads=[p1, b31], writes=[e1])
                    else:
                        c0 = 512 * j - 128 * ki + 384
                        for pc, ec in ((p0, e0), (p1, e1)):
                            ef = Ef[cnt["fk"] % 2]; cnt["fk"] += 1
                            S.op("scalar", lambda e, pc=pc, ef=ef: e.activation(ef[:], pc[:], AF.Exp), reads=[pc], writes=[ef])
                            S.op("vector",
                                 lambda e, ef=ef, ec=ec: e.tensor_tensor(ec[:], ef[:], expB[h][:, c0:c0 + 512], op=ALU.mult),
                                 reads=[ef, expB[h]], writes=[ec])
                    live[idx] = (e0, e1)

                def stageB(idx):
                    i, s, h, j, ki, nk = steps[idx]
                    if j == 0 and ki == 0 and i + 1 < len(heads):
                        loadE(i + 1)
                    vt = vts[i % 2]
                    e0, e1 = live.pop(idx)
                    first, lastk = (ki == 0), (ki == nk - 1)
                    S.op("tensor", lambda e: e.matmul(pO0[:], lhsT=vt[:, ki, :], rhs=e0[:], start=first, stop=lastk), reads=[vt, e0], writes=[pO0])
                    S.op("tensor", lambda e: e.matmul(pZ0[:], lhsT=onesB[:], rhs=e0[:], start=first, stop=lastk), reads=[onesB, e0], writes=[pZ0])
                    S.op("tensor", lambda e: e.matmul(pO1[:], lhsT=vt[:, ki, :], rhs=e1[:], start=first, stop=lastk), reads=[vt, e1], writes=[pO1])
                    S.op("tensor", lambda e: e.matmul(pZ1[:], lhsT=onesB[:], rhs=e1[:], start=first, stop=lastk), reads=[onesB, e1], writes=[pZ1])
                    if not lastk:
                        return
                    S.op("scalar", lambda e: e.copy(zc0[:], pZ0[:]), reads=[pZ0], writes=[zc0])
                    S.op("vector", lambda e: e.tensor_copy(zc1[:], pZ1[:]), reads=[pZ1], writes=[zc1])
                    S.op("scalar", lambda e: e.copy(oc0[:], pO0[:]), reads=[pO0], writes=[oc0])
                    S.op("vector", lambda e: e.tensor_copy(oc1[:], pO1[:]), reads=[pO1], writes=[oc1])
                    pend_epi.append((s, h, j))
                    return

                def epilogue_tail():
                    s, h, j = pend_epi.pop(0)
                    S.op("vector", lambda e: e.reciprocal(rz0[:], zc0[:]), reads=[zc0], writes=[rz0])
                    S.op("vector", lambda e: e.reciprocal(rz1[:], zc1[:]), reads=[zc1], writes=[rz1])
                    S.op("gpsimd", lambda e: e.tensor_tensor(rz0[:], oc0[:], rz0[:], op=ALU.mult), reads=[oc0, rz0], writes=[rz0])
                    S.op("gpsimd", lambda e: e.tensor_tensor(rz1[:], oc1[:], rz1[:], op=ALU.mult), reads=[oc1, rz1], writes=[rz1])
                    S.op("vector", lambda e: e.scalar_tensor_tensor(oo[:], in0=rz1[:], scalar=nlam[:, 0:1], in1=rz0[:], op0=ALU.mult, op1=ALU.add),
                         reads=[rz0, rz1, nlam], writes=[oo])
                    S.op("gpsimd", lambda e: e.tensor_tensor(osq[:], oo[:], oo[:], op=ALU.mult), reads=[oo], writes=[osq])
                    pend_epi2.append([s, h, j, 8])

                def epilogue_tail2():
                    s, h, j, _ = pend_epi2.pop(0)
                    ek = cnt["ek"]; cnt["ek"] += 1
                    pss = pS0[ek % 2]
                    S.op("tensor", lambda e: e.matmul(pss[:], lhsT=ONESF(), rhs=osq[:], start=True, stop=True), reads=[cst, osq], writes=[pss])
                    S.op("scalar", lambda e: e.activation(rin[:], pss[:], AF.Ln, bias=epsc(), scale=1.0 / 128), reads=[pss, cst], writes=[rin])
                    S.op("scalar", lambda e: e.activation(rin[:], rin[:], AF.Exp, scale=-0.5), reads=[rin], writes=[rin])
                    ot = oTs[cnt["ok"] % 2]; cnt["ok"] += 1
                    S.op("vector", lambda e: e.scalar_tensor_tensor(ot[:], in0=oo[:], scalar=subg[:, 0:1], in1=rin[:], op0=ALU.mult, op1=ALU.mult),
                         reads=[oo, subg, rin], writes=[ot])
                    S.dma("sync", lambda e: e.dma_start(out=OT[s, 4 + h, :, j * 512:(j + 1) * 512], in_=ot[:]), ot, reads=[ot])

                pend_epi = []
                pend_epi2 = []
                LOOK = 2
                for idx in range(min(LOOK, len(steps))):
                    stageA(idx)
                for idx in range(len(steps)):
                    had = len(pend_epi)
                    stageB(idx)
                    if idx + LOOK < len(steps):
                        stageA(idx + LOOK)
                    for it in pend_epi2:
                        it[3] -= 1
                    if had:
                        while pend_epi2:
                            epilogue_tail2()
                        epilogue_tail()
                    while pend_epi2 and pend_epi2[0][3] <= 0:
                        epilogue_tail2()
                while pend_epi:
                    while pend_epi2:
                        epilogue_tail2()
                    epilogue_tail()
                while pend_epi2:
                    epilogue_tail2()
            if upto == "E":
                break
            with Phase(S, "F%d" % l) as P:
                Wo = load_w_bf16(P, "Wo", w_out[l, :, :], 8, DM)
                gtB = []
                for b in range(2):
                    t = P.sb("gtB%d" % b, [128, DM], F32)
                    S.dma("sync", lambda e, b=b, t=t, l=l: e.dma_start(out=t[:], in_=MOD[l, b, 2 * DM:3 * DM].partition_broadcast(128)), t, writes=[t])
                    gtB.append(t)
                xts = [P.sb("xt%d" % i, [128, 4, DM], F32) for i in range(2)]
                ots = [P.sb("ot%d" % i, [128, 8, 512], BF16) for i in range(2)]
                tmp = [P.sb("tmp%d" % i, [128, 512], F32) for i in range(2)]
                pY = [P.ps("pY%d" % i, [128, 512], F32) for i in range(4)]
                blocks = [(s, blk) for s in range(2) for blk in range(8)]

                def loadF(i):
                    s, blk = blocks[i]
                    S.dma("sync", lambda e: e.dma_start(out=xts[i % 2][:], in_=xsrc[s, blk * 512:(blk + 1) * 512, :].rearrange("(a p) d -> p a d", p=128)),
                          xts[i % 2], writes=[xts[i % 2]])
                    S.dma("sync", lambda e: e.dma_start(out=ots[i % 2][:], in_=OT[s, :, :, blk * 512:(blk + 1) * 512].rearrange("c p t -> p c t")),
                          ots[i % 2], writes=[ots[i % 2]])

                loadF(0)
                yk = 0
                for i, (s, blk) in enumerate(blocks):
                    if i + 1 < len(blocks):
                        loadF(i + 1)
                    xt, ot = xts[i % 2], ots[i % 2]
                    for sub in range(4):
                        for dh in range(2):
                            py = pY[yk % 4]; tp = tmp[yk % 2]; yk += 1
                            for c in range(8):
                                S.op("tensor", lambda e, c=c, sub=sub, dh=dh, py=py, ot=ot: e.matmul(py[:], lhsT=ot[:, c, sub * 128:(sub + 1) * 128],
                                                                                                  rhs=Wo[:, c, dh * 512:(dh + 1) * 512], start=(c == 0), stop=(c == 7)),
                                     reads=[ot, Wo], writes=[py])
                            S.op("vector", lambda e, py=py, tp=tp, dh=dh, s=s: e.tensor_tensor(tp[:], py[:], gtB[s][:, dh * 512:(dh + 1) * 512], op=ALU.mult),
                                 reads=[py, gtB[s]], writes=[tp])
                            S.op("gpsimd", lambda e, tp=tp, sub=sub, dh=dh, xt=xt: e.tensor_tensor(xt[:, sub, dh * 512:(dh + 1) * 512], xt[:, sub, dh * 512:(dh + 1) * 512],
                                                                                                   tp[:], op=ALU.add), reads=[tp, xt], writes=[xt])
                    S.dma("sync", lambda e, xt=xt, s=s, blk=blk: e.dma_start(out=XA[s, blk * 512:(blk + 1) * 512, :].rearrange("(a p) d -> p a d", p=128), in_=xt[:]),
                          xt, reads=[xt])
            if upto == "F":
                break
            with Phase(S, "G%d" % l) as P:
                pp = P.sb("pp", [128, NPP], F32)
                S.dma("sync", lambda e, l=l: e.dma_start(out=pp[:], in_=pp_in[l, :, :]), pp, writes=[pp])
                Wup = load_w_bf16(P, "Wup", ffn_up[l, :, :], 8, 2 * DFF)
                AB = {}
                for b in range(2):
                    AB[b] = make_AB(P, l, b, pp, PP_NFFN, 3, 4, "ffn%d" % b)
                NB = 512
                xts = [P.sb("xt%d" % i, [128, 4, DM], F32) for i in range(2)]
                ss = P.sb("ss", [128, 4], F32); rs = P.sb("rs", [128, 4], F32)
                sq = P.sb("sq", [128, DM], BF16); xn = P.sb("xn", [128, 4, DM], BF16)
                tmpf = P.sb("tmpf", [128, 8, 128], F32)
                hT = P.sb("hT", [128, 8, NB], BF16)
                gts = [P.sb("gt%d" % i, [128, NB], BF16) for i in range(4)]
                pT = P.ps("pT", [128, 8, 128], BF16)
                pUf = [P.ps("pU%d" % i, [128, 512], F32) for i in range(6)]
                upad = [P.sb("upad%d" % i, [128, NB + 2], F32) for i in range(4)]
                acc = [P.sb("acc%d" % i, [128, NB], F32) for i in range(4)]
                sg = [P.sb("sg%d" % i, [128, NB], F32) for i in range(2)]
                halo = P.sb("halo", [128, 44, 2], F32)
                blocks = [(s, blk) for s in range(2) for blk in range(T // NB)]

                def loadG(i):
                    s, blk = blocks[i]
                    S.dma("sync", lambda e: e.dma_start(out=xts[i % 2][:], in_=XA[s, blk * NB:(blk + 1) * NB, :].rearrange("(a p) d -> p a d", p=128)),
                          xts[i % 2], writes=[xts[i % 2]])

                loadG(0)
                uk = 0; pk = 0; gk = 0
                for i, (s, blk) in enumerate(blocks):
                    if i + 1 < len(blocks):
                        loadG(i + 1)
                    xt = xts[i % 2]
                    A, Bsh = AB[s]
                    if blk == 0:
                        S.op("gpsimd", lambda e: e.memset(halo[:], 0.0), writes=[halo])
                    norm_to_hT(P, xt, 4, A, Bsh, hT, (ss, rs, sq, xn, tmpf), pT, None)
                    for fc in range(22):
                        res = []
                        for half in range(2):
                            f = fc + 22 * half
                            pu = pUf[pk % 6]; pk += 1
                            up = upad[uk % 4]; ac = acc[uk % 4]; uk += 1
                            for kc in range(8):
                                S.op("tensor", lambda e, kc=kc, f=f, pu=pu: e.matmul(pu[:], lhsT=Wup[:, kc, f * 128:(f + 1) * 128], rhs=hT[:, kc, :],
                                                                                    start=(kc == 0), stop=(kc == 7)), reads=[Wup, hT], writes=[pu])
                            S.op("gpsimd", lambda e, f=f, up=up: e.tensor_copy(up[:, 0:2], halo[:, f, :]), reads=[halo], writes=[up])
                            S.op("scalar", lambda e, up=up, pu=pu: e.copy(up[:, 2:NB + 2], pu[:]), reads=[pu], writes=[up])
                            S.op("gpsimd", lambda e, f=f, up=up: e.tensor_copy(halo[:, f, :], up[:, NB:NB + 2]), reads=[up], writes=[halo])
                            cw = lambda j, f=f: pp[:, PP_FCW + f * 3 + j:PP_FCW + f * 3 + j + 1]
                            S.op("vector", lambda e, up=up, ac=ac, cw=cw, f=f: e.tensor_scalar(ac[:], up[:, 2:NB + 2], cw(2), pp[:, PP_FCB + f:PP_FCB + f + 1],
                                                                                               op0=ALU.mult, op1=ALU.add), reads=[up, pp], writes=[ac])
                            for j in (1, 0):
                                S.op("vector", lambda e, up=up, ac=ac, cw=cw, j=j: e.scalar_tensor_tensor(ac[:], in0=up[:, j:j + NB], scalar=cw(j), in1=ac[:],
                                                                                                          op0=ALU.mult, op1=ALU.add), reads=[up, pp, ac], writes=[ac])
                            res.append(ac)
                        sgt = sg[fc % 2]
                        gt = gts[gk % 4]; gk += 1
                        S.op("scalar", lambda e, sgt=sgt, g_=res[1]: e.activation(sgt[:], g_[:], AF.Silu), reads=[res[1]], writes=[sgt])
                        S.op("gpsimd", lambda e, sgt=sgt, a_=res[0], gt=gt: e.tensor_tensor(gt[:], a_[:], sgt[:], op=ALU.mult), reads=[res[0], sgt], writes=[gt])
                        S.dma("sync", lambda e, gt=gt, s=s, blk=blk, fc=fc: e.dma_start(out=GTD[s, fc, :, blk * NB:(blk + 1) * NB], in_=gt[:]), gt, reads=[gt])
            with Phase(S, "H%d" % l) as P:
                Wdn = load_w_bf16(P, "Wdn", ffn_down[l, :, :], 22, DM)
                gtB = []
                for b in range(2):
                    t = P.sb("gtB%d" % b, [128, DM], F32)
                    S.dma("sync", lambda e, b=b, t=t, l=l: e.dma_start(out=t[:], in_=MOD[l, b, 5 * DM:6 * DM].partition_broadcast(128)), t, writes=[t])
                    gtB.append(t)
                NB = 512
                xts = [P.sb("xt%d" % i, [128, 4, DM], F32) for i in range(2)]
                gin = [P.sb("gin%d" % i, [128, 22, NB], BF16) for i in range(2)]
                tmp = [P.sb("tmp%d" % i, [128, 512], F32) for i in range(2)]
                pY = [P.ps("pY%d" % i, [128, 512], F32) for i in range(4)]
                blocks = [(s, blk) for s in range(2) for blk in range(T // NB)]

                def loadH(i):
                    s, blk = blocks[i]
                    S.dma("sync", lambda e: e.dma_start(out=xts[i % 2][:], in_=XA[s, blk * NB:(blk + 1) * NB, :].rearrange("(a p) d -> p a d", p=128)),
                          xts[i % 2], writes=[xts[i % 2]])
                    for f0 in (0, 11):
                        S.dma("sync", lambda e, f0=f0: e.dma_start(out=gin[i % 2][:, f0:f0 + 11, :],
                                                                   in_=GTD[s, f0:f0 + 11, :, blk * NB:(blk + 1) * NB].rearrange("f p t -> p f t")),
                              gin[i % 2], writes=[gin[i % 2]])

                loadH(0)
                yk = 0
                for i, (s, blk) in enumerate(blocks):
                    if i + 1 < len(blocks):
                        loadH(i + 1)
                    xt, gi = xts[i % 2], gin[i % 2]
                    for sub in range(4):
                        for dh in range(2):
                            py = pY[yk % 4]; tp = tmp[yk % 2]; yk += 1
                            for fc in range(22):
                                S.op("tensor", lambda e, fc=fc, sub=sub, dh=dh, py=py, gi=gi: e.matmul(py[:], lhsT=gi[:, fc, sub * 128:(sub + 1) * 128],
                                                                                                           rhs=Wdn[:, fc, dh * 512:(dh + 1) * 512], start=(fc == 0), stop=(fc == 21)),
                                     reads=[gi, Wdn], writes=[py])
                            S.op("vector", lambda e, py=py, tp=tp, dh=dh, s=s: e.tensor_tensor(tp[:], py[:], gtB[s][:, dh * 512:(dh + 1) * 512], op=ALU.mult),
                                 reads=[py, gtB[s]], writes=[tp])
                            S.op("gpsimd", lambda e, tp=tp, sub=sub, dh=dh, xt=xt: e.tensor_tensor(xt[:, sub, dh * 512:(dh + 1) * 512], xt[:, sub, dh * 512:(dh + 1) * 512],
                                                                                                   tp[:], op=ALU.add), reads=[tp, xt], writes=[xt])
                    S.dma("sync", lambda e, xt=xt, s=s, blk=blk: e.dma_start(out=xdst[s, blk * NB:(blk + 1) * NB, :].rearrange("(a p) d -> p a d", p=128), in_=xt[:]),
                          xt, reads=[xt])
            xsrc = xdst
        G.__exit__(None, None, None)
    return nc


def _prep(inputs):
    inp = {k: np.asarray(v) for k, v in inputs.items()}
    consts = _consts()
    pp = _pack_pp(inp)
    lamv = np.stack([inp["diff_lambda_q1"], inp["diff_lambda_k1"], inp["diff_lambda_q2"], inp["diff_lambda_k2"]], axis=1)
    lamv = np.ascontiguousarray(lamv.astype(np.float32))
    kk = np.arange(128)[:, None]
    cc = np.arange(1024)[None, :]
    dist = cc - kk - 384
    bidx = _t5_bucket(np.maximum(dist, 0))
    rb = inp["rel_bias"].astype(np.float32)
    tb = np.empty((4, 128, 1024), np.float32)
    for h in range(4):
        tb[h] = np.where(dist >= 0, rb[bidx, h], np.float32(NEG))
    b31 = np.ascontiguousarray(np.broadcast_to(rb[31][None, :], (128, 4))).astype(np.float32)
    shared = dict(consts=consts, pp=pp, lamv=lamv, tb=tb, b31=b31,
                  w_ada=inp["w_ada"], b_ada=inp["b_ada"], w_in=inp["w_in"], w_out=inp["w_out"],
                  ffn_up=inp["ffn_up"], ffn_down=inp["ffn_down"])
    in_maps = []
    for c in range(NCORES):
        m = dict(shared)
        m["x"] = np.ascontiguousarray(inp["x"][2 * c:2 * c + 2])
        cc_ = inp["c"][2 * c:2 * c + 2]
        m["cT"] = np.ascontiguousarray(cc_.reshape(2, 8, 128).transpose(2, 1, 0))
        in_maps.append(m)
    return in_maps


def kernel(**inputs):
    in_maps = _prep(inputs)
    nc = build()
    res = run_bass_kernel_spmd(nc, in_maps, core_ids=list(range(NCORES)))
    return np.concatenate([r["out"] for r in res.results], axis=0).astype(np.float32)
```

```python
import math
from contextlib import ExitStack
import numpy as np
import concourse.bass as bass
import concourse.mybir as mybir
from concourse.bass_utils import run_bass_kernel_spmd

F32 = mybir.dt.float32
BF16 = mybir.dt.bfloat16
AF = mybir.ActivationFunctionType
ALU = mybir.AluOpType

NCORES = 8
DEPTH = 4
T = 4096
DM = 1024
NEG = -30000.0
EPS = 1e-6
DFF = 2816


class Buf:
    def __init__(self, name, t=None):
        self.name = name
        self.t = t
        self.last_w = None
        self.readers = []
        self.dsem = None
        self.dcount = 0

    def __getitem__(self, k):
        return self.t[k]


class Eng:
    def __init__(self, name, sem):
        self.name = name
        self.sem = sem
        self.count = 0
        self.waited = {}
        self.prog = []


class Sched:
    SEM_WRAP = 30000

    def __init__(self, nc, es):
        self.nc = nc
        self.es = es
        self.engs = {}
        for name in ("sync", "scalar", "vector", "gpsimd", "tensor"):
            self.engs[name] = Eng(name, es.enter_context(nc.semaphore("s_" + name)))
        self.n_instr = 0
        self.dbufs = []
        self.sem_pool = []
        self.rr = 0

    def _waits(self, e, reads, writes):
        toks = []
        own = e.sem
        for b in reads:
            if b.last_w is not None:
                toks.append(b.last_w)
        for b in writes:
            if b.last_w is not None and b.last_w[0] is not own:
                toks.append(b.last_w)
            for r in b.readers:
                if r[0] is not own:
                    toks.append(r)
        need = {}
        for sem, val in toks:
            if e.name == "tensor" and sem is own:
                continue
            k = id(sem)
            if e.waited.get(k, 0) >= val:
                continue
            if k not in need or need[k][1] < val:
                need[k] = (sem, val)
        out = []
        for k, (sem, val) in need.items():
            e.waited[k] = val
            out.append((sem, val))
        return out

    def _record(self, e, fn, waits, sem, inc, reads, writes, tok):
        def run(eng, fn=fn, waits=waits, sem=sem, inc=inc):
            for s, v in waits:
                eng.wait_ge(s, v)
            fn(eng).then_inc(sem, inc)
        e.prog.append(run)
        for b in reads:
            b.readers.append(tok)
            if len(b.readers) > 64:
                b.readers = b.readers[-48:]
        for b in writes:
            b.last_w = tok
            b.readers = []
        self.n_instr += 1

    def op(self, engname, fn, reads=(), writes=()):
        e = self.engs[engname]
        if e.count >= self.SEM_WRAP:
            e.sem = self.es.enter_context(self.nc.semaphore("s_%s_%d" % (engname, self.n_instr)))
            e.count = 0
        waits = self._waits(e, reads, writes)
        e.count += 1
        tok = (e.sem, e.count)
        self._record(e, fn, waits, e.sem, 1, reads, writes, tok)
        return tok

    def dma(self, engname, fn, sembuf, reads=(), writes=()):
        e = self.engs[engname]
        waits = self._waits(e, reads, writes)
        if sembuf.dsem is None:
            if self.sem_pool:
                sembuf.dsem, sembuf.dcount = self.sem_pool.pop()
            else:
                sembuf.dsem = self.es.enter_context(self.nc.semaphore("d%d_%s" % (self.n_instr, sembuf.name)))
        if sembuf not in self.dbufs:
            self.dbufs.append(sembuf)
        sembuf.dcount += 16
        tok = (sembuf.dsem, sembuf.dcount)
        self._record(e, fn, waits, sembuf.dsem, 16, reads, writes, tok)
        return tok

    def barrier(self):
        toks = [(e.sem, e.count) for e in self.engs.values() if e.count > 0]
        toks += [(b.dsem, b.dcount) for b in self.dbufs]
        for name in self.engs:
            self.wait_tokens(name, toks)
        for b in self.dbufs:
            self.sem_pool.append((b.dsem, b.dcount))
            b.dsem = None
        self.dbufs = []

    def wait_tokens(self, engname, toks):
        e = self.engs[engname]
        for sem, val in toks:
            k = id(sem)
            if e.waited.get(k, 0) >= val:
                continue
            e.waited[k] = val
            e.prog.append(lambda eng, s=sem, v=val: eng.wait_ge(s, v))

    def emit(self):
        with self.nc.Block() as block:
            for name in ("sync", "scalar", "vector", "gpsimd", "tensor"):
                def body(eng, name=name):
                    for f in self.engs[name].prog:
                        f(eng)
                getattr(block, name)(body)
        for e in self.engs.values():
            e.prog = []

    def alt(self):
        self.rr ^= 1
        return "vector" if self.rr else "gpsimd"


PHASE_LOG = []


class Phase:
    def __init__(self, S, name):
        self.S = S
        self.nc = S.nc
        self.name = name
        self.es = ExitStack()
        self.k = 0

    def __enter__(self):
        self.es.__enter__()
        return self

    def sb(self, name, shape, dt):
        self.k += 1
        nm = "%s_%s_%d" % (self.name, name, self.k)
        return Buf(nm, self.es.enter_context(self.nc.sbuf_tensor(nm, list(shape), dt)))

    def ps(self, name, shape, dt):
        self.k += 1
        nm = "%s_%s_%d" % (self.name, name, self.k)
        return Buf(nm, self.es.enter_context(self.nc.psum_tensor(nm, list(shape), dt)))

    def __exit__(self, *a):
        if a[0] is None:
            PHASE_LOG.append((self.name, {k: (v.count, id(v.sem)) for k, v in self.S.engs.items()}, self.S.n_instr))
            self.S.barrier()
            self.S.emit()
        return self.es.__exit__(*a)


C_IDENT, C_TRI, C_ONES, C_MASKNEG, C_STRICT, C_BLK64, C_DELTA, C_EPS, C_ONE = 0, 128, 256, 384, 512, 640, 768, 769, 770
C_BM16, C_OFF32, C_OFF64, C_OFF128 = 771, 899, 1027, 1155
NCONST = 1283

PP_CONVW = 0
PP_DTB = 48
PP_ALOG = 52
PP_ONORM = 56
PP_QN = 184
PP_KN = 185
PP_SUBLN = 186
PP_FCW = 187
PP_FCB = 319
PP_NMIX = 363
PP_NFFN = 371
PP_BADA = 379
NPP = 380


def _consts():
    c = np.zeros((128, NCONST), np.float32)
    i = np.arange(128)
    c[:, C_IDENT:C_IDENT + 128] = np.eye(128)
    c[:, C_TRI:C_TRI + 128] = (i[:, None] <= i[None, :])
    c[:, C_ONES:C_ONES + 128] = 1.0
    c[:, C_MASKNEG:C_MASKNEG + 128] = np.where(i[:, None] <= i[None, :], 0.0, NEG)
    c[:, C_STRICT:C_STRICT + 128] = (i[:, None] < i[None, :])
    c[:, C_BLK64:C_BLK64 + 128] = ((i[:, None] // 64) == (i[None, :] // 64))
    c[0, C_DELTA] = 1.0
    bm = lambda m: ((i[:, None] // m) == (i[None, :] // m)).astype(np.float32)
    c[:, C_BM16:C_BM16 + 128] = bm(16)
    c[:, C_OFF32:C_OFF32 + 128] = bm(32) - bm(16)
    c[:, C_OFF64:C_OFF64 + 128] = bm(64) - bm(32)
    c[:, C_OFF128:C_OFF128 + 128] = 1.0 - bm(64)
    c[:, C_EPS] = EPS
    c[:, C_ONE] = 1.0
    return c


def _t5_bucket(n):
    n = np.asarray(n)
    nf = np.maximum(n, 1).astype(np.float32)
    large = 16 + (np.log(nf / np.float32(16)) / np.float32(math.log(128 / 16)) * np.float32(16)).astype(np.int32)
    large = np.minimum(large, 31)
    return np.where(n < 16, n, large)


def _pack_pp(inp):
    pp = np.zeros((DEPTH, 128, NPP), np.float32)
    p = np.arange(128)
    for l in range(DEPTH):
        cw = inp["gdn_conv_w"][l]
        pp[l, :, PP_CONVW:PP_CONVW + 48] = cw.reshape(4, 12, 128).transpose(2, 1, 0).reshape(128, 48)
        pp[l, :, PP_DTB:PP_DTB + 4] = inp["gdn_dt_bias"][l][None, :]
        pp[l, :, PP_ALOG:PP_ALOG + 4] = inp["gdn_a_log"][l][None, :]
        pp[l, :, PP_ONORM:PP_ONORM + 128] = inp["gdn_out_norm"][l][None, :]
        pp[l, :, PP_QN] = inp["diff_q_norm"][l][p % 64]
        pp[l, :, PP_KN] = inp["diff_k_norm"][l][p % 64]
        pp[l, :, PP_SUBLN] = inp["diff_subln"][l]
        fw = inp["ffn_conv_w"][l]
        pp[l, :, PP_FCW:PP_FCW + 132] = fw.reshape(3, 44, 128).transpose(2, 1, 0).reshape(128, 132)
        pp[l, :, PP_FCB:PP_FCB + 44] = inp["ffn_conv_b"][l].reshape(44, 128).T
        pp[l, :, PP_NMIX:PP_NMIX + 8] = inp["norm_mix"][l].reshape(8, 128).T
        pp[l, :, PP_NFFN:PP_NFFN + 8] = inp["norm_ffn"][l].reshape(8, 128).T
    return pp


def build(nlayers=DEPTH, upto="G", debug=False):
    nc = bass.Bass("TRN2", target_bir_lowering=False)
    dk = "ExternalOutput" if debug else "Internal"

    def din(name, shape, dt=F32):
        return nc.dram_tensor(name, list(shape), dt, kind="ExternalInput").ap()

    def dscr(name, shape, dt, dbg=True):
        return nc.dram_tensor(name, list(shape), dt, kind=(dk if dbg else "Internal")).ap()

    x_in = din("x", [2, T, DM])
    cT_in = din("cT", [128, 8, 2])
    consts_in = din("consts", [128, NCONST])
    pp_in = din("pp", [DEPTH, 128, NPP])
    lam_in = din("lamv", [DEPTH, 4, 64])
    tb_in = din("tb", [4, 128, 1024])
    b31_in = din("b31", [128, 4])
    w_ada = din("w_ada", [DEPTH, DM, 6 * DM])
    b_ada = din("b_ada", [DEPTH, 6 * DM])
    w_in = din("w_in", [DEPTH, DM, 3592])
    w_out = din("w_out", [DEPTH, DM, DM])
    ffn_up = din("ffn_up", [DEPTH, DM, 2 * DFF])
    ffn_down = din("ffn_down", [DEPTH, DFF, DM])
    out = nc.dram_tensor("out", [2, T, DM], F32, kind="ExternalOutput").ap()

    MOD = dscr("MOD", [DEPTH, 2, 6 * DM], F32)
    XA = dscr("XA", [2, T, DM], F32)
    XB = dscr("XB", [2, T, DM], F32, dbg=False)
    NCH = T // 128
    KT = dscr("KT", [2, NCH, 128, 4, 128], BF16)
    QGT = dscr("QGT", [2, NCH, 128, 4, 128], BF16)
    QT = dscr("QT", [2, NCH, 128, 4, 128], BF16)
    KBT = dscr("KBT", [2, NCH, 128, 4, 128], BF16)
    KBG = dscr("KBG", [2, NCH, 128, 4, 128], BF16)
    KDEC = dscr("KDEC", [2, NCH, 128, 4, 128], BF16)
    VB = dscr("VB", [2, NCH, 128, 4, 128], BF16)
    DEC = dscr("DEC", [2, NCH, 128, 4, 128], F32)
    EGL = dscr("EGL", [2, NCH, 128, 4], F32)
    GATE = dscr("GATE", [2, NCH, 128, 512], BF16)
    DQT = dscr("DQT", [2, 4, 128, T], BF16)
    DKT = dscr("DKT", [2, 4, 128, T], BF16)
    DV = dscr("DV", [2, NCH, 128, 512], BF16)
    OT = dscr("OT", [2, 8, 128, T], BF16)
    GTD = dscr("GTD", [2, 22, 128, T], BF16, dbg=False)

    DBG = {}
    if debug:
        for nm in ("Y", "X", "Q", "U", "T1", "X1", "Y1b", "S0", "U1", "PW1", "O1", "VN0"):
            DBG[nm] = dscr("DBG_" + nm, [128, 4, 128], F32)

    with ExitStack() as es0:
        S = Sched(nc, es0)
        G = Phase(S, "glob")
        G.__enter__()
        cst = G.sb("cst", [128, NCONST], F32)
        identB = G.sb("identB", [128, 128], BF16)
        onesB = G.sb("onesB", [128, 128], BF16)
        S.dma("sync", lambda e: e.dma_start(out=cst[:], in_=consts_in[:, :]), cst, writes=[cst])
        S.op("vector", lambda e: e.tensor_copy(identB[:], cst[:, C_IDENT:C_IDENT + 128]), reads=[cst], writes=[identB])
        S.op("vector", lambda e: e.tensor_copy(onesB[:], cst[:, C_ONES:C_ONES + 128]), reads=[cst], writes=[onesB])
        identF = lambda: cst[:, C_IDENT:C_IDENT + 128]
        TRI = lambda: cst[:, C_TRI:C_TRI + 128]
        ONESF = lambda: cst[:, C_ONES:C_ONES + 128]
        epsc = lambda: cst[:, C_EPS:C_EPS + 1]
        onec = lambda: cst[:, C_ONE:C_ONE + 1]

        with Phase(S, "A") as P:
            cT = P.sb("cT", [128, 8, 2], F32)
            S.dma("sync", lambda e: e.dma_start(out=cT[:], in_=cT_in[:, :, :]), cT, writes=[cT])
            S.op("scalar", lambda e: e.activation(cT[:], cT[:], AF.Silu), reads=[cT], writes=[cT])
            wts = [P.sb("wa%d" % i, [128, 8, 512], F32) for i in range(3)]
            pM = [P.ps("pM%d" % i, [2, 512], F32) for i in range(2)]
            k = 0
            bada = P.sb("bada", [2, 6 * DM], F32)
            modt = P.sb("modt", [2, 6 * DM], F32)
            for l in range(nlayers):
                S.dma("sync", lambda e, l=l, bada=bada: e.dma_start(out=bada[:], in_=b_ada[l, :].partition_broadcast(2)), bada, writes=[bada])
                for fb in range(12):
                    wt = wts[k % 3]
                    pm = pM[k % 2]
                    k += 1
                    S.dma("sync", lambda e, l=l, fb=fb, wt=wt: e.dma_start(
                        out=wt[:], in_=w_ada[l, :, fb * 512:(fb + 1) * 512].rearrange("(c p) f -> p c f", p=128)), wt, writes=[wt])
                    for kc in range(8):
                        S.op("tensor", lambda e, kc=kc, wt=wt, pm=pm: e.matmul(pm[:], lhsT=cT[:, kc, :], rhs=wt[:, kc, :],
                                                                              start=(kc == 0), stop=(kc == 7)),
                             reads=[cT, wt], writes=[pm])
                    S.op("vector", lambda e, fb=fb, pm=pm, modt=modt, bada=bada: e.tensor_tensor(
                        modt[:, fb * 512:(fb + 1) * 512], pm[:], bada[:, fb * 512:(fb + 1) * 512], op=ALU.add),
                        reads=[pm, bada], writes=[modt])
                S.dma("sync", lambda e, l=l, modt=modt: e.dma_start(out=MOD[l, :, :], in_=modt[:]), modt, reads=[modt])

        def load_mod_cols(P, l, b, seg, name):
            t = P.sb(name, [128, 8], F32)
            S.dma("sync", lambda e: e.dma_start(out=t[:], in_=MOD[l, b, seg * DM:(seg + 1) * DM].rearrange("(c p) -> p c", p=128),
                                                allow_slow_non_contiguous=True), t, writes=[t])
            return t

        def make_AB(P, l, b, pp, ncol, seg_sh, seg_sc, name):
            sh = load_mod_cols(P, l, b, seg_sh, name + "sh")
            sc = load_mod_cols(P, l, b, seg_sc, name + "sc")
            A = P.sb(name + "A", [128, 8], F32)
            S.op("vector", lambda e: e.scalar_tensor_tensor(A[:], in0=sc[:], scalar=1.0, in1=pp[:, ncol:ncol + 8],
                                                            op0=ALU.add, op1=ALU.mult), reads=[sc, pp], writes=[A])
            return A, sh

        def norm_to_hT(P, xt, nsub, A, Bsh, hT, tmps, pT, W):
            ss, rs, sq, xn, tmpf = tmps
            for sub in range(nsub):
                S.op("scalar", lambda e, sub=sub: e.activation(sq[:], xt[:, sub, :], AF.Square, scale=1.0 / 32,
                                                                accum_out=ss[:, sub:sub + 1]), reads=[xt], writes=[sq, ss])
            S.op("scalar", lambda e: e.activation(rs[:, 0:nsub], ss[:, 0:nsub], AF.Ln, bias=epsc()), reads=[ss, cst], writes=[rs])
            S.op("scalar", lambda e: e.activation(rs[:, 0:nsub], rs[:, 0:nsub], AF.Exp, scale=-0.5), reads=[rs], writes=[rs])
            for sub in range(nsub):
                S.op("vector", lambda e, sub=sub: e.tensor_scalar_mul(xn[:, sub, :], xt[:, sub, :], rs[:, sub:sub + 1]),
                     reads=[xt, rs], writes=[xn])
                for c in range(8):
                    S.op("tensor", lambda e, sub=sub, c=c: e.transpose(pT[:, c, :], xn[:, sub, c * 128:(c + 1) * 128], identB[:]),
                         reads=[xn, identB], writes=[pT])
                S.op("vector", lambda e: e.tensor_tensor(tmpf[:], pT[:], A[:, :, None].to_broadcast([128, 8, 128]), op=ALU.mult),
                     reads=[pT, A], writes=[tmpf])
                S.op("gpsimd", lambda e, sub=sub: e.tensor_tensor(hT[:, :, sub * 128:(sub + 1) * 128], tmpf[:],
                                                                   Bsh[:, :, None].to_broadcast([128, 8, 128]), op=ALU.add),
                     reads=[tmpf, Bsh], writes=[hT])

        def load_w_bf16(P, name, src_ap, kc, ncols):
            t = P.sb(name, [128, kc, ncols], BF16)
            step = max(1, 4096 // ncols)
            for c0 in range(0, kc, step):
                c1 = min(kc, c0 + step)
                S.dma("gpsimd", lambda e, c0=c0, c1=c1: e.dma_start(
                    out=t[:, c0:c1, :], in_=src_ap[c0 * 128:c1 * 128, :].rearrange("(c p) f -> p c f", p=128)), t, writes=[t])
            return t

        xsrc = x_in
        for l in range(nlayers):
            last = (l == nlayers - 1)
            xdst = out if last else XB
            with Phase(S, "C%d" % l) as P:
                pp = P.sb("pp", [128, NPP], F32)
                S.dma("sync", lambda e, l=l: e.dma_start(out=pp[:], in_=pp_in[l, :, :]), pp, writes=[pp])
                Wqkv = load_w_bf16(P, "Wqkv", w_in[l, :, 0:1536], 8, 1536)
                Wgate = load_w_bf16(P, "Wgate", w_in[l, :, 1536:2048], 8, 512)
                Wba = load_w_bf16(P, "Wba", w_in[l, :, 2048:2056], 8, 8)
                Wdqk = load_w_bf16(P, "Wdqk", w_in[l, :, 2056:3080], 8, 1024)
                Wdv = load_w_bf16(P, "Wdv", w_in[l, :, 3080:3592], 8, 512)
                negA = P.sb("negA", [128, 4], F32)
                S.op("scalar", lambda e: e.activation(negA[:], pp[:, PP_ALOG:PP_ALOG + 4], AF.Exp), reads=[pp], writes=[negA])
                S.op("vector", lambda e: e.tensor_scalar_mul(negA[:], negA[:], -1.0), reads=[negA], writes=[negA])
                qgain = P.sb("qgain", [128, 1], F32)
                S.op("vector", lambda e: e.tensor_scalar_mul(qgain[:], pp[:, PP_QN:PP_QN + 1], 0.125), reads=[pp], writes=[qgain])
                xts = [P.sb("xt%d" % i, [128, 4, DM], F32) for i in range(2)]
                ss = P.sb("ss", [128, 4], F32); rs = P.sb("rs", [128, 4], F32)
                sq = P.sb("sq", [128, DM], BF16); xn = P.sb("xn", [128, 4, DM], BF16)
                tmpf = P.sb("tmpf", [128, 8, 128], F32)
                hT = P.sb("hT", [128, 8, 512], BF16)
                pT = P.ps("pT", [128, 8, 128], BF16)
                pU = [P.ps("pU%d" % i, [128, 512], F32) for i in range(2)]
                pS = P.ps("pS", [128, 512], F32)
                pSm = P.ps("pSm", [128, 512], F32)
                pD = P.ps("pD", [128, 512], F32)
                pSmv = lambda: pSm[:].rearrange("p (a b) -> p a b", b=16)
                pDv = lambda: pD[:].rearrange("p (h c) -> p h c", c=128)
                pTr = P.ps("pTr", [128, 8, 128], BF16)
                pTok = P.ps("pTok", [128, 512], F32)
                PUS = [pU[0], pU[1], pSm, pD, pTok]
                upad = [P.sb("upad%d" % i, [128, 515], F32) for i in range(2)]
                acc = [P.sb("acc%d" % i, [128, 512], F32) for i in range(2)]
                act = [P.sb("act%d" % i, [128, 512], F32) for i in range(4)]
                sqq = [P.sb("sqq%d" % i, [128, 512], F32) for i in range(4)]
                rinv = [P.sb("rinv%d" % i, [128, 512], F32) for i in range(2)]
                halo = P.sb("halo", [128, 12, 3], F32)
                kT = P.sb("kT", [128, 4, 4, 128], BF16)
                qT = P.sb("qT", [128, 4, 4, 128], BF16)
                vT = P.sb("vT", [128, 4, 4, 128], BF16)
                kbT = P.sb("kbT", [128, 4, 4, 128], BF16)
                qgT = P.sb("qgT", [128, 4, 4, 128], BF16)
                kbg = P.sb("kbg", [128, 4, 4, 128], BF16)
                kdec = P.sb("kdec", [128, 4, 4, 128], BF16)
                kb = P.sb("kb", [128, 4, 128], BF16)
                qg = P.sb("qg", [128, 4, 128], BF16)
                vb = P.sb("vb", [128, 4, 4, 128], BF16)
                dec = P.sb("dec", [128, 4, 4, 128], F32)
                gatet = P.sb("gatet", [128, 4, 512], BF16)
                dvt = P.sb("dvt", [128, 4, 512], BF16)
                dqkT = P.sb("dqkT", [128, 8, 512], BF16)
                braw = P.sb("braw", [128, 4, 8], F32)
                beta = P.sb("beta", [128, 4, 4], F32)
                gg = P.sb("gg", [128, 4, 4], F32)
                gcl = P.sb("gcl", [128, 4, 8], F32)
                ngc = P.sb("ngc", [128, 4, 4], F32)
                egc = P.sb("egc", [128, 4, 4], F32)
                bgs = P.sb("bgs", [128, 4, 4], F32)
                qsc = P.sb("qsc", [128, 4, 4], F32)
                kds = P.sb("kds", [128, 4, 4], F32)
                egl = P.sb("egl", [128, 4, 4], F32)
                gbc = P.sb("gbc", [128, 4, 128], F32)

                blocks = [(s, blk) for s in range(2) for blk in range(8)]

                def load_x(i):
                    s, blk = blocks[i]
                    xt = xts[i % 2]
                    S.dma("sync", lambda e: e.dma_start(
                        out=xt[:], in_=xsrc[s, blk * 512:(blk + 1) * 512, :].rearrange("(a p) d -> p a d", p=128)), xt, writes=[xt])

                load_x(0)
                AB = {}
                for b in range(2):
                    AB[b] = make_AB(P, l, b, pp, PP_NMIX, 0, 1, "mix%d" % b)
                cc = 0
                for i, (s, blk) in enumerate(blocks):
                    xt = xts[i % 2]
                    if i + 1 < len(blocks):
                        load_x(i + 1)
                    A, Bsh = AB[s]
                    if blk == 0:
                        S.op("gpsimd", lambda e: e.memset(halo[:], 0.0), writes=[halo])
                    norm_to_hT(P, xt, 4, A, Bsh, hT, (ss, rs, sq, xn, tmpf), pT, None)

                    for sub in range(4):
                        for kc in range(8):
                            S.op("tensor", lambda e, sub=sub, kc=kc: e.matmul(pSmv()[:, sub, 0:8], lhsT=hT[:, kc, sub * 128:(sub + 1) * 128],
                                                                            rhs=Wba[:, kc, :], start=(kc == 0), stop=(kc == 7)),
                                 reads=[hT, Wba], writes=[pSm])
                    S.op("vector", lambda e: e.tensor_copy(braw[:], pSmv()[:, 0:4, 0:8]), reads=[pSm], writes=[braw])
                    S.op("scalar", lambda e: e.activation(beta[:], braw[:, :, 0:4], AF.Sigmoid), reads=[braw], writes=[beta])
                    S.op("vector", lambda e: e.tensor_tensor(gg[:], braw[:, :, 4:8], pp[:, None, PP_DTB:PP_DTB + 4].to_broadcast([128, 4, 4]),
                                                             op=ALU.add), reads=[braw, pp], writes=[gg])
                    S.op("scalar", lambda e: e.activation(gg[:], gg[:], AF.Exp), reads=[gg], writes=[gg])
                    S.op("scalar", lambda e: e.activation(gg[:], gg[:], AF.Ln, bias=onec()), reads=[gg, cst], writes=[gg])
                    S.op("vector", lambda e: e.tensor_tensor(gg[:], gg[:], negA[:, None, :].to_broadcast([128, 4, 4]), op=ALU.mult),
                         reads=[gg, negA], writes=[gg])
                    pending = []
                    for fc in range(12):
                        pu = PUS[cc % 5]; up = upad[cc % 2]; ac = acc[cc % 2]; at = act[cc % 4]
                        sqt = sqq[cc % 4]; rv = rinv[cc % 2]
                        cc += 1
                        for kc in range(8):
                            S.op("tensor", lambda e, kc=kc, fc=fc, pu=pu: e.matmul(pu[:], lhsT=Wqkv[:, kc, fc * 128:(fc + 1) * 128],
                                                                                    rhs=hT[:, kc, :], start=(kc == 0), stop=(kc == 7)),
                                 reads=[Wqkv, hT], writes=[pu])
                        while len(pending) > 2:
                            pending.pop(0)()
                        S.op("vector", lambda e, fc=fc, up=up: e.tensor_copy(up[:, 0:3], halo[:, fc, :]), reads=[halo], writes=[up])
                        S.op("scalar", lambda e, up=up, pu=pu: e.copy(up[:, 3:515], pu[:]), reads=[pu], writes=[up])
                        S.op("gpsimd", lambda e, fc=fc, up=up: e.tensor_copy(halo[:, fc, :], up[:, 512:515]), reads=[up], writes=[halo])
                        cw = lambda j, fc=fc: pp[:, PP_CONVW + fc * 4 + j:PP_CONVW + fc * 4 + j + 1]
                        S.op("vector", lambda e, up=up, ac=ac, cw=cw: e.tensor_scalar_mul(ac[:], up[:, 3:515], cw(3)), reads=[up, pp], writes=[ac])
                        for j in (2, 1, 0):
                            S.op("vector",
                                 lambda e, up=up, ac=ac, cw=cw, j=j: e.scalar_tensor_tensor(ac[:], in0=up[:, j:j + 512], scalar=cw(j), in1=ac[:],
                                                                                           op0=ALU.mult, op1=ALU.add),
                                 reads=[up, pp, ac], writes=[ac])
                        kind, h = fc // 4, fc % 4
                        if kind == 2:
                            S.op("scalar", lambda e, ac=ac, h=h: e.activation(vT[:, :, h, :], ac[:].rearrange("p (a t) -> p a t", a=4), AF.Silu),
                                 reads=[ac], writes=[vT])
                        else:
                            dst = qT if kind == 0 else kT
                            S.op("scalar", lambda e, ac=ac, at=at: e.activation(at[:], ac[:], AF.Silu), reads=[ac], writes=[at])
                            S.op("gpsimd", lambda e, at=at, sqt=sqt: e.tensor_tensor(sqt[:], at[:], at[:], op=ALU.mult), reads=[at], writes=[sqt])

                            def fin(sqt=sqt, rv=rv, at=at, dst=dst, h=h):
                                S.op("tensor", lambda e: e.matmul(pS[:], lhsT=ONESF(), rhs=sqt[:], start=True, stop=True),
                                     reads=[cst, sqt], writes=[pS])
                                S.op("scalar", lambda e: e.activation(rv[:], pS[:], AF.Ln, bias=epsc()), reads=[pS, cst], writes=[rv])
                                S.op("scalar", lambda e: e.activation(rv[:], rv[:], AF.Exp, scale=-0.5), reads=[rv], writes=[rv])
                                S.op("vector", lambda e: e.tensor_tensor(
                                    dst[:, :, h, :], at[:].rearrange("p (a t) -> p a t", a=4), rv[:].rearrange("p (a t) -> p a t", a=4), op=ALU.mult),
                                    reads=[at, rv], writes=[dst])
                            pending.append(fin)

                    while pending:
                        pending.pop(0)()
                    for sub in range(4):
                        S.op("tensor", lambda e, sub=sub: e.matmul(pSmv()[:, sub, 8:12], lhsT=TRI(), rhs=gg[:, sub, :], start=True, stop=True),
                             reads=[cst, gg], writes=[pSm])
                        S.op("tensor", lambda e, sub=sub: e.matmul(pSmv()[:, sub, 12:16], lhsT=ONESF(), rhs=gg[:, sub, :], start=True, stop=True),
                             reads=[cst, gg], writes=[pSm])
                    S.op("vector", lambda e: e.tensor_copy(gcl[:], pSmv()[:, 0:4, 8:16]), reads=[pSm], writes=[gcl])
                    S.op("vector", lambda e: e.tensor_scalar_mul(ngc[:], gcl[:, :, 0:4], -1.0), reads=[gcl], writes=[ngc])
                    S.op("scalar", lambda e: e.activation(egc[:], gcl[:, :, 0:4], AF.Exp), reads=[gcl], writes=[egc])
                    S.op("scalar", lambda e: e.activation(egl[:], gcl[:, :, 4:8], AF.Exp), reads=[gcl], writes=[egl])
                    S.op("vector", lambda e: e.tensor_tensor(kds[:], gcl[:, :, 4:8], gcl[:, :, 0:4], op=ALU.subtract), reads=[gcl], writes=[kds])
                    S.op("scalar", lambda e: e.activation(kds[:], kds[:], AF.Exp), reads=[kds], writes=[kds])
                    S.op("vector", lambda e: e.tensor_tensor(bgs[:], beta[:], egc[:], op=ALU.mult), reads=[beta, egc], writes=[bgs])
                    S.op("vector", lambda e: e.tensor_scalar_mul(qsc[:], egc[:], 128.0 ** -0.5), reads=[egc], writes=[qsc])
                    S.dma("sync", lambda e, s=s, blk=blk: e.dma_start(out=EGL[s, blk * 4:(blk + 1) * 4, :, :].rearrange("n p h -> p n h"),
                                                                       in_=egl[:]), egl, reads=[egl])
                    for sub in range(4):
                        for h in range(4):
                            S.op("vector", lambda e, sub=sub, h=h: e.tensor_copy(gbc[:, h, :], gg[:, sub, h:h + 1].to_broadcast([128, 128])),
                                 reads=[gg], writes=[gbc])
                        for h in range(4):
                            S.op("tensor", lambda e, h=h: e.matmul(pDv()[:, h, :], lhsT=gbc[:, h, :], rhs=TRI(), start=True, stop=False),
                                 reads=[gbc, cst], writes=[pD])
                            S.op("tensor", lambda e, h=h: e.matmul(pDv()[:, h, :], lhsT=identF(), rhs=cst[:, C_MASKNEG:C_MASKNEG + 128],
                                                                   start=False, stop=True), reads=[cst], writes=[pD])
                        for h in range(4):
                            S.op("scalar", lambda e, sub=sub, h=h: e.activation(dec[:, sub, h, :], pDv()[:, h, :], AF.Exp,
                                                                                  bias=ngc[:, sub, h:h + 1]),
                                 reads=[pD, ngc], writes=[dec])
                    S.dma("sync", lambda e, s=s, blk=blk: e.dma_start(
                        out=DEC[s, blk * 4:(blk + 1) * 4, :, :, :].rearrange("n p h c -> p n h c"), in_=dec[:]), dec, reads=[dec])

                    for fc in range(8):
                        pu = PUS[cc % 5]; sqt = sqq[cc % 4]; rv = rinv[cc % 2]
                        cc += 1
                        for kc in range(8):
                            S.op("tensor", lambda e, kc=kc, fc=fc, pu=pu: e.matmul(pu[:], lhsT=Wdqk[:, kc, fc * 128:(fc + 1) * 128],
                                                                                    rhs=hT[:, kc, :], start=(kc == 0), stop=(kc == 7)),
                                 reads=[Wdqk, hT], writes=[pu])
                        while len(pending) > 2:
                            pending.pop(0)()
                        S.op("scalar", lambda e, pu=pu, sqt=sqt: e.activation(sqt[:], pu[:], AF.Square), reads=[pu], writes=[sqt])
                        gain = qgain[:, 0:1] if fc < 4 else pp[:, PP_KN:PP_KN + 1]

                        def fin2(pu=pu, sqt=sqt, rv=rv, fc=fc, gain=gain):
                            S.op("tensor", lambda e: e.matmul(pS[:], lhsT=cst[:, C_BLK64:C_BLK64 + 128], rhs=sqt[:], start=True, stop=True),
                                 reads=[cst, sqt], writes=[pS])
                            S.op("scalar", lambda e: e.activation(rv[:], pS[:], AF.Ln, bias=epsc(), scale=1.0 / 64), reads=[pS, cst], writes=[rv])
                            S.op("scalar", lambda e: e.activation(rv[:], rv[:], AF.Exp, scale=-0.5), reads=[rv], writes=[rv])
                            S.op("vector", lambda e: e.scalar_tensor_tensor(
                                dqkT[:, fc, :], in0=pu[:], scalar=gain, in1=rv[:], op0=ALU.mult, op1=ALU.mult),
                                reads=[pu, rv, qgain, pp], writes=[dqkT])
                        pending.append(fin2)
                    while pending:
                        pending.pop(0)()
                    S.dma("sync", lambda e, s=s, blk=blk: e.dma_start(out=DQT[s, :, :, blk * 512:(blk + 1) * 512].rearrange("h p t -> p h t"),
                                                                       in_=dqkT[:, 0:4, :]), dqkT, reads=[dqkT])
                    S.dma("sync", lambda e, s=s, blk=blk: e.dma_start(out=DKT[s, :, :, blk * 512:(blk + 1) * 512].rearrange("h p t -> p h t"),
                                                                       in_=dqkT[:, 4:8, :]), dqkT, reads=[dqkT])

                    for W, dstt, fn in ((Wgate, gatet, AF.Silu), (Wdv, dvt, AF.Copy)):
                        for sub in range(4):
                            for kc in range(8):
                                S.op("tensor", lambda e, sub=sub, kc=kc, W=W: e.matmul(pTok[:], lhsT=hT[:, kc, sub * 128:(sub + 1) * 128],
                                                                                        rhs=W[:, kc, :], start=(kc == 0), stop=(kc == 7)),
                                     reads=[hT, W], writes=[pTok])
                            S.op("scalar", lambda e, sub=sub, dstt=dstt, fn=fn: e.activation(dstt[:, sub, :], pTok[:], fn),
                                 reads=[pTok], writes=[dstt])
                        dd = GATE if dstt is gatet else DV
                        S.dma("sync", lambda e, dd=dd, dstt=dstt, s=s, blk=blk: e.dma_start(
                            out=dd[s, blk * 4:(blk + 1) * 4, :, :].rearrange("n p f -> p n f"), in_=dstt[:]), dstt, reads=[dstt])
                    for sub in range(4):
                        for h in range(4):
                            S.op("tensor", lambda e, sub=sub, h=h: e.transpose(pTr[:, h, :], kT[:, sub, h, :], identB[:]),
                                 reads=[kT, identB], writes=[pTr])
                        S.op("vector", lambda e, sub=sub: e.tensor_tensor(kbg[:, sub, :, :], pTr[:, 0:4, :], bgs[:, sub, :, None].to_broadcast([128, 4, 128]),
                                                                          op=ALU.mult), reads=[pTr, bgs], writes=[kbg])
                        S.op("vector", lambda e, sub=sub: e.tensor_tensor(kdec[:, sub, :, :], pTr[:, 0:4, :], kds[:, sub, :, None].to_broadcast([128, 4, 128]),
                                                                          op=ALU.mult), reads=[pTr, kds], writes=[kdec])
                        S.op("vector", lambda e, sub=sub: e.tensor_tensor(kb[:], pTr[:, 0:4, :], beta[:, sub, :, None].to_broadcast([128, 4, 128]),
                                                                          op=ALU.mult), reads=[pTr, beta], writes=[kb])
                        for h in range(4):
                            S.op("tensor", lambda e, h=h: e.transpose(pTr[:, h, :], kb[:, h, :], identB[:]), reads=[kb, identB], writes=[pTr])
                        S.op("scalar", lambda e, sub=sub: e.copy(kbT[:, sub, :, :], pTr[:, 0:4, :]), reads=[pTr], writes=[kbT])
                        for h in range(4):
                            S.op("tensor", lambda e, sub=sub, h=h: e.transpose(pTr[:, h, :], qT[:, sub, h, :], identB[:]),
                                 reads=[qT, identB], writes=[pTr])
                        S.op("vector", lambda e, sub=sub: e.tensor_tensor(qg[:], pTr[:, 0:4, :], qsc[:, sub, :, None].to_broadcast([128, 4, 128]),
                                                                          op=ALU.mult), reads=[pTr, qsc], writes=[qg])
                        for h in range(4):
                            S.op("tensor", lambda e, h=h: e.transpose(pTr[:, h, :], qg[:, h, :], identB[:]), reads=[qg, identB], writes=[pTr])
                        S.op("scalar", lambda e, sub=sub: e.copy(qgT[:, sub, :, :], pTr[:, 0:4, :]), reads=[pTr], writes=[qgT])
                        for h in range(4):
                            S.op("tensor", lambda e, sub=sub, h=h: e.transpose(pTr[:, h, :], vT[:, sub, h, :], identB[:]),
                                 reads=[vT, identB], writes=[pTr])
                        S.op("vector", lambda e, sub=sub: e.tensor_tensor(vb[:, sub, :, :], pTr[:, 0:4, :], beta[:, sub, :, None].to_broadcast([128, 4, 128]),
                                                                          op=ALU.mult), reads=[pTr, beta], writes=[vb])
                    for dst, src in ((KT, kT), (QT, qT), (QGT, qgT), (KBT, kbT), (KBG, kbg), (KDEC, kdec), (VB, vb)):
                        S.dma("sync", lambda e, dst=dst, src=src, s=s, blk=blk: e.dma_start(
                            out=dst[s, blk * 4:(blk + 1) * 4, :, :, :].rearrange("n p h t -> p n h t"), in_=src[:]), src, reads=[src])

            if upto == "C":
                break
            with Phase(S, "D%d" % l) as P:
                pp = P.sb("pp", [128, NPP], F32)
                S.dma("sync", lambda e, l=l: e.dma_start(out=pp[:], in_=pp_in[l, :, :]), pp, writes=[pp])
                NSLOT = 3
                names = ("kT", "qT", "qgT", "kbT", "kbg", "kdec", "vb")
                srcs = dict(kT=KT, qT=QT, qgT=QGT, kbT=KBT, kbg=KBG, kdec=KDEC, vb=VB)
                slots = {}
                for s in range(2):
                    for k in range(NSLOT):
                        d_ = {nm: P.sb("%s_%d_%d" % (nm, s, k), [128, 4, 128], BF16) for nm in names}
                        d_["dec"] = P.sb("dec_%d_%d" % (s, k), [128, 4, 128], F32)
                        d_["egl"] = P.sb("egl_%d_%d" % (s, k), [128, 4], F32)
                        d_["gate"] = P.sb("gate_%d_%d" % (s, k), [128, 4, 128], BF16)
                        slots[(s, k)] = d_
                strict = cst[:, None, C_STRICT:C_STRICT + 128].to_broadcast([128, 4, 128])
                identq = cst[:, None, C_IDENT:C_IDENT + 128].to_broadcast([128, 4, 128])
                onormb = pp[:, None, PP_ONORM:PP_ONORM + 128].to_broadcast([128, 4, 128])
                pre = [P.ps("pre%d" % i, [128, 4, 128], F32) for i in range(2)]
                pinv = [P.ps("pinv%d" % i, [128, 4, 128], F32) for i in range(2)]
                pW = P.ps("pW", [128, 4, 128], F32)
                pO = P.ps("pO", [128, 4, 128], F32)
                pSt = P.ps("pSt", [128, 4, 128], F32)
                pTr = P.ps("pTr", [128, 8, 128], BF16)
                st = {}
                for s in range(2):
                    st[s] = dict(
                        Y=[P.sb("Y%d_%d" % (s, i), [128, 4, 128], F32) for i in range(2)],
                        X=[P.sb("X%d_%d" % (s, i), [128, 4, 128], F32) for i in range(2)],
                        Q=P.sb("Q%d" % s, [128, 4, 128], F32), Qb=P.sb("Qb%d" % s, [128, 4, 128], BF16),
                        P=P.sb("Pm%d" % s, [128, 4, 128], F32), Yf=P.sb("Yf%d" % s, [128, 4, 128], F32), Xf=P.sb("Xf%d" % s, [128, 4, 128], F32),
                        t1=P.sb("t1%d" % s, [128, 4, 128], F32),
                        aT=P.sb("aT%d" % s, [128, 4, 128], BF16), u=P.sb("u%d" % s, [128, 4, 128], F32),
                        wT=P.sb("wT%d" % s, [128, 4, 128], BF16), vn=P.sb("vn%d" % s, [128, 4, 128], BF16),
                        Sf=P.sb("Sf%d" % s, [128, 4, 128], F32), Sb=P.sb("Sb%d" % s, [128, 4, 128], BF16),
                        osq=P.sb("osq%d" % s, [128, 4, 128], F32), oss=P.sb("oss%d" % s, [128, 4], F32),
                        o1=P.sb("o1%d" % s, [128, 4, 128], F32), g2=P.sb("g2%d" % s, [128, 4, 128], F32),
                        y=P.sb("y%d" % s, [128, 4, 128], BF16),
                        oT=P.sb("oT%d" % s, [128, 4, 512], BF16),
                    )
                    S.op("gpsimd", lambda e, s=s: e.memset(st[s]["Sf"][:], 0.0), writes=[st[s]["Sf"]])
                    S.op("gpsimd", lambda e, s=s: e.memset(st[s]["Sb"][:], 0.0), writes=[st[s]["Sb"]])

                def dbg(nm, tile, s, n, nn=0):
                    if debug and s == 0 and n == nn and l == 0:
                        S.dma("sync", lambda e: e.dma_start(out=DBG[nm][:, :, :], in_=tile[:]), tile, reads=[tile])

                def loadD(s, n):
                    sl = slots[(s, n % NSLOT)]
                    for nm in names:
                        S.dma("sync", lambda e, nm=nm, sl=sl: e.dma_start(out=sl[nm][:], in_=srcs[nm][s, n, :, :, :]), sl[nm], writes=[sl[nm]])
                    S.dma("sync", lambda e, sl=sl: e.dma_start(out=sl["dec"][:], in_=DEC[s, n, :, :, :]), sl["dec"], writes=[sl["dec"]])
                    S.dma("sync", lambda e, sl=sl: e.dma_start(out=sl["egl"][:], in_=EGL[s, n, :, :]), sl["egl"], writes=[sl["egl"]])
                    S.dma("sync", lambda e, sl=sl: e.dma_start(out=sl["gate"][:], in_=GATE[s, n, :, :].rearrange("p (h e) -> p h e", h=4)),
                          sl["gate"], writes=[sl["gate"]])

                for s in range(2):
                    loadD(s, 0)
                    loadD(s, 1)
                pk = [0, 0]

                def mm4(out, lhs, rhs, reads, acc=None):
                    for h in range(4):
                        S.op("tensor", lambda e, h=h: e.matmul(out[:, h, :], lhsT=lhs[:, h, :], rhs=rhs[:, h, :],
                                                               start=(acc in (None, "start")), stop=(acc in (None, "stop"))),
                             reads=reads, writes=[out])

                for n in range(NCH):
                    SL = {s: slots[(s, n % NSLOT)] for s in range(2)}
                    if n + 2 < NCH:
                        for s in range(2):
                            loadD(s, n + 2)
                    for s in range(2):
                        sl, q = SL[s], st[s]
                        pg = pre[s]
                        q["pg"] = pg
                        mm4(pg, sl["kT"], sl["kbT"], [sl["kT"], sl["kbT"]])
                    bc = lambda c0: cst[:, None, c0:c0 + 128].to_broadcast([128, 4, 128])
                    for s in range(2):
                        sl, q = SL[s], st[s]
                        pg = q["pg"]
                        S.op("vector", lambda e, q=q, sl=sl, pg=pg: e.scalar_tensor_tensor(q["t1"][:], in0=pg[:], scalar=-1.0, in1=sl["dec"][:],
                                                                                        op0=ALU.mult, op1=ALU.mult), reads=[pg, sl["dec"]], writes=[q["t1"]])
                        S.op("gpsimd", lambda e, q=q: e.tensor_tensor(q["Yf"][:], q["t1"][:], strict, op=ALU.mult),
                             reads=[q["t1"], cst], writes=[q["Yf"]])
                    for s in range(2):
                        q = st[s]
                        for h in range(4):
                            S.op("tensor", lambda e, h=h, q=q, pi=pinv[s]: e.transpose(pi[:, h, :], q["Yf"][:, h, :], identF()),
                                 reads=[q["Yf"], cst], writes=[pinv[s]])
                    for s in range(2):
                        q = st[s]
                        S.op("scalar", lambda e, q=q, pi_=pinv[s]: e.copy(q["Xf"][:], pi_[:]), reads=[pinv[s]], writes=[q["Xf"]])
                        S.op("gpsimd", lambda e, q=q: e.tensor_tensor(q["Y"][0][:], q["Yf"][:], bc(C_BM16), op=ALU.mult), reads=[q["Yf"], cst], writes=[q["Y"][0]])
                        S.op("gpsimd", lambda e, q=q: e.tensor_tensor(q["X"][0][:], q["Xf"][:], bc(C_BM16), op=ALU.mult), reads=[q["Xf"], cst], writes=[q["X"][0]])
                        S.op("gpsimd", lambda e, q=q: e.tensor_tensor(q["Q"][:], q["Y"][0][:], identq, op=ALU.add), reads=[q["Y"][0], cst], writes=[q["Q"]])
                        S.op("gpsimd", lambda e, q=q: e.tensor_tensor(q["P"][:], q["X"][0][:], identq, op=ALU.add), reads=[q["X"][0], cst], writes=[q["P"]])
                    for lev in range(1, 4):
                        a, b = (lev - 1) % 2, lev % 2
                        for s in range(2):
                            q = st[s]
                            mm4(pinv[s], q["Y"][a], q["X"][a], [q["Y"][a], q["X"][a]])
                            mm4(pre[s], q["X"][a], q["Y"][a], [q["Y"][a], q["X"][a]])
                        for s in range(2):
                            q = st[s]
                            S.op("scalar", lambda e, q=q, b=b, px_=pinv[s]: e.copy(q["X"][b][:], px_[:]), reads=[pinv[s]], writes=[q["X"][b]])
                            S.op("vector", lambda e, q=q, b=b, py_=pre[s]: e.tensor_copy(q["Y"][b][:], py_[:]), reads=[pre[s]], writes=[q["Y"][b]])
                        for s in range(2):
                            q = st[s]
                            mm4(pinv[s], q["X"][b], q["Q"], [q["X"][b], q["Q"]])
                            mm4(pre[s], q["Y"][b], q["P"], [q["Y"][b], q["P"]])
                        for s in range(2):
                            q = st[s]
                            S.op("vector", lambda e, q=q, pq_=pinv[s]: e.tensor_tensor(q["Q"][:], q["Q"][:], pq_[:], op=ALU.add),
                                 reads=[q["Q"], pinv[s]], writes=[q["Q"]])
                            S.op("vector", lambda e, q=q, pp_=pre[s]: e.tensor_tensor(q["P"][:], q["P"][:], pp_[:], op=ALU.add),
                                 reads=[q["P"], pre[s]], writes=[q["P"]])
                    for mi, coff in enumerate((C_OFF32, C_OFF64, C_OFF128)):
                        lastm = (mi == 2)
                        for s in range(2):
                            q = st[s]
                            S.op("gpsimd", lambda e, q=q, coff=coff: e.tensor_tensor(q["Y"][0][:], q["Yf"][:], bc(coff), op=ALU.mult), reads=[q["Yf"], cst], writes=[q["Y"][0]])
                            S.op("gpsimd", lambda e, q=q, coff=coff: e.tensor_tensor(q["X"][0][:], q["Xf"][:], bc(coff), op=ALU.mult), reads=[q["Xf"], cst], writes=[q["X"][0]])
                        for s in range(2):
                            q = st[s]
                            mm4(pinv[s], q["X"][0], q["Q"], [q["X"][0], q["Q"]])
                            if not lastm:
                                mm4(pre[s], q["Y"][0], q["P"], [q["Y"][0], q["P"]])
                        for s in range(2):
                            q = st[s]
                            S.op("scalar", lambda e, q=q, p_=pinv[s]: e.copy(q["X"][1][:], p_[:]), reads=[pinv[s]], writes=[q["X"][1]])
                            if not lastm:
                                S.op("vector", lambda e, q=q, p_=pre[s]: e.tensor_copy(q["Y"][1][:], p_[:]), reads=[pre[s]], writes=[q["Y"][1]])
                        for s in range(2):
                            q = st[s]
                            mm4(pinv[s], q["P"], q["X"][1], [q["P"], q["X"][1]])
                            if not lastm:
                                mm4(pre[s], q["Q"], q["Y"][1], [q["Q"], q["Y"][1]])
                        for s in range(2):
                            q = st[s]
                            S.op("vector", lambda e, q=q, p_=pinv[s]: e.tensor_tensor(q["Q"][:], q["Q"][:], p_[:], op=ALU.add),
                                 reads=[q["Q"], pinv[s]], writes=[q["Q"]])
                            if not lastm:
                                S.op("vector", lambda e, q=q, p_=pre[s]: e.tensor_tensor(q["P"][:], q["P"][:], p_[:], op=ALU.add),
                                     reads=[q["P"], pre[s]], writes=[q["P"]])
                    for s in range(2):
                        sl, q = SL[s], st[s]
                        S.op("scalar", lambda e, q=q: e.copy(q["Qb"][:], q["Q"][:]), reads=[q["Q"]], writes=[q["Qb"]])
                        dbg("Q", q["Q"], s, n)
                        pa = pre[s]
                        q["pa"] = pa
                        mm4(pa, sl["kT"], sl["qT"], [sl["kT"], sl["qT"]])
                    for s in range(2):
                        sl, q = SL[s], st[s]
                        S.op("vector", lambda e, q=q, sl=sl, pa_=q["pa"]: e.scalar_tensor_tensor(q["aT"][:], in0=pa_[:], scalar=128.0 ** -0.5, in1=sl["dec"][:],
                                                                                     op0=ALU.mult, op1=ALU.mult), reads=[q["pa"], sl["dec"]], writes=[q["aT"]])
                        pu_ = pinv[s]
                        q["pu"] = pu_
                        mm4(pu_, q["Qb"], sl["vb"], [q["Qb"], sl["vb"]])
                    for s in range(2):
                        sl, q = SL[s], st[s]
                        S.op("scalar", lambda e, q=q, pu_=q["pu"]: e.copy(q["u"][:], pu_[:]), reads=[q["pu"]], writes=[q["u"]])
                        dbg("U", q["u"], s, n)
                        pw_ = pre[s]
                        q["pw"] = pw_
                        mm4(pw_, sl["kbg"], q["Qb"], [q["Qb"], sl["kbg"]])
                    for s in range(2):
                        q = st[s]
                        S.op("scalar", lambda e, q=q, pw_=q["pw"]: e.copy(q["wT"][:], pw_[:]), reads=[q["pw"]], writes=[q["wT"]])
                    for s in range(2):
                        sl, q = SL[s], st[s]
                        mm4(pW, q["wT"], q["Sb"], [q["wT"], q["Sb"]])
                        S.op("vector", lambda e, q=q: e.tensor_tensor(q["vn"][:], q["u"][:], pW[:], op=ALU.subtract),
                             reads=[q["u"], pW], writes=[q["vn"]])
                        if debug and s == 0 and n == 1 and l == 0:
                            S.op("vector", lambda e, q=q: e.tensor_copy(q["o1"][:], pW[:]), reads=[pW], writes=[q["o1"]])
                            dbg("PW1", q["o1"], s, n, 1)
                        for h in range(4):
                            S.op("tensor", lambda e, h=h, q=q, sl=sl: e.matmul(pO[:, h, :], lhsT=sl["qgT"][:, h, :], rhs=q["Sb"][:, h, :], start=True, stop=False),
                                 reads=[sl["qgT"], q["Sb"]], writes=[pO])
                            S.op("tensor", lambda e, h=h, q=q: e.matmul(pO[:, h, :], lhsT=q["aT"][:, h, :], rhs=q["vn"][:, h, :], start=False, stop=True),
                                 reads=[q["aT"], q["vn"]], writes=[pO])
                        mm4(pSt, sl["kdec"], q["vn"], [sl["kdec"], q["vn"]])
                        S.op("gpsimd", lambda e, q=q, sl=sl: e.tensor_tensor(q["Sf"][:], q["Sf"][:], sl["egl"][:, :, None].to_broadcast([128, 4, 128]),
                                                                            op=ALU.mult), reads=[q["Sf"], sl["egl"]], writes=[q["Sf"]])
                        S.op("vector", lambda e, q=q: e.tensor_tensor(q["Sf"][:], q["Sf"][:], pSt[:], op=ALU.add), reads=[q["Sf"], pSt], writes=[q["Sf"]])
                        S.op("scalar", lambda e, q=q: e.copy(q["Sb"][:], q["Sf"][:]), reads=[q["Sf"]], writes=[q["Sb"]])
                        dbg("S0", q["Sf"], s, n)
                        dbg("U1", q["u"], s, n, 1)
                        S.op("scalar", lambda e, q=q: e.activation(q["osq"][:], pO[:], AF.Square), reads=[pO], writes=[q["osq"]])
                        S.op("vector", lambda e, q=q: e.reduce_sum(q["oss"][:], q["osq"][:], axis=mybir.AxisListType.X), reads=[q["osq"]], writes=[q["oss"]])
                        S.op("scalar", lambda e, q=q: e.activation(q["oss"][:], q["oss"][:], AF.Ln, bias=epsc(), scale=1.0 / 128), reads=[q["oss"], cst], writes=[q["oss"]])
                        S.op("scalar", lambda e, q=q: e.activation(q["oss"][:], q["oss"][:], AF.Exp, scale=-0.5), reads=[q["oss"]], writes=[q["oss"]])
                        S.op("vector", lambda e, q=q: e.tensor_tensor(q["o1"][:], pO[:], q["oss"][:, :, None].to_broadcast([128, 4, 128]), op=ALU.mult),
                             reads=[pO, q["oss"]], writes=[q["o1"]])
                        S.op("gpsimd", lambda e, q=q, sl=sl: e.tensor_tensor(q["g2"][:], sl["gate"][:], onormb, op=ALU.mult), reads=[sl["gate"], pp], writes=[q["g2"]])
                        S.op("gpsimd", lambda e, q=q: e.tensor_tensor(q["y"][:], q["o1"][:], q["g2"][:], op=ALU.mult), reads=[q["o1"], q["g2"]], writes=[q["y"]])
                        for h in range(4):
                            S.op("tensor", lambda e, h=h, q=q: e.transpose(pTr[:, h, :], q["y"][:, h, :], identB[:]), reads=[q["y"], identB], writes=[pTr])
                        j = n % 4
                        S.op("scalar", lambda e, q=q, j=j: e.copy(q["oT"][:, :, j * 128:(j + 1) * 128], pTr[:, 0:4, :]), reads=[pTr], writes=[q["oT"]])
                        if j == 3:
                            t0 = (n - 3) * 128
                            S.dma("sync", lambda e, q=q, s=s, t0=t0: e.dma_start(out=OT[s, 0:4, :, t0:t0 + 512].rearrange("h p t -> p h t"), in_=q["oT"][:]),
                                  q["oT"], reads=[q["oT"]])
            if upto == "D":
                break
            with Phase(S, "E%d" % l) as P:
                pp = P.sb("pp", [128, NPP], F32)
                S.dma("sync", lambda e, l=l: e.dma_start(out=pp[:], in_=pp_in[l, :, :]), pp, writes=[pp])
                lam_init = 0.8 - 0.6 * math.exp(-0.3 * l)
                lamr = P.sb("lamr", [1, 4, 64], F32)
                S.dma("sync", lambda e, l=l: e.dma_start(out=lamr[:], in_=lam_in[l:l + 1, :, :]), lamr, writes=[lamr])
                lprod = P.sb("lprod", [1, 2, 64], F32)
                lsum = P.sb("lsum", [1, 2], F32)
                nlam1 = P.sb("nlam1", [1, 1], F32)
                nlam = P.sb("nlam", [128, 1], F32)
                subg = P.sb("subg", [128, 1], F32)
                b31 = P.sb("b31", [128, 4], F32)
                S.dma("sync", lambda e: e.dma_start(out=b31[:], in_=b31_in[:, :]), b31, writes=[b31])
                S.op("vector", lambda e: e.tensor_tensor(lprod[:], lamr[:, 0:4:2, :], lamr[:, 1:4:2, :], op=ALU.mult), reads=[lamr], writes=[lprod])
                S.op("vector", lambda e: e.reduce_sum(lsum[:], lprod[:], axis=mybir.AxisListType.X), reads=[lprod], writes=[lsum])
                S.op("scalar", lambda e: e.activation(lsum[:], lsum[:], AF.Exp), reads=[lsum], writes=[lsum])
                S.op("vector", lambda e: e.scalar_tensor_tensor(nlam1[:], in0=lsum[:, 1:2], scalar=-lam_init, in1=lsum[:, 0:1],
                                                                op0=ALU.add, op1=ALU.subtract), reads=[lsum], writes=[nlam1])
                pS0 = [P.ps("pS0_%d" % i, [128, 512], F32) for i in range(2)]
                pS1 = [P.ps("pS1_%d" % i, [128, 512], F32) for i in range(2)]
                pO0 = P.ps("pO0", [128, 512], F32); pO1 = P.ps("pO1", [128, 512], F32)
                pZ0 = P.ps("pZ0", [128, 512], F32); pZ1 = P.ps("pZ1", [128, 512], F32)
                S.op("tensor", lambda e: e.matmul(pZ0[:, 0:1], lhsT=cst[0:1, C_ONES:C_ONES + 128], rhs=nlam1[:], start=True, stop=True),
                     reads=[cst, nlam1], writes=[pZ0])
                S.op("vector", lambda e: e.tensor_copy(nlam[:], pZ0[:, 0:1]), reads=[pZ0], writes=[nlam])
                S.op("vector", lambda e: e.tensor_scalar_mul(subg[:], pp[:, PP_SUBLN:PP_SUBLN + 1], 1.0 - lam_init), reads=[pp], writes=[subg])
                expB = []
                for h in range(4):
                    t = P.sb("expB%d" % h, [128, 1024], F32)
                    S.dma("sync", lambda e, h=h, t=t: e.dma_start(out=t[:], in_=tb_in[h, :, :]), t, writes=[t])
                    S.op("scalar", lambda e, t=t: e.activation(t[:], t[:], AF.Exp), reads=[t], writes=[t])
                    expB.append(t)
                qts = [P.sb("qt%d" % i, [128, T], BF16) for i in range(2)]
                kts = [P.sb("kt%d" % i, [128, T], BF16) for i in range(2)]
                vts = [P.sb("vt%d" % i, [128, NCH, 128], BF16) for i in range(2)]
                E0 = [P.sb("E0_%d" % i, [128, 512], BF16) for i in range(3)]
                E1 = [P.sb("E1_%d" % i, [128, 512], BF16) for i in range(3)]
                Ef = [P.sb("Ef_%d" % i, [128, 512], F32) for i in range(2)]
                rz0 = P.sb("rz0", [128, 512], F32); rz1 = P.sb("rz1", [128, 512], F32)
                zc0 = P.sb("zc0", [128, 512], F32); zc1 = P.sb("zc1", [128, 512], F32)
                oc0 = P.sb("oc0", [128, 512], F32); oc1 = P.sb("oc1", [128, 512], F32)
                oo = P.sb("oo", [128, 512], F32); osq = P.sb("osq", [128, 512], F32)
                rin = P.sb("rin", [128, 512], F32)
                oTs = [P.sb("oTs%d" % i, [128, 512], BF16) for i in range(2)]
                heads = [(s, h) for s in range(2) for h in range(4)]

                def loadE(i):
                    s, h = heads[i]
                    S.dma("sync", lambda e: e.dma_start(out=qts[i % 2][:], in_=DQT[s, h, :, :]), qts[i % 2], writes=[qts[i % 2]])
                    S.dma("sync", lambda e: e.dma_start(out=kts[i % 2][:], in_=DKT[s, h, :, :]), kts[i % 2], writes=[kts[i % 2]])
                    S.dma("sync", lambda e: e.dma_start(out=vts[i % 2][:], in_=DV[s, :, :, h * 128:(h + 1) * 128].rearrange("n p e -> p n e")),
                          vts[i % 2], writes=[vts[i % 2]])

                loadE(0)
                steps = []
                for i, (s, h) in enumerate(heads):
                    for j in range(8):
                        nk = 4 * j + 4
                        for ki in range(nk):
                            steps.append((i, s, h, j, ki, nk))
                cnt = {"ek": 0, "fk": 0, "ok": 0, "loaded": 0}
                live = {}

                def stageA(idx):
                    i, s, h, j, ki, nk = steps[idx]
                    qt, kt = qts[i % 2], kts[i % 2]
                    ek = cnt["ek"]; cnt["ek"] += 1
                    p0, p1 = pS0[ek % 2], pS1[ek % 2]
                    e0, e1 = E0[ek % 3], E1[ek % 3]
                    S.op("tensor", lambda e: e.matmul(p0[:], lhsT=kt[0:64, ki * 128:(ki + 1) * 128], rhs=qt[0:64, j * 512:(j + 1) * 512],
                                                      start=True, stop=True), reads=[kt, qt], writes=[p0])
                    S.op("tensor", lambda e: e.matmul(p1[:], lhsT=kt[64:128, ki * 128:(ki + 1) * 128], rhs=qt[64:128, j * 512:(j + 1) * 512],
                                                      start=True, stop=True), reads=[kt, qt], writes=[p1])
                    near = ki >= 4 * j - 1
                    if not near:
                        S.op("scalar", lambda e: e.activation(e0[:], p0[:], AF.Exp, bias=b31[:, h:h + 1]), reads=[p0, b31], writes=[e0])
                        S.op("scalar", lambda e: e.activation(e1[:], p1[:], AF.Exp, bias=b31[:, h:h + 1]), reads=[p1, b31], writes=[e1])
                    else:
                        c0 = 512 * j - 128 * ki + 384
                        for pc, ec in ((p0, e0), (p1, e1)):
                            ef = Ef[cnt["fk"] % 2]; cnt["fk"] += 1
                            S.op("scalar", lambda e, pc=pc, ef=ef: e.activation(ef[:], pc[:], AF.Exp), reads=[pc], writes=[ef])
                            S.op("vector",
                                 lambda e, ef=ef, ec=ec: e.tensor_tensor(ec[:], ef[:], expB[h][:, c0:c0 + 512], op=ALU.mult),
                                 reads=[ef, expB[h]], writes=[ec])
                    live[idx] = (e0, e1)

                def stageB(idx):
                    i, s, h, j, ki, nk = steps[idx]
                    if j == 0 and ki == 0 and i + 1 < len(heads):
                        loadE(i + 1)
                    vt = vts[i % 2]
                    e0, e1 = live.pop(idx)
                    first, lastk = (ki == 0), (ki == nk - 1)
                    S.op("tensor", lambda e: e.matmul(pO0[:], lhsT=vt[:, ki, :], rhs=e0[:], start=first, stop=lastk), reads=[vt, e0], writes=[pO0])
                    S.op("tensor", lambda e: e.matmul(pZ0[:], lhsT=onesB[:], rhs=e0[:], start=first, stop=lastk), reads=[onesB, e0], writes=[pZ0])
                    S.op("tensor", lambda e: e.matmul(pO1[:], lhsT=vt[:, ki, :], rhs=e1[:], start=first, stop=lastk), reads=[vt, e1], writes=[pO1])
                    S.op("tensor", lambda e: e.matmul(pZ1[:], lhsT=onesB[:], rhs=e1[:], start=first, stop=lastk), reads=[onesB, e1], writes=[pZ1])
                    if not lastk:
                        return
                    S.op("scalar", lambda e: e.copy(zc0[:], pZ0[:]), reads=[pZ0], writes=[zc0])
                    S.op("vector", lambda e: e.tensor_copy(zc1[:], pZ1[:]), reads=[pZ1], writes=[zc1])
                    S.op("scalar", lambda e: e.copy(oc0[:], pO0[:]), reads=[pO0], writes=[oc0])
                    S.op("vector", lambda e: e.tensor_copy(oc1[:], pO1[:]), reads=[pO1], writes=[oc1])
                    pend_epi.append((s, h, j))
                    return

                def epilogue_tail():
                    s, h, j = pend_epi.pop(0)
                    S.op("vector", lambda e: e.reciprocal(rz0[:], zc0[:]), reads=[zc0], writes=[rz0])
                    S.op("vector", lambda e: e.reciprocal(rz1[:], zc1[:]), reads=[zc1], writes=[rz1])
                    S.op("gpsimd", lambda e: e.tensor_tensor(rz0[:], oc0[:], rz0[:], op=ALU.mult), reads=[oc0, rz0], writes=[rz0])
                    S.op("gpsimd", lambda e: e.tensor_tensor(rz1[:], oc1[:], rz1[:], op=ALU.mult), reads=[oc1, rz1], writes=[rz1])
                    S.op("vector", lambda e: e.scalar_tensor_tensor(oo[:], in0=rz1[:], scalar=nlam[:, 0:1], in1=rz0[:], op0=ALU.mult, op1=ALU.add),
                         reads=[rz0, rz1, nlam], writes=[oo])
                    S.op("gpsimd", lambda e: e.tensor_tensor(osq[:], oo[:], oo[:], op=ALU.mult), reads=[oo], writes=[osq])
                    pend_epi2.append([s, h, j, 8])

                def epilogue_tail2():
                    s, h, j, _ = pend_epi2.pop(0)
                    ek = cnt["ek"]; cnt["ek"] += 1
                    pss = pS0[ek % 2]
                    S.op("tensor", lambda e: e.matmul(pss[:], lhsT=ONESF(), rhs=osq[:], start=True, stop=True), reads=[cst, osq], writes=[pss])
                    S.op("scalar", lambda e: e.activation(rin[:], pss[:], AF.Ln, bias=epsc(), scale=1.0 / 128), reads=[pss, cst], writes=[rin])
                    S.op("scalar", lambda e: e.activation(rin[:], rin[:], AF.Exp, scale=-0.5), reads=[rin], writes=[rin])
                    ot = oTs[cnt["ok"] % 2]; cnt["ok"] += 1
                    S.op("vector", lambda e: e.scalar_tensor_tensor(ot[:], in0=oo[:], scalar=subg[:, 0:1], in1=rin[:], op0=ALU.mult, op1=ALU.mult),
                         reads=[oo, subg, rin], writes=[ot])
                    S.dma("sync", lambda e: e.dma_start(out=OT[s, 4 + h, :, j * 512:(j + 1) * 512], in_=ot[:]), ot, reads=[ot])

                pend_epi = []
                pend_epi2 = []
                LOOK = 2
                for idx in range(min(LOOK, len(steps))):
                    stageA(idx)
                for idx in range(len(steps)):
                    had = len(pend_epi)
                    stageB(idx)
                    if idx + LOOK < len(steps):
                        stageA(idx + LOOK)
                    for it in pend_epi2:
                        it[3] -= 1
                    if had:
                        while pend_epi2:
                            epilogue_tail2()
                        epilogue_tail()
                    while pend_epi2 and pend_epi2[0][3] <= 0:
                        epilogue_tail2()
                while pend_epi:
                    while pend_epi2:
                        epilogue_tail2()
                    epilogue_tail()
                while pend_epi2:
                    epilogue_tail2()
            if upto == "E":
                break
            with Phase(S, "F%d" % l) as P:
                Wo = load_w_bf16(P, "Wo", w_out[l, :, :], 8, DM)
                gtB = []
                for b in range(2):
                    t = P.sb("gtB%d" % b, [128, DM], F32)
                    S.dma("sync", lambda e, b=b, t=t, l=l: e.dma_start(out=t[:], in_=MOD[l, b, 2 * DM:3 * DM].partition_broadcast(128)), t, writes=[t])
                    gtB.append(t)
                xts = [P.sb("xt%d" % i, [128, 4, DM], F32) for i in range(2)]
                ots = [P.sb("ot%d" % i, [128, 8, 512], BF16) for i in range(2)]
                tmp = [P.sb("tmp%d" % i, [128, 512], F32) for i in range(2)]
                pY = [P.ps("pY%d" % i, [128, 512], F32) for i in range(4)]
                blocks = [(s, blk) for s in range(2) for blk in range(8)]

                def loadF(i):
                    s, blk = blocks[i]
                    S.dma("sync", lambda e: e.dma_start(out=xts[i % 2][:], in_=xsrc[s, blk * 512:(blk + 1) * 512, :].rearrange("(a p) d -> p a d", p=128)),
                          xts[i % 2], writes=[xts[i % 2]])
                    S.dma("sync", lambda e: e.dma_start(out=ots[i % 2][:], in_=OT[s, :, :, blk * 512:(blk + 1) * 512].rearrange("c p t -> p c t")),
                          ots[i % 2], writes=[ots[i % 2]])

                loadF(0)
                yk = 0
                for i, (s, blk) in enumerate(blocks):
                    if i + 1 < len(blocks):
                        loadF(i + 1)
                    xt, ot = xts[i % 2], ots[i % 2]
                    for sub in range(4):
                        for dh in range(2):
                            py = pY[yk % 4]; tp = tmp[yk % 2]; yk += 1
                            for c in range(8):
                                S.op("tensor", lambda e, c=c, sub=sub, dh=dh, py=py, ot=ot: e.matmul(py[:], lhsT=ot[:, c, sub * 128:(sub + 1) * 128],
                                                                                                  rhs=Wo[:, c, dh * 512:(dh + 1) * 512], start=(c == 0), stop=(c == 7)),
                                     reads=[ot, Wo], writes=[py])
                            S.op("vector", lambda e, py=py, tp=tp, dh=dh, s=s: e.tensor_tensor(tp[:], py[:], gtB[s][:, dh * 512:(dh + 1) * 512], op=ALU.mult),
                                 reads=[py, gtB[s]], writes=[tp])
                            S.op("gpsimd", lambda e, tp=tp, sub=sub, dh=dh, xt=xt: e.tensor_tensor(xt[:, sub, dh * 512:(dh + 1) * 512], xt[:, sub, dh * 512:(dh + 1) * 512],
                                                                                                   tp[:], op=ALU.add), reads=[tp, xt], writes=[xt])
                    S.dma("sync", lambda e, xt=xt, s=s, blk=blk: e.dma_start(out=XA[s, blk * 512:(blk + 1) * 512, :].rearrange("(a p) d -> p a d", p=128), in_=xt[:]),
                          xt, reads=[xt])
            if upto == "F":
                break
            with Phase(S, "G%d" % l) as P:
                pp = P.sb("pp", [128, NPP], F32)
                S.dma("sync", lambda e, l=l: e.dma_start(out=pp[:], in_=pp_in[l, :, :]), pp, writes=[pp])
                Wup = load_w_bf16(P, "Wup", ffn_up[l, :, :], 8, 2 * DFF)
                AB = {}
                for b in range(2):
                    AB[b] = make_AB(P, l, b, pp, PP_NFFN, 3, 4, "ffn%d" % b)
                NB = 512
                xts = [P.sb("xt%d" % i, [128, 4, DM], F32) for i in range(2)]
                ss = P.sb("ss", [128, 4], F32); rs = P.sb("rs", [128, 4], F32)
                sq = P.sb("sq", [128, DM], BF16); xn = P.sb("xn", [128, 4, DM], BF16)
                tmpf = P.sb("tmpf", [128, 8, 128], F32)
                hTs = [P.sb("hT%d" % i, [128, 8, NB], BF16) for i in range(2)]
                gts = [P.sb("gt%d" % i, [128, NB], BF16) for i in range(4)]
                pT = P.ps("pT", [128, 8, 128], BF16)
                pUf = [P.ps("pU%d" % i, [128, 512], F32) for i in range(6)]
                upad = [P.sb("upad%d" % i, [128, NB + 2], F32) for i in range(4)]
                acc = [P.sb("acc%d" % i, [128, NB], F32) for i in range(4)]
                sg = [P.sb("sg%d" % i, [128, NB], F32) for i in range(2)]
                halo = P.sb("halo", [128, 44, 2], F32)
                blocks = [(s, blk) for s in range(2) for blk in range(T // NB)]

                def loadG(i):
                    s, blk = blocks[i]
                    S.dma("sync", lambda e: e.dma_start(out=xts[i % 2][:], in_=XA[s, blk * NB:(blk + 1) * NB, :].rearrange("(a p) d -> p a d", p=128)),
                          xts[i % 2], writes=[xts[i % 2]])

                def normG(i):
                    s_, _ = blocks[i]
                    A_, B_ = AB[s_]
                    norm_to_hT(P, xts[i % 2], 4, A_, B_, hTs[i % 2], (ss, rs, sq, xn, tmpf), pT, None)

                loadG(0)
                loadG(1)
                normG(0)
                uk = 0; pk = 0; gk = 0
                for i, (s, blk) in enumerate(blocks):
                    if i + 1 < len(blocks):
                        normG(i + 1)
                    if i + 2 < len(blocks):
                        loadG(i + 2)
                    hT = hTs[i % 2]
                    if blk == 0:
                        S.op("gpsimd", lambda e: e.memset(halo[:], 0.0), writes=[halo])
                    for fc in range(22):
                        res = []
                        for half in range(2):
                            f = fc + 22 * half
                            pu = pUf[pk % 6]; pk += 1
                            up = upad[uk % 4]; ac = acc[uk % 4]; uk += 1
                            for kc in range(8):
                                S.op("tensor", lambda e, kc=kc, f=f, pu=pu, hT=hT: e.matmul(pu[:], lhsT=Wup[:, kc, f * 128:(f + 1) * 128], rhs=hT[:, kc, :],
                                                                                    start=(kc == 0), stop=(kc == 7)), reads=[Wup, hT], writes=[pu])
                            S.op("gpsimd", lambda e, f=f, up=up: e.tensor_copy(up[:, 0:2], halo[:, f, :]), reads=[halo], writes=[up])
                            S.op("scalar", lambda e, up=up, pu=pu: e.copy(up[:, 2:NB + 2], pu[:]), reads=[pu], writes=[up])
                            S.op("gpsimd", lambda e, f=f, up=up: e.tensor_copy(halo[:, f, :], up[:, NB:NB + 2]), reads=[up], writes=[halo])
                            cw = lambda j, f=f: pp[:, PP_FCW + f * 3 + j:PP_FCW + f * 3 + j + 1]
                            S.op("vector", lambda e, up=up, ac=ac, cw=cw, f=f: e.tensor_scalar(ac[:], up[:, 2:NB + 2], cw(2), pp[:, PP_FCB + f:PP_FCB + f + 1],
                                                                                               op0=ALU.mult, op1=ALU.add), reads=[up, pp], writes=[ac])
                            for j in (1, 0):
                                S.op("vector", lambda e, up=up, ac=ac, cw=cw, j=j: e.scalar_tensor_tensor(ac[:], in0=up[:, j:j + NB], scalar=cw(j), in1=ac[:],
                                                                                                          op0=ALU.mult, op1=ALU.add), reads=[up, pp, ac], writes=[ac])
                            res.append(ac)
                        sgt = sg[fc % 2]
                        gt = gts[gk % 4]; gk += 1
                        S.op("scalar", lambda e, sgt=sgt, g_=res[1]: e.activation(sgt[:], g_[:], AF.Silu), reads=[res[1]], writes=[sgt])
                        S.op("gpsimd", lambda e, sgt=sgt, a_=res[0], gt=gt: e.tensor_tensor(gt[:], a_[:], sgt[:], op=ALU.mult), reads=[res[0], sgt], writes=[gt])
                        S.dma("sync", lambda e, gt=gt, s=s, blk=blk, fc=fc: e.dma_start(out=GTD[s, fc, :, blk * NB:(blk + 1) * NB], in_=gt[:]), gt, reads=[gt])
            with Phase(S, "H%d" % l) as P:
                Wdn = load_w_bf16(P, "Wdn", ffn_down[l, :, :], 22, DM)
                gtB = []
                for b in range(2):
                    t = P.sb("gtB%d" % b, [128, DM], F32)
                    S.dma("sync", lambda e, b=b, t=t, l=l: e.dma_start(out=t[:], in_=MOD[l, b, 5 * DM:6 * DM].partition_broadcast(128)), t, writes=[t])
                    gtB.append(t)
                NB = 512
                xts = [P.sb("xt%d" % i, [128, 4, DM], F32) for i in range(2)]
                gin = [P.sb("gin%d" % i, [128, 22, NB], BF16) for i in range(2)]
                tmp = [P.sb("tmp%d" % i, [128, 512], F32) for i in range(2)]
                pY = [P.ps("pY%d" % i, [128, 512], F32) for i in range(4)]
                blocks = [(s, blk) for s in range(2) for blk in range(T // NB)]

                def loadH(i):
                    s, blk = blocks[i]
                    S.dma("sync", lambda e: e.dma_start(out=xts[i % 2][:], in_=XA[s, blk * NB:(blk + 1) * NB, :].rearrange("(a p) d -> p a d", p=128)),
                          xts[i % 2], writes=[xts[i % 2]])
                    for f0 in (0, 11):
                        S.dma("sync", lambda e, f0=f0: e.dma_start(out=gin[i % 2][:, f0:f0 + 11, :],
                                                                   in_=GTD[s, f0:f0 + 11, :, blk * NB:(blk + 1) * NB].rearrange("f p t -> p f t")),
                              gin[i % 2], writes=[gin[i % 2]])

                loadH(0)
                yk = 0
                for i, (s, blk) in enumerate(blocks):
                    if i + 1 < len(blocks):
                        loadH(i + 1)
                    xt, gi = xts[i % 2], gin[i % 2]
                    for sub in range(4):
                        for dh in range(2):
                            py = pY[yk % 4]; tp = tmp[yk % 2]; yk += 1
                            for fc in range(22):
                                S.op("tensor", lambda e, fc=fc, sub=sub, dh=dh, py=py, gi=gi: e.matmul(py[:], lhsT=gi[:, fc, sub * 128:(sub + 1) * 128],
                                                                                                           rhs=Wdn[:, fc, dh * 512:(dh + 1) * 512], start=(fc == 0), stop=(fc == 21)),
                                     reads=[gi, Wdn], writes=[py])
                            S.op("vector", lambda e, py=py, tp=tp, dh=dh, s=s: e.tensor_tensor(tp[:], py[:], gtB[s][:, dh * 512:(dh + 1) * 512], op=ALU.mult),
                                 reads=[py, gtB[s]], writes=[tp])
                            S.op("gpsimd", lambda e, tp=tp, sub=sub, dh=dh, xt=xt: e.tensor_tensor(xt[:, sub, dh * 512:(dh + 1) * 512], xt[:, sub, dh * 512:(dh + 1) * 512],
                                                                                                   tp[:], op=ALU.add), reads=[tp, xt], writes=[xt])
                    S.dma("sync", lambda e, xt=xt, s=s, blk=blk: e.dma_start(out=xdst[s, blk * NB:(blk + 1) * NB, :].rearrange("(a p) d -> p a d", p=128), in_=xt[:]),
                          xt, reads=[xt])
            xsrc = xdst
        G.__exit__(None, None, None)
    return nc


def _prep(inputs):
    inp = {k: np.asarray(v) for k, v in inputs.items()}
    consts = _consts()
    pp = _pack_pp(inp)
    lamv = np.stack([inp["diff_lambda_q1"], inp["diff_lambda_k1"], inp["diff_lambda_q2"], inp["diff_lambda_k2"]], axis=1)
    lamv = np.ascontiguousarray(lamv.astype(np.float32))
    kk = np.arange(128)[:, None]
    cc = np.arange(1024)[None, :]
    dist = cc - kk - 384
    bidx = _t5_bucket(np.maximum(dist, 0))
    rb = inp["rel_bias"].astype(np.float32)
    tb = np.empty((4, 128, 1024), np.float32)
    for h in range(4):
        tb[h] = np.where(dist >= 0, rb[bidx, h], np.float32(NEG))
    b31 = np.ascontiguousarray(np.broadcast_to(rb[31][None, :], (128, 4))).astype(np.float32)
    shared = dict(consts=consts, pp=pp, lamv=lamv, tb=tb, b31=b31,
                  w_ada=inp["w_ada"], b_ada=inp["b_ada"], w_in=inp["w_in"], w_out=inp["w_out"],
                  ffn_up=inp["ffn_up"], ffn_down=inp["ffn_down"])
    in_maps = []
    for c in range(NCORES):
        m = dict(shared)
        m["x"] = np.ascontiguousarray(inp["x"][2 * c:2 * c + 2])
        cc_ = inp["c"][2 * c:2 * c + 2]
        m["cT"] = np.ascontiguousarray(cc_.reshape(2, 8, 128).transpose(2, 1, 0))
        in_maps.append(m)
    return in_maps


def kernel(**inputs):
    in_maps = _prep(inputs)
    nc = build()
    res = run_bass_kernel_spmd(nc, in_maps, core_ids=list(range(NCORES)))
    return np.concatenate([r["out"] for r in res.results], axis=0).astype(np.float32)
```

```python
import math
from contextlib import ExitStack
import numpy as np
import concourse.bass as bass
import concourse.mybir as mybir
from concourse.bass_utils import run_bass_kernel_spmd

F32 = mybir.dt.float32
BF16 = mybir.dt.bfloat16
AF = mybir.ActivationFunctionType
ALU = mybir.AluOpType

NCORES = 8
DEPTH = 4
T = 4096
DM = 1024
NEG = -30000.0
EPS = 1e-6
DFF = 2816


class Buf:
    def __init__(self, name, t=None):
        self.name = name
        self.t = t
        self.last_w = None
        self.readers = []
        self.dsem = None
        self.dcount = 0

    def __getitem__(self, k):
        return self.t[k]


class Eng:
    def __init__(self, name, sem):
        self.name = name
        self.sem = sem
        self.count = 0
        self.waited = {}
        self.prog = []


class Sched:
    SEM_WRAP = 30000

    def __init__(self, nc, es):
        self.nc = nc
        self.es = es
        self.engs = {}
        for name in ("sync", "scalar", "vector", "gpsimd", "tensor"):
            self.engs[name] = Eng(name, es.enter_context(nc.semaphore("s_" + name)))
        self.n_instr = 0
        self.dbufs = []
        self.sem_pool = []
        self.rr = 0

    def _waits(self, e, reads, writes):
        toks = []
        own = e.sem
        for b in reads:
            if b.last_w is not None:
                toks.append(b.last_w)
        for b in writes:
            if b.last_w is not None and b.last_w[0] is not own:
                toks.append(b.last_w)
            for r in b.readers:
                if r[0] is not own:
                    toks.append(r)
        need = {}
        for sem, val in toks:
            if e.name == "tensor" and sem is own:
                continue
            k = id(sem)
            if e.waited.get(k, 0) >= val:
                continue
            if k not in need or need[k][1] < val:
                need[k] = (sem, val)
        out = []
        for k, (sem, val) in need.items():
            e.waited[k] = val
            out.append((sem, val))
        return out

    def _record(self, e, fn, waits, sem, inc, reads, writes, tok):
        def run(eng, fn=fn, waits=waits, sem=sem, inc=inc):
            for s, v in waits:
                eng.wait_ge(s, v)
            fn(eng).then_inc(sem, inc)
        e.prog.append(run)
        for b in reads:
            b.readers.append(tok)
            if len(b.readers) > 64:
                b.readers = b.readers[-48:]
        for b in writes:
            b.last_w = tok
            b.readers = []
        self.n_instr += 1

    def op(self, engname, fn, reads=(), writes=()):
        e = self.engs[engname]
        if e.count >= self.SEM_WRAP:
            e.sem = self.es.enter_context(self.nc.semaphore("s_%s_%d" % (engname, self.n_instr)))
            e.count = 0
        waits = self._waits(e, reads, writes)
        e.count += 1
        tok = (e.sem, e.count)
        self._record(e, fn, waits, e.sem, 1, reads, writes, tok)
        return tok

    def dma(self, engname, fn, sembuf, reads=(), writes=()):
        e = self.engs[engname]
        waits = self._waits(e, reads, writes)
        if sembuf.dsem is None:
            if self.sem_pool:
                sembuf.dsem, sembuf.dcount = self.sem_pool.pop()
            else:
                sembuf.dsem = self.es.enter_context(self.nc.semaphore("d%d_%s" % (self.n_instr, sembuf.name)))
        if sembuf not in self.dbufs:
            self.dbufs.append(sembuf)
        sembuf.dcount += 16
        tok = (sembuf.dsem, sembuf.dcount)
        self._record(e, fn, waits, sembuf.dsem, 16, reads, writes, tok)
        return tok

    def barrier(self):
        toks = [(e.sem, e.count) for e in self.engs.values() if e.count > 0]
        toks += [(b.dsem, b.dcount) for b in self.dbufs]
        for name in self.engs:
            self.wait_tokens(name, toks)
        for b in self.dbufs:
            self.sem_pool.append((b.dsem, b.dcount))
            b.dsem = None
        self.dbufs = []

    def wait_tokens(self, engname, toks):
        e = self.engs[engname]
        for sem, val in toks:
            k = id(sem)
            if e.waited.get(k, 0) >= val:
                continue
            e.waited[k] = val
            e.prog.append(lambda eng, s=sem, v=val: eng.wait_ge(s, v))

    def emit(self):
        with self.nc.Block() as block:
            for name in ("sync", "scalar", "vector", "gpsimd", "tensor"):
                def body(eng, name=name):
                    for f in self.engs[name].prog:
                        f(eng)
                getattr(block, name)(body)
        for e in self.engs.values():
            e.prog = []

    def alt(self):
        self.rr ^= 1
        return "vector" if self.rr else "gpsimd"


PHASE_LOG = []


class Phase:
    def __init__(self, S, name):
        self.S = S
        self.nc = S.nc
        self.name = name
        self.es = ExitStack()
        self.k = 0

    def __enter__(self):
        self.es.__enter__()
        return self

    def sb(self, name, shape, dt):
        self.k += 1
        nm = "%s_%s_%d" % (self.name, name, self.k)
        return Buf(nm, self.es.enter_context(self.nc.sbuf_tensor(nm, list(shape), dt)))

    def ps(self, name, shape, dt):
        self.k += 1
        nm = "%s_%s_%d" % (self.name, name, self.k)
        return Buf(nm, self.es.enter_context(self.nc.psum_tensor(nm, list(shape), dt)))

    def __exit__(self, *a):
        if a[0] is None:
            PHASE_LOG.append((self.name, {k: (v.count, id(v.sem)) for k, v in self.S.engs.items()}, self.S.n_instr))
            self.S.barrier()
            self.S.emit()
        return self.es.__exit__(*a)


C_IDENT, C_TRI, C_ONES, C_MASKNEG, C_STRICT, C_BLK64, C_DELTA, C_EPS, C_ONE = 0, 128, 256, 384, 512, 640, 768, 769, 770
C_BM16, C_OFF32, C_OFF64, C_OFF128 = 771, 899, 1027, 1155
NCONST = 1283

PP_CONVW = 0
PP_DTB = 48
PP_ALOG = 52
PP_ONORM = 56
PP_QN = 184
PP_KN = 185
PP_SUBLN = 186
PP_FCW = 187
PP_FCB = 319
PP_NMIX = 363
PP_NFFN = 371
PP_BADA = 379
NPP = 380


def _consts():
    c = np.zeros((128, NCONST), np.float32)
    i = np.arange(128)
    c[:, C_IDENT:C_IDENT + 128] = np.eye(128)
    c[:, C_TRI:C_TRI + 128] = (i[:, None] <= i[None, :])
    c[:, C_ONES:C_ONES + 128] = 1.0
    c[:, C_MASKNEG:C_MASKNEG + 128] = np.where(i[:, None] <= i[None, :], 0.0, NEG)
    c[:, C_STRICT:C_STRICT + 128] = (i[:, None] < i[None, :])
    c[:, C_BLK64:C_BLK64 + 128] = ((i[:, None] // 64) == (i[None, :] // 64))
    c[0, C_DELTA] = 1.0
    bm = lambda m: ((i[:, None] // m) == (i[None, :] // m)).astype(np.float32)
    c[:, C_BM16:C_BM16 + 128] = bm(16)
    c[:, C_OFF32:C_OFF32 + 128] = bm(32) - bm(16)
    c[:, C_OFF64:C_OFF64 + 128] = bm(64) - bm(32)
    c[:, C_OFF128:C_OFF128 + 128] = 1.0 - bm(64)
    c[:, C_EPS] = EPS
    c[:, C_ONE] = 1.0
    return c


def _t5_bucket(n):
    n = np.asarray(n)
    nf = np.maximum(n, 1).astype(np.float32)
    large = 16 + (np.log(nf / np.float32(16)) / np.float32(math.log(128 / 16)) * np.float32(16)).astype(np.int32)
    large = np.minimum(large, 31)
    return np.where(n < 16, n, large)


def _pack_pp(inp):
    pp = np.zeros((DEPTH, 128, NPP), np.float32)
    p = np.arange(128)
    for l in range(DEPTH):
        cw = inp["gdn_conv_w"][l]
        pp[l, :, PP_CONVW:PP_CONVW + 48] = cw.reshape(4, 12, 128).transpose(2, 1, 0).reshape(128, 48)
        pp[l, :, PP_DTB:PP_DTB + 4] = inp["gdn_dt_bias"][l][None, :]
        pp[l, :, PP_ALOG:PP_ALOG + 4] = inp["gdn_a_log"][l][None, :]
        pp[l, :, PP_ONORM:PP_ONORM + 128] = inp["gdn_out_norm"][l][None, :]
        pp[l, :, PP_QN] = inp["diff_q_norm"][l][p % 64]
        pp[l, :, PP_KN] = inp["diff_k_norm"][l][p % 64]
        pp[l, :, PP_SUBLN] = inp["diff_subln"][l]
        fw = inp["ffn_conv_w"][l]
        pp[l, :, PP_FCW:PP_FCW + 132] = fw.reshape(3, 44, 128).transpose(2, 1, 0).reshape(128, 132)
        pp[l, :, PP_FCB:PP_FCB + 44] = inp["ffn_conv_b"][l].reshape(44, 128).T
        pp[l, :, PP_NMIX:PP_NMIX + 8] = inp["norm_mix"][l].reshape(8, 128).T
        pp[l, :, PP_NFFN:PP_NFFN + 8] = inp["norm_ffn"][l].reshape(8, 128).T
    return pp


def build(nlayers=DEPTH, upto="G", debug=False):
    nc = bass.Bass("TRN2", target_bir_lowering=False)
    dk = "ExternalOutput" if debug else "Internal"

    def din(name, shape, dt=F32):
        return nc.dram_tensor(name, list(shape), dt, kind="ExternalInput").ap()

    def dscr(name, shape, dt, dbg=True):
        return nc.dram_tensor(name, list(shape), dt, kind=(dk if dbg else "Internal")).ap()

    x_in = din("x", [2, T, DM])
    cT_in = din("cT", [128, 8, 2])
    consts_in = din("consts", [128, NCONST])
    pp_in = din("pp", [DEPTH, 128, NPP])
    lam_in = din("lamv", [DEPTH, 4, 64])
    tb_in = din("tb", [4, 128, 1024])
    b31_in = din("b31", [128, 4])
    w_ada = din("w_ada", [DEPTH, DM, 6 * DM])
    b_ada = din("b_ada", [DEPTH, 6 * DM])
    w_in = din("w_in", [DEPTH, DM, 3592])
    w_out = din("w_out", [DEPTH, DM, DM])
    ffn_up = din("ffn_up", [DEPTH, DM, 2 * DFF])
    ffn_down = din("ffn_down", [DEPTH, DFF, DM])
    out = nc.dram_tensor("out", [2, T, DM], F32, kind="ExternalOutput").ap()

    MOD = dscr("MOD", [DEPTH, 2, 6 * DM], F32)
    XA = dscr("XA", [2, T, DM], F32)
    XB = dscr("XB", [2, T, DM], F32, dbg=False)
    NCH = T // 128
    KT = dscr("KT", [2, NCH, 128, 4, 128], BF16)
    QGT = dscr("QGT", [2, NCH, 128, 4, 128], BF16)
    QT = dscr("QT", [2, NCH, 128, 4, 128], BF16)
    KBT = dscr("KBT", [2, NCH, 128, 4, 128], BF16)
    KBG = dscr("KBG", [2, NCH, 128, 4, 128], BF16)
    KDEC = dscr("KDEC", [2, NCH, 128, 4, 128], BF16)
    VB = dscr("VB", [2, NCH, 128, 4, 128], BF16)
    DEC = dscr("DEC", [2, NCH, 128, 4, 128], F32)
    EGL = dscr("EGL", [2, NCH, 128, 4], F32)
    GATE = dscr("GATE", [2, NCH, 128, 512], BF16)
    DQT = dscr("DQT", [2, 4, 128, T], BF16)
    DKT = dscr("DKT", [2, 4, 128, T], BF16)
    DV = dscr("DV", [2, NCH, 128, 512], BF16)
    OT = dscr("OT", [2, 8, 128, T], BF16)
    GTD = dscr("GTD", [2, 22, 128, T], BF16, dbg=False)

    DBG = {}
    if debug:
        for nm in ("Y", "X", "Q", "U", "T1", "X1", "Y1b", "S0", "U1", "PW1", "O1", "VN0"):
            DBG[nm] = dscr("DBG_" + nm, [128, 4, 128], F32)

    with ExitStack() as es0:
        S = Sched(nc, es0)
        G = Phase(S, "glob")
        G.__enter__()
        cst = G.sb("cst", [128, NCONST], F32)
        identB = G.sb("identB", [128, 128], BF16)
        onesB = G.sb("onesB", [128, 128], BF16)
        S.dma("sync", lambda e: e.dma_start(out=cst[:], in_=consts_in[:, :]), cst, writes=[cst])
        S.op("vector", lambda e: e.tensor_copy(identB[:], cst[:, C_IDENT:C_IDENT + 128]), reads=[cst], writes=[identB])
        S.op("vector", lambda e: e.tensor_copy(onesB[:], cst[:, C_ONES:C_ONES + 128]), reads=[cst], writes=[onesB])
        identF = lambda: cst[:, C_IDENT:C_IDENT + 128]
        TRI = lambda: cst[:, C_TRI:C_TRI + 128]
        ONESF = lambda: cst[:, C_ONES:C_ONES + 128]
        epsc = lambda: cst[:, C_EPS:C_EPS + 1]
        onec = lambda: cst[:, C_ONE:C_ONE + 1]

        with Phase(S, "A") as P:
            cT = P.sb("cT", [128, 8, 2], F32)
            S.dma("sync", lambda e: e.dma_start(out=cT[:], in_=cT_in[:, :, :]), cT, writes=[cT])
            S.op("scalar", lambda e: e.activation(cT[:], cT[:], AF.Silu), reads=[cT], writes=[cT])
            wts = [P.sb("wa%d" % i, [128, 8, 512], F32) for i in range(3)]
            pM = [P.ps("pM%d" % i, [2, 512], F32) for i in range(2)]
            k = 0
            bada = P.sb("bada", [2, 6 * DM], F32)
            modt = P.sb("modt", [2, 6 * DM], F32)
            for l in range(nlayers):
                S.dma("sync", lambda e, l=l, bada=bada: e.dma_start(out=bada[:], in_=b_ada[l, :].partition_broadcast(2)), bada, writes=[bada])
                for fb in range(12):
                    wt = wts[k % 3]
                    pm = pM[k % 2]
                    k += 1
                    S.dma("sync", lambda e, l=l, fb=fb, wt=wt: e.dma_start(
                        out=wt[:], in_=w_ada[l, :, fb * 512:(fb + 1) * 512].rearrange("(c p) f -> p c f", p=128)), wt, writes=[wt])
                    for kc in range(8):
                        S.op("tensor", lambda e, kc=kc, wt=wt, pm=pm: e.matmul(pm[:], lhsT=cT[:, kc, :], rhs=wt[:, kc, :],
                                                                              start=(kc == 0), stop=(kc == 7)),
                             reads=[cT, wt], writes=[pm])
                    S.op("vector", lambda e, fb=fb, pm=pm, modt=modt, bada=bada: e.tensor_tensor(
                        modt[:, fb * 512:(fb + 1) * 512], pm[:], bada[:, fb * 512:(fb + 1) * 512], op=ALU.add),
                        reads=[pm, bada], writes=[modt])
                S.dma("sync", lambda e, l=l, modt=modt: e.dma_start(out=MOD[l, :, :], in_=modt[:]), modt, reads=[modt])

        def load_mod_cols(P, l, b, seg, name):
            t = P.sb(name, [128, 8], F32)
            S.dma("sync", lambda e: e.dma_start(out=t[:], in_=MOD[l, b, seg * DM:(seg + 1) * DM].rearrange("(c p) -> p c", p=128),
                                                allow_slow_non_contiguous=True), t, writes=[t])
            return t

        def make_AB(P, l, b, pp, ncol, seg_sh, seg_sc, name):
            sh = load_mod_cols(P, l, b, seg_sh, name + "sh")
            sc = load_mod_cols(P, l, b, seg_sc, name + "sc")
            A = P.sb(name + "A", [128, 8], F32)
            S.op("vector", lambda e: e.scalar_tensor_tensor(A[:], in0=sc[:], scalar=1.0, in1=pp[:, ncol:ncol + 8],
                                                            op0=ALU.add, op1=ALU.mult), reads=[sc, pp], writes=[A])
            return A, sh

        def norm_to_hT(P, xt, nsub, A, Bsh, hT, tmps, pT, W):
            ss, rs, sq, xn, tmpf = tmps
            for sub in range(nsub):
                S.op("scalar", lambda e, sub=sub: e.activation(sq[:], xt[:, sub, :], AF.Square, scale=1.0 / 32,
                                                                accum_out=ss[:, sub:sub + 1]), reads=[xt], writes=[sq, ss])
            S.op("scalar", lambda e: e.activation(rs[:, 0:nsub], ss[:, 0:nsub], AF.Ln, bias=epsc()), reads=[ss, cst], writes=[rs])
            S.op("scalar", lambda e: e.activation(rs[:, 0:nsub], rs[:, 0:nsub], AF.Exp, scale=-0.5), reads=[rs], writes=[rs])
            for sub in range(nsub):
                S.op("vector", lambda e, sub=sub: e.tensor_scalar_mul(xn[:, sub, :], xt[:, sub, :], rs[:, sub:sub + 1]),
                     reads=[xt, rs], writes=[xn])
                for c in range(8):
                    S.op("tensor", lambda e, sub=sub, c=c: e.transpose(pT[:, c, :], xn[:, sub, c * 128:(c + 1) * 128], identB[:]),
                         reads=[xn, identB], writes=[pT])
                S.op("vector", lambda e: e.tensor_tensor(tmpf[:], pT[:], A[:, :, None].to_broadcast([128, 8, 128]), op=ALU.mult),
                     reads=[pT, A], writes=[tmpf])
                S.op("gpsimd", lambda e, sub=sub: e.tensor_tensor(hT[:, :, sub * 128:(sub + 1) * 128], tmpf[:],
                                                                   Bsh[:, :, None].to_broadcast([128, 8, 128]), op=ALU.add),
                     reads=[tmpf, Bsh], writes=[hT])

        def load_w_bf16(P, name, src_ap, kc, ncols):
            t = P.sb(name, [128, kc, ncols], BF16)
            step = max(1, 4096 // ncols)
            for c0 in range(0, kc, step):
                c1 = min(kc, c0 + step)
                S.dma("gpsimd", lambda e, c0=c0, c1=c1: e.dma_start(
                    out=t[:, c0:c1, :], in_=src_ap[c0 * 128:c1 * 128, :].rearrange("(c p) f -> p c f", p=128)), t, writes=[t])
            return t

        xsrc = x_in
        for l in range(nlayers):
            last = (l == nlayers - 1)
            xdst = out if last else XB
            with Phase(S, "C%d" % l) as P:
                pp = P.sb("pp", [128, NPP], F32)
                S.dma("sync", lambda e, l=l: e.dma_start(out=pp[:], in_=pp_in[l, :, :]), pp, writes=[pp])
                Wqkv = load_w_bf16(P, "Wqkv", w_in[l, :, 0:1536], 8, 1536)
                Wgate = load_w_bf16(P, "Wgate", w_in[l, :, 1536:2048], 8, 512)
                Wba = load_w_bf16(P, "Wba", w_in[l, :, 2048:2056], 8, 8)
                Wdqk = load_w_bf16(P, "Wdqk", w_in[l, :, 2056:3080], 8, 1024)
                Wdv = load_w_bf16(P, "Wdv", w_in[l, :, 3080:3592], 8, 512)
                negA = P.sb("negA", [128, 4], F32)
                S.op("scalar", lambda e: e.activation(negA[:], pp[:, PP_ALOG:PP_ALOG + 4], AF.Exp), reads=[pp], writes=[negA])
                S.op("vector", lambda e: e.tensor_scalar_mul(negA[:], negA[:], -1.0), reads=[negA], writes=[negA])
                qgain = P.sb("qgain", [128, 1], F32)
                S.op("vector", lambda e: e.tensor_scalar_mul(qgain[:], pp[:, PP_QN:PP_QN + 1], 0.125), reads=[pp], writes=[qgain])
                xts = [P.sb("xt%d" % i, [128, 4, DM], F32) for i in range(2)]
                ss = P.sb("ss", [128, 4], F32); rs = P.sb("rs", [128, 4], F32)
                sq = P.sb("sq", [128, DM], BF16); xn = P.sb("xn", [128, 4, DM], BF16)
                tmpf = P.sb("tmpf", [128, 8, 128], F32)
                hT = P.sb("hT", [128, 8, 512], BF16)
                pT = P.ps("pT", [128, 8, 128], BF16)
                pU = [P.ps("pU%d" % i, [128, 512], F32) for i in range(2)]
                pS = P.ps("pS", [128, 512], F32)
                pSm = P.ps("pSm", [128, 512], F32)
                pD = P.ps("pD", [128, 512], F32)
                pSmv = lambda: pSm[:].rearrange("p (a b) -> p a b", b=16)
                pDv = lambda: pD[:].rearrange("p (h c) -> p h c", c=128)
                pTr = P.ps("pTr", [128, 8, 128], BF16)
                pTok = P.ps("pTok", [128, 512], F32)
                PUS = [pU[0], pU[1], pSm, pD, pTok]
                upad = [P.sb("upad%d" % i, [128, 515], F32) for i in range(2)]
                acc = [P.sb("acc%d" % i, [128, 512], F32) for i in range(2)]
                act = [P.sb("act%d" % i, [128, 512], F32) for i in range(4)]
                sqq = [P.sb("sqq%d" % i, [128, 512], F32) for i in range(4)]
                rinv = [P.sb("rinv%d" % i, [128, 512], F32) for i in range(2)]
                halo = P.sb("halo", [128, 12, 3], F32)
                kT = P.sb("kT", [128, 4, 4, 128], BF16)
                qT = P.sb("qT", [128, 4, 4, 128], BF16)
                vT = P.sb("vT", [128, 4, 4, 128], BF16)
                kbT = P.sb("kbT", [128, 4, 4, 128], BF16)
                qgT = P.sb("qgT", [128, 4, 4, 128], BF16)
                kbg = P.sb("kbg", [128, 4, 4, 128], BF16)
                kdec = P.sb("kdec", [128, 4, 4, 128], BF16)
                kb = P.sb("kb", [128, 4, 128], BF16)
                qg = P.sb("qg", [128, 4, 128], BF16)
                vb = P.sb("vb", [128, 4, 4, 128], BF16)
                dec = P.sb("dec", [128, 4, 4, 128], F32)
                gatet = P.sb("gatet", [128, 4, 512], BF16)
                dvt = P.sb("dvt", [128, 4, 512], BF16)
                dqkT = P.sb("dqkT", [128, 8, 512], BF16)
                braw = P.sb("braw", [128, 4, 8], F32)
                beta = P.sb("beta", [128, 4, 4], F32)
                gg = P.sb("gg", [128, 4, 4], F32)
                gcl = P.sb("gcl", [128, 4, 8], F32)
                ngc = P.sb("ngc", [128, 4, 4], F32)
                egc = P.sb("egc", [128, 4, 4], F32)
                bgs = P.sb("bgs", [128, 4, 4], F32)
                qsc = P.sb("qsc", [128, 4, 4], F32)
                kds = P.sb("kds", [128, 4, 4], F32)
                egl = P.sb("egl", [128, 4, 4], F32)
                gbc = P.sb("gbc", [128, 4, 128], F32)

                blocks = [(s, blk) for s in range(2) for blk in range(8)]

                def load_x(i):
                    s, blk = blocks[i]
                    xt = xts[i % 2]
                    S.dma("sync", lambda e: e.dma_start(
                        out=xt[:], in_=xsrc[s, blk * 512:(blk + 1) * 512, :].rearrange("(a p) d -> p a d", p=128)), xt, writes=[xt])

                load_x(0)
                AB = {}
                for b in range(2):
                    AB[b] = make_AB(P, l, b, pp, PP_NMIX, 0, 1, "mix%d" % b)
                cc = 0
                for i, (s, blk) in enumerate(blocks):
                    xt = xts[i % 2]
                    if i + 1 < len(blocks):
                        load_x(i + 1)
                    A, Bsh = AB[s]
                    if blk == 0:
                        S.op("gpsimd", lambda e: e.memset(halo[:], 0.0), writes=[halo])
                    norm_to_hT(P, xt, 4, A, Bsh, hT, (ss, rs, sq, xn, tmpf), pT, None)

                    for sub in range(4):
                        for kc in range(8):
                            S.op("tensor", lambda e, sub=sub, kc=kc: e.matmul(pSmv()[:, sub, 0:8], lhsT=hT[:, kc, sub * 128:(sub + 1) * 128],
                                                                            rhs=Wba[:, kc, :], start=(kc == 0), stop=(kc == 7)),
                                 reads=[hT, Wba], writes=[pSm])
                    S.op("vector", lambda e: e.tensor_copy(braw[:], pSmv()[:, 0:4, 0:8]), reads=[pSm], writes=[braw])
                    S.op("scalar", lambda e: e.activation(beta[:], braw[:, :, 0:4], AF.Sigmoid), reads=[braw], writes=[beta])
                    S.op("vector", lambda e: e.tensor_tensor(gg[:], braw[:, :, 4:8], pp[:, None, PP_DTB:PP_DTB + 4].to_broadcast([128, 4, 4]),
                                                             op=ALU.add), reads=[braw, pp], writes=[gg])
                    S.op("scalar", lambda e: e.activation(gg[:], gg[:], AF.Exp), reads=[gg], writes=[gg])
                    S.op("scalar", lambda e: e.activation(gg[:], gg[:], AF.Ln, bias=onec()), reads=[gg, cst], writes=[gg])
                    S.op("vector", lambda e: e.tensor_tensor(gg[:], gg[:], negA[:, None, :].to_broadcast([128, 4, 4]), op=ALU.mult),
                         reads=[gg, negA], writes=[gg])
                    pending = []
                    for fc in range(12):
                        pu = PUS[cc % 5]; up = upad[cc % 2]; ac = acc[cc % 2]; at = act[cc % 4]
                        sqt = sqq[cc % 4]; rv = rinv[cc % 2]
                        cc += 1
                        for kc in range(8):
                            S.op("tensor", lambda e, kc=kc, fc=fc, pu=pu: e.matmul(pu[:], lhsT=Wqkv[:, kc, fc * 128:(fc + 1) * 128],
                                                                                    rhs=hT[:, kc, :], start=(kc == 0), stop=(kc == 7)),
                                 reads=[Wqkv, hT], writes=[pu])
                        while len(pending) > 2:
                            pending.pop(0)()
                        S.op("vector", lambda e, fc=fc, up=up: e.tensor_copy(up[:, 0:3], halo[:, fc, :]), reads=[halo], writes=[up])
                        S.op("scalar", lambda e, up=up, pu=pu: e.copy(up[:, 3:515], pu[:]), reads=[pu], writes=[up])
                        S.op("gpsimd", lambda e, fc=fc, up=up: e.tensor_copy(halo[:, fc, :], up[:, 512:515]), reads=[up], writes=[halo])
                        cw = lambda j, fc=fc: pp[:, PP_CONVW + fc * 4 + j:PP_CONVW + fc * 4 + j + 1]
                        S.op("vector", lambda e, up=up, ac=ac, cw=cw: e.tensor_scalar_mul(ac[:], up[:, 3:515], cw(3)), reads=[up, pp], writes=[ac])
                        for j in (2, 1, 0):
                            S.op("vector",
                                 lambda e, up=up, ac=ac, cw=cw, j=j: e.scalar_tensor_tensor(ac[:], in0=up[:, j:j + 512], scalar=cw(j), in1=ac[:],
                                                                                           op0=ALU.mult, op1=ALU.add),
                                 reads=[up, pp, ac], writes=[ac])
                        kind, h = fc // 4, fc % 4
                        if kind == 2:
                            S.op("scalar", lambda e, ac=ac, h=h: e.activation(vT[:, :, h, :], ac[:].rearrange("p (a t) -> p a t", a=4), AF.Silu),
                                 reads=[ac], writes=[vT])
                        else:
                            dst = qT if kind == 0 else kT
                            S.op("scalar", lambda e, ac=ac, at=at: e.activation(at[:], ac[:], AF.Silu), reads=[ac], writes=[at])
                            S.op("gpsimd", lambda e, at=at, sqt=sqt: e.tensor_tensor(sqt[:], at[:], at[:], op=ALU.mult), reads=[at], writes=[sqt])

                            def fin(sqt=sqt, rv=rv, at=at, dst=dst, h=h):
                                S.op("tensor", lambda e: e.matmul(pS[:], lhsT=ONESF(), rhs=sqt[:], start=True, stop=True),
                                     reads=[cst, sqt], writes=[pS])
                                S.op("scalar", lambda e: e.activation(rv[:], pS[:], AF.Ln, bias=epsc()), reads=[pS, cst], writes=[rv])
                                S.op("scalar", lambda e: e.activation(rv[:], rv[:], AF.Exp, scale=-0.5), reads=[rv], writes=[rv])
                                S.op("vector", lambda e: e.tensor_tensor(
                                    dst[:, :, h, :], at[:].rearrange("p (a t) -> p a t", a=4), rv[:].rearrange("p (a t) -> p a t", a=4), op=ALU.mult),
                                    reads=[at, rv], writes=[dst])
                            pending.append(fin)

                    while pending:
                        pending.pop(0)()
                    for sub in range(4):
                        S.op("tensor", lambda e, sub=sub: e.matmul(pSmv()[:, sub, 8:12], lhsT=TRI(), rhs=gg[:, sub, :], start=True, stop=True),
                             reads=[cst, gg], writes=[pSm])
                        S.op("tensor", lambda e, sub=sub: e.matmul(pSmv()[:, sub, 12:16], lhsT=ONESF(), rhs=gg[:, sub, :], start=True, stop=True),
                             reads=[cst, gg], writes=[pSm])
                    S.op("vector", lambda e: e.tensor_copy(gcl[:], pSmv()[:, 0:4, 8:16]), reads=[pSm], writes=[gcl])
                    S.op("vector", lambda e: e.tensor_scalar_mul(ngc[:], gcl[:, :, 0:4], -1.0), reads=[gcl], writes=[ngc])
                    S.op("scalar", lambda e: e.activation(egc[:], gcl[:, :, 0:4], AF.Exp), reads=[gcl], writes=[egc])
                    S.op("scalar", lambda e: e.activation(egl[:], gcl[:, :, 4:8], AF.Exp), reads=[gcl], writes=[egl])
                    S.op("vector", lambda e: e.tensor_tensor(kds[:], gcl[:, :, 4:8], gcl[:, :, 0:4], op=ALU.subtract), reads=[gcl], writes=[kds])
                    S.op("scalar", lambda e: e.activation(kds[:], kds[:], AF.Exp), reads=[kds], writes=[kds])
                    S.op("vector", lambda e: e.tensor_tensor(bgs[:], beta[:], egc[:], op=ALU.mult), reads=[beta, egc], writes=[bgs])
                    S.op("vector", lambda e: e.tensor_scalar_mul(qsc[:], egc[:], 128.0 ** -0.5), reads=[egc], writes=[qsc])
                    S.dma("sync", lambda e, s=s, blk=blk: e.dma_start(out=EGL[s, blk * 4:(blk + 1) * 4, :, :].rearrange("n p h -> p n h"),
                                                                       in_=egl[:]), egl, reads=[egl])
                    for sub in range(4):
                        for h in range(4):
                            S.op("vector", lambda e, sub=sub, h=h: e.tensor_copy(gbc[:, h, :], gg[:, sub, h:h + 1].to_broadcast([128, 128])),
                                 reads=[gg], writes=[gbc])
                        for h in range(4):
                            S.op("tensor", lambda e, h=h: e.matmul(pDv()[:, h, :], lhsT=gbc[:, h, :], rhs=TRI(), start=True, stop=False),
                                 reads=[gbc, cst], writes=[pD])
                            S.op("tensor", lambda e, h=h: e.matmul(pDv()[:, h, :], lhsT=identF(), rhs=cst[:, C_MASKNEG:C_MASKNEG + 128],
                                                                   start=False, stop=True), reads=[cst], writes=[pD])
                        for h in range(4):
                            S.op("scalar", lambda e, sub=sub, h=h: e.activation(dec[:, sub, h, :], pDv()[:, h, :], AF.Exp,
                                                                                  bias=ngc[:, sub, h:h + 1]),
                                 reads=[pD, ngc], writes=[dec])
                    S.dma("sync", lambda e, s=s, blk=blk: e.dma_start(
                        out=DEC[s, blk * 4:(blk + 1) * 4, :, :, :].rearrange("n p h c -> p n h c"), in_=dec[:]), dec, reads=[dec])

                    for fc in range(8):
                        pu = PUS[cc % 5]; sqt = sqq[cc % 4]; rv = rinv[cc % 2]
                        cc += 1
                        for kc in range(8):
                            S.op("tensor", lambda e, kc=kc, fc=fc, pu=pu: e.matmul(pu[:], lhsT=Wdqk[:, kc, fc * 128:(fc + 1) * 128],
                                                                                    rhs=hT[:, kc, :], start=(kc == 0), stop=(kc == 7)),
                                 reads=[Wdqk, hT], writes=[pu])
                        while len(pending) > 2:
                            pending.pop(0)()
                        S.op("scalar", lambda e, pu=pu, sqt=sqt: e.activation(sqt[:], pu[:], AF.Square), reads=[pu], writes=[sqt])
                        gain = qgain[:, 0:1] if fc < 4 else pp[:, PP_KN:PP_KN + 1]

                        def fin2(pu=pu, sqt=sqt, rv=rv, fc=fc, gain=gain):
                            S.op("tensor", lambda e: e.matmul(pS[:], lhsT=cst[:, C_BLK64:C_BLK64 + 128], rhs=sqt[:], start=True, stop=True),
                                 reads=[cst, sqt], writes=[pS])
                            S.op("scalar", lambda e: e.activation(rv[:], pS[:], AF.Ln, bias=epsc(), scale=1.0 / 64), reads=[pS, cst], writes=[rv])
                            S.op("scalar", lambda e: e.activation(rv[:], rv[:], AF.Exp, scale=-0.5), reads=[rv], writes=[rv])
                            S.op("vector", lambda e: e.scalar_tensor_tensor(
                                dqkT[:, fc, :], in0=pu[:], scalar=gain, in1=rv[:], op0=ALU.mult, op1=ALU.mult),
                                reads=[pu, rv, qgain, pp], writes=[dqkT])
                        pending.append(fin2)
                    while pending:
                        pending.pop(0)()
                    S.dma("sync", lambda e, s=s, blk=blk: e.dma_start(out=DQT[s, :, :, blk * 512:(blk + 1) * 512].rearrange("h p t -> p h t"),
                                                                       in_=dqkT[:, 0:4, :]), dqkT, reads=[dqkT])
                    S.dma("sync", lambda e, s=s, blk=blk: e.dma_start(out=DKT[s, :, :, blk * 512:(blk + 1) * 512].rearrange("h p t -> p h t"),
                                                                       in_=dqkT[:, 4:8, :]), dqkT, reads=[dqkT])

                    for W, dstt, fn in ((Wgate, gatet, AF.Silu), (Wdv, dvt, AF.Copy)):
                        for sub in range(4):
                            for kc in range(8):
                                S.op("tensor", lambda e, sub=sub, kc=kc, W=W: e.matmul(pTok[:], lhsT=hT[:, kc, sub * 128:(sub + 1) * 128],
                                                                                        rhs=W[:, kc, :], start=(kc == 0), stop=(kc == 7)),
                                     reads=[hT, W], writes=[pTok])
                            S.op("scalar", lambda e, sub=sub, dstt=dstt, fn=fn: e.activation(dstt[:, sub, :], pTok[:], fn),
                                 reads=[pTok], writes=[dstt])
                        dd = GATE if dstt is gatet else DV
                        S.dma("sync", lambda e, dd=dd, dstt=dstt, s=s, blk=blk: e.dma_start(
                            out=dd[s, blk * 4:(blk + 1) * 4, :, :].rearrange("n p f -> p n f"), in_=dstt[:]), dstt, reads=[dstt])
                    for sub in range(4):
                        for h in range(4):
                            S.op("tensor", lambda e, sub=sub, h=h: e.transpose(pTr[:, h, :], kT[:, sub, h, :], identB[:]),
                                 reads=[kT, identB], writes=[pTr])
                        S.op("vector", lambda e, sub=sub: e.tensor_tensor(kbg[:, sub, :, :], pTr[:, 0:4, :], bgs[:, sub, :, None].to_broadcast([128, 4, 128]),
                                                                          op=ALU.mult), reads=[pTr, bgs], writes=[kbg])
                        S.op("vector", lambda e, sub=sub: e.tensor_tensor(kdec[:, sub, :, :], pTr[:, 0:4, :], kds[:, sub, :, None].to_broadcast([128, 4, 128]),
                                                                          op=ALU.mult), reads=[pTr, kds], writes=[kdec])
                        S.op("vector", lambda e, sub=sub: e.tensor_tensor(kb[:], pTr[:, 0:4, :], beta[:, sub, :, None].to_broadcast([128, 4, 128]),
                                                                          op=ALU.mult), reads=[pTr, beta], writes=[kb])
                        for h in range(4):
                            S.op("tensor", lambda e, h=h: e.transpose(pTr[:, h, :], kb[:, h, :], identB[:]), reads=[kb, identB], writes=[pTr])
                        S.op("scalar", lambda e, sub=sub: e.copy(kbT[:, sub, :, :], pTr[:, 0:4, :]), reads=[pTr], writes=[kbT])
                        for h in range(4):
                            S.op("tensor", lambda e, sub=sub, h=h: e.transpose(pTr[:, h, :], qT[:, sub, h, :], identB[:]),
                                 reads=[qT, identB], writes=[pTr])
                        S.op("vector", lambda e, sub=sub: e.tensor_tensor(qg[:], pTr[:, 0:4, :], qsc[:, sub, :, None].to_broadcast([128, 4, 128]),
                                                                          op=ALU.mult), reads=[pTr, qsc], writes=[qg])
                        for h in range(4):
                            S.op("tensor", lambda e, h=h: e.transpose(pTr[:, h, :], qg[:, h, :], identB[:]), reads=[qg, identB], writes=[pTr])
                        S.op("scalar", lambda e, sub=sub: e.copy(qgT[:, sub, :, :], pTr[:, 0:4, :]), reads=[pTr], writes=[qgT])
                        for h in range(4):
                            S.op("tensor", lambda e, sub=sub, h=h: e.transpose(pTr[:, h, :], vT[:, sub, h, :], identB[:]),
                                 reads=[vT, identB], writes=[pTr])
                        S.op("vector", lambda e, sub=sub: e.tensor_tensor(vb[:, sub, :, :], pTr[:, 0:4, :], beta[:, sub, :, None].to_broadcast([128, 4, 128]),
                                                                          op=ALU.mult), reads=[pTr, beta], writes=[vb])
                    for dst, src in ((KT, kT), (QT, qT), (QGT, qgT), (KBT, kbT), (KBG, kbg), (KDEC, kdec), (VB, vb)):
                        S.dma("sync", lambda e, dst=dst, src=src, s=s, blk=blk: e.dma_start(
                            out=dst[s, blk * 4:(blk + 1) * 4, :, :, :].rearrange("n p h t -> p n h t"), in_=src[:]), src, reads=[src])

            if upto == "C":
                break
            with Phase(S, "D%d" % l) as P:
                pp = P.sb("pp", [128, NPP], F32)
                S.dma("sync", lambda e, l=l: e.dma_start(out=pp[:], in_=pp_in[l, :, :]), pp, writes=[pp])
                NSLOT = 3
                names = ("kT", "qT", "qgT", "kbT", "kbg", "kdec", "vb")
                srcs = dict(kT=KT, qT=QT, qgT=QGT, kbT=KBT, kbg=KBG, kdec=KDEC, vb=VB)
                slots = {}
                for s in range(2):
                    for k in range(NSLOT):
                        d_ = {nm: P.sb("%s_%d_%d" % (nm, s, k), [128, 4, 128], BF16) for nm in names}
                        d_["dec"] = P.sb("dec_%d_%d" % (s, k), [128, 4, 128], F32)
                        d_["egl"] = P.sb("egl_%d_%d" % (s, k), [128, 4], F32)
                        d_["gate"] = P.sb("gate_%d_%d" % (s, k), [128, 4, 128], BF16)
                        slots[(s, k)] = d_
                strict = cst[:, None, C_STRICT:C_STRICT + 128].to_broadcast([128, 4, 128])
                identq = cst[:, None, C_IDENT:C_IDENT + 128].to_broadcast([128, 4, 128])
                onormb = pp[:, None, PP_ONORM:PP_ONORM + 128].to_broadcast([128, 4, 128])
                pre = [P.ps("pre%d" % i, [128, 4, 128], F32) for i in range(2)]
                pinv = [P.ps("pinv%d" % i, [128, 4, 128], F32) for i in range(2)]
                pW = P.ps("pW", [128, 4, 128], F32)
                pO = P.ps("pO", [128, 4, 128], F32)
                pSt = P.ps("pSt", [128, 4, 128], F32)
                pTr = P.ps("pTr", [128, 8, 128], BF16)
                st = {}
                for s in range(2):
                    st[s] = dict(
                        Y=[P.sb("Y%d_%d" % (s, i), [128, 4, 128], F32) for i in range(2)],
                        X=[P.sb("X%d_%d" % (s, i), [128, 4, 128], F32) for i in range(2)],
                        Q=P.sb("Q%d" % s, [128, 4, 128], F32), Qb=P.sb("Qb%d" % s, [128, 4, 128], BF16),
                        P=P.sb("Pm%d" % s, [128, 4, 128], F32), Yf=P.sb("Yf%d" % s, [128, 4, 128], F32), Xf=P.sb("Xf%d" % s, [128, 4, 128], F32),
                        t1=P.sb("t1%d" % s, [128, 4, 128], F32),
                        aT=P.sb("aT%d" % s, [128, 4, 128], BF16), u=P.sb("u%d" % s, [128, 4, 128], F32),
                        wT=P.sb("wT%d" % s, [128, 4, 128], BF16), vn=P.sb("vn%d" % s, [128, 4, 128], BF16),
                        Sf=P.sb("Sf%d" % s, [128, 4, 128], F32), Sb=P.sb("Sb%d" % s, [128, 4, 128], BF16),
                        osq=P.sb("osq%d" % s, [128, 4, 128], F32), oss=P.sb("oss%d" % s, [128, 4], F32),
                        o1=P.sb("o1%d" % s, [128, 4, 128], F32), g2=P.sb("g2%d" % s, [128, 4, 128], F32),
                        y=P.sb("y%d" % s, [128, 4, 128], BF16),
                        oT=P.sb("oT%d" % s, [128, 4, 512], BF16),
                    )
                    S.op("gpsimd", lambda e, s=s: e.memset(st[s]["Sf"][:], 0.0), writes=[st[s]["Sf"]])
                    S.op("gpsimd", lambda e, s=s: e.memset(st[s]["Sb"][:], 0.0), writes=[st[s]["Sb"]])

                def dbg(nm, tile, s, n, nn=0):
                    if debug and s == 0 and n == nn and l == 0:
                        S.dma("sync", lambda e: e.dma_start(out=DBG[nm][:, :, :], in_=tile[:]), tile, reads=[tile])

                def loadD(s, n):
                    sl = slots[(s, n % NSLOT)]
                    for nm in names:
                        S.dma("sync", lambda e, nm=nm, sl=sl: e.dma_start(out=sl[nm][:], in_=srcs[nm][s, n, :, :, :]), sl[nm], writes=[sl[nm]])
                    S.dma("sync", lambda e, sl=sl: e.dma_start(out=sl["dec"][:], in_=DEC[s, n, :, :, :]), sl["dec"], writes=[sl["dec"]])
                    S.dma("sync", lambda e, sl=sl: e.dma_start(out=sl["egl"][:], in_=EGL[s, n, :, :]), sl["egl"], writes=[sl["egl"]])
                    S.dma("sync", lambda e, sl=sl: e.dma_start(out=sl["gate"][:], in_=GATE[s, n, :, :].rearrange("p (h e) -> p h e", h=4)),
                          sl["gate"], writes=[sl["gate"]])

                for s in range(2):
                    loadD(s, 0)
                    loadD(s, 1)
                pk = [0, 0]

                def mm4(out, lhs, rhs, reads, acc=None):
                    for h in range(4):
                        S.op("tensor", lambda e, h=h: e.matmul(out[:, h, :], lhsT=lhs[:, h, :], rhs=rhs[:, h, :],
                                                               start=(acc in (None, "start")), stop=(acc in (None, "stop"))),
                             reads=reads, writes=[out])

                for n in range(NCH):
                    SL = {s: slots[(s, n % NSLOT)] for s in range(2)}
                    if n + 2 < NCH:
                        for s in range(2):
                            loadD(s, n + 2)
                    for s in range(2):
                        sl, q = SL[s], st[s]
                        pg = pre[s]
                        q["pg"] = pg
                        mm4(pg, sl["kT"], sl["kbT"], [sl["kT"], sl["kbT"]])
                    bc = lambda c0: cst[:, None, c0:c0 + 128].to_broadcast([128, 4, 128])
                    for s in range(2):
                        sl, q = SL[s], st[s]
                        pg = q["pg"]
                        S.op("vector", lambda e, q=q, sl=sl, pg=pg: e.scalar_tensor_tensor(q["t1"][:], in0=pg[:], scalar=-1.0, in1=sl["dec"][:],
                                                                                        op0=ALU.mult, op1=ALU.mult), reads=[pg, sl["dec"]], writes=[q["t1"]])
                        S.op("gpsimd", lambda e, q=q: e.tensor_tensor(q["Yf"][:], q["t1"][:], strict, op=ALU.mult),
                             reads=[q["t1"], cst], writes=[q["Yf"]])
                    for s in range(2):
                        q = st[s]
                        for h in range(4):
                            S.op("tensor", lambda e, h=h, q=q, pi=pinv[s]: e.transpose(pi[:, h, :], q["Yf"][:, h, :], identF()),
                                 reads=[q["Yf"], cst], writes=[pinv[s]])
                    for s in range(2):
                        q = st[s]
                        S.op("scalar", lambda e, q=q, pi_=pinv[s]: e.copy(q["Xf"][:], pi_[:]), reads=[pinv[s]], writes=[q["Xf"]])
                        S.op("gpsimd", lambda e, q=q: e.tensor_tensor(q["Y"][0][:], q["Yf"][:], bc(C_BM16), op=ALU.mult), reads=[q["Yf"], cst], writes=[q["Y"][0]])
                        S.op("gpsimd", lambda e, q=q: e.tensor_tensor(q["X"][0][:], q["Xf"][:], bc(C_BM16), op=ALU.mult), reads=[q["Xf"], cst], writes=[q["X"][0]])
                        S.op("gpsimd", lambda e, q=q: e.tensor_tensor(q["Q"][:], q["Y"][0][:], identq, op=ALU.add), reads=[q["Y"][0], cst], writes=[q["Q"]])
                        S.op("gpsimd", lambda e, q=q: e.tensor_tensor(q["P"][:], q["X"][0][:], identq, op=ALU.add), reads=[q["X"][0], cst], writes=[q["P"]])
                    for lev in range(1, 4):
                        a, b = (lev - 1) % 2, lev % 2
                        for s in range(2):
                            q = st[s]
                            mm4(pinv[s], q["Y"][a], q["X"][a], [q["Y"][a], q["X"][a]])
                            mm4(pre[s], q["X"][a], q["Y"][a], [q["Y"][a], q["X"][a]])
                        for s in range(2):
                            q = st[s]
                            S.op("scalar", lambda e, q=q, b=b, px_=pinv[s]: e.copy(q["X"][b][:], px_[:]), reads=[pinv[s]], writes=[q["X"][b]])
                            S.op("vector", lambda e, q=q, b=b, py_=pre[s]: e.tensor_copy(q["Y"][b][:], py_[:]), reads=[pre[s]], writes=[q["Y"][b]])
                        for s in range(2):
                            q = st[s]
                            mm4(pinv[s], q["X"][b], q["Q"], [q["X"][b], q["Q"]])
                            mm4(pre[s], q["Y"][b], q["P"], [q["Y"][b], q["P"]])
                        for s in range(2):
                            q = st[s]
                            S.op("vector", lambda e, q=q, pq_=pinv[s]: e.tensor_tensor(q["Q"][:], q["Q"][:], pq_[:], op=ALU.add),
                                 reads=[q["Q"], pinv[s]], writes=[q["Q"]])
                            S.op("vector", lambda e, q=q, pp_=pre[s]: e.tensor_tensor(q["P"][:], q["P"][:], pp_[:], op=ALU.add),
                                 reads=[q["P"], pre[s]], writes=[q["P"]])
                    for mi, coff in enumerate((C_OFF32, C_OFF64, C_OFF128)):
                        lastm = (mi == 2)
                        for s in range(2):
                            q = st[s]
                            S.op("gpsimd", lambda e, q=q, coff=coff: e.tensor_tensor(q["Y"][0][:], q["Yf"][:], bc(coff), op=ALU.mult), reads=[q["Yf"], cst], writes=[q["Y"][0]])
                            S.op("gpsimd", lambda e, q=q, coff=coff: e.tensor_tensor(q["X"][0][:], q["Xf"][:], bc(coff), op=ALU.mult), reads=[q["Xf"], cst], writes=[q["X"][0]])
                        for s in range(2):
                            q = st[s]
                            mm4(pinv[s], q["X"][0], q["Q"], [q["X"][0], q["Q"]])
                            if not lastm:
                                mm4(pre[s], q["Y"][0], q["P"], [q["Y"][0], q["P"]])
                        for s in range(2):
                            q = st[s]
                            S.op("scalar", lambda e, q=q, p_=pinv[s]: e.copy(q["X"][1][:], p_[:]), reads=[pinv[s]], writes=[q["X"][1]])
                            if not lastm:
                                S.op("vector", lambda e, q=q, p_=pre[s]: e.tensor_copy(q["Y"][1][:], p_[:]), reads=[pre[s]], writes=[q["Y"][1]])
                        for s in range(2):
                            q = st[s]
                            mm4(pinv[s], q["P"], q["X"][1], [q["P"], q["X"][1]])
                            if not lastm:
                                mm4(pre[s], q["Q"], q["Y"][1], [q["Q"], q["Y"][1]])
                        for s in range(2):
                            q = st[s]
                            S.op("vector", lambda e, q=q, p_=pinv[s]: e.tensor_tensor(q["Q"][:], q["Q"][:], p_[:], op=ALU.add),
                                 reads=[q["Q"], pinv[s]], writes=[q["Q"]])
                            if not lastm:
                                S.op("vector", lambda e, q=q, p_=pre[s]: e.tensor_tensor(q["P"][:], q["P"][:], p_[:], op=ALU.add),
                                     reads=[q["P"], pre[s]], writes=[q["P"]])
                    for s in range(2):
                        sl, q = SL[s], st[s]
                        S.op("scalar", lambda e, q=q: e.copy(q["Qb"][:], q["Q"][:]), reads=[q["Q"]], writes=[q["Qb"]])
                        dbg("Q", q["Q"], s, n)
                        pa = pre[s]
                        q["pa"] = pa
                        mm4(pa, sl["kT"], sl["qT"], [sl["kT"], sl["qT"]])
                    for s in range(2):
                        sl, q = SL[s], st[s]
                        S.op("vector", lambda e, q=q, sl=sl, pa_=q["pa"]: e.scalar_tensor_tensor(q["aT"][:], in0=pa_[:], scalar=128.0 ** -0.5, in1=sl["dec"][:],
                                                                                     op0=ALU.mult, op1=ALU.mult), reads=[q["pa"], sl["dec"]], writes=[q["aT"]])
                        pu_ = pinv[s]
                        q["pu"] = pu_
                        mm4(pu_, q["Qb"], sl["vb"], [q["Qb"], sl["vb"]])
                    for s in range(2):
                        sl, q = SL[s], st[s]
                        S.op("scalar", lambda e, q=q, pu_=q["pu"]: e.copy(q["u"][:], pu_[:]), reads=[q["pu"]], writes=[q["u"]])
                        dbg("U", q["u"], s, n)
                        pw_ = pre[s]
                        q["pw"] = pw_
                        mm4(pw_, sl["kbg"], q["Qb"], [q["Qb"], sl["kbg"]])
                    for s in range(2):
                        q = st[s]
                        S.op("scalar", lambda e, q=q, pw_=q["pw"]: e.copy(q["wT"][:], pw_[:]), reads=[q["pw"]], writes=[q["wT"]])
                    for s in range(2):
                        sl, q = SL[s], st[s]
                        mm4(pW, q["wT"], q["Sb"], [q["wT"], q["Sb"]])
                        S.op("vector", lambda e, q=q: e.tensor_tensor(q["vn"][:], q["u"][:], pW[:], op=ALU.subtract),
                             reads=[q["u"], pW], writes=[q["vn"]])
                        if debug and s == 0 and n == 1 and l == 0:
                            S.op("vector", lambda e, q=q: e.tensor_copy(q["o1"][:], pW[:]), reads=[pW], writes=[q["o1"]])
                            dbg("PW1", q["o1"], s, n, 1)
                        for h in range(4):
                            S.op("tensor", lambda e, h=h, q=q, sl=sl: e.matmul(pO[:, h, :], lhsT=sl["qgT"][:, h, :], rhs=q["Sb"][:, h, :], start=True, stop=False),
                                 reads=[sl["qgT"], q["Sb"]], writes=[pO])
                            S.op("tensor", lambda e, h=h, q=q: e.matmul(pO[:, h, :], lhsT=q["aT"][:, h, :], rhs=q["vn"][:, h, :], start=False, stop=True),
                                 reads=[q["aT"], q["vn"]], writes=[pO])
                        mm4(pSt, sl["kdec"], q["vn"], [sl["kdec"], q["vn"]])
                        S.op("gpsimd", lambda e, q=q, sl=sl: e.tensor_tensor(q["Sf"][:], q["Sf"][:], sl["egl"][:, :, None].to_broadcast([128, 4, 128]),
                                                                            op=ALU.mult), reads=[q["Sf"], sl["egl"]], writes=[q["Sf"]])
                        S.op("vector", lambda e, q=q: e.tensor_tensor(q["Sf"][:], q["Sf"][:], pSt[:], op=ALU.add), reads=[q["Sf"], pSt], writes=[q["Sf"]])
                        S.op("scalar", lambda e, q=q: e.copy(q["Sb"][:], q["Sf"][:]), reads=[q["Sf"]], writes=[q["Sb"]])
                        dbg("S0", q["Sf"], s, n)
                        dbg("U1", q["u"], s, n, 1)
                        S.op("scalar", lambda e, q=q: e.activation(q["osq"][:], pO[:], AF.Square), reads=[pO], writes=[q["osq"]])
                        S.op("vector", lambda e, q=q: e.reduce_sum(q["oss"][:], q["osq"][:], axis=mybir.AxisListType.X), reads=[q["osq"]], writes=[q["oss"]])
                        S.op("scalar", lambda e, q=q: e.activation(q["oss"][:], q["oss"][:], AF.Ln, bias=epsc(), scale=1.0 / 128), reads=[q["oss"], cst], writes=[q["oss"]])
                        S.op("scalar", lambda e, q=q: e.activation(q["oss"][:], q["oss"][:], AF.Exp, scale=-0.5), reads=[q["oss"]], writes=[q["oss"]])
                        S.op("vector", lambda e, q=q: e.tensor_tensor(q["o1"][:], pO[:], q["oss"][:, :, None].to_broadcast([128, 4, 128]), op=ALU.mult),
                             reads=[pO, q["oss"]], writes=[q["o1"]])
                        S.op("gpsimd", lambda e, q=q, sl=sl: e.tensor_tensor(q["g2"][:], sl["gate"][:], onormb, op=ALU.mult), reads=[sl["gate"], pp], writes=[q["g2"]])
                        S.op("gpsimd", lambda e, q=q: e.tensor_tensor(q["y"][:], q["o1"][:], q["g2"][:], op=ALU.mult), reads=[q["o1"], q["g2"]], writes=[q["y"]])
                        for h in range(4):
                            S.op("tensor", lambda e, h=h, q=q: e.transpose(pTr[:, h, :], q["y"][:, h, :], identB[:]), reads=[q["y"], identB], writes=[pTr])
                        j = n % 4
                        S.op("scalar", lambda e, q=q, j=j: e.copy(q["oT"][:, :, j * 128:(j + 1) * 128], pTr[:, 0:4, :]), reads=[pTr], writes=[q["oT"]])
                        if j == 3:
                            t0 = (n - 3) * 128
                            S.dma("sync", lambda e, q=q, s=s, t0=t0: e.dma_start(out=OT[s, 0:4, :, t0:t0 + 512].rearrange("h p t -> p h t"), in_=q["oT"][:]),
                                  q["oT"], reads=[q["oT"]])
            if upto == "D":
                break
            with Phase(S, "E%d" % l) as P:
                pp = P.sb("pp", [128, NPP], F32)
                S.dma("sync", lambda e, l=l: e.dma_start(out=pp[:], in_=pp_in[l, :, :]), pp, writes=[pp])
                lam_init = 0.8 - 0.6 * math.exp(-0.3 * l)
                lamr = P.sb("lamr", [1, 4, 64], F32)
                S.dma("sync", lambda e, l=l: e.dma_start(out=lamr[:], in_=lam_in[l:l + 1, :, :]), lamr, writes=[lamr])
                lprod = P.sb("lprod", [1, 2, 64], F32)
                lsum = P.sb("lsum", [1, 2], F32)
                nlam1 = P.sb("nlam1", [1, 1], F32)
                nlam = P.sb("nlam", [128, 1], F32)
                subg = P.sb("subg", [128, 1], F32)
                b31 = P.sb("b31", [128, 4], F32)
                S.dma("sync", lambda e: e.dma_start(out=b31[:], in_=b31_in[:, :]), b31, writes=[b31])
                S.op("vector", lambda e: e.tensor_tensor(lprod[:], lamr[:, 0:4:2, :], lamr[:, 1:4:2, :], op=ALU.mult), reads=[lamr], writes=[lprod])
                S.op("vector", lambda e: e.reduce_sum(lsum[:], lprod[:], axis=mybir.AxisListType.X), reads=[lprod], writes=[lsum])
                S.op("scalar", lambda e: e.activation(lsum[:], lsum[:], AF.Exp), reads=[lsum], writes=[lsum])
                S.op("vector", lambda e: e.scalar_tensor_tensor(nlam1[:], in0=lsum[:, 1:2], scalar=-lam_init, in1=lsum[:, 0:1],
                                                                op0=ALU.add, op1=ALU.subtract), reads=[lsum], writes=[nlam1])
                pS0 = [P.ps("pS0_%d" % i, [128, 512], F32) for i in range(2)]
                pS1 = [P.ps("pS1_%d" % i, [128, 512], F32) for i in range(2)]
                pO0 = P.ps("pO0", [128, 512], F32); pO1 = P.ps("pO1", [128, 512], F32)
                pZ0 = P.ps("pZ0", [128, 512], F32); pZ1 = P.ps("pZ1", [128, 512], F32)
                S.op("tensor", lambda e: e.matmul(pZ0[:, 0:1], lhsT=cst[0:1, C_ONES:C_ONES + 128], rhs=nlam1[:], start=True, stop=True),
                     reads=[cst, nlam1], writes=[pZ0])
                S.op("vector", lambda e: e.tensor_copy(nlam[:], pZ0[:, 0:1]), reads=[pZ0], writes=[nlam])
                S.op("vector", lambda e: e.tensor_scalar_mul(subg[:], pp[:, PP_SUBLN:PP_SUBLN + 1], 1.0 - lam_init), reads=[pp], writes=[subg])
                expB = []
                for h in range(4):
                    t = P.sb("expB%d" % h, [128, 1024], F32)
                    S.dma("sync", lambda e, h=h, t=t: e.dma_start(out=t[:], in_=tb_in[h, :, :]), t, writes=[t])
                    S.op("scalar", lambda e, t=t: e.activation(t[:], t[:], AF.Exp), reads=[t], writes=[t])
                    expB.append(t)
                qts = [P.sb("qt%d" % i, [128, T], BF16) for i in range(2)]
                kts = [P.sb("kt%d" % i, [128, T], BF16) for i in range(2)]
                vts = [P.sb("vt%d" % i, [128, NCH, 128], BF16) for i in range(2)]
                E0 = [P.sb("E0_%d" % i, [128, 512], BF16) for i in range(3)]
                E1 = [P.sb("E1_%d" % i, [128, 512], BF16) for i in range(3)]
                Ef = [P.sb("Ef_%d" % i, [128, 512], F32) for i in range(2)]
                rz0 = P.sb("rz0", [128, 512], F32); rz1 = P.sb("rz1", [128, 512], F32)
                zc0 = P.sb("zc0", [128, 512], F32); zc1 = P.sb("zc1", [128, 512], F32)
                oc0 = P.sb("oc0", [128, 512], F32); oc1 = P.sb("oc1", [128, 512], F32)
                oo = P.sb("oo", [128, 512], F32); osq = P.sb("osq", [128, 512], F32)
                rin = P.sb("rin", [128, 512], F32)
                oTs = [P.sb("oTs%d" % i, [128, 512], BF16) for i in range(2)]
                heads = [(s, h) for s in range(2) for h in range(4)]

                def loadE(i):
                    s, h = heads[i]
                    S.dma("sync", lambda e: e.dma_start(out=qts[i % 2][:], in_=DQT[s, h, :, :]), qts[i % 2], writes=[qts[i % 2]])
                    S.dma("sync", lambda e: e.dma_start(out=kts[i % 2][:], in_=DKT[s, h, :, :]), kts[i % 2], writes=[kts[i % 2]])
                    S.dma("sync", lambda e: e.dma_start(out=vts[i % 2][:], in_=DV[s, :, :, h * 128:(h + 1) * 128].rearrange("n p e -> p n e")),
                          vts[i % 2], writes=[vts[i % 2]])

                loadE(0)
                steps = []
                for i, (s, h) in enumerate(heads):
                    for j in range(8):
                        nk = 4 * j + 4
                        for ki in range(nk):
                            steps.append((i, s, h, j, ki, nk))
                cnt = {"ek": 0, "fk": 0, "ok": 0, "loaded": 0}
                live = {}

                def stageA(idx):
                    i, s, h, j, ki, nk = steps[idx]
                    qt, kt = qts[i % 2], kts[i % 2]
                    ek = cnt["ek"]; cnt["ek"] += 1
                    p0, p1 = pS0[ek % 2], pS1[ek % 2]
                    ec = cnt.get("ec", 0); cnt["ec"] = ec + 1
                    e0, e1 = E0[ec % 3], E1[ec % 3]
                    S.op("tensor", lambda e: e.matmul(p0[:], lhsT=kt[0:64, ki * 128:(ki + 1) * 128], rhs=qt[0:64, j * 512:(j + 1) * 512],
                                                      start=True, stop=True), reads=[kt, qt], writes=[p0])
                    S.op("tensor", lambda e: e.matmul(p1[:], lhsT=kt[64:128, ki * 128:(ki + 1) * 128], rhs=qt[64:128, j * 512:(j + 1) * 512],
                                                      start=True, stop=True), reads=[kt, qt], writes=[p1])
                    near = ki >= 4 * j - 1
                    if not near:
                        S.op("scalar", lambda e: e.activation(e0[:], p0[:], AF.Exp, bias=b31[:, h:h + 1]), reads=[p0, b31], writes=[e0])
                        S.op("scalar", lambda e: e.activation(e1[:], p1[:], AF.Exp, bias=b31[:, h:h + 1]), reads=[p1, b31], writes=[e1])
                    else:
                        c0 = 512 * j - 128 * ki + 384
                        for pc, ec in ((p0, e0), (p1, e1)):
                            ef = Ef[cnt["fk"] % 2]; cnt["fk"] += 1
                            S.op("scalar", lambda e, pc=pc, ef=ef: e.activation(ef[:], pc[:], AF.Exp), reads=[pc], writes=[ef])
                            S.op("vector",
                                 lambda e, ef=ef, ec=ec: e.tensor_tensor(ec[:], ef[:], expB[h][:, c0:c0 + 512], op=ALU.mult),
                                 reads=[ef, expB[h]], writes=[ec])
                    live[idx] = (e0, e1)

                def stageB(idx):
                    i, s, h, j, ki, nk = steps[idx]
                    if j == 0 and ki == 0 and i + 1 < len(heads):
                        loadE(i + 1)
                    vt = vts[i % 2]
                    e0, e1 = live.pop(idx)
                    first, lastk = (ki == 0), (ki == nk - 1)
                    S.op("tensor", lambda e: e.matmul(pO0[:], lhsT=vt[:, ki, :], rhs=e0[:], start=first, stop=lastk), reads=[vt, e0], writes=[pO0])
                    S.op("tensor", lambda e: e.matmul(pZ0[:], lhsT=onesB[:], rhs=e0[:], start=first, stop=lastk), reads=[onesB, e0], writes=[pZ0])
                    S.op("tensor", lambda e: e.matmul(pO1[:], lhsT=vt[:, ki, :], rhs=e1[:], start=first, stop=lastk), reads=[vt, e1], writes=[pO1])
                    S.op("tensor", lambda e: e.matmul(pZ1[:], lhsT=onesB[:], rhs=e1[:], start=first, stop=lastk), reads=[onesB, e1], writes=[pZ1])
                    if not lastk:
                        return
                    S.op("scalar", lambda e: e.copy(zc0[:], pZ0[:]), reads=[pZ0], writes=[zc0])
                    S.op("vector", lambda e: e.tensor_copy(zc1[:], pZ1[:]), reads=[pZ1], writes=[zc1])
                    S.op("scalar", lambda e: e.copy(oc0[:], pO0[:]), reads=[pO0], writes=[oc0])
                    S.op("vector", lambda e: e.tensor_copy(oc1[:], pO1[:]), reads=[pO1], writes=[oc1])
                    pend_epi.append((s, h, j))
                    return

                def epilogue_tail():
                    s, h, j = pend_epi.pop(0)
                    S.op("vector", lambda e: e.reciprocal(rz0[:], zc0[:]), reads=[zc0], writes=[rz0])
                    S.op("vector", lambda e: e.reciprocal(rz1[:], zc1[:]), reads=[zc1], writes=[rz1])
                    S.op("gpsimd", lambda e: e.tensor_tensor(rz0[:], oc0[:], rz0[:], op=ALU.mult), reads=[oc0, rz0], writes=[rz0])
                    S.op("gpsimd", lambda e: e.tensor_tensor(rz1[:], oc1[:], rz1[:], op=ALU.mult), reads=[oc1, rz1], writes=[rz1])
                    S.op("vector", lambda e: e.scalar_tensor_tensor(oo[:], in0=rz1[:], scalar=nlam[:, 0:1], in1=rz0[:], op0=ALU.mult, op1=ALU.add),
                         reads=[rz0, rz1, nlam], writes=[oo])
                    S.op("gpsimd", lambda e: e.tensor_tensor(osq[:], oo[:], oo[:], op=ALU.mult), reads=[oo], writes=[osq])
                    pend_epi2.append([s, h, j, 8])

                def epilogue_tail2():
                    s, h, j, _ = pend_epi2.pop(0)
                    ek = cnt["ek"]; cnt["ek"] += 1
                    pss = pS0[ek % 2]
                    S.op("tensor", lambda e: e.matmul(pss[:], lhsT=ONESF(), rhs=osq[:], start=True, stop=True), reads=[cst, osq], writes=[pss])
                    S.op("scalar", lambda e: e.activation(rin[:], pss[:], AF.Ln, bias=epsc(), scale=1.0 / 128), reads=[pss, cst], writes=[rin])
                    S.op("scalar", lambda e: e.activation(rin[:], rin[:], AF.Exp, scale=-0.5), reads=[rin], writes=[rin])
                    ot = oTs[cnt["ok"] % 2]; cnt["ok"] += 1
                    S.op("vector", lambda e: e.scalar_tensor_tensor(ot[:], in0=oo[:], scalar=subg[:, 0:1], in1=rin[:], op0=ALU.mult, op1=ALU.mult),
                         reads=[oo, subg, rin], writes=[ot])
                    S.dma("sync", lambda e: e.dma_start(out=OT[s, 4 + h, :, j * 512:(j + 1) * 512], in_=ot[:]), ot, reads=[ot])

                pend_epi = []
                pend_epi2 = []
                LOOK = 3
                for idx in range(min(LOOK, len(steps))):
                    stageA(idx)
                for idx in range(len(steps)):
                    had = len(pend_epi)
                    stageB(idx)
                    if idx + LOOK < len(steps):
                        stageA(idx + LOOK)
                    for it in pend_epi2:
                        it[3] -= 1
                    if had:
                        while pend_epi2:
                            epilogue_tail2()
                        epilogue_tail()
                    while pend_epi2 and pend_epi2[0][3] <= 0:
                        epilogue_tail2()
                while pend_epi:
                    while pend_epi2:
                        epilogue_tail2()
                    epilogue_tail()
                while pend_epi2:
                    epilogue_tail2()
            if upto == "E":
                break
            with Phase(S, "F%d" % l) as P:
                Wo = load_w_bf16(P, "Wo", w_out[l, :, :], 8, DM)
                gtB = []
                for b in range(2):
                    t = P.sb("gtB%d" % b, [128, DM], F32)
                    S.dma("sync", lambda e, b=b, t=t, l=l: e.dma_start(out=t[:], in_=MOD[l, b, 2 * DM:3 * DM].partition_broadcast(128)), t, writes=[t])
                    gtB.append(t)
                xts = [P.sb("xt%d" % i, [128, 4, DM], F32) for i in range(2)]
                ots = [P.sb("ot%d" % i, [128, 8, 512], BF16) for i in range(2)]
                tmp = [P.sb("tmp%d" % i, [128, 512], F32) for i in range(2)]
                pY = [P.ps("pY%d" % i, [128, 512], F32) for i in range(4)]
                blocks = [(s, blk) for s in range(2) for blk in range(8)]

                def loadF(i):
                    s, blk = blocks[i]
                    S.dma("sync", lambda e: e.dma_start(out=xts[i % 2][:], in_=xsrc[s, blk * 512:(blk + 1) * 512, :].rearrange("(a p) d -> p a d", p=128)),
                          xts[i % 2], writes=[xts[i % 2]])
                    S.dma("sync", lambda e: e.dma_start(out=ots[i % 2][:], in_=OT[s, :, :, blk * 512:(blk + 1) * 512].rearrange("c p t -> p c t")),
                          ots[i % 2], writes=[ots[i % 2]])

                loadF(0)
                yk = 0
                for i, (s, blk) in enumerate(blocks):
                    if i + 1 < len(blocks):
                        loadF(i + 1)
                    xt, ot = xts[i % 2], ots[i % 2]
                    for sub in range(4):
                        for dh in range(2):
                            py = pY[yk % 4]; tp = tmp[yk % 2]; yk += 1
                            for c in range(8):
                                S.op("tensor", lambda e, c=c, sub=sub, dh=dh, py=py, ot=ot: e.matmul(py[:], lhsT=ot[:, c, sub * 128:(sub + 1) * 128],
                                                                                                  rhs=Wo[:, c, dh * 512:(dh + 1) * 512], start=(c == 0), stop=(c == 7)),
                                     reads=[ot, Wo], writes=[py])
                            S.op("vector", lambda e, py=py, tp=tp, dh=dh, s=s: e.tensor_tensor(tp[:], py[:], gtB[s][:, dh * 512:(dh + 1) * 512], op=ALU.mult),
                                 reads=[py, gtB[s]], writes=[tp])
                            S.op("gpsimd", lambda e, tp=tp, sub=sub, dh=dh, xt=xt: e.tensor_tensor(xt[:, sub, dh * 512:(dh + 1) * 512], xt[:, sub, dh * 512:(dh + 1) * 512],
                                                                                                   tp[:], op=ALU.add), reads=[tp, xt], writes=[xt])
                    S.dma("sync", lambda e, xt=xt, s=s, blk=blk: e.dma_start(out=XA[s, blk * 512:(blk + 1) * 512, :].rearrange("(a p) d -> p a d", p=128), in_=xt[:]),
                          xt, reads=[xt])
            if upto == "F":
                break
            with Phase(S, "G%d" % l) as P:
                pp = P.sb("pp", [128, NPP], F32)
                S.dma("sync", lambda e, l=l: e.dma_start(out=pp[:], in_=pp_in[l, :, :]), pp, writes=[pp])
                Wup = load_w_bf16(P, "Wup", ffn_up[l, :, :], 8, 2 * DFF)
                AB = {}
                for b in range(2):
                    AB[b] = make_AB(P, l, b, pp, PP_NFFN, 3, 4, "ffn%d" % b)
                NB = 512
                xts = [P.sb("xt%d" % i, [128, 4, DM], F32) for i in range(2)]
                ss = P.sb("ss", [128, 4], F32); rs = P.sb("rs", [128, 4], F32)
                sq = P.sb("sq", [128, DM], BF16); xn = P.sb("xn", [128, 4, DM], BF16)
                tmpf = P.sb("tmpf", [128, 8, 128], F32)
                hTs = [P.sb("hT%d" % i, [128, 8, NB], BF16) for i in range(2)]
                gts = [P.sb("gt%d" % i, [128, NB], BF16) for i in range(4)]
                pT = P.ps("pT", [128, 8, 128], BF16)
                pUf = [P.ps("pU%d" % i, [128, 512], F32) for i in range(6)]
                upad = [P.sb("upad%d" % i, [128, NB + 2], F32) for i in range(4)]
                acc = [P.sb("acc%d" % i, [128, NB], F32) for i in range(4)]
                sg = [P.sb("sg%d" % i, [128, NB], F32) for i in range(2)]
                halo = P.sb("halo", [128, 44, 2], F32)
                blocks = [(s, blk) for s in range(2) for blk in range(T // NB)]

                def loadG(i):
                    s, blk = blocks[i]
                    S.dma("sync", lambda e: e.dma_start(out=xts[i % 2][:], in_=XA[s, blk * NB:(blk + 1) * NB, :].rearrange("(a p) d -> p a d", p=128)),
                          xts[i % 2], writes=[xts[i % 2]])

                def normG(i):
                    s_, _ = blocks[i]
                    A_, B_ = AB[s_]
                    norm_to_hT(P, xts[i % 2], 4, A_, B_, hTs[i % 2], (ss, rs, sq, xn, tmpf), pT, None)

                loadG(0)
                loadG(1)
                normG(0)
                uk = 0; pk = 0; gk = 0
                for i, (s, blk) in enumerate(blocks):
                    if i + 1 < len(blocks):
                        normG(i + 1)
                    if i + 2 < len(blocks):
                        loadG(i + 2)
                    hT = hTs[i % 2]
                    if blk == 0:
                        S.op("gpsimd", lambda e: e.memset(halo[:], 0.0), writes=[halo])
                    for fc in range(22):
                        res = []
                        for half in range(2):
                            f = fc + 22 * half
                            pu = pUf[pk % 6]; pk += 1
                            up = upad[uk % 4]; ac = acc[uk % 4]; uk += 1
                            for kc in range(8):
                                S.op("tensor", lambda e, kc=kc, f=f, pu=pu, hT=hT: e.matmul(pu[:], lhsT=Wup[:, kc, f * 128:(f + 1) * 128], rhs=hT[:, kc, :],
                                                                                    start=(kc == 0), stop=(kc == 7)), reads=[Wup, hT], writes=[pu])
                            S.op("gpsimd", lambda e, f=f, up=up: e.tensor_copy(up[:, 0:2], halo[:, f, :]), reads=[halo], writes=[up])
                            S.op("scalar", lambda e, up=up, pu=pu: e.copy(up[:, 2:NB + 2], pu[:]), reads=[pu], writes=[up])
                            S.op("gpsimd", lambda e, f=f, up=up: e.tensor_copy(halo[:, f, :], up[:, NB:NB + 2]), reads=[up], writes=[halo])
                            cw = lambda j, f=f: pp[:, PP_FCW + f * 3 + j:PP_FCW + f * 3 + j + 1]
                            S.op("vector", lambda e, up=up, ac=ac, cw=cw, f=f: e.tensor_scalar(ac[:], up[:, 2:NB + 2], cw(2), pp[:, PP_FCB + f:PP_FCB + f + 1],
                                                                                               op0=ALU.mult, op1=ALU.add), reads=[up, pp], writes=[ac])
                            for j in (1, 0):
                                S.op("vector", lambda e, up=up, ac=ac, cw=cw, j=j: e.scalar_tensor_tensor(ac[:], in0=up[:, j:j + NB], scalar=cw(j), in1=ac[:],
                                                                                                          op0=ALU.mult, op1=ALU.add), reads=[up, pp, ac], writes=[ac])
                            res.append(ac)
                        sgt = sg[fc % 2]
                        gt = gts[gk % 4]; gk += 1
                        S.op("scalar", lambda e, sgt=sgt, g_=res[1]: e.activation(sgt[:], g_[:], AF.Silu), reads=[res[1]], writes=[sgt])
                        S.op("gpsimd", lambda e, sgt=sgt, a_=res[0], gt=gt: e.tensor_tensor(gt[:], a_[:], sgt[:], op=ALU.mult), reads=[res[0], sgt], writes=[gt])
                        S.dma("sync", lambda e, gt=gt, s=s, blk=blk, fc=fc: e.dma_start(out=GTD[s, fc, :, blk * NB:(blk + 1) * NB], in_=gt[:]), gt, reads=[gt])
            with Phase(S, "H%d" % l) as P:
                Wdn = load_w_bf16(P, "Wdn", ffn_down[l, :, :], 22, DM)
                gtB = []
                for b in range(2):
                    t = P.sb("gtB%d" % b, [128, DM], F32)
                    S.dma("sync", lambda e, b=b, t=t, l=l: e.dma_start(out=t[:], in_=MOD[l, b, 5 * DM:6 * DM].partition_broadcast(128)), t, writes=[t])
                    gtB.append(t)
                NB = 512
                xts = [P.sb("xt%d" % i, [128, 4, DM], F32) for i in range(2)]
                gin = [P.sb("gin%d" % i, [128, 22, NB], BF16) for i in range(2)]
                tmp = [P.sb("tmp%d" % i, [128, 512], F32) for i in range(2)]
                pY = [P.ps("pY%d" % i, [128, 512], F32) for i in range(4)]
                blocks = [(s, blk) for s in range(2) for blk in range(T // NB)]

                def loadH(i):
                    s, blk = blocks[i]
                    S.dma("sync", lambda e: e.dma_start(out=xts[i % 2][:], in_=XA[s, blk * NB:(blk + 1) * NB, :].rearrange("(a p) d -> p a d", p=128)),
                          xts[i % 2], writes=[xts[i % 2]])
                    for f0 in (0, 11):
                        S.dma("sync", lambda e, f0=f0: e.dma_start(out=gin[i % 2][:, f0:f0 + 11, :],
                                                                   in_=GTD[s, f0:f0 + 11, :, blk * NB:(blk + 1) * NB].rearrange("f p t -> p f t")),
                              gin[i % 2], writes=[gin[i % 2]])

                loadH(0)
                yk = 0
                for i, (s, blk) in enumerate(blocks):
                    if i + 1 < len(blocks):
                        loadH(i + 1)
                    xt, gi = xts[i % 2], gin[i % 2]
                    for sub in range(4):
                        for dh in range(2):
                            py = pY[yk % 4]; tp = tmp[yk % 2]; yk += 1
                            for fc in range(22):
                                S.op("tensor", lambda e, fc=fc, sub=sub, dh=dh, py=py, gi=gi: e.matmul(py[:], lhsT=gi[:, fc, sub * 128:(sub + 1) * 128],
                                                                                                           rhs=Wdn[:, fc, dh * 512:(dh + 1) * 512], start=(fc == 0), stop=(fc == 21)),
                                     reads=[gi, Wdn], writes=[py])
                            S.op("vector", lambda e, py=py, tp=tp, dh=dh, s=s: e.tensor_tensor(tp[:], py[:], gtB[s][:, dh * 512:(dh + 1) * 512], op=ALU.mult),
                                 reads=[py, gtB[s]], writes=[tp])
                            S.op("gpsimd", lambda e, tp=tp, sub=sub, dh=dh, xt=xt: e.tensor_tensor(xt[:, sub, dh * 512:(dh + 1) * 512], xt[:, sub, dh * 512:(dh + 1) * 512],
                                                                                                   tp[:], op=ALU.add), reads=[tp, xt], writes=[xt])
                    S.dma("sync", lambda e, xt=xt, s=s, blk=blk: e.dma_start(out=xdst[s, blk * NB:(blk + 1) * NB, :].rearrange("(a p) d -> p a d", p=128), in_=xt[:]),
                          xt, reads=[xt])
            xsrc = xdst
        G.__exit__(None, None, None)
    return nc


def _prep(inputs):
    inp = {k: np.asarray(v) for k, v in inputs.items()}
    consts = _consts()
    pp = _pack_pp(inp)
    lamv = np.stack([inp["diff_lambda_q1"], inp["diff_lambda_k1"], inp["diff_lambda_q2"], inp["diff_lambda_k2"]], axis=1)
    lamv = np.ascontiguousarray(lamv.astype(np.float32))
    kk = np.arange(128)[:, None]
    cc = np.arange(1024)[None, :]
    dist = cc - kk - 384
    bidx = _t5_bucket(np.maximum(dist, 0))
    rb = inp["rel_bias"].astype(np.float32)
    tb = np.empty((4, 128, 1024), np.float32)
    for h in range(4):
        tb[h] = np.where(dist >= 0, rb[bidx, h], np.float32(NEG))
    b31 = np.ascontiguousarray(np.broadcast_to(rb[31][None, :], (128, 4))).astype(np.float32)
    shared = dict(consts=consts, pp=pp, lamv=lamv, tb=tb, b31=b31,
                  w_ada=inp["w_ada"], b_ada=inp["b_ada"], w_in=inp["w_in"], w_out=inp["w_out"],
                  ffn_up=inp["ffn_up"], ffn_down=inp["ffn_down"])
    in_maps = []
    for c in range(NCORES):
        m = dict(shared)
        m["x"] = np.ascontiguousarray(inp["x"][2 * c:2 * c + 2])
        cc_ = inp["c"][2 * c:2 * c + 2]
        m["cT"] = np.ascontiguousarray(cc_.reshape(2, 8, 128).transpose(2, 1, 0))
        in_maps.append(m)
    return in_maps


def kernel(**inputs):
    in_maps = _prep(inputs)
    nc = build()
    res = run_bass_kernel_spmd(nc, in_maps, core_ids=list(range(NCORES)))
    return np.concatenate([r["out"] for r in res.results], axis=0).astype(np.float32)
```
